# Optimizing a Trainium2 kernel written in Bass

```python
import jax, jax.numpy as jnp
from jax import lax
import numpy as np

D_MODEL = 1024
BATCH = 32
SEQ = 2048
DEPTH = 2

N_HEADS = 16
HEAD_DIM = D_MODEL // N_HEADS
D_FF = ((8 * D_MODEL // 3 + 127) // 128) * 128
CONV_WIDTH = 3
CHUNK = 64
Q_BLOCK = 128
N_A_LAYERS = DEPTH // 2
N_B_LAYERS = DEPTH - N_A_LAYERS
RMS_EPS = 1e-6
FORGET_BIAS = 3.0

kernel_name = "yoco_stickbreak_fox_convffn"


def rmsnorm(x, g):
    xf = x.astype(jnp.float32)
    y = xf * lax.rsqrt(jnp.mean(xf * xf, axis=-1, keepdims=True) + RMS_EPS)
    return (y * g.astype(jnp.float32)).astype(x.dtype)


def split_heads(t):
    b, s, _ = t.shape
    return t.reshape(b, s, N_HEADS, HEAD_DIM).transpose(0, 2, 1, 3)


def merge_heads(t):
    b, h, s, d = t.shape
    return t.transpose(0, 2, 1, 3).reshape(b, s, h * d)


def stick_breaking_attention(q, k, v):
    b, h, s, d = q.shape
    scale = HEAD_DIM ** -0.5
    n_blocks = s // Q_BLOCK
    strict = jnp.arange(Q_BLOCK)[:, None] > jnp.arange(Q_BLOCK)[None, :]
    outs = []
    for qi in range(n_blocks):
        lo, hi = qi * Q_BLOCK, (qi + 1) * Q_BLOCK
        qb = q[:, :, lo:hi].astype(jnp.float32) * scale
        kd = k[:, :, lo:hi].astype(jnp.float32)
        vd = v[:, :, lo:hi].astype(jnp.float32)
        z = jnp.einsum('bhtd,bhsd->bhts', qb, kd)
        log_keep = jnp.where(strict, jax.nn.log_sigmoid(-z), 0.0)
        later = lax.cumsum(log_keep, axis=3, reverse=True) - log_keep
        attn = jnp.where(strict, jnp.exp(jax.nn.log_sigmoid(z) + later), 0.0)
        out = jnp.einsum('bhts,bhsd->bhtd', attn, vd)
        log_surv = jnp.sum(log_keep, axis=-1)
        if qi > 0:
            kp = k[:, :, :lo].astype(jnp.float32).reshape(b, h, qi, Q_BLOCK, d)
            vp = v[:, :, :lo].astype(jnp.float32).reshape(b, h, qi, Q_BLOCK, d)
            kp = jnp.flip(jnp.moveaxis(kp, 2, 0), axis=0)
            vp = jnp.flip(jnp.moveaxis(vp, 2, 0), axis=0)

            def step(carry, kv, qb=qb):
                acc, surv = carry
                kb, vb = kv
                zb = jnp.einsum('bhtd,bhsd->bhts', qb, kb)
                lk = jax.nn.log_sigmoid(-zb)
                lat = lax.cumsum(lk, axis=3, reverse=True) - lk + surv[..., None]
                wb = jnp.exp(jax.nn.log_sigmoid(zb) + lat)
                acc = acc + jnp.einsum('bhts,bhsd->bhtd', wb, vb)
                return (acc, surv + jnp.sum(lk, axis=-1)), None

            (out, _), _ = lax.scan(step, (out, log_surv), (kp, vp))
        outs.append(out)
    return jnp.concatenate(outs, axis=2).astype(v.dtype)


def forgetting_attention(q, k, v, fcum):
    s = q.shape[2]
    scale = HEAD_DIM ** -0.5
    outs = []
    for qi in range(s // Q_BLOCK):
        lo, hi = qi * Q_BLOCK, (qi + 1) * Q_BLOCK
        qb = q[:, :, lo:hi].astype(jnp.float32) * scale
        kb = k[:, :, :hi].astype(jnp.float32)
        vb = v[:, :, :hi].astype(jnp.float32)
        logits = (jnp.einsum('bhtd,bhsd->bhts', qb, kb)
                  + fcum[:, :, lo:hi, None] - fcum[:, :, None, :hi])
        causal = (lo + jnp.arange(Q_BLOCK))[:, None] >= jnp.arange(hi)[None, :]
        p = jax.nn.softmax(jnp.where(causal, logits, -jnp.inf), axis=-1)
        outs.append(jnp.einsum('bhts,bhsd->bhtd', p, vb))
    return jnp.concatenate(outs, axis=2).astype(v.dtype)


def conv_ffn(x, w_up, conv_w, conv_b, w_down):
    hdn = x @ w_up
    hdn = lax.conv_general_dilated(
        hdn, conv_w[:, None, :], window_strides=(1,),
        padding=[(CONV_WIDTH - 1, 0)],
        dimension_numbers=('NWC', 'WIO', 'NWC'),
        feature_group_count=2 * D_FF) + conv_b
    gate, up = jnp.split(hdn, 2, axis=-1)
    return (jax.nn.gelu(gate, approximate=True) * up) @ w_down


def setup_inputs(seed: int = 0) -> dict:
    key = jax.random.key(seed)
    ks = jax.random.split(key, 20)
    D, F, H = D_MODEL, D_FF, N_HEADS

    def w(k, shape, fan_in):
        return jax.random.normal(k, shape, jnp.float32) * fan_in ** -0.5

    def gain(k, shape):
        return 1.0 + 0.1 * jax.random.normal(k, shape, jnp.float32)

    return {
        "x": jax.random.normal(ks[0], (BATCH, SEQ, D), jnp.float32),
        "sb_pre_g": gain(ks[1], (N_A_LAYERS, D)),
        "sb_w_qkv": w(ks[2], (N_A_LAYERS, D, 3 * D), D),
        "sb_w_o": w(ks[3], (N_A_LAYERS, D, D), D),
        "sb_post_g": gain(ks[4], (N_A_LAYERS, D)),
        "kv_norm_g": gain(ks[5], (D,)),
        "w_kvf": w(ks[6], (D, 2 * D + H), D),
        "b_f": FORGET_BIAS + 0.5 * jax.random.normal(ks[7], (H,), jnp.float32),
        "fox_pre_g": gain(ks[8], (N_B_LAYERS, D)),
        "fox_w_q": w(ks[9], (N_B_LAYERS, D, D), D),
        "fox_w_o": w(ks[10], (N_B_LAYERS, D, D), D),
        "fox_post_g": gain(ks[11], (N_B_LAYERS, D)),
        "ffn_pre_g": gain(ks[12], (DEPTH, D)),
        "w_up": w(ks[13], (DEPTH, D, 2 * F), D),
        "conv_w": w(ks[14], (DEPTH, CONV_WIDTH, 2 * F), CONV_WIDTH),
        "conv_b": 0.02 * jax.random.normal(ks[15], (DEPTH, 2 * F), jnp.float32),
        "w_down": w(ks[16], (DEPTH, F, D), F),
        "ffn_post_g": gain(ks[17], (DEPTH, D)),
    }


def reference(x, sb_pre_g, sb_w_qkv, sb_w_o, sb_post_g, kv_norm_g, w_kvf, b_f,
              fox_pre_g, fox_w_q, fox_w_o, fox_post_g,
              ffn_pre_g, w_up, conv_w, conv_b, w_down, ffn_post_g):
    D = D_MODEL
    h = x
    k_sh = v_sh = fcum_sh = None
    for layer in range(DEPTH):
        if layer < N_A_LAYERS:
            a = rmsnorm(h, sb_pre_g[layer])
            q, k, v = jnp.split(a @ sb_w_qkv[layer], 3, axis=-1)
            o = stick_breaking_attention(split_heads(q), split_heads(k), split_heads(v))
            h = h + rmsnorm(merge_heads(o) @ sb_w_o[layer], sb_post_g[layer])
        else:
            if layer == N_A_LAYERS:
                s_in = rmsnorm(h, kv_norm_g)
                kvf = s_in @ w_kvf
                k_sh = split_heads(kvf[..., :D])
                v_sh = split_heads(kvf[..., D:2 * D])
                log_f = jax.nn.log_sigmoid(kvf[..., 2 * D:].astype(jnp.float32)
                                           + b_f.astype(jnp.float32))
                fcum_sh = jnp.cumsum(jnp.transpose(log_f, (0, 2, 1)), axis=-1)
            j = layer - N_A_LAYERS
            a = rmsnorm(h, fox_pre_g[j])
            q = split_heads(a @ fox_w_q[j])
            o = forgetting_attention(q, k_sh, v_sh, fcum_sh)
            h = h + rmsnorm(merge_heads(o) @ fox_w_o[j], fox_post_g[j])
        f = conv_ffn(rmsnorm(h, ffn_pre_g[layer]), w_up[layer], conv_w[layer],
                     conv_b[layer], w_down[layer])
        h = h + rmsnorm(f, ffn_post_g[layer])
    return h
```

```python
from contextlib import ExitStack
import numpy as np
import concourse.bass as bass
import concourse.mybir as mybir
from concourse.bass_utils import run_bass_kernel_spmd

F32 = mybir.dt.float32
BF16 = mybir.dt.bfloat16
AF = mybir.ActivationFunctionType
ALU = mybir.AluOpType

D = 1024
H = 16
DH = 64
FF = 2816
NCH = FF // 128
EPS = 1e-6
N_CORES = 8


class Res:
    __slots__ = ("name", "lw", "rd", "sem", "semcnt")

    def __init__(self, name, sem=None):
        self.name = name
        self.lw = None
        self.rd = {}
        self.sem = sem
        self.semcnt = 0


class Prog:
    ENGS = ("pe", "act", "dve", "pool", "sp")

    def __init__(self, nc, ctx):
        self.nc = nc
        self.ctx = ctx
        self.streams = {e: [] for e in self.ENGS}
        self.count = {e: 0 for e in self.ENGS}
        self.waited = {e: {} for e in self.ENGS}
        self.esem = {e: ctx.enter_context(nc.semaphore("es_" + e)) for e in self.ENGS}
        self.nres = 0

    def res(self, name=None, dma=False):
        self.nres += 1
        name = name or ("r%d" % self.nres)
        sem = self.ctx.enter_context(self.nc.semaphore("ds%d" % self.nres)) if dma else None
        return Res(name, sem)

    def _deps(self, reads, writes):
        deps = []
        for r in reads:
            if r.lw is not None:
                deps.append(r.lw)
        for w in writes:
            if w.lw is not None:
                deps.append(w.lw)
            deps.extend(w.rd.items())
        return deps

    def _waits_for(self, eng, deps):
        best = {}
        for (key, val) in deps:
            if key == "pe" and eng == "pe":
                continue
            if val > best.get(key, 0):
                best[key] = val
        out = []
        wd = self.waited[eng]
        for key, val in best.items():
            if wd.get(key, 0) >= val:
                continue
            wd[key] = val
            out.append((key, val))
        return out

    def _sem_of(self, key):
        if isinstance(key, str):
            return self.esem[key]
        return key.sem

    def _record(self, ev, reads, writes):
        k, v = ev
        for r in reads:
            if r.rd.get(k, 0) < v:
                r.rd[k] = v
        for w in writes:
            w.lw = ev
            w.rd = {}

    def op(self, eng, fn, reads=(), writes=()):
        waits = self._waits_for(eng, self._deps(reads, writes))
        self.count[eng] += 1
        ev = (eng, self.count[eng])
        self.streams[eng].append((waits, fn, None))
        self._record(ev, reads, writes)
        return ev

    def dma(self, eng, fn, semres, reads=(), writes=()):
        waits = self._waits_for(eng, self._deps(reads, writes))
        semres.semcnt += 1
        ev = (semres, 16 * semres.semcnt)
        self.streams[eng].append((waits, fn, semres))
        self._record(ev, reads, writes)
        return ev

    def alias(self, olds, news):
        evs = {}
        for o in olds:
            if o.lw is not None:
                k, v = o.lw
                evs[k] = max(evs.get(k, 0), v)
            for k, v in o.rd.items():
                evs[k] = max(evs.get(k, 0), v)
        for n in news:
            for k, v in evs.items():
                if n.rd.get(k, 0) < v:
                    n.rd[k] = v

    def wait_all(self, eng, resources):
        deps = []
        for r in resources:
            if r.lw is not None:
                deps.append(r.lw)
            deps.extend(r.rd.items())
        waits = self._waits_for(eng, deps)
        self.streams[eng].append((waits, None, None))

    def emit(self):
        nc = self.nc
        with nc.Block() as block:
            def run(engname):
                def body(e):
                    esem = self.esem[engname]
                    for (waits, fn, semres) in self.streams[engname]:
                        for (key, val) in waits:
                            e.wait_ge(self._sem_of(key), val)
                        if fn is None:
                            continue
                        ins = fn(e)
                        if semres is None:
                            ins.then_inc(esem, 1)
                        else:
                            ins.then_inc(semres.sem, 16)
                return body
            block.tensor(run("pe"))
            block.scalar(run("act"))
            block.vector(run("dve"))
            block.gpsimd(run("pool"))
            block.sync(run("sp"))


def build_nc(NB, S):
    NT = S // 128
    NG = S // 512
    nc = bass.Bass("TRN2", target_bir_lowering=False)
    dt_in = lambda name, shape: nc.dram_tensor(name, list(shape), F32, kind="ExternalInput").ap()
    x = dt_in("x", [NB, S, D])
    sb_pre_g = dt_in("sb_pre_g", [1, D])
    sb_w_qkv = dt_in("sb_w_qkv", [1, D, 3 * D])
    sb_w_o = dt_in("sb_w_o", [1, D, D])
    sb_post_g = dt_in("sb_post_g", [1, D])
    kv_norm_g = dt_in("kv_norm_g", [D])
    w_kvf = dt_in("w_kvf", [D, 2 * D + H])
    b_f = dt_in("b_f", [H])
    fox_pre_g = dt_in("fox_pre_g", [1, D])
    fox_w_q = dt_in("fox_w_q", [1, D, D])
    fox_w_o = dt_in("fox_w_o", [1, D, D])
    fox_post_g = dt_in("fox_post_g", [1, D])
    ffn_pre_g = dt_in("ffn_pre_g", [2, D])
    w_up = dt_in("w_up", [2, D, 2 * FF])
    conv_w = dt_in("conv_w", [2, 3, 2 * FF])
    conv_b = dt_in("conv_b", [2, 2 * FF])
    w_down = dt_in("w_down", [2, FF, D])
    ffn_post_g = dt_in("ffn_post_g", [2, D])
    y = nc.dram_tensor("y", [NB, S, D], F32, kind="ExternalOutput").ap()

    ws_qkv = nc.dram_tensor("ws_qkv", [5, 8, 128, 1024], BF16).ap()
    ws_kv = nc.dram_tensor("ws_kv", [2, 8, 128, 1024], BF16).ap()
    ws_f = nc.dram_tensor("ws_f", [128, 8 * 16], BF16).ap()
    ws_o = nc.dram_tensor("ws_o", [2, 128, 8192], BF16).ap()
    ws_up = nc.dram_tensor("ws_up", [2, NCH, 128, 2048], BF16).ap()
    ws_dn = nc.dram_tensor("ws_dn", [2, NCH, 128, 1024], BF16).ap()

    with ExitStack() as ctx:
        P = Prog(nc, ctx)
        sbt = lambda name, shape, dt: ctx.enter_context(nc.sbuf_tensor(name, list(shape), dt))
        pst = lambda name, shape, dt: ctx.enter_context(nc.psum_tensor(name, list(shape), dt))

        Hbuf = sbt("Hbuf", [128, NT * D], F32)
        hview = Hbuf[:].rearrange("p (t d) -> p t d", d=D)
        r_h = [P.res("h%d" % t, dma=True) for t in range(NT)]
        R1N = max(8 * S, 16384)
        R2N = max(8 * S, 16384)
        R3N = max(4 * S + NT * 256, 8192)
        R1 = sbt("R1", [128, R1N], BF16)
        R2 = sbt("R2", [128, R2N], BF16)
        R3 = sbt("R3", [128, R3N], BF16)
        r_R1 = P.res("R1")
        r_R2 = P.res("R2")
        aT = R1[:, 0:8 * S].rearrange("p (k s) -> p k s", s=S)
        bT = R2[:, 0:8 * S].rearrange("p (k s) -> p k s", s=S)
        qa = [R3[:, 0:S], R3[:, S:2 * S]]
        ka = [R3[:, 2 * S:3 * S], R3[:, 3 * S:4 * S]]
        vpad = R3[:, 4 * S:4 * S + NT * 256].rearrange("p (t h c) -> p t h c", h=2, c=128)
        r_qa = [P.res("qa0", dma=True), P.res("qa1", dma=True)]
        r_ka = [P.res("ka0", dma=True), P.res("ka1", dma=True)]
        r_vpad = P.res("vpad")
        WoT = R3[:, 0:8192].rearrange("p (k n) -> p k n", n=D)
        r_Wo = P.res("Wo", dma=True)
        gT = R2[:, 0:NCH * 512].rearrange("p (c t) -> p c t", t=512)
        r_gT = [P.res("gT%d" % c) for c in range(NCH)]
        wupb = [R2[:, NCH * 512 + i * 2048: NCH * 512 + (i + 1) * 2048].rearrange("p (g k j) -> p g k j", g=2, k=8) for i in range(2)]
        r_wup = [P.res("wup%d" % i, dma=True) for i in range(2)]
        wdnb = [R3[:, i * 1024:(i + 1) * 1024] for i in range(3)]
        r_wdn = [P.res("wdn%d" % i, dma=True) for i in range(3)]

        Wq = sbt("Wq", [128, 8, 128], BF16); r_Wq = P.res("Wq", dma=True)
        Wk = sbt("Wk", [128, 8, 128], BF16); r_Wk = P.res("Wk", dma=True)
        Wv = sbt("Wv", [128, 8, 128], BF16); r_Wv = P.res("Wv", dma=True)
        Wf = sbt("Wf", [128, 8, 16], BF16); r_Wf = P.res("Wf", dma=True)
        e32 = [sbt("e32_%d" % i, [128, 516], F32) for i in range(2)]; r_e32 = [P.res() for _ in range(2)]
        spb = [sbt("spb_%d" % i, [128, 512], BF16) for i in range(2)]; r_sp = [P.res() for _ in range(2)]
        E2 = [sbt("E2_%d" % i, [128, 512], F32) for i in range(2)]; r_E2 = [P.res() for _ in range(2)]
        Ab = [sbt("Ab_%d" % i, [128, 512], BF16) for i in range(3)]; r_A = [P.res() for _ in range(3)]
        Ls32 = [sbt("Ls32_%d" % i, [128, 512], F32) for i in range(2)]; r_Ls32 = [P.res() for _ in range(2)]
        Lsbf = [sbt("Lsbf_%d" % i, [128, 512], BF16) for i in range(2)]; r_Lsbf = [P.res() for _ in range(2)]
        rinv = sbt("rinv", [128, 512], F32); r_rinv = P.res()
        junk = sbt("junk", [128, 1024], BF16); r_junk = P.res()
        xn = sbt("xn", [128, 1024], BF16); r_xn = P.res()
        tmpf = sbt("tmpf", [128, 1024], F32); r_tmpf = P.res()
        Gpre = sbt("Gpre", [128, 1024], F32); r_Gpre = P.res("Gpre", dma=True)
        Gpost = Gpre; r_Gpost = r_Gpre
        st = sbt("stats", [128, 8], F32); r_st = P.res()
        frowA = sbt("frowA", [80, S], BF16)
        frowB = sbt("frowB", [16, S], BF16)
        frow = [frowA[0:16, :], frowA[32:48, :], frowA[64:80, :], frowB[0:16, :]]
        r_frow = P.res()
        carry = sbt("carry", [16, 1], F32); r_carry = P.res()
        bfneg = sbt("bfneg", [16, 1], F32); r_bf = P.res("bf", dma=True)
        CW = [sbt("CW%d" % l, [128, 4, 2 * NCH], F32) for l in range(2)]; r_CW = [P.res("CW%d" % l, dma=True) for l in range(2)]
        hraw = e32; r_hraw = r_e32
        cacc = E2; r_cacc = r_E2
        ctmp = Ls32; r_ctmp = r_Ls32
        gl = rinv; r_gl = r_rinv
        halo = sbt("halo", [128, 2 * NCH, 2], F32); r_halo = P.res()
        ident = sbt("ident", [128, 128], BF16); r_ident = P.res()
        mstrict = sbt("mstrict", [128, 128], BF16); r_mstrict = P.res()
        mincl = sbt("mincl", [128, 128], BF16); r_mincl = P.res()
        UI = sbt("UI", [128, 128], BF16); r_UI = P.res()
        ones = sbt("ones", [128, 128], BF16); r_ones = P.res()
        PB = [pst("PB%d" % i, [128, 512], F32) for i in range(7)]; r_PB = [P.res("PB%d" % i) for i in range(7)]
        PT = pst("PT", [128, 1024], BF16); r_PT = P.res("PT")

        def mm(out, lhsT, rhs, start, stop, reads, writes):
            P.op("pe", lambda e: e.matmul(out, lhsT=lhsT, rhs=rhs, start=start, stop=stop, skip_group_check=True), reads, writes)

        def act(out, in_, func, reads, writes, **kw):
            P.op("act", lambda e: e.activation(out=out, in_=in_, func=func, **kw), reads, writes)

        def tcopy(eng, out, in_, reads, writes):
            P.op(eng, lambda e: e.tensor_copy(out=out, in_=in_), reads, writes)

        def tt(eng, out, in0, in1, op, reads, writes):
            P.op(eng, lambda e: e.tensor_tensor(out=out, in0=in0, in1=in1, op=op), reads, writes)

        def ts(eng, out, in0, s1, s2, op0, op1, reads, writes):
            if s2 is None:
                P.op(eng, lambda e: e.tensor_scalar(out=out, in0=in0, scalar1=s1, scalar2=None, op0=op0), reads, writes)
            else:
                P.op(eng, lambda e: e.tensor_scalar(out=out, in0=in0, scalar1=s1, scalar2=s2, op0=op0, op1=op1), reads, writes)

        def stt(eng, out, in0, scalar, in1, op0, op1, reads, writes):
            P.op(eng, lambda e: e.scalar_tensor_tensor(out=out, in0=in0, scalar=scalar, in1=in1, op0=op0, op1=op1), reads, writes)

        def memset(eng, ap, val, writes):
            P.op(eng, lambda e: e.memset(ap, val), (), writes)

        def dma(out, in_, semres, reads, writes, slow=False):
            if slow:
                P.dma("sp", lambda e: e.dma_start(out=out, in_=in_, allow_slow_non_contiguous=True), semres, reads, writes)
            else:
                P.dma("sp", lambda e: e.dma_start(out=out, in_=in_), semres, reads, writes)

        def aff(ap, pattern, cmp, cm, writes):
            P.op("pool", lambda e: e.affine_select(out=ap, in_=ap, pattern=pattern, compare_op=cmp, fill=0.0, base=0, channel_multiplier=cm), writes, writes)
        for (t_, r_) in ((ident, r_ident), (mstrict, r_mstrict), (mincl, r_mincl), (UI, r_UI), (ones, r_ones)):
            memset("pool", t_[:], 1.0, [r_])
        aff(ident[:], [[-1, 128]], ALU.is_equal, 1, [r_ident])
        aff(mstrict[:], [[1, 128]], ALU.is_gt, -1, [r_mstrict])
        aff(mincl[:], [[1, 128]], ALU.is_ge, -1, [r_mincl])
        aff(UI[:], [[-1, 128]], ALU.is_ge, 1, [r_UI])
        memset("pool", halo[:], 0.0, [r_halo])
        for l in range(2):
            for k in range(3):
                dma(CW[l][:, k, :], conv_w[l, k].rearrange("(c p) -> p c", p=128), r_CW[l], [], [r_CW[l]], slow=True)
            dma(CW[l][:, 3, :], conv_b[l].rearrange("(c p) -> p c", p=128), r_CW[l], [], [r_CW[l]], slow=True)
        dma(bfneg[:], b_f.rearrange("(h o) -> h o", o=1), r_bf, [], [r_bf], slow=True)
        ts("dve", bfneg[:], bfneg[:], -1.0, None, ALU.mult, None, [r_bf], [r_bf])

        stg32 = [R1[:, i * 8192:(i + 1) * 8192].bitcast(F32) for i in range(2)]
        stg16 = [R2[:, i * 4096:(i + 1) * 4096] for i in range(2)]
        r_s32 = [P.res("s32_%d" % i, dma=True) for i in range(2)]
        r_s16 = [P.res("s16_%d" % i) for i in range(2)]
        r_s16d = [P.res("s16d_%d" % i, dma=True) for i in range(2)]
        cast_engs = ["dve", "pool", "act"]
        ucount = [0]

        def cast_unit(srcs, E, dst, outview=None):
            i = ucount[0] % 2
            eng = cast_engs[ucount[0] % 3]
            ucount[0] += 1
            for (vf, src) in srcs:
                dma(vf(stg32[i]), src, r_s32[i], [], [r_s32[i]])
            if eng == "act":
                act(stg16[i][:, 0:E], stg32[i][:, 0:E], AF.Copy, [r_s32[i]], [r_s16[i]])
            else:
                tcopy(eng, stg16[i][:, 0:E], stg32[i][:, 0:E], [r_s32[i]], [r_s16[i]])
            src16 = stg16[i][:, 0:E] if outview is None else outview(stg16[i][:, 0:E])
            dma(dst, src16, r_s16d[i], [r_s16[i]], [r_s16d[i]])

        def img4(stage, a, b, c):
            return stage[:, 0:a * b * c].rearrange("p (a b c) -> p a b c", a=a, b=b)

        def img3(stage, a, b):
            return stage[:, 0:a * b].rearrange("p (a b) -> p a b", a=a)

        def cast_cols(src2d, col0, dst_units):
            for half in range(2):
                srcs = []
                for hp4 in range(4):
                    hp = half * 4 + hp4
                    srcs.append((lambda s, hp4=hp4: img4(s, 4, 8, 128)[:, hp4],
                                 src2d[:, col0 + hp * 128: col0 + (hp + 1) * 128].rearrange("(k p) j -> p k j", p=128)))
                cast_unit(srcs, 4096, dst_units[half * 4:half * 4 + 4].rearrange("u p e -> p u e"),
                          outview=lambda v: v.rearrange("p (u e) -> p u e", u=4))
        cast_cols(sb_w_qkv[0], 0, ws_qkv[0])
        cast_cols(sb_w_qkv[0], D, ws_qkv[1])
        cast_cols(sb_w_qkv[0], 2 * D, ws_qkv[2])

        def cast_wo(src2d, dst):
            for half in range(2):
                srcs = [(lambda s: img3(s, 4, 1024),
                         src2d[half * 512:(half + 1) * 512, :].rearrange("(k p) n -> p k n", p=128))]
                cast_unit(srcs, 4096, dst[:, half * 4096:(half + 1) * 4096])
        cast_wo(sb_w_o[0], ws_o[0])

        def cast_ffn(l):
            for c0 in range(0, NCH, 2):
                srcs = []
                for gu in range(2):
                    for ci in range(2):
                        c = c0 + ci
                        srcs.append((lambda s, gu=gu, ci=ci: s[:, 0:4096].rearrange("p (c g k j) -> p c g k j", c=2, g=2, k=8)[:, ci, gu],
                                     w_up[l][:, gu * FF + c * 128: gu * FF + (c + 1) * 128].rearrange("(k p) j -> p k j", p=128)))
                cast_unit(srcs, 4096, ws_up[l, c0:c0 + 2].rearrange("c p e -> p c e"),
                          outview=lambda v: v.rearrange("p (c e) -> p c e", c=2))
            for c0 in range(0, NCH, 4):
                n = min(4, NCH - c0)
                srcs = [(lambda s, n=n: img3(s, n, 1024),
                         w_down[l][c0 * 128:(c0 + n) * 128, :].rearrange("(c p) n -> p c n", p=128))]
                cast_unit(srcs, n * 1024, ws_dn[l, c0:c0 + n].rearrange("c p e -> p c e"),
                          outview=lambda v, n=n: v.rearrange("p (c e) -> p c e", c=n))
        cast_ffn(0)
        cast_cols(w_kvf, 0, ws_kv[0])
        cast_cols(w_kvf, D, ws_kv[1])
        cast_unit([(lambda s: img3(s, 8, 16), w_kvf[:, 2 * D:2 * D + H].rearrange("(k p) j -> p k j", p=128))], 128, ws_f)
        cast_cols(fox_w_q[0], 0, ws_qkv[3])
        cast_wo(fox_w_o[0], ws_o[1])
        cast_ffn(1)
        r_scr = r_s16d
        P.alias(r_s32 + r_s16 + r_s16d, [r_R1, r_R2])

        def prenorm(g_ap, dstT, r_dst, first_alias=None):
            dma(Gpre[:], g_ap.partition_broadcast(128), r_Gpre, [], [r_Gpre])
            for t in range(NT):
                act(junk[:], hview[:, t, :], AF.Square, [r_h[t]], [r_junk, r_st], accum_out=st[:, 0:1])
                act(st[:, 1:2], st[:, 0:1], AF.Ln, [r_st], [r_st], scale=1.0 / D, bias=EPS)
                act(st[:, 2:3], st[:, 1:2], AF.Exp, [r_st], [r_st], scale=-0.5)
                stt("dve", xn[:], hview[:, t, :], st[:, 2:3], Gpre[:], ALU.mult, ALU.mult, [r_h[t], r_st, r_Gpre], [r_xn])
                for kc in range(8):
                    P.op("pe", lambda e, kc=kc: e.transpose(PT[:, kc * 128:(kc + 1) * 128], xn[:, kc * 128:(kc + 1) * 128], ident[:]),
                         [r_xn, r_ident], [r_PT])
                tcopy("dve", dstT[:, :, t * 128:(t + 1) * 128], PT[:].rearrange("p (k j) -> p k j", j=128), [r_PT], [r_dst])

        def postnorm_residual(t, ba, bb, g_res):
            act(junk[:, 0:512], PB[ba][:], AF.Square, [r_PB[ba]], [r_junk, r_st], accum_out=st[:, 3:4])
            act(junk[:, 512:1024], PB[bb][:], AF.Square, [r_PB[bb]], [r_junk, r_st], accum_out=st[:, 4:5])
            tt("dve", st[:, 5:6], st[:, 3:4], st[:, 4:5], ALU.add, [r_st], [r_st])
            act(st[:, 6:7], st[:, 5:6], AF.Ln, [r_st], [r_st], scale=1.0 / D, bias=EPS)
            act(st[:, 7:8], st[:, 6:7], AF.Exp, [r_st], [r_st], scale=-0.5)
            stt("dve", tmpf[:, 0:512], PB[ba][:], st[:, 7:8], Gpost[:, 0:512], ALU.mult, ALU.mult, [r_PB[ba], r_st, g_res], [r_tmpf])
            stt("dve", tmpf[:, 512:1024], PB[bb][:], st[:, 7:8], Gpost[:, 512:1024], ALU.mult, ALU.mult, [r_PB[bb], r_st, g_res], [r_tmpf])
            tt("pool", hview[:, t, :], hview[:, t, :], tmpf[:], ALU.add, [r_h[t], r_tmpf], [r_h[t]])

        def out_proj(widx, g_ap, srcT, r_src):
            P.alias(r_qa + r_ka + [r_vpad], [r_Wo])
            dma(WoT[:, 0:4, :], ws_o[widx][:, 0:4096].rearrange("p (k n) -> p k n", n=D), r_Wo, r_scr, [r_Wo])
            dma(WoT[:, 4:8, :], ws_o[widx][:, 4096:8192].rearrange("p (k n) -> p k n", n=D), r_Wo, r_scr, [r_Wo])
            dma(Gpost[:], g_ap.partition_broadcast(128), r_Gpost, [], [r_Gpost])
            for t in range(NT):
                ba, bb = (0, 1) if t % 2 == 0 else (2, 3)
                for n, bk in ((0, ba), (1, bb)):
                    for kc in range(8):
                        mm(PB[bk][:], srcT[:, kc, t * 128:(t + 1) * 128], WoT[:, kc, n * 512:(n + 1) * 512],
                           kc == 0, kc == 7, [r_src, r_Wo], [r_PB[bk]])
                postnorm_residual(t, ba, bb, r_Gpost)
            P.alias([r_Wo], r_qa + r_ka + [r_vpad])

        def pipeline(n, stages):
            ns = len(stages)
            for tick in range(n + ns - 1):
                for k in range(ns - 1, -1, -1):
                    i = tick - k
                    if 0 <= i < n:
                        stages[k](i)

        def load_pair_weights(qsrc, ksrc, vsrc, hp):
            if qsrc is not None:
                dma(Wq[:], qsrc[hp].rearrange("p (k j) -> p k j", j=128), r_Wq, r_scr, [r_Wq])
            dma(Wk[:], ksrc[hp].rearrange("p (k j) -> p k j", j=128), r_Wk, r_scr, [r_Wk])
            dma(Wv[:], vsrc[hp].rearrange("p (k j) -> p k j", j=128), r_Wv, r_scr, [r_Wv])

        def proj_featmajor(Wt, r_W, srcT, r_src, bank, sink):
            for tg in range(NG):
                for kc in range(8):
                    mm(PB[bank][:], Wt[:, kc, :], srcT[:, kc, tg * 512:(tg + 1) * 512], kc == 0, kc == 7, [r_W, r_src], [r_PB[bank]])
                sink(tg, bank)

        def proj_v(srcT, r_src, bank, padval):
            memset("pool", vpad[:, :, 0, 64:128], padval, [r_vpad])
            memset("pool", vpad[:, :, 1, 0:64], padval, [r_vpad])
            for t4 in range(0, NT, 4):
                for ti in range(4):
                    t = t4 + ti
                    for kc in range(8):
                        mm(PB[bank][:, ti * 128:(ti + 1) * 128], srcT[:, kc, t * 128:(t + 1) * 128], Wv[:, kc, :], kc == 0, kc == 7,
                           [r_src, r_Wv], [r_PB[bank]])
                pv = PB[bank][:].rearrange("p (t c) -> p t c", c=128)
                tcopy("dve", vpad[:, t4:t4 + 4, 0, 0:64], pv[:, :, 0:64], [r_PB[bank]], [r_vpad])
                tcopy("dve", vpad[:, t4:t4 + 4, 1, 64:128], pv[:, :, 64:128], [r_PB[bank]], [r_vpad])

        def attn_sb_pair(hp, obank_base):
            units = []
            for g in range(NG):
                for sb in range(4 * g + 3, -1, -1):
                    for hd in range(2):
                        units.append((g, sb, hd))
            n = len(units)

            def geom(u):
                g, sb, hd = units[u]
                c0 = max(sb * 128, g * 512) - g * 512
                return g, sb, hd, c0, (sb >= 4 * g)

            def s_z(u):
                g, sb, hd, c0, diag = geom(u)
                zb = u % 2
                if sb == 4 * g + 3:
                    memset("pool", Ls32[hd][:], 0.0, [r_Ls32[hd]])
                    memset("pool", Lsbf[hd][:], 0.0, [r_Lsbf[hd]])
                mm(PB[zb][:, c0:512], ka[hd][0:64, sb * 128:(sb + 1) * 128], qa[hd][0:64, g * 512 + c0:(g + 1) * 512], True, True,
                   [r_ka[hd], r_qa[hd]], [r_PB[zb]])

            def s_esp(u):
                g, sb, hd, c0, diag = geom(u)
                zb = u % 2
                act(e32[zb][:, c0:512], PB[zb][:, c0:512], AF.Exp, [r_PB[zb]], [r_e32[zb]])
                act(spb[zb][:, c0:512], e32[zb][:, c0:512], AF.Ln, [r_e32[zb]], [r_sp[zb]], bias=1.0)
                if diag:
                    tt("pool", spb[zb][:, c0:c0 + 128], spb[zb][:, c0:c0 + 128], mstrict[:], ALU.mult, [r_sp[zb], r_mstrict], [r_sp[zb]])

            def s_x(u):
                g, sb, hd, c0, diag = geom(u)
                zb = u % 2
                xb = 2 + zb
                first = (sb == 4 * g + 3)
                mm(PB[xb][:, c0:512], UI[:], spb[zb][:, c0:512], True, first, [r_UI, r_sp[zb]], [r_PB[xb]])
                if not first:
                    mm(PB[xb][:, c0:512], ones[:], Lsbf[hd][:, c0:512], False, True, [r_ones, r_Lsbf[hd]], [r_PB[xb]])
                if sb > 0:
                    tt("pool", Ls32[hd][:, c0:512], Ls32[hd][:, c0:512], spb[zb][:, c0:512], ALU.add, [r_Ls32[hd], r_sp[zb]], [r_Ls32[hd]])
                    tcopy("pool", Lsbf[hd][:, c0:512], Ls32[hd][:, c0:512], [r_Ls32[hd]], [r_Lsbf[hd]])

            def s_a(u):
                g, sb, hd, c0, diag = geom(u)
                zb = u % 2
                xb = 2 + zb
                ab = u % 3
                act(E2[zb][:, c0:512], PB[xb][:, c0:512], AF.Exp, [r_PB[xb]], [r_E2[zb]], scale=-1.0)
                tt("dve", Ab[ab][:, c0:512], e32[zb][:, c0:512], E2[zb][:, c0:512], ALU.mult, [r_e32[zb], r_E2[zb]], [r_A[ab]])
                if diag:
                    tt("pool", Ab[ab][:, c0:c0 + 128], Ab[ab][:, c0:c0 + 128], mstrict[:], ALU.mult, [r_A[ab], r_mstrict], [r_A[ab]])

            def s_pv(u):
                g, sb, hd, c0, diag = geom(u)
                ab = u % 3
                ob = obank_base + (g % 2)
                firstm = (sb == 4 * g + 3 and hd == 0)
                mm(PB[ob][:, c0:512], vpad[:, sb, hd, :], Ab[ab][:, c0:512], firstm, False, [r_vpad, r_A[ab]], [r_PB[ob]])
                if sb == 0 and hd == 1:
                    tcopy("dve", bT[:, hp, g * 512:(g + 1) * 512], PB[ob][:], [r_PB[ob]], [r_R2])

            pipeline(n, [s_z, s_esp, s_x, s_a, s_pv])

        def attn_fox_pair(hp):
            units = []
            for g in range(NG):
                for sb in range(4 * g + 3, -1, -1):
                    for hd in range(2):
                        units.append((g, sb, hd))
            n = len(units)

            def geom(u):
                g, sb, hd = units[u]
                c0 = max(sb * 128, g * 512) - g * 512
                return g, sb, hd, c0, (sb >= 4 * g)

            def obank(g, hd):
                return (4 + hd) if g % 2 == 0 else (2 + hd)

            def s_z(u):
                g, sb, hd, c0, diag = geom(u)
                zb = u % 2
                mm(PB[zb][:, c0:512], ka[hd][0:68, sb * 128:(sb + 1) * 128], qa[hd][0:68, g * 512 + c0:(g + 1) * 512], True, True,
                   [r_ka[hd], r_qa[hd]], [r_PB[zb]])

            def s_a(u):
                g, sb, hd, c0, diag = geom(u)
                zb = u % 2
                ab = u % 3
                act(Ab[ab][:, c0:512], PB[zb][:, c0:512], AF.Exp, [r_PB[zb]], [r_A[ab]])
                if diag:
                    tt("pool", Ab[ab][:, c0:c0 + 128], Ab[ab][:, c0:c0 + 128], mincl[:], ALU.mult, [r_A[ab], r_mincl], [r_A[ab]])

            def s_pv(u):
                g, sb, hd, c0, diag = geom(u)
                ab = u % 3
                ob = obank(g, hd)
                mm(PB[ob][:, c0:512], vpad[:, sb, hd, :], Ab[ab][:, c0:512], sb == 4 * g + 3, False, [r_vpad, r_A[ab]], [r_PB[ob]])
                if sb == 0:
                    if hd == 0:
                        P.op("dve", lambda e: e.reciprocal(out=rinv[0:64, :], in_=PB[ob][64:128, :]), [r_PB[ob]], [r_rinv])
                        tt("dve", bT[0:64, hp, g * 512:(g + 1) * 512], PB[ob][0:64, :], rinv[0:64, :], ALU.mult, [r_PB[ob], r_rinv], [r_R2])
                    else:
                        P.op("dve", lambda e: e.reciprocal(out=rinv[64:128, :], in_=PB[ob][0:64, :]), [r_PB[ob]], [r_rinv])
                        tt("dve", bT[64:128, hp, g * 512:(g + 1) * 512], PB[ob][64:128, :], rinv[64:128, :], ALU.mult, [r_PB[ob], r_rinv], [r_R2])

            pipeline(n, [s_z, s_a, s_pv])

        def ffn(l):
            prenorm(ffn_pre_g[l], aT, r_R1)
            dma(Gpost[:], ffn_post_g[l].partition_broadcast(128), r_Gpost, [], [r_Gpost])
            P.alias([r_R2], r_gT + r_wup)
            P.alias(r_qa + r_ka + [r_vpad], r_wdn)
            for tg in range(NG):
                def load_up(c):
                    dma(wupb[c % 2][:], ws_up[l, c].rearrange("p (g k j) -> p g k j", g=2, k=8), r_wup[c % 2], r_scr, [r_wup[c % 2]])
                load_up(0)
                for c in range(NCH):
                    if c + 1 < NCH:
                        load_up(c + 1)
                    for gu in range(2):
                        bk = 2 * (c % 2) + gu
                        for kc in range(8):
                            mm(PB[bk][:], wupb[c % 2][:, gu, kc, :], aT[:, kc, tg * 512:(tg + 1) * 512], kc == 0, kc == 7,
                               [r_wup[c % 2], r_R1], [r_PB[bk]])
                    for gu in range(2):
                        bk = 2 * (c % 2) + gu
                        ci = gu * NCH + c
                        hr, rh = hraw[gu], r_hraw[gu]
                        if tg == 0:
                            memset("pool", hr[:, 2:4], 0.0, [rh])
                        else:
                            tcopy("pool", hr[:, 2:4], halo[:, ci, :], [r_halo], [rh])
                        act(hr[:, 4:516], PB[bk][:], AF.Copy, [r_PB[bk]], [rh])
                        tcopy("pool", halo[:, ci, :], hr[:, 514:516], [rh], [r_halo])
                        ts("dve", cacc[gu][:], hr[:, 4:516], CW[l][:, 2, ci:ci + 1], CW[l][:, 3, ci:ci + 1], ALU.mult, ALU.add,
                           [rh, r_CW[l]], [r_cacc[gu]])
                        stt("dve", cacc[gu][:], hr[:, 3:515], CW[l][:, 1, ci:ci + 1], cacc[gu][:], ALU.mult, ALU.add,
                            [rh, r_CW[l], r_cacc[gu]], [r_cacc[gu]])
                        ts("pool", ctmp[gu][:], hr[:, 2:514], CW[l][:, 0, ci:ci + 1], None, ALU.mult, None, [rh, r_CW[l]], [r_ctmp[gu]])
                        tt("pool", cacc[gu][:], cacc[gu][:], ctmp[gu][:], ALU.add, [r_cacc[gu], r_ctmp[gu]], [r_cacc[gu]])
                    act(gl[:], cacc[0][:], AF.Gelu_apprx_tanh, [r_cacc[0]], [r_gl])
                    tt("pool", gT[:, c, :], gl[:], cacc[1][:], ALU.mult, [r_gl, r_cacc[1]], [r_gT[c]])
                for half in range(2):
                    def load_dn(c):
                        dma(wdnb[c % 3], ws_dn[l, c], r_wdn[c % 3], r_scr, [r_wdn[c % 3]])
                    load_dn(0)
                    load_dn(1)
                    for c in range(NCH):
                        if c + 2 < NCH:
                            load_dn(c + 2)
                        for t2 in range(2):
                            tl = 2 * half + t2
                            for nn in range(2):
                                bk = 2 * t2 + nn
                                mm(PB[bk][:], gT[:, c, tl * 128:(tl + 1) * 128], wdnb[c % 3][:, nn * 512:(nn + 1) * 512], c == 0, c == NCH - 1,
                                   [r_gT[c], r_wdn[c % 3]], [r_PB[bk]])
                    for t2 in range(2):
                        postnorm_residual(tg * 4 + 2 * half + t2, 2 * t2, 2 * t2 + 1, r_Gpost)
            P.alias(r_gT + r_wup, [r_R2])
            P.alias(r_wdn, r_qa + r_ka + [r_vpad])

        for b in range(NB):
            for t in range(NT):
                dma(hview[:, t, :], x[b, t * 128:(t + 1) * 128, :], r_h[t], [], [r_h[t]])
            prenorm(sb_pre_g[0], aT, r_R1)
            for hp in range(8):
                load_pair_weights(ws_qkv[0], ws_qkv[1], ws_qkv[2], hp)

                def sink_q(tg, bank):
                    for hd in range(2):
                        ts("dve", qa[hd][0:64, tg * 512:(tg + 1) * 512], PB[bank][hd * 64:(hd + 1) * 64, :], 0.125, None, ALU.mult, None,
                           [r_PB[bank]], [r_qa[hd]])

                def sink_k(tg, bank):
                    for hd in range(2):
                        tcopy("dve", ka[hd][0:64, tg * 512:(tg + 1) * 512], PB[bank][hd * 64:(hd + 1) * 64, :], [r_PB[bank]], [r_ka[hd]])
                proj_featmajor(Wq, r_Wq, aT, r_R1, 6, sink_q)
                proj_featmajor(Wk, r_Wk, aT, r_R1, 6, sink_k)
                proj_v(aT, r_R1, 6, 0.0)
                attn_sb_pair(hp, 4)
            out_proj(0, sb_post_g[0], bT, r_R2)
            ffn(0)
            prenorm(fox_pre_g[0], aT, r_R1)
            for hp in range(8):
                dma(Wq[:], ws_qkv[3][hp].rearrange("p (k j) -> p k j", j=128), r_Wq, r_scr, [r_Wq])

                def sink_qall(tg, bank, hp=hp):
                    ts("dve", bT[:, hp, tg * 512:(tg + 1) * 512], PB[bank][:], 0.125, None, ALU.mult, None, [r_PB[bank]], [r_R2])
                proj_featmajor(Wq, r_Wq, aT, r_R1, 6, sink_qall)
            prenorm(kv_norm_g, aT, r_R1)
            dma(Wf[:], ws_f.rearrange("p (k j) -> p k j", j=16), r_Wf, r_scr, [r_Wf])
            memset("pool", E2[1][0:16, 0:512], 1.0, [r_E2[1]])
            for tg in range(NG):
                seg = slice(tg * 512, (tg + 1) * 512)
                for kc in range(8):
                    mm(PB[6][0:16, :], Wf[:, kc, :], aT[:, kc, seg], kc == 0, kc == 7, [r_Wf, r_R1], [r_PB[6]])
                act(e32[0][0:16, 0:512], PB[6][0:16, :], AF.Exp, [r_PB[6], r_bf], [r_e32[0]], scale=-1.0, bias=bfneg[:, 0:1])
                act(e32[1][0:16, 0:512], e32[0][0:16, 0:512], AF.Ln, [r_e32[0]], [r_e32[1]], bias=1.0)
                if tg == 0:
                    P.op("dve", lambda e: e.tensor_tensor_scan(out=E2[0][0:16, 0:512], data0=E2[1][0:16, 0:512], data1=e32[1][0:16, 0:512],
                                                               initial=0.0, op0=ALU.mult, op1=ALU.add),
                         [r_E2[1], r_e32[1]], [r_E2[0]])
                else:
                    P.op("dve", lambda e: e.tensor_tensor_scan(out=E2[0][0:16, 0:512], data0=E2[1][0:16, 0:512], data1=e32[1][0:16, 0:512],
                                                               initial=carry[:, 0:1], op0=ALU.mult, op1=ALU.add),
                         [r_E2[1], r_e32[1], r_carry], [r_E2[0]])
                tcopy("dve", carry[:, 0:1], E2[0][0:16, 511:512], [r_E2[0]], [r_carry])
                tcopy("dve", frow[0][:, seg], E2[0][0:16, 0:512], [r_E2[0]], [r_frow])
                tt("dve", Ls32[0][0:16, :], E2[0][0:16, 0:512], frow[0][:, seg], ALU.subtract, [r_E2[0], r_frow], [r_Ls32[0]])
                tcopy("dve", frow[1][:, seg], Ls32[0][0:16, :], [r_Ls32[0]], [r_frow])
                ts("dve", frow[2][:, seg], E2[0][0:16, 0:512], -1.0, None, ALU.mult, None, [r_E2[0]], [r_frow])
                ts("dve", frow[3][:, seg], Ls32[0][0:16, :], -1.0, None, ALU.mult, None, [r_Ls32[0]], [r_frow])
            for hd in range(2):
                memset("pool", qa[hd][64:68, :], 1.0, [r_qa[hd]])
                memset("pool", ka[hd][64:68, :], 1.0, [r_ka[hd]])
            for hp in range(8):
                load_pair_weights(None, ws_kv[0], ws_kv[1], hp)
                for hd in range(2):
                    hh = 2 * hp + hd
                    tcopy("pool", qa[hd][0:64, :], bT[hd * 64:(hd + 1) * 64, hp, :], [r_R2], [r_qa[hd]])
                    dma(qa[hd][64:65, :], frow[2][hh:hh + 1, :], r_qa[hd], [r_frow], [r_qa[hd]])
                    dma(qa[hd][65:66, :], frow[3][hh:hh + 1, :], r_qa[hd], [r_frow], [r_qa[hd]])
                    dma(ka[hd][66:67, :], frow[0][hh:hh + 1, :], r_ka[hd], [r_frow], [r_ka[hd]])
                    dma(ka[hd][67:68, :], frow[1][hh:hh + 1, :], r_ka[hd], [r_frow], [r_ka[hd]])

                def sink_k1(tg, bank):
                    for hd in range(2):
                        tcopy("dve", ka[hd][0:64, tg * 512:(tg + 1) * 512], PB[bank][hd * 64:(hd + 1) * 64, :], [r_PB[bank]], [r_ka[hd]])
                proj_featmajor(Wk, r_Wk, aT, r_R1, 6, sink_k1)
                proj_v(aT, r_R1, 6, 1.0)
                attn_fox_pair(hp)
            out_proj(1, fox_post_g[0], bT, r_R2)
            ffn(1)
            for t in range(NT):
                dma(y[b, t * 128:(t + 1) * 128, :], hview[:, t, :], r_h[t], [r_h[t]], [])
        P.wait_all("sp", r_h)
        P.emit()
    return nc


_NC_CACHE = {}


def kernel(**inputs):
    x = np.ascontiguousarray(inputs["x"], dtype=np.float32)
    B, S, _ = x.shape
    NB = B // N_CORES
    key = (NB, S)
    if key not in _NC_CACHE:
        _NC_CACHE[key] = build_nc(NB, S)
    nc = _NC_CACHE[key]
    wnames = ["sb_pre_g", "sb_w_qkv", "sb_w_o", "sb_post_g", "kv_norm_g", "w_kvf", "b_f", "fox_pre_g", "fox_w_q",
              "fox_w_o", "fox_post_g", "ffn_pre_g", "w_up", "conv_w", "conv_b", "w_down", "ffn_post_g"]
    ws = {k: np.ascontiguousarray(inputs[k], dtype=np.float32) for k in wnames}
    in_maps = []
    for c in range(N_CORES):
        m = {"x": x[c * NB:(c + 1) * NB]}
        m.update(ws)
        in_maps.append(m)
    res = run_bass_kernel_spmd(nc, in_maps, core_ids=list(range(N_CORES)))
    return np.concatenate([r["y"] for r in res.results], axis=0)
```

```python
from contextlib import ExitStack
import numpy as np
import concourse.bass as bass
import concourse.mybir as mybir
from concourse.bass_utils import run_bass_kernel_spmd

F32 = mybir.dt.float32
BF16 = mybir.dt.bfloat16
AF = mybir.ActivationFunctionType
ALU = mybir.AluOpType

D = 1024
H = 16
DH = 64
FF = 2816
NCH = FF // 128
EPS = 1e-6
N_CORES = 8


class Res:
    __slots__ = ("name", "lw", "rd", "sem", "semcnt")

    def __init__(self, name, sem=None):
        self.name = name
        self.lw = None
        self.rd = {}
        self.sem = sem
        self.semcnt = 0


class Prog:
    ENGS = ("pe", "act", "dve", "pool", "sp")

    def __init__(self, nc, ctx):
        self.nc = nc
        self.ctx = ctx
        self.streams = {e: [] for e in self.ENGS}
        self.count = {e: 0 for e in self.ENGS}
        self.waited = {e: {} for e in self.ENGS}
        self.esem = {e: ctx.enter_context(nc.semaphore("es_" + e)) for e in self.ENGS}
        self.nres = 0

    def res(self, name=None, dma=False):
        self.nres += 1
        name = name or ("r%d" % self.nres)
        sem = self.ctx.enter_context(self.nc.semaphore("ds%d" % self.nres)) if dma else None
        return Res(name, sem)

    def _deps(self, reads, writes):
        deps = []
        for r in reads:
            if r.lw is not None:
                deps.append(r.lw)
        for w in writes:
            if w.lw is not None:
                deps.append(w.lw)
            deps.extend(w.rd.items())
        return deps

    def _waits_for(self, eng, deps):
        best = {}
        for (key, val) in deps:
            if key == "pe" and eng == "pe":
                continue
            if val > best.get(key, 0):
                best[key] = val
        out = []
        wd = self.waited[eng]
        for key, val in best.items():
            if wd.get(key, 0) >= val:
                continue
            wd[key] = val
            out.append((key, val))
        return out

    def _sem_of(self, key):
        if isinstance(key, str):
            return self.esem[key]
        return key.sem

    def _record(self, ev, reads, writes):
        k, v = ev
        for r in reads:
            if r.rd.get(k, 0) < v:
                r.rd[k] = v
        for w in writes:
            w.lw = ev
            w.rd = {}

    def op(self, eng, fn, reads=(), writes=()):
        waits = self._waits_for(eng, self._deps(reads, writes))
        self.count[eng] += 1
        ev = (eng, self.count[eng])
        self.streams[eng].append((waits, fn, None))
        self._record(ev, reads, writes)
        return ev

    def dma(self, eng, fn, semres, reads=(), writes=()):
        waits = self._waits_for(eng, self._deps(reads, writes))
        semres.semcnt += 1
        ev = (semres, 16 * semres.semcnt)
        self.streams[eng].append((waits, fn, semres))
        self._record(ev, reads, writes)
        return ev

    def alias(self, olds, news):
        evs = {}
        for o in olds:
            if o.lw is not None:
                k, v = o.lw
                evs[k] = max(evs.get(k, 0), v)
            for k, v in o.rd.items():
                evs[k] = max(evs.get(k, 0), v)
        for n in news:
            for k, v in evs.items():
                if n.rd.get(k, 0) < v:
                    n.rd[k] = v

    def wait_all(self, eng, resources):
        deps = []
        for r in resources:
            if r.lw is not None:
                deps.append(r.lw)
            deps.extend(r.rd.items())
        waits = self._waits_for(eng, deps)
        self.streams[eng].append((waits, None, None))

    def emit(self):
        nc = self.nc
        with nc.Block() as block:
            def run(engname):
                def body(e):
                    esem = self.esem[engname]
                    for (waits, fn, semres) in self.streams[engname]:
                        for (key, val) in waits:
                            e.wait_ge(self._sem_of(key), val)
                        if fn is None:
                            continue
                        ins = fn(e)
                        if semres is None:
                            ins.then_inc(esem, 1)
                        else:
                            ins.then_inc(semres.sem, 16)
                return body
            block.tensor(run("pe"))
            block.scalar(run("act"))
            block.vector(run("dve"))
            block.gpsimd(run("pool"))
            block.sync(run("sp"))


def build_nc(NB, S):
    NT = S // 128
    NG = S // 512
    nc = bass.Bass("TRN2", target_bir_lowering=False)
    dt_in = lambda name, shape: nc.dram_tensor(name, list(shape), F32, kind="ExternalInput").ap()
    x = dt_in("x", [NB, S, D])
    sb_pre_g = dt_in("sb_pre_g", [1, D])
    sb_w_qkv = dt_in("sb_w_qkv", [1, D, 3 * D])
    sb_w_o = dt_in("sb_w_o", [1, D, D])
    sb_post_g = dt_in("sb_post_g", [1, D])
    kv_norm_g = dt_in("kv_norm_g", [D])
    w_kvf = dt_in("w_kvf", [D, 2 * D + H])
    b_f = dt_in("b_f", [H])
    fox_pre_g = dt_in("fox_pre_g", [1, D])
    fox_w_q = dt_in("fox_w_q", [1, D, D])
    fox_w_o = dt_in("fox_w_o", [1, D, D])
    fox_post_g = dt_in("fox_post_g", [1, D])
    ffn_pre_g = dt_in("ffn_pre_g", [2, D])
    w_up = dt_in("w_up", [2, D, 2 * FF])
    conv_w = dt_in("conv_w", [2, 3, 2 * FF])
    conv_b = dt_in("conv_b", [2, 2 * FF])
    w_down = dt_in("w_down", [2, FF, D])
    ffn_post_g = dt_in("ffn_post_g", [2, D])
    y = nc.dram_tensor("y", [NB, S, D], F32, kind="ExternalOutput").ap()

    ws_qkv = nc.dram_tensor("ws_qkv", [5, 8, 128, 1024], BF16).ap()
    ws_kv = nc.dram_tensor("ws_kv", [2, 8, 128, 1024], BF16).ap()
    ws_f = nc.dram_tensor("ws_f", [128, 8 * 16], BF16).ap()
    ws_o = nc.dram_tensor("ws_o", [2, 128, 8192], BF16).ap()
    ws_up = nc.dram_tensor("ws_up", [2, NCH, 128, 2048], BF16).ap()
    ws_dn = nc.dram_tensor("ws_dn", [2, NCH, 128, 1024], BF16).ap()

    with ExitStack() as ctx:
        P = Prog(nc, ctx)
        sbt = lambda name, shape, dt: ctx.enter_context(nc.sbuf_tensor(name, list(shape), dt))
        pst = lambda name, shape, dt: ctx.enter_context(nc.psum_tensor(name, list(shape), dt))

        Hbuf = sbt("Hbuf", [128, NT * D], F32)
        hview = Hbuf[:].rearrange("p (t d) -> p t d", d=D)
        r_h = [P.res("h%d" % t, dma=True) for t in range(NT)]
        R1N = max(8 * S, 16384)
        R2N = max(8 * S, 16384)
        R3N = max(4 * S + NT * 256, 8192)
        R1 = sbt("R1", [128, R1N], BF16)
        R2 = sbt("R2", [128, R2N], BF16)
        R3 = sbt("R3", [128, R3N], BF16)
        r_R1 = P.res("R1")
        r_R2 = P.res("R2")
        aT = R1[:, 0:8 * S].rearrange("p (k s) -> p k s", s=S)
        bT = R2[:, 0:8 * S].rearrange("p (k s) -> p k s", s=S)
        qa = [R3[:, 0:S], R3[:, S:2 * S]]
        ka = [R3[:, 2 * S:3 * S], R3[:, 3 * S:4 * S]]
        vpad = R3[:, 4 * S:4 * S + NT * 256].rearrange("p (t h c) -> p t h c", h=2, c=128)
        r_qa = [P.res("qa0", dma=True), P.res("qa1", dma=True)]
        r_ka = [P.res("ka0", dma=True), P.res("ka1", dma=True)]
        r_vpad = P.res("vpad")
        WoT = R3[:, 0:8192].rearrange("p (k n) -> p k n", n=D)
        r_Wo = P.res("Wo", dma=True)
        gT = R2[:, 0:NCH * 512].rearrange("p (c t) -> p c t", t=512)
        r_gT = [P.res("gT%d" % c) for c in range(NCH)]
        wupb = [R2[:, NCH * 512 + i * 2048: NCH * 512 + (i + 1) * 2048].rearrange("p (g k j) -> p g k j", g=2, k=8) for i in range(2)]
        r_wup = [P.res("wup%d" % i, dma=True) for i in range(2)]
        wdnb = [R3[:, i * 1024:(i + 1) * 1024] for i in range(3)]
        r_wdn = [P.res("wdn%d" % i, dma=True) for i in range(3)]

        Wq = sbt("Wq", [128, 8, 128], BF16); r_Wq = P.res("Wq", dma=True)
        Wk = sbt("Wk", [128, 8, 128], BF16); r_Wk = P.res("Wk", dma=True)
        Wv = sbt("Wv", [128, 8, 128], BF16); r_Wv = P.res("Wv", dma=True)
        Wf = sbt("Wf", [128, 8, 16], BF16); r_Wf = P.res("Wf", dma=True)
        e32 = [sbt("e32_%d" % i, [128, 516], F32) for i in range(4)]; r_e32 = [P.res() for _ in range(4)]
        spb = [sbt("spb_%d" % i, [128, 512], BF16) for i in range(4)]; r_sp = [P.res() for _ in range(4)]
        E2 = [sbt("E2_%d" % i, [128, 512], F32) for i in range(2)]; r_E2 = [P.res() for _ in range(2)]
        Ab = [sbt("Ab_%d" % i, [128, 512], BF16) for i in range(3)]; r_A = [P.res() for _ in range(3)]
        Ls32 = [sbt("Ls32_%d" % i, [128, 512], F32) for i in range(1)]; r_Ls32 = [P.res() for _ in range(1)]
        rinv = sbt("rinv", [128, 512], F32); r_rinv = P.res()
        junk = sbt("junk", [128, 1024], BF16); r_junk = P.res()
        xn = sbt("xn", [128, 1024], BF16); r_xn = P.res()
        tmpf = sbt("tmpf", [128, 1024], F32); r_tmpf = P.res()
        Gpre = sbt("Gpre", [128, 1024], F32); r_Gpre = P.res("Gpre", dma=True)
        Gpost = Gpre; r_Gpost = r_Gpre
        st = sbt("stats", [128, 8], F32); r_st = P.res()
        frowA = sbt("frowA", [80, S], BF16)
        frowB = sbt("frowB", [16, S], BF16)
        frow = [frowA[0:16, :], frowA[32:48, :], frowA[64:80, :], frowB[0:16, :]]
        r_frow = P.res()
        carry = sbt("carry", [16, 1], F32); r_carry = P.res()
        bfneg = sbt("bfneg", [16, 1], F32); r_bf = P.res("bf", dma=True)
        CW = [sbt("CW%d" % l, [128, 4, 2 * NCH], F32) for l in range(2)]; r_CW = [P.res("CW%d" % l, dma=True) for l in range(2)]
        hraw = e32; r_hraw = r_e32
        cacc = E2; r_cacc = r_E2
        gl = rinv; r_gl = r_rinv
        halo = sbt("halo", [128, 2 * NCH, 2], F32); r_halo = P.res()
        ident = sbt("ident", [128, 128], BF16); r_ident = P.res()
        mstrict = sbt("mstrict", [128, 128], BF16); r_mstrict = P.res()
        mincl = sbt("mincl", [128, 128], BF16); r_mincl = P.res()
        UI = sbt("UI", [128, 128], BF16); r_UI = P.res()
        Lc = sbt("Lc", [128, 128], BF16); r_Lc = P.res()
        PB = [pst("PB%d" % i, [128, 512], F32) for i in range(7)]; r_PB = [P.res("PB%d" % i) for i in range(7)]
        PT = pst("PT", [128, 1024], BF16); r_PT = P.res("PT")

        def mm(out, lhsT, rhs, start, stop, reads, writes):
            P.op("pe", lambda e: e.matmul(out, lhsT=lhsT, rhs=rhs, start=start, stop=stop, skip_group_check=True), reads, writes)

        def act(out, in_, func, reads, writes, **kw):
            P.op("act", lambda e: e.activation(out=out, in_=in_, func=func, **kw), reads, writes)

        def tcopy(eng, out, in_, reads, writes):
            P.op(eng, lambda e: e.tensor_copy(out=out, in_=in_), reads, writes)

        def tt(eng, out, in0, in1, op, reads, writes):
            P.op(eng, lambda e: e.tensor_tensor(out=out, in0=in0, in1=in1, op=op), reads, writes)

        def ts(eng, out, in0, s1, s2, op0, op1, reads, writes):
            if s2 is None:
                P.op(eng, lambda e: e.tensor_scalar(out=out, in0=in0, scalar1=s1, scalar2=None, op0=op0), reads, writes)
            else:
                P.op(eng, lambda e: e.tensor_scalar(out=out, in0=in0, scalar1=s1, scalar2=s2, op0=op0, op1=op1), reads, writes)

        def stt(eng, out, in0, scalar, in1, op0, op1, reads, writes):
            P.op(eng, lambda e: e.scalar_tensor_tensor(out=out, in0=in0, scalar=scalar, in1=in1, op0=op0, op1=op1), reads, writes)

        def memset(eng, ap, val, writes):
            P.op(eng, lambda e: e.memset(ap, val), (), writes)

        def dma(out, in_, semres, reads, writes, slow=False):
            if slow:
                P.dma("sp", lambda e: e.dma_start(out=out, in_=in_, allow_slow_non_contiguous=True), semres, reads, writes)
            else:
                P.dma("sp", lambda e: e.dma_start(out=out, in_=in_), semres, reads, writes)

        def aff(ap, pattern, cmp, cm, writes):
            P.op("pool", lambda e: e.affine_select(out=ap, in_=ap, pattern=pattern, compare_op=cmp, fill=0.0, base=0, channel_multiplier=cm), writes, writes)
        for (t_, r_) in ((ident, r_ident), (mstrict, r_mstrict), (mincl, r_mincl), (UI, r_UI), (Lc, r_Lc)):
            memset("pool", t_[:], 1.0, [r_])
        aff(ident[:], [[-1, 128]], ALU.is_equal, 1, [r_ident])
        aff(mstrict[:], [[1, 128]], ALU.is_gt, -1, [r_mstrict])
        aff(mincl[:], [[1, 128]], ALU.is_ge, -1, [r_mincl])
        aff(UI[:], [[-1, 128]], ALU.is_ge, 1, [r_UI])
        aff(Lc[:], [[1, 128]], ALU.is_gt, -1, [r_Lc])
        memset("pool", halo[:], 0.0, [r_halo])
        for l in range(2):
            for k in range(3):
                dma(CW[l][:, k, :], conv_w[l, k].rearrange("(c p) -> p c", p=128), r_CW[l], [], [r_CW[l]], slow=True)
            dma(CW[l][:, 3, :], conv_b[l].rearrange("(c p) -> p c", p=128), r_CW[l], [], [r_CW[l]], slow=True)
        dma(bfneg[:], b_f.rearrange("(h o) -> h o", o=1), r_bf, [], [r_bf], slow=True)
        ts("dve", bfneg[:], bfneg[:], -1.0, None, ALU.mult, None, [r_bf], [r_bf])

        stg32 = [R1[:, i * 8192:(i + 1) * 8192].bitcast(F32) for i in range(2)]
        stg16 = [R2[:, i * 4096:(i + 1) * 4096] for i in range(2)]
        r_s32 = [P.res("s32_%d" % i, dma=True) for i in range(2)]
        r_s16 = [P.res("s16_%d" % i) for i in range(2)]
        r_s16d = [P.res("s16d_%d" % i, dma=True) for i in range(2)]
        cast_engs = ["dve", "pool", "act"]
        ucount = [0]

        def cast_unit(srcs, E, dst, outview=None):
            i = ucount[0] % 2
            eng = cast_engs[ucount[0] % 3]
            ucount[0] += 1
            for (vf, src) in srcs:
                dma(vf(stg32[i]), src, r_s32[i], [], [r_s32[i]])
            if eng == "act":
                act(stg16[i][:, 0:E], stg32[i][:, 0:E], AF.Copy, [r_s32[i]], [r_s16[i]])
            else:
                tcopy(eng, stg16[i][:, 0:E], stg32[i][:, 0:E], [r_s32[i]], [r_s16[i]])
            src16 = stg16[i][:, 0:E] if outview is None else outview(stg16[i][:, 0:E])
            dma(dst, src16, r_s16d[i], [r_s16[i]], [r_s16d[i]])

        def img4(stage, a, b, c):
            return stage[:, 0:a * b * c].rearrange("p (a b c) -> p a b c", a=a, b=b)

        def img3(stage, a, b):
            return stage[:, 0:a * b].rearrange("p (a b) -> p a b", a=a)

        def cast_cols(src2d, col0, dst_units):
            for half in range(2):
                srcs = []
                for hp4 in range(4):
                    hp = half * 4 + hp4
                    srcs.append((lambda s, hp4=hp4: img4(s, 4, 8, 128)[:, hp4],
                                 src2d[:, col0 + hp * 128: col0 + (hp + 1) * 128].rearrange("(k p) j -> p k j", p=128)))
                cast_unit(srcs, 4096, dst_units[half * 4:half * 4 + 4].rearrange("u p e -> p u e"),
                          outview=lambda v: v.rearrange("p (u e) -> p u e", u=4))
        cast_cols(sb_w_qkv[0], 0, ws_qkv[0])
        cast_cols(sb_w_qkv[0], D, ws_qkv[1])
        cast_cols(sb_w_qkv[0], 2 * D, ws_qkv[2])

        def cast_wo(src2d, dst):
            for half in range(2):
                srcs = [(lambda s: img3(s, 4, 1024),
                         src2d[half * 512:(half + 1) * 512, :].rearrange("(k p) n -> p k n", p=128))]
                cast_unit(srcs, 4096, dst[:, half * 4096:(half + 1) * 4096])
        cast_wo(sb_w_o[0], ws_o[0])

        def cast_ffn(l):
            for c0 in range(0, NCH, 2):
                srcs = []
                for gu in range(2):
                    for ci in range(2):
                        c = c0 + ci
                        srcs.append((lambda s, gu=gu, ci=ci: s[:, 0:4096].rearrange("p (c g k j) -> p c g k j", c=2, g=2, k=8)[:, ci, gu],
                                     w_up[l][:, gu * FF + c * 128: gu * FF + (c + 1) * 128].rearrange("(k p) j -> p k j", p=128)))
                cast_unit(srcs, 4096, ws_up[l, c0:c0 + 2].rearrange("c p e -> p c e"),
                          outview=lambda v: v.rearrange("p (c e) -> p c e", c=2))
            for c0 in range(0, NCH, 4):
                n = min(4, NCH - c0)
                srcs = [(lambda s, n=n: img3(s, n, 1024),
                         w_down[l][c0 * 128:(c0 + n) * 128, :].rearrange("(c p) n -> p c n", p=128))]
                cast_unit(srcs, n * 1024, ws_dn[l, c0:c0 + n].rearrange("c p e -> p c e"),
                          outview=lambda v, n=n: v.rearrange("p (c e) -> p c e", c=n))
        cast_ffn(0)
        cast_cols(w_kvf, 0, ws_kv[0])
        cast_cols(w_kvf, D, ws_kv[1])
        cast_unit([(lambda s: img3(s, 8, 16), w_kvf[:, 2 * D:2 * D + H].rearrange("(k p) j -> p k j", p=128))], 128, ws_f)
        cast_cols(fox_w_q[0], 0, ws_qkv[3])
        cast_wo(fox_w_o[0], ws_o[1])
        cast_ffn(1)
        r_scr = r_s16d
        P.alias(r_s32 + r_s16 + r_s16d, [r_R1, r_R2])

        def prenorm(g_ap, dstT, r_dst, first_alias=None):
            dma(Gpre[:], g_ap.partition_broadcast(128), r_Gpre, [], [r_Gpre])
            for t in range(NT):
                act(junk[:], hview[:, t, :], AF.Square, [r_h[t]], [r_junk, r_st], accum_out=st[:, 0:1])
                act(st[:, 1:2], st[:, 0:1], AF.Ln, [r_st], [r_st], scale=1.0 / D, bias=EPS)
                act(st[:, 2:3], st[:, 1:2], AF.Exp, [r_st], [r_st], scale=-0.5)
                stt("dve", xn[:], hview[:, t, :], st[:, 2:3], Gpre[:], ALU.mult, ALU.mult, [r_h[t], r_st, r_Gpre], [r_xn])
                for kc in range(8):
                    P.op("pe", lambda e, kc=kc: e.transpose(PT[:, kc * 128:(kc + 1) * 128], xn[:, kc * 128:(kc + 1) * 128], ident[:]),
                         [r_xn, r_ident], [r_PT])
                tcopy("dve", dstT[:, :, t * 128:(t + 1) * 128], PT[:].rearrange("p (k j) -> p k j", j=128), [r_PT], [r_dst])

        def postnorm_residual(t, ba, bb, g_res):
            act(junk[:, 0:512], PB[ba][:], AF.Square, [r_PB[ba]], [r_junk, r_st], accum_out=st[:, 3:4])
            act(junk[:, 512:1024], PB[bb][:], AF.Square, [r_PB[bb]], [r_junk, r_st], accum_out=st[:, 4:5])
            tt("dve", st[:, 5:6], st[:, 3:4], st[:, 4:5], ALU.add, [r_st], [r_st])
            act(st[:, 6:7], st[:, 5:6], AF.Ln, [r_st], [r_st], scale=1.0 / D, bias=EPS)
            act(st[:, 7:8], st[:, 6:7], AF.Exp, [r_st], [r_st], scale=-0.5)
            stt("dve", tmpf[:, 0:512], PB[ba][:], st[:, 7:8], Gpost[:, 0:512], ALU.mult, ALU.mult, [r_PB[ba], r_st, g_res], [r_tmpf])
            stt("dve", tmpf[:, 512:1024], PB[bb][:], st[:, 7:8], Gpost[:, 512:1024], ALU.mult, ALU.mult, [r_PB[bb], r_st, g_res], [r_tmpf])
            tt("pool", hview[:, t, :], hview[:, t, :], tmpf[:], ALU.add, [r_h[t], r_tmpf], [r_h[t]])

        def out_proj(widx, g_ap, srcT, r_src):
            P.alias(r_qa + r_ka + [r_vpad], [r_Wo])
            dma(WoT[:, 0:4, :], ws_o[widx][:, 0:4096].rearrange("p (k n) -> p k n", n=D), r_Wo, r_scr, [r_Wo])
            dma(WoT[:, 4:8, :], ws_o[widx][:, 4096:8192].rearrange("p (k n) -> p k n", n=D), r_Wo, r_scr, [r_Wo])
            dma(Gpost[:], g_ap.partition_broadcast(128), r_Gpost, [], [r_Gpost])
            for t in range(NT):
                ba, bb = (0, 1) if t % 2 == 0 else (2, 3)
                for n, bk in ((0, ba), (1, bb)):
                    for kc in range(8):
                        mm(PB[bk][:], srcT[:, kc, t * 128:(t + 1) * 128], WoT[:, kc, n * 512:(n + 1) * 512],
                           kc == 0, kc == 7, [r_src, r_Wo], [r_PB[bk]])
                postnorm_residual(t, ba, bb, r_Gpost)
            P.alias([r_Wo], r_qa + r_ka + [r_vpad])

        def pipeline(n, stages):
            ns = len(stages)
            for tick in range(n + ns - 1):
                for k in range(ns - 1, -1, -1):
                    i = tick - k
                    if 0 <= i < n:
                        stages[k](i)

        def load_pair_weights(qsrc, ksrc, vsrc, hp):
            if qsrc is not None:
                dma(Wq[:], qsrc[hp].rearrange("p (k j) -> p k j", j=128), r_Wq, r_scr, [r_Wq])
            dma(Wk[:], ksrc[hp].rearrange("p (k j) -> p k j", j=128), r_Wk, r_scr, [r_Wk])
            dma(Wv[:], vsrc[hp].rearrange("p (k j) -> p k j", j=128), r_Wv, r_scr, [r_Wv])

        def proj_featmajor(Wt, r_W, srcT, r_src, bank, sink):
            for tg in range(NG):
                for kc in range(8):
                    mm(PB[bank][:], Wt[:, kc, :], srcT[:, kc, tg * 512:(tg + 1) * 512], kc == 0, kc == 7, [r_W, r_src], [r_PB[bank]])
                sink(tg, bank)

        def proj_v(srcT, r_src, bank, padval):
            memset("pool", vpad[:, :, 0, 64:128], padval, [r_vpad])
            memset("pool", vpad[:, :, 1, 0:64], padval, [r_vpad])
            for t4 in range(0, NT, 4):
                for ti in range(4):
                    t = t4 + ti
                    for kc in range(8):
                        mm(PB[bank][:, ti * 128:(ti + 1) * 128], srcT[:, kc, t * 128:(t + 1) * 128], Wv[:, kc, :], kc == 0, kc == 7,
                           [r_src, r_Wv], [r_PB[bank]])
                pv = PB[bank][:].rearrange("p (t c) -> p t c", c=128)
                tcopy("dve", vpad[:, t4:t4 + 4, 0, 0:64], pv[:, :, 0:64], [r_PB[bank]], [r_vpad])
                tcopy("dve", vpad[:, t4:t4 + 4, 1, 64:128], pv[:, :, 64:128], [r_PB[bank]], [r_vpad])

        def attn_sb_pair(hp, obank_base):
            units = []
            for g in range(NG):
                for sb in range(4 * g + 3, -1, -1):
                    for hd in range(2):
                        units.append((g, sb, hd))
            n = len(units)

            def geom(u):
                g, sb, hd = units[u]
                c0 = max(sb * 128, g * 512) - g * 512
                return g, sb, hd, c0, (sb >= 4 * g)

            def s_z(u):
                g, sb, hd, c0, diag = geom(u)
                zb = u % 2
                mm(PB[zb][:, c0:512], ka[hd][0:64, sb * 128:(sb + 1) * 128], qa[hd][0:64, g * 512 + c0:(g + 1) * 512], True, True,
                   [r_ka[hd], r_qa[hd]], [r_PB[zb]])

            def s_esp(u):
                g, sb, hd, c0, diag = geom(u)
                zb = u % 2
                eb = u % 4
                act(e32[eb][:, c0:512], PB[zb][:, c0:512], AF.Exp, [r_PB[zb]], [r_e32[eb]])
                act(spb[eb][:, c0:512], e32[eb][:, c0:512], AF.Ln, [r_e32[eb]], [r_sp[eb]], bias=1.0)
                if diag:
                    tt("pool", spb[eb][:, c0:c0 + 128], spb[eb][:, c0:c0 + 128], mstrict[:], ALU.mult, [r_sp[eb], r_mstrict], [r_sp[eb]])

            def s_x(u):
                g, sb, hd, c0, diag = geom(u)
                eb = u % 4
                xb = 2 + hd
                mm(PB[xb][:, c0:512], UI[:], spb[eb][:, c0:512], sb == 4 * g + 3, False, [r_UI, r_sp[eb]], [r_PB[xb]])

            def s_a(u):
                g, sb, hd, c0, diag = geom(u)
                eb = u % 4
                xb = 2 + hd
                ab = u % 3
                act(E2[hd][:, c0:512], PB[xb][:, c0:512], AF.Exp, [r_PB[xb]], [r_E2[hd]], scale=-1.0)
                if sb > 0:
                    mm(PB[xb][:, c0:512], Lc[:], spb[eb][:, c0:512], False, False, [r_Lc, r_sp[eb]], [r_PB[xb]])
                tt("dve", Ab[ab][:, c0:512], e32[eb][:, c0:512], E2[hd][:, c0:512], ALU.mult, [r_e32[eb], r_E2[hd]], [r_A[ab]])
                if diag:
                    tt("pool", Ab[ab][:, c0:c0 + 128], Ab[ab][:, c0:c0 + 128], mstrict[:], ALU.mult, [r_A[ab], r_mstrict], [r_A[ab]])

            def s_pv(u):
                g, sb, hd, c0, diag = geom(u)
                ab = u % 3
                ob = obank_base + (g % 2)
                firstm = (sb == 4 * g + 3 and hd == 0)
                mm(PB[ob][:, c0:512], vpad[:, sb, hd, :], Ab[ab][:, c0:512], firstm, False, [r_vpad, r_A[ab]], [r_PB[ob]])
                if sb == 0 and hd == 1:
                    tcopy("dve", bT[:, hp, g * 512:(g + 1) * 512], PB[ob][:], [r_PB[ob]], [r_R2])

            pipeline(n, [s_z, s_esp, s_x, s_a, s_pv])

        def attn_fox_pair(hp):
            units = []
            for g in range(NG):
                for sb in range(4 * g + 3, -1, -1):
                    for hd in range(2):
                        units.append((g, sb, hd))
            n = len(units)

            def geom(u):
                g, sb, hd = units[u]
                c0 = max(sb * 128, g * 512) - g * 512
                return g, sb, hd, c0, (sb >= 4 * g)

            def obank(g, hd):
                return (4 + hd) if g % 2 == 0 else (2 + hd)

            def s_z(u):
                g, sb, hd, c0, diag = geom(u)
                zb = u % 2
                mm(PB[zb][:, c0:512], ka[hd][0:68, sb * 128:(sb + 1) * 128], qa[hd][0:68, g * 512 + c0:(g + 1) * 512], True, True,
                   [r_ka[hd], r_qa[hd]], [r_PB[zb]])

            def s_a(u):
                g, sb, hd, c0, diag = geom(u)
                zb = u % 2
                ab = u % 3
                act(Ab[ab][:, c0:512], PB[zb][:, c0:512], AF.Exp, [r_PB[zb]], [r_A[ab]])
                if diag:
                    tt("pool", Ab[ab][:, c0:c0 + 128], Ab[ab][:, c0:c0 + 128], mincl[:], ALU.mult, [r_A[ab], r_mincl], [r_A[ab]])

            def s_pv(u):
                g, sb, hd, c0, diag = geom(u)
                ab = u % 3
                ob = obank(g, hd)
                mm(PB[ob][:, c0:512], vpad[:, sb, hd, :], Ab[ab][:, c0:512], sb == 4 * g + 3, False, [r_vpad, r_A[ab]], [r_PB[ob]])
                if sb == 0:
                    if hd == 0:
                        P.op("dve", lambda e: e.reciprocal(out=rinv[0:64, :], in_=PB[ob][64:128, :]), [r_PB[ob]], [r_rinv])
                        tt("dve", bT[0:64, hp, g * 512:(g + 1) * 512], PB[ob][0:64, :], rinv[0:64, :], ALU.mult, [r_PB[ob], r_rinv], [r_R2])
                    else:
                        P.op("dve", lambda e: e.reciprocal(out=rinv[64:128, :], in_=PB[ob][0:64, :]), [r_PB[ob]], [r_rinv])
                        tt("dve", bT[64:128, hp, g * 512:(g + 1) * 512], PB[ob][64:128, :], rinv[64:128, :], ALU.mult, [r_PB[ob], r_rinv], [r_R2])

            pipeline(n, [s_z, s_a, s_pv])

        def ffn(l):
            prenorm(ffn_pre_g[l], aT, r_R1)
            dma(Gpost[:], ffn_post_g[l].partition_broadcast(128), r_Gpost, [], [r_Gpost])
            P.alias([r_R2], r_gT + r_wup)
            P.alias(r_qa + r_ka + [r_vpad], r_wdn)
            for tg in range(NG):
                def load_up(c):
                    dma(wupb[c % 2][:], ws_up[l, c].rearrange("p (g k j) -> p g k j", g=2, k=8), r_wup[c % 2], r_scr, [r_wup[c % 2]])
                load_up(0)
                for c in range(NCH):
                    if c + 1 < NCH:
                        load_up(c + 1)
                    for gu in range(2):
                        bk = 2 * (c % 2) + gu
                        for kc in range(8):
                            mm(PB[bk][:], wupb[c % 2][:, gu, kc, :], aT[:, kc, tg * 512:(tg + 1) * 512], kc == 0, kc == 7,
                               [r_wup[c % 2], r_R1], [r_PB[bk]])
                    for gu in range(2):
                        bk = 2 * (c % 2) + gu
                        ci = gu * NCH + c
                        hr, rh = hraw[gu], r_hraw[gu]
                        if tg == 0:
                            memset("pool", hr[:, 2:4], 0.0, [rh])
                        else:
                            tcopy("pool", hr[:, 2:4], halo[:, ci, :], [r_halo], [rh])
                        act(hr[:, 4:516], PB[bk][:], AF.Copy, [r_PB[bk]], [rh])
                        tcopy("pool", halo[:, ci, :], hr[:, 514:516], [rh], [r_halo])
                        act(cacc[gu][:], PB[bk][:], AF.Identity, [r_PB[bk], r_CW[l]], [r_cacc[gu]],
                            scale=CW[l][:, 2, ci:ci + 1], bias=CW[l][:, 3, ci:ci + 1])
                        stt("dve", cacc[gu][:], hr[:, 3:515], CW[l][:, 1, ci:ci + 1], cacc[gu][:], ALU.mult, ALU.add,
                            [rh, r_CW[l], r_cacc[gu]], [r_cacc[gu]])
                        stt("dve", cacc[gu][:], hr[:, 2:514], CW[l][:, 0, ci:ci + 1], cacc[gu][:], ALU.mult, ALU.add,
                            [rh, r_CW[l], r_cacc[gu]], [r_cacc[gu]])
                    act(gl[:], cacc[0][:], AF.Gelu_apprx_tanh, [r_cacc[0]], [r_gl])
                    tt("pool", gT[:, c, :], gl[:], cacc[1][:], ALU.mult, [r_gl, r_cacc[1]], [r_gT[c]])
                for half in range(2):
                    def load_dn(c):
                        dma(wdnb[c % 3], ws_dn[l, c], r_wdn[c % 3], r_scr, [r_wdn[c % 3]])
                    load_dn(0)
                    load_dn(1)
                    for c in range(NCH):
                        if c + 2 < NCH:
                            load_dn(c + 2)
                        for t2 in range(2):
                            tl = 2 * half + t2
                            for nn in range(2):
                                bk = 2 * t2 + nn
                                mm(PB[bk][:], gT[:, c, tl * 128:(tl + 1) * 128], wdnb[c % 3][:, nn * 512:(nn + 1) * 512], c == 0, c == NCH - 1,
                                   [r_gT[c], r_wdn[c % 3]], [r_PB[bk]])
                    for t2 in range(2):
                        postnorm_residual(tg * 4 + 2 * half + t2, 2 * t2, 2 * t2 + 1, r_Gpost)
            P.alias(r_gT + r_wup, [r_R2])
            P.alias(r_wdn, r_qa + r_ka + [r_vpad])

        for b in range(NB):
            for t in range(NT):
                dma(hview[:, t, :], x[b, t * 128:(t + 1) * 128, :], r_h[t], [], [r_h[t]])
            prenorm(sb_pre_g[0], aT, r_R1)
            for hp in range(8):
                load_pair_weights(ws_qkv[0], ws_qkv[1], ws_qkv[2], hp)

                def sink_q(tg, bank):
                    for hd in range(2):
                        ts("dve", qa[hd][0:64, tg * 512:(tg + 1) * 512], PB[bank][hd * 64:(hd + 1) * 64, :], 0.125, None, ALU.mult, None,
                           [r_PB[bank]], [r_qa[hd]])

                def sink_k(tg, bank):
                    for hd in range(2):
                        tcopy("dve", ka[hd][0:64, tg * 512:(tg + 1) * 512], PB[bank][hd * 64:(hd + 1) * 64, :], [r_PB[bank]], [r_ka[hd]])
                proj_featmajor(Wq, r_Wq, aT, r_R1, 6, sink_q)
                proj_featmajor(Wk, r_Wk, aT, r_R1, 6, sink_k)
                proj_v(aT, r_R1, 6, 0.0)
                attn_sb_pair(hp, 4)
            out_proj(0, sb_post_g[0], bT, r_R2)
            ffn(0)
            prenorm(fox_pre_g[0], aT, r_R1)
            for hp in range(8):
                dma(Wq[:], ws_qkv[3][hp].rearrange("p (k j) -> p k j", j=128), r_Wq, r_scr, [r_Wq])

                def sink_qall(tg, bank, hp=hp):
                    ts("dve", bT[:, hp, tg * 512:(tg + 1) * 512], PB[bank][:], 0.125, None, ALU.mult, None, [r_PB[bank]], [r_R2])
                proj_featmajor(Wq, r_Wq, aT, r_R1, 6, sink_qall)
            prenorm(kv_norm_g, aT, r_R1)
            dma(Wf[:], ws_f.rearrange("p (k j) -> p k j", j=16), r_Wf, r_scr, [r_Wf])
            memset("pool", E2[1][0:16, 0:512], 1.0, [r_E2[1]])
            for tg in range(NG):
                seg = slice(tg * 512, (tg + 1) * 512)
                for kc in range(8):
                    mm(PB[6][0:16, :], Wf[:, kc, :], aT[:, kc, seg], kc == 0, kc == 7, [r_Wf, r_R1], [r_PB[6]])
                act(e32[0][0:16, 0:512], PB[6][0:16, :], AF.Exp, [r_PB[6], r_bf], [r_e32[0]], scale=-1.0, bias=bfneg[:, 0:1])
                act(e32[1][0:16, 0:512], e32[0][0:16, 0:512], AF.Ln, [r_e32[0]], [r_e32[1]], bias=1.0)
                if tg == 0:
                    P.op("dve", lambda e: e.tensor_tensor_scan(out=E2[0][0:16, 0:512], data0=E2[1][0:16, 0:512], data1=e32[1][0:16, 0:512],
                                                               initial=0.0, op0=ALU.mult, op1=ALU.add),
                         [r_E2[1], r_e32[1]], [r_E2[0]])
                else:
                    P.op("dve", lambda e: e.tensor_tensor_scan(out=E2[0][0:16, 0:512], data0=E2[1][0:16, 0:512], data1=e32[1][0:16, 0:512],
                                                               initial=carry[:, 0:1], op0=ALU.mult, op1=ALU.add),
                         [r_E2[1], r_e32[1], r_carry], [r_E2[0]])
                tcopy("dve", carry[:, 0:1], E2[0][0:16, 511:512], [r_E2[0]], [r_carry])
                tcopy("dve", frow[0][:, seg], E2[0][0:16, 0:512], [r_E2[0]], [r_frow])
                tt("dve", Ls32[0][0:16, :], E2[0][0:16, 0:512], frow[0][:, seg], ALU.subtract, [r_E2[0], r_frow], [r_Ls32[0]])
                tcopy("dve", frow[1][:, seg], Ls32[0][0:16, :], [r_Ls32[0]], [r_frow])
                ts("dve", frow[2][:, seg], E2[0][0:16, 0:512], -1.0, None, ALU.mult, None, [r_E2[0]], [r_frow])
                ts("dve", frow[3][:, seg], Ls32[0][0:16, :], -1.0, None, ALU.mult, None, [r_Ls32[0]], [r_frow])
            for hd in range(2):
                memset("pool", qa[hd][64:68, :], 1.0, [r_qa[hd]])
                memset("pool", ka[hd][64:68, :], 1.0, [r_ka[hd]])
            for hp in range(8):
                load_pair_weights(None, ws_kv[0], ws_kv[1], hp)
                for hd in range(2):
                    hh = 2 * hp + hd
                    tcopy("pool", qa[hd][0:64, :], bT[hd * 64:(hd + 1) * 64, hp, :], [r_R2], [r_qa[hd]])
                    dma(qa[hd][64:65, :], frow[2][hh:hh + 1, :], r_qa[hd], [r_frow], [r_qa[hd]])
                    dma(qa[hd][65:66, :], frow[3][hh:hh + 1, :], r_qa[hd], [r_frow], [r_qa[hd]])
                    dma(ka[hd][66:67, :], frow[0][hh:hh + 1, :], r_ka[hd], [r_frow], [r_ka[hd]])
                    dma(ka[hd][67:68, :], frow[1][hh:hh + 1, :], r_ka[hd], [r_frow], [r_ka[hd]])

                def sink_k1(tg, bank):
                    for hd in range(2):
                        tcopy("dve", ka[hd][0:64, tg * 512:(tg + 1) * 512], PB[bank][hd * 64:(hd + 1) * 64, :], [r_PB[bank]], [r_ka[hd]])
                proj_featmajor(Wk, r_Wk, aT, r_R1, 6, sink_k1)
                proj_v(aT, r_R1, 6, 1.0)
                attn_fox_pair(hp)
            out_proj(1, fox_post_g[0], bT, r_R2)
            ffn(1)
            for t in range(NT):
                dma(y[b, t * 128:(t + 1) * 128, :], hview[:, t, :], r_h[t], [r_h[t]], [])
        P.wait_all("sp", r_h)
        P.emit()
    return nc


_NC_CACHE = {}


def kernel(**inputs):
    x = np.ascontiguousarray(inputs["x"], dtype=np.float32)
    B, S, _ = x.shape
    NB = B // N_CORES
    key = (NB, S)
    if key not in _NC_CACHE:
        _NC_CACHE[key] = build_nc(NB, S)
    nc = _NC_CACHE[key]
    wnames = ["sb_pre_g", "sb_w_qkv", "sb_w_o", "sb_post_g", "kv_norm_g", "w_kvf", "b_f", "fox_pre_g", "fox_w_q",
              "fox_w_o", "fox_post_g", "ffn_pre_g", "w_up", "conv_w", "conv_b", "w_down", "ffn_post_g"]
    ws = {k: np.ascontiguousarray(inputs[k], dtype=np.float32) for k in wnames}
    in_maps = []
    for c in range(N_CORES):
        m = {"x": x[c * NB:(c + 1) * NB]}
        m.update(ws)
        in_maps.append(m)
    res = run_bass_kernel_spmd(nc, in_maps, core_ids=list(range(N_CORES)))
    return np.concatenate([r["y"] for r in res.results], axis=0)
```

```python
from contextlib import ExitStack
import numpy as np
import concourse.bass as bass
import concourse.mybir as mybir
from concourse.bass_utils import run_bass_kernel_spmd

F32 = mybir.dt.float32
BF16 = mybir.dt.bfloat16
AF = mybir.ActivationFunctionType
ALU = mybir.AluOpType

D = 1024
H = 16
DH = 64
FF = 2816
NCH = FF // 128
EPS = 1e-6
N_CORES = 8


class Res:
    __slots__ = ("name", "lw", "rd", "sem", "semcnt")

    def __init__(self, name, sem=None):
        self.name = name
        self.lw = None
        self.rd = {}
        self.sem = sem
        self.semcnt = 0


class Prog:
    ENGS = ("pe", "act", "dve", "pool", "sp")

    def __init__(self, nc, ctx):
        self.nc = nc
        self.ctx = ctx
        self.streams = {e: [] for e in self.ENGS}
        self.count = {e: 0 for e in self.ENGS}
        self.waited = {e: {} for e in self.ENGS}
        self.esem = {e: ctx.enter_context(nc.semaphore("es_" + e)) for e in self.ENGS}
        self.nres = 0

    def res(self, name=None, dma=False):
        self.nres += 1
        name = name or ("r%d" % self.nres)
        sem = self.ctx.enter_context(self.nc.semaphore("ds%d" % self.nres)) if dma else None
        return Res(name, sem)

    def _deps(self, reads, writes):
        deps = []
        for r in reads:
            if r.lw is not None:
                deps.append(r.lw)
        for w in writes:
            if w.lw is not None:
                deps.append(w.lw)
            deps.extend(w.rd.items())
        return deps

    def _waits_for(self, eng, deps):
        best = {}
        for (key, val) in deps:
            if key == "pe" and eng == "pe":
                continue
            if val > best.get(key, 0):
                best[key] = val
        out = []
        wd = self.waited[eng]
        for key, val in best.items():
            if wd.get(key, 0) >= val:
                continue
            wd[key] = val
            out.append((key, val))
        return out

    def _sem_of(self, key):
        if isinstance(key, str):
            return self.esem[key]
        return key.sem

    def _record(self, ev, reads, writes):
        k, v = ev
        for r in reads:
            if r.rd.get(k, 0) < v:
                r.rd[k] = v
        for w in writes:
            w.lw = ev
            w.rd = {}

    def op(self, eng, fn, reads=(), writes=()):
        waits = self._waits_for(eng, self._deps(reads, writes))
        self.count[eng] += 1
        ev = (eng, self.count[eng])
        self.streams[eng].append((waits, fn, None))
        self._record(ev, reads, writes)
        return ev

    def dma(self, eng, fn, semres, reads=(), writes=()):
        waits = self._waits_for(eng, self._deps(reads, writes))
        semres.semcnt += 1
        ev = (semres, 16 * semres.semcnt)
        self.streams[eng].append((waits, fn, semres))
        self._record(ev, reads, writes)
        return ev

    def alias(self, olds, news):
        evs = {}
        for o in olds:
            if o.lw is not None:
                k, v = o.lw
                evs[k] = max(evs.get(k, 0), v)
            for k, v in o.rd.items():
                evs[k] = max(evs.get(k, 0), v)
        for n in news:
            for k, v in evs.items():
                if n.rd.get(k, 0) < v:
                    n.rd[k] = v

    def wait_all(self, eng, resources):
        deps = []
        for r in resources:
            if r.lw is not None:
                deps.append(r.lw)
            deps.extend(r.rd.items())
        waits = self._waits_for(eng, deps)
        self.streams[eng].append((waits, None, None))

    def emit(self):
        nc = self.nc
        with nc.Block() as block:
            def run(engname):
                def body(e):
                    esem = self.esem[engname]
                    for (waits, fn, semres) in self.streams[engname]:
                        for (key, val) in waits:
                            e.wait_ge(self._sem_of(key), val)
                        if fn is None:
                            continue
                        ins = fn(e)
                        if semres is None:
                            ins.then_inc(esem, 1)
                        else:
                            ins.then_inc(semres.sem, 16)
                return body
            block.tensor(run("pe"))
            block.scalar(run("act"))
            block.vector(run("dve"))
            block.gpsimd(run("pool"))
            block.sync(run("sp"))


def build_nc(NB, S):
    NT = S // 128
    NG = S // 512
    nc = bass.Bass("TRN2", target_bir_lowering=False)
    dt_in = lambda name, shape: nc.dram_tensor(name, list(shape), F32, kind="ExternalInput").ap()
    x = dt_in("x", [NB, S, D])
    sb_pre_g = dt_in("sb_pre_g", [1, D])
    sb_w_qkv = dt_in("sb_w_qkv", [1, D, 3 * D])
    sb_w_o = dt_in("sb_w_o", [1, D, D])
    sb_post_g = dt_in("sb_post_g", [1, D])
    kv_norm_g = dt_in("kv_norm_g", [D])
    w_kvf = dt_in("w_kvf", [D, 2 * D + H])
    b_f = dt_in("b_f", [H])
    fox_pre_g = dt_in("fox_pre_g", [1, D])
    fox_w_q = dt_in("fox_w_q", [1, D, D])
    fox_w_o = dt_in("fox_w_o", [1, D, D])
    fox_post_g = dt_in("fox_post_g", [1, D])
    ffn_pre_g = dt_in("ffn_pre_g", [2, D])
    w_up = dt_in("w_up", [2, D, 2 * FF])
    conv_w = dt_in("conv_w", [2, 3, 2 * FF])
    conv_b = dt_in("conv_b", [2, 2 * FF])
    w_down = dt_in("w_down", [2, FF, D])
    ffn_post_g = dt_in("ffn_post_g", [2, D])
    y = nc.dram_tensor("y", [NB, S, D], F32, kind="ExternalOutput").ap()

    ws_qkv = nc.dram_tensor("ws_qkv", [5, 8, 128, 1024], BF16).ap()
    ws_kv = nc.dram_tensor("ws_kv", [2, 8, 128, 1024], BF16).ap()
    ws_f = nc.dram_tensor("ws_f", [128, 8 * 16], BF16).ap()
    ws_o = nc.dram_tensor("ws_o", [2, 128, 8192], BF16).ap()
    ws_up = nc.dram_tensor("ws_up", [2, NCH, 128, 2048], BF16).ap()
    ws_dn = nc.dram_tensor("ws_dn", [2, NCH, 128, 1024], BF16).ap()

    with ExitStack() as ctx:
        P = Prog(nc, ctx)
        sbt = lambda name, shape, dt: ctx.enter_context(nc.sbuf_tensor(name, list(shape), dt))
        pst = lambda name, shape, dt: ctx.enter_context(nc.psum_tensor(name, list(shape), dt))

        Hbuf = sbt("Hbuf", [128, NT * D], F32)
        hview = Hbuf[:].rearrange("p (t d) -> p t d", d=D)
        r_h = [P.res("h%d" % t, dma=True) for t in range(NT)]
        R1N = max(8 * S, 16384)
        R2N = max(8 * S, 16384)
        R3N = max(4 * S + NT * 256, 8192)
        R1 = sbt("R1", [128, R1N], BF16)
        R2 = sbt("R2", [128, R2N], BF16)
        R3 = sbt("R3", [128, R3N], BF16)
        r_R1 = P.res("R1")
        r_R2 = P.res("R2")
        aT = R1[:, 0:8 * S].rearrange("p (k s) -> p k s", s=S)
        bT = R2[:, 0:8 * S].rearrange("p (k s) -> p k s", s=S)
        qa = [R3[:, 0:S], R3[:, S:2 * S]]
        ka = [R3[:, 2 * S:3 * S], R3[:, 3 * S:4 * S]]
        vpad = R3[:, 4 * S:4 * S + NT * 256].rearrange("p (t h c) -> p t h c", h=2, c=128)
        r_qa = [P.res("qa0", dma=True), P.res("qa1", dma=True)]
        r_ka = [P.res("ka0", dma=True), P.res("ka1", dma=True)]
        r_vpad = P.res("vpad")
        r_qrow = [[P.res("qrow%d%d" % (i, j), dma=True) for j in range(2)] for i in range(2)]
        r_krow = [[P.res("krow%d%d" % (i, j), dma=True) for j in range(2)] for i in range(2)]
        r_attn_all = r_qa + r_ka + [r_vpad] + r_qrow[0] + r_qrow[1] + r_krow[0] + r_krow[1]
        WoT = R3[:, 0:8192].rearrange("p (k n) -> p k n", n=D)
        r_Wo = P.res("Wo", dma=True)
        gT = R2[:, 0:NCH * 512].rearrange("p (c t) -> p c t", t=512)
        r_gT = [P.res("gT%d" % c) for c in range(NCH)]
        wupb = [R2[:, NCH * 512 + i * 2048: NCH * 512 + (i + 1) * 2048].rearrange("p (g k j) -> p g k j", g=2, k=8) for i in range(2)]
        r_wup = [P.res("wup%d" % i, dma=True) for i in range(2)]
        wdnb = [R3[:, i * 1024:(i + 1) * 1024] for i in range(3)]
        r_wdn = [P.res("wdn%d" % i, dma=True) for i in range(3)]

        Wq = sbt("Wq", [128, 8, 128], BF16); r_Wq = P.res("Wq", dma=True)
        Wk = sbt("Wk", [128, 8, 128], BF16); r_Wk = P.res("Wk", dma=True)
        Wv = sbt("Wv", [128, 8, 128], BF16); r_Wv = P.res("Wv", dma=True)
        Wf = sbt("Wf", [128, 8, 16], BF16); r_Wf = P.res("Wf", dma=True)
        e32 = [sbt("e32_%d" % i, [128, 516], F32) for i in range(4)]; r_e32 = [P.res() for _ in range(4)]
        spb = [sbt("spb_%d" % i, [128, 512], BF16) for i in range(4)]; r_sp = [P.res() for _ in range(4)]
        E2 = [sbt("E2_%d" % i, [128, 512], F32) for i in range(2)]; r_E2 = [P.res() for _ in range(2)]
        Ab = [sbt("Ab_%d" % i, [128, 512], BF16) for i in range(3)]; r_A = [P.res() for _ in range(3)]
        Ls32 = [sbt("Ls32_%d" % i, [128, 512], F32) for i in range(1)]; r_Ls32 = [P.res() for _ in range(1)]
        rinv = sbt("rinv", [128, 512], F32); r_rinv = P.res()
        junk = sbt("junk", [128, 1024], BF16); r_junk = P.res()
        xn = sbt("xn", [128, 1024], BF16); r_xn = P.res()
        tmpf = sbt("tmpf", [128, 1024], F32); r_tmpA = P.res(); r_tmpB = P.res()
        Gpre = sbt("Gpre", [128, 1024], F32); r_Gpre = P.res("Gpre", dma=True)
        Gpost = Gpre; r_Gpost = r_Gpre
        st = sbt("stats", [128, 8], F32); r_st = P.res()
        pst_ = sbt("pstats", [128, 3 * 16], F32); r_pst = P.res()
        frowA = sbt("frowA", [80, S], BF16)
        frowB = sbt("frowB", [16, S], BF16)
        frow = [frowA[0:16, :], frowA[32:48, :], frowA[64:80, :], frowB[0:16, :]]
        r_frow = P.res()
        carry = sbt("carry", [16, 1], F32); r_carry = P.res()
        bfneg = sbt("bfneg", [16, 1], F32); r_bf = P.res("bf", dma=True)
        CW = [sbt("CW%d" % l, [128, 4, 2 * NCH], F32) for l in range(2)]; r_CW = [P.res("CW%d" % l, dma=True) for l in range(2)]
        hraw = e32; r_hraw = r_e32
        cacc = [E2[0][:, :], E2[1][:, :], tmpf[:, 0:512], tmpf[:, 512:1024]]; r_cacc = [r_E2[0], r_E2[1], r_tmpA, r_tmpB]
        gl = rinv; r_gl = r_rinv
        halo = sbt("halo", [128, 2 * NCH, 2], F32); r_halo = P.res()
        ident = sbt("ident", [128, 128], BF16); r_ident = P.res()
        mstrict = sbt("mstrict", [128, 128], BF16); r_mstrict = P.res()
        mincl = sbt("mincl", [128, 128], BF16); r_mincl = P.res()
        UI = sbt("UI", [128, 128], BF16); r_UI = P.res()
        Lc = sbt("Lc", [128, 128], BF16); r_Lc = P.res()
        PB = [pst("PB%d" % i, [128, 512], F32) for i in range(7)]; r_PB = [P.res("PB%d" % i) for i in range(7)]
        PT = pst("PT", [128, 1024], BF16); r_PT = P.res("PT")

        def mm(out, lhsT, rhs, start, stop, reads, writes):
            P.op("pe", lambda e: e.matmul(out, lhsT=lhsT, rhs=rhs, start=start, stop=stop, skip_group_check=True), reads, writes)

        def act(out, in_, func, reads, writes, **kw):
            P.op("act", lambda e: e.activation(out=out, in_=in_, func=func, **kw), reads, writes)

        def tcopy(eng, out, in_, reads, writes):
            P.op(eng, lambda e: e.tensor_copy(out=out, in_=in_), reads, writes)

        def tt(eng, out, in0, in1, op, reads, writes):
            P.op(eng, lambda e: e.tensor_tensor(out=out, in0=in0, in1=in1, op=op), reads, writes)

        def ts(eng, out, in0, s1, s2, op0, op1, reads, writes):
            if s2 is None:
                P.op(eng, lambda e: e.tensor_scalar(out=out, in0=in0, scalar1=s1, scalar2=None, op0=op0), reads, writes)
            else:
                P.op(eng, lambda e: e.tensor_scalar(out=out, in0=in0, scalar1=s1, scalar2=s2, op0=op0, op1=op1), reads, writes)

        def stt(eng, out, in0, scalar, in1, op0, op1, reads, writes):
            P.op(eng, lambda e: e.scalar_tensor_tensor(out=out, in0=in0, scalar=scalar, in1=in1, op0=op0, op1=op1), reads, writes)

        def memset(eng, ap, val, writes):
            P.op(eng, lambda e: e.memset(ap, val), (), writes)

        def dma(out, in_, semres, reads, writes, slow=False):
            if slow:
                P.dma("sp", lambda e: e.dma_start(out=out, in_=in_, allow_slow_non_contiguous=True), semres, reads, writes)
            else:
                P.dma("sp", lambda e: e.dma_start(out=out, in_=in_), semres, reads, writes)

        def aff(ap, pattern, cmp, cm, writes):
            P.op("pool", lambda e: e.affine_select(out=ap, in_=ap, pattern=pattern, compare_op=cmp, fill=0.0, base=0, channel_multiplier=cm), writes, writes)
        for (t_, r_) in ((ident, r_ident), (mstrict, r_mstrict), (mincl, r_mincl), (UI, r_UI), (Lc, r_Lc)):
            memset("pool", t_[:], 1.0, [r_])
        aff(ident[:], [[-1, 128]], ALU.is_equal, 1, [r_ident])
        aff(mstrict[:], [[1, 128]], ALU.is_gt, -1, [r_mstrict])
        aff(mincl[:], [[1, 128]], ALU.is_ge, -1, [r_mincl])
        aff(UI[:], [[-1, 128]], ALU.is_ge, 1, [r_UI])
        aff(Lc[:], [[1, 128]], ALU.is_gt, -1, [r_Lc])
        memset("pool", halo[:], 0.0, [r_halo])
        for l in range(2):
            for k in range(3):
                dma(CW[l][:, k, :], conv_w[l, k].rearrange("(c p) -> p c", p=128), r_CW[l], [], [r_CW[l]], slow=True)
            dma(CW[l][:, 3, :], conv_b[l].rearrange("(c p) -> p c", p=128), r_CW[l], [], [r_CW[l]], slow=True)
        dma(bfneg[:], b_f.rearrange("(h o) -> h o", o=1), r_bf, [], [r_bf], slow=True)
        ts("dve", bfneg[:], bfneg[:], -1.0, None, ALU.mult, None, [r_bf], [r_bf])

        NSLOT = 4 if NT * D >= 4 * 4096 else 2
        if NSLOT == 4:
            stg32 = [Hbuf[:, i * 4096:(i + 1) * 4096] for i in range(4)]
        else:
            stg32 = [R1[:, i * 8192:(i + 1) * 8192].bitcast(F32) for i in range(2)]
        stg16 = [R2[:, i * 4096:(i + 1) * 4096] for i in range(NSLOT)]
        r_s32 = [P.res("s32_%d" % i, dma=True) for i in range(NSLOT)]
        r_s16 = [P.res("s16_%d" % i) for i in range(NSLOT)]
        r_s16d = [P.res("s16d_%d" % i, dma=True) for i in range(NSLOT)]
        cast_engs = ["dve", "pool", "act"]
        ucount = [0]

        def cast_unit(srcs, E, dst, outview=None):
            i = ucount[0] % NSLOT
            eng = cast_engs[ucount[0] % 3]
            ucount[0] += 1
            for (vf, src) in srcs:
                dma(vf(stg32[i]), src, r_s32[i], [], [r_s32[i]])
            if eng == "act":
                act(stg16[i][:, 0:E], stg32[i][:, 0:E], AF.Copy, [r_s32[i]], [r_s16[i]])
            else:
                tcopy(eng, stg16[i][:, 0:E], stg32[i][:, 0:E], [r_s32[i]], [r_s16[i]])
            src16 = stg16[i][:, 0:E] if outview is None else outview(stg16[i][:, 0:E])
            dma(dst, src16, r_s16d[i], [r_s16[i]], [r_s16d[i]])

        def img4(stage, a, b, c):
            return stage[:, 0:a * b * c].rearrange("p (a b c) -> p a b c", a=a, b=b)

        def img3(stage, a, b):
            return stage[:, 0:a * b].rearrange("p (a b) -> p a b", a=a)

        def cast_cols(src2d, col0, dst_units):
            for half in range(2):
                srcs = []
                for hp4 in range(4):
                    hp = half * 4 + hp4
                    srcs.append((lambda s, hp4=hp4: img4(s, 4, 8, 128)[:, hp4],
                                 src2d[:, col0 + hp * 128: col0 + (hp + 1) * 128].rearrange("(k p) j -> p k j", p=128)))
                cast_unit(srcs, 4096, dst_units[half * 4:half * 4 + 4].rearrange("u p e -> p u e"),
                          outview=lambda v: v.rearrange("p (u e) -> p u e", u=4))
        cast_cols(sb_w_qkv[0], 0, ws_qkv[0])
        cast_cols(sb_w_qkv[0], D, ws_qkv[1])
        cast_cols(sb_w_qkv[0], 2 * D, ws_qkv[2])

        def cast_wo(src2d, dst):
            for half in range(2):
                srcs = [(lambda s: img3(s, 4, 1024),
                         src2d[half * 512:(half + 1) * 512, :].rearrange("(k p) n -> p k n", p=128))]
                cast_unit(srcs, 4096, dst[:, half * 4096:(half + 1) * 4096])
        cast_wo(sb_w_o[0], ws_o[0])

        def cast_ffn(l):
            for c0 in range(0, NCH, 2):
                srcs = []
                for gu in range(2):
                    for ci in range(2):
                        c = c0 + ci
                        srcs.append((lambda s, gu=gu, ci=ci: s[:, 0:4096].rearrange("p (c g k j) -> p c g k j", c=2, g=2, k=8)[:, ci, gu],
                                     w_up[l][:, gu * FF + c * 128: gu * FF + (c + 1) * 128].rearrange("(k p) j -> p k j", p=128)))
                cast_unit(srcs, 4096, ws_up[l, c0:c0 + 2].rearrange("c p e -> p c e"),
                          outview=lambda v: v.rearrange("p (c e) -> p c e", c=2))
            for c0 in range(0, NCH, 4):
                n = min(4, NCH - c0)
                srcs = [(lambda s, n=n: img3(s, n, 1024),
                         w_down[l][c0 * 128:(c0 + n) * 128, :].rearrange("(c p) n -> p c n", p=128))]
                cast_unit(srcs, n * 1024, ws_dn[l, c0:c0 + n].rearrange("c p e -> p c e"),
                          outview=lambda v, n=n: v.rearrange("p (c e) -> p c e", c=n))
        cast_ffn(0)
        cast_cols(w_kvf, 0, ws_kv[0])
        cast_cols(w_kvf, D, ws_kv[1])
        cast_unit([(lambda s: img3(s, 8, 16), w_kvf[:, 2 * D:2 * D + H].rearrange("(k p) j -> p k j", p=128))], 128, ws_f)
        cast_cols(fox_w_q[0], 0, ws_qkv[3])
        cast_wo(fox_w_o[0], ws_o[1])
        cast_ffn(1)
        r_scr = r_s16d
        P.alias(r_s32 + r_s16 + r_s16d, [r_R1, r_R2] + r_h)

        def prenorm(g_ap, dstT, r_dst, first_alias=None):
            dma(Gpre[:], g_ap.partition_broadcast(128), r_Gpre, [], [r_Gpre])
            for t in range(NT):
                act(junk[:], hview[:, t, :], AF.Square, [r_h[t]], [r_junk, r_pst], accum_out=pst_[:, t:t + 1])
            act(pst_[:, 16:16 + NT], pst_[:, 0:NT], AF.Ln, [r_pst], [r_pst], scale=1.0 / D, bias=EPS)
            act(pst_[:, 32:32 + NT], pst_[:, 16:16 + NT], AF.Exp, [r_pst], [r_pst], scale=-0.5)
            for t in range(NT):
                stt("dve", xn[:], hview[:, t, :], pst_[:, 32 + t:33 + t], Gpre[:], ALU.mult, ALU.mult, [r_h[t], r_pst, r_Gpre], [r_xn])
                for kc in range(8):
                    P.op("pe", lambda e, kc=kc: e.transpose(PT[:, kc * 128:(kc + 1) * 128], xn[:, kc * 128:(kc + 1) * 128], ident[:]),
                         [r_xn, r_ident], [r_PT])
                tcopy("dve", dstT[:, :, t * 128:(t + 1) * 128], PT[:].rearrange("p (k j) -> p k j", j=128), [r_PT], [r_dst])

        def postnorm_residual(t, ba, bb, g_res):
            act(junk[:, 0:512], PB[ba][:], AF.Square, [r_PB[ba]], [r_junk, r_st], accum_out=st[:, 3:4])
            act(junk[:, 512:1024], PB[bb][:], AF.Square, [r_PB[bb]], [r_junk, r_st], accum_out=st[:, 4:5])
            tt("dve", st[:, 5:6], st[:, 3:4], st[:, 4:5], ALU.add, [r_st], [r_st])
            act(st[:, 6:7], st[:, 5:6], AF.Ln, [r_st], [r_st], scale=1.0 / D, bias=EPS)
            act(st[:, 7:8], st[:, 6:7], AF.Exp, [r_st], [r_st], scale=-0.5)
            stt("dve", tmpf[:, 0:512], PB[ba][:], st[:, 7:8], Gpost[:, 0:512], ALU.mult, ALU.mult, [r_PB[ba], r_st, g_res], [r_tmpA])
            stt("dve", tmpf[:, 512:1024], PB[bb][:], st[:, 7:8], Gpost[:, 512:1024], ALU.mult, ALU.mult, [r_PB[bb], r_st, g_res], [r_tmpB])
            tt("pool", hview[:, t, :], hview[:, t, :], tmpf[:], ALU.add, [r_h[t], r_tmpA, r_tmpB], [r_h[t]])

        def out_proj(widx, g_ap, srcT, r_src):
            P.alias(r_attn_all, [r_Wo])
            dma(WoT[:, 0:4, :], ws_o[widx][:, 0:4096].rearrange("p (k n) -> p k n", n=D), r_Wo, r_scr, [r_Wo])
            dma(WoT[:, 4:8, :], ws_o[widx][:, 4096:8192].rearrange("p (k n) -> p k n", n=D), r_Wo, r_scr, [r_Wo])
            dma(Gpost[:], g_ap.partition_broadcast(128), r_Gpost, [], [r_Gpost])
            for t in range(NT):
                ba, bb = (0, 1) if t % 2 == 0 else (2, 3)
                for n, bk in ((0, ba), (1, bb)):
                    for kc in range(8):
                        mm(PB[bk][:], srcT[:, kc, t * 128:(t + 1) * 128], WoT[:, kc, n * 512:(n + 1) * 512],
                           kc == 0, kc == 7, [r_src, r_Wo], [r_PB[bk]])
                postnorm_residual(t, ba, bb, r_Gpost)
            P.alias([r_Wo], r_attn_all)

        def pipeline(n, stages):
            ns = len(stages)
            for tick in range(n + ns - 1):
                for k in range(ns - 1, -1, -1):
                    i = tick - k
                    if 0 <= i < n:
                        stages[k](i)

        def load_pair_weights(qsrc, ksrc, vsrc, hp):
            if qsrc is not None:
                dma(Wq[:], qsrc[hp].rearrange("p (k j) -> p k j", j=128), r_Wq, r_scr, [r_Wq])
            dma(Wk[:], ksrc[hp].rearrange("p (k j) -> p k j", j=128), r_Wk, r_scr, [r_Wk])
            dma(Wv[:], vsrc[hp].rearrange("p (k j) -> p k j", j=128), r_Wv, r_scr, [r_Wv])

        def proj_featmajor(Wt, r_W, srcT, r_src, bank, sink):
            for tg in range(NG):
                for kc in range(8):
                    mm(PB[bank][:], Wt[:, kc, :], srcT[:, kc, tg * 512:(tg + 1) * 512], kc == 0, kc == 7, [r_W, r_src], [r_PB[bank]])
                sink(tg, bank)

        def proj_v(srcT, r_src, bank, padval):
            memset("pool", vpad[:, :, 0, 64:128], padval, [r_vpad])
            memset("pool", vpad[:, :, 1, 0:64], padval, [r_vpad])
            for t4 in range(0, NT, 4):
                for ti in range(4):
                    t = t4 + ti
                    for kc in range(8):
                        mm(PB[bank][:, ti * 128:(ti + 1) * 128], srcT[:, kc, t * 128:(t + 1) * 128], Wv[:, kc, :], kc == 0, kc == 7,
                           [r_src, r_Wv], [r_PB[bank]])
                pv = PB[bank][:].rearrange("p (t c) -> p t c", c=128)
                tcopy("dve", vpad[:, t4:t4 + 4, 0, 0:64], pv[:, :, 0:64], [r_PB[bank]], [r_vpad])
                tcopy("dve", vpad[:, t4:t4 + 4, 1, 64:128], pv[:, :, 64:128], [r_PB[bank]], [r_vpad])

        def attn_sb_pair(hp, obank_base):
            units = []
            for g in range(NG):
                for sb in range(4 * g + 3, -1, -1):
                    for hd in range(2):
                        units.append((g, sb, hd))
            n = len(units)

            def geom(u):
                g, sb, hd = units[u]
                c0 = max(sb * 128, g * 512) - g * 512
                return g, sb, hd, c0, (sb >= 4 * g)

            def s_z(u):
                g, sb, hd, c0, diag = geom(u)
                zb = u % 2
                mm(PB[zb][:, c0:512], ka[hd][0:64, sb * 128:(sb + 1) * 128], qa[hd][0:64, g * 512 + c0:(g + 1) * 512], True, True,
                   [r_ka[hd], r_qa[hd]], [r_PB[zb]])

            def s_esp(u):
                g, sb, hd, c0, diag = geom(u)
                zb = u % 2
                eb = u % 4
                act(e32[eb][:, c0:512], PB[zb][:, c0:512], AF.Exp, [r_PB[zb]], [r_e32[eb]])
                act(spb[eb][:, c0:512], e32[eb][:, c0:512], AF.Ln, [r_e32[eb]], [r_sp[eb]], bias=1.0)
                if diag:
                    tt("pool", spb[eb][:, c0:c0 + 128], spb[eb][:, c0:c0 + 128], mstrict[:], ALU.mult, [r_sp[eb], r_mstrict], [r_sp[eb]])

            def s_x(u):
                g, sb, hd, c0, diag = geom(u)
                eb = u % 4
                xb = 2 + hd
                mm(PB[xb][:, c0:512], UI[:], spb[eb][:, c0:512], sb == 4 * g + 3, False, [r_UI, r_sp[eb]], [r_PB[xb]])

            def s_a(u):
                g, sb, hd, c0, diag = geom(u)
                eb = u % 4
                xb = 2 + hd
                ab = u % 3
                act(E2[hd][:, c0:512], PB[xb][:, c0:512], AF.Exp, [r_PB[xb]], [r_E2[hd]], scale=-1.0)
                if sb > 0:
                    mm(PB[xb][:, c0:512], Lc[:], spb[eb][:, c0:512], False, False, [r_Lc, r_sp[eb]], [r_PB[xb]])
                tt("dve", Ab[ab][:, c0:512], e32[eb][:, c0:512], E2[hd][:, c0:512], ALU.mult, [r_e32[eb], r_E2[hd]], [r_A[ab]])
                if diag:
                    tt("pool", Ab[ab][:, c0:c0 + 128], Ab[ab][:, c0:c0 + 128], mstrict[:], ALU.mult, [r_A[ab], r_mstrict], [r_A[ab]])

            def s_pv(u):
                g, sb, hd, c0, diag = geom(u)
                ab = u % 3
                ob = obank_base + (g % 2)
                firstm = (sb == 4 * g + 3 and hd == 0)
                mm(PB[ob][:, c0:512], vpad[:, sb, hd, :], Ab[ab][:, c0:512], firstm, False, [r_vpad, r_A[ab]], [r_PB[ob]])
                if sb == 0 and hd == 1:
                    tcopy("dve", bT[:, hp, g * 512:(g + 1) * 512], PB[ob][:], [r_PB[ob]], [r_R2])

            pipeline(n, [s_z, s_esp, s_x, s_a, s_pv])

        def attn_fox_pair(hp):
            units = []
            for g in range(NG):
                for sb in range(4 * g + 3, -1, -1):
                    for hd in range(2):
                        units.append((g, sb, hd))
            n = len(units)

            def geom(u):
                g, sb, hd = units[u]
                c0 = max(sb * 128, g * 512) - g * 512
                return g, sb, hd, c0, (sb >= 4 * g)

            def obank(g, hd):
                return (4 + hd) if g % 2 == 0 else (2 + hd)

            def s_z(u):
                g, sb, hd, c0, diag = geom(u)
                zb = u % 2
                mm(PB[zb][:, c0:512], ka[hd][0:68, sb * 128:(sb + 1) * 128], qa[hd][0:68, g * 512 + c0:(g + 1) * 512], True, True,
                   [r_ka[hd], r_qa[hd]] + r_qrow[hd] + r_krow[hd], [r_PB[zb]])

            def s_a(u):
                g, sb, hd, c0, diag = geom(u)
                zb = u % 2
                ab = u % 3
                act(Ab[ab][:, c0:512], PB[zb][:, c0:512], AF.Exp, [r_PB[zb]], [r_A[ab]])
                if diag:
                    tt("pool", Ab[ab][:, c0:c0 + 128], Ab[ab][:, c0:c0 + 128], mincl[:], ALU.mult, [r_A[ab], r_mincl], [r_A[ab]])

            def s_pv(u):
                g, sb, hd, c0, diag = geom(u)
                ab = u % 3
                ob = obank(g, hd)
                mm(PB[ob][:, c0:512], vpad[:, sb, hd, :], Ab[ab][:, c0:512], sb == 4 * g + 3, False, [r_vpad, r_A[ab]], [r_PB[ob]])
                if sb == 0:
                    if hd == 0:
                        P.op("dve", lambda e: e.reciprocal(out=rinv[0:64, :], in_=PB[ob][64:128, :]), [r_PB[ob]], [r_rinv])
                        tt("dve", bT[0:64, hp, g * 512:(g + 1) * 512], PB[ob][0:64, :], rinv[0:64, :], ALU.mult, [r_PB[ob], r_rinv], [r_R2])
                    else:
                        P.op("dve", lambda e: e.reciprocal(out=rinv[64:128, :], in_=PB[ob][0:64, :]), [r_PB[ob]], [r_rinv])
                        tt("dve", bT[64:128, hp, g * 512:(g + 1) * 512], PB[ob][64:128, :], rinv[64:128, :], ALU.mult, [r_PB[ob], r_rinv], [r_R2])

            pipeline(n, [s_z, s_a, s_pv])

        def ffn(l):
            prenorm(ffn_pre_g[l], aT, r_R1)
            dma(Gpost[:], ffn_post_g[l].partition_broadcast(128), r_Gpost, [], [r_Gpost])
            P.alias([r_R2], r_gT + r_wup)
            P.alias(r_attn_all, r_wdn)
            for tg in range(NG):
                def load_up(c):
                    dma(wupb[c % 2][:], ws_up[l, c].rearrange("p (g k j) -> p g k j", g=2, k=8), r_wup[c % 2], r_scr, [r_wup[c % 2]])
                load_up(0)
                for c in range(NCH):
                    if c + 1 < NCH:
                        load_up(c + 1)
                    for gu in range(2):
                        bk = 2 * (c % 2) + gu
                        for kc in range(8):
                            mm(PB[bk][:], wupb[c % 2][:, gu, kc, :], aT[:, kc, tg * 512:(tg + 1) * 512], kc == 0, kc == 7,
                               [r_wup[c % 2], r_R1], [r_PB[bk]])
                    for gu in range(2):
                        bk = 2 * (c % 2) + gu
                        ci = gu * NCH + c
                        hb = 2 * (c % 2) + gu
                        hr, rh = hraw[hb], r_hraw[hb]
                        ca, rca = cacc[hb], r_cacc[hb]
                        if tg == 0:
                            memset("pool", hr[:, 2:4], 0.0, [rh])
                        else:
                            tcopy("pool", hr[:, 2:4], halo[:, ci, :], [r_halo], [rh])
                        act(hr[:, 4:516], PB[bk][:], AF.Copy, [r_PB[bk]], [rh])
                        tcopy("pool", halo[:, ci, :], hr[:, 514:516], [rh], [r_halo])
                        act(ca, PB[bk][:], AF.Identity, [r_PB[bk], r_CW[l]], [rca],
                            scale=CW[l][:, 2, ci:ci + 1], bias=CW[l][:, 3, ci:ci + 1])
                        stt("dve", ca, hr[:, 3:515], CW[l][:, 1, ci:ci + 1], ca, ALU.mult, ALU.add, [rh, r_CW[l], rca], [rca])
                        stt("dve", ca, hr[:, 2:514], CW[l][:, 0, ci:ci + 1], ca, ALU.mult, ALU.add, [rh, r_CW[l], rca], [rca])
                    def gate_mul(cc):
                        cg, cu = 2 * (cc % 2), 2 * (cc % 2) + 1
                        act(gl[:], cacc[cg], AF.Gelu_apprx_tanh, [r_cacc[cg]], [r_gl])
                        tt("pool", gT[:, cc, :], gl[:], cacc[cu], ALU.mult, [r_gl, r_cacc[cu]], [r_gT[cc]])
                    if c > 0:
                        gate_mul(c - 1)
                    if c == NCH - 1:
                        gate_mul(c)
                for half in range(2):
                    def load_dn(c):
                        dma(wdnb[c % 3], ws_dn[l, c], r_wdn[c % 3], r_scr, [r_wdn[c % 3]])
                    load_dn(0)
                    load_dn(1)
                    for c in range(NCH):
                        if c + 2 < NCH:
                            load_dn(c + 2)
                        for t2 in range(2):
                            tl = 2 * half + t2
                            for nn in range(2):
                                bk = 2 * t2 + nn
                                mm(PB[bk][:], gT[:, c, tl * 128:(tl + 1) * 128], wdnb[c % 3][:, nn * 512:(nn + 1) * 512], c == 0, c == NCH - 1,
                                   [r_gT[c], r_wdn[c % 3]], [r_PB[bk]])
                    for t2 in range(2):
                        postnorm_residual(tg * 4 + 2 * half + t2, 2 * t2, 2 * t2 + 1, r_Gpost)
            P.alias(r_gT + r_wup, [r_R2])
            P.alias(r_wdn, r_attn_all)

        for b in range(NB):
            for t in range(NT):
                dma(hview[:, t, :], x[b, t * 128:(t + 1) * 128, :], r_h[t], [], [r_h[t]])
            prenorm(sb_pre_g[0], aT, r_R1)
            for hp in range(8):
                load_pair_weights(ws_qkv[0], ws_qkv[1], ws_qkv[2], hp)

                def sink_q(tg, bank):
                    for hd in range(2):
                        ts("dve", qa[hd][0:64, tg * 512:(tg + 1) * 512], PB[bank][hd * 64:(hd + 1) * 64, :], 0.125, None, ALU.mult, None,
                           [r_PB[bank]], [r_qa[hd]])

                def sink_k(tg, bank):
                    for hd in range(2):
                        tcopy("dve", ka[hd][0:64, tg * 512:(tg + 1) * 512], PB[bank][hd * 64:(hd + 1) * 64, :], [r_PB[bank]], [r_ka[hd]])
                proj_featmajor(Wq, r_Wq, aT, r_R1, 6, sink_q)
                proj_featmajor(Wk, r_Wk, aT, r_R1, 6, sink_k)
                proj_v(aT, r_R1, 6, 0.0)
                attn_sb_pair(hp, 4)
            out_proj(0, sb_post_g[0], bT, r_R2)
            ffn(0)
            prenorm(fox_pre_g[0], aT, r_R1)
            for hp in range(8):
                dma(Wq[:], ws_qkv[3][hp].rearrange("p (k j) -> p k j", j=128), r_Wq, r_scr, [r_Wq])

                def sink_qall(tg, bank, hp=hp):
                    ts("dve", bT[:, hp, tg * 512:(tg + 1) * 512], PB[bank][:], 0.125, None, ALU.mult, None, [r_PB[bank]], [r_R2])
                proj_featmajor(Wq, r_Wq, aT, r_R1, 6, sink_qall)
            prenorm(kv_norm_g, aT, r_R1)
            dma(Wf[:], ws_f.rearrange("p (k j) -> p k j", j=16), r_Wf, r_scr, [r_Wf])
            memset("pool", E2[1][0:16, 0:512], 1.0, [r_E2[1]])
            for tg in range(NG):
                seg = slice(tg * 512, (tg + 1) * 512)
                for kc in range(8):
                    mm(PB[6][0:16, :], Wf[:, kc, :], aT[:, kc, seg], kc == 0, kc == 7, [r_Wf, r_R1], [r_PB[6]])
                act(e32[0][0:16, 0:512], PB[6][0:16, :], AF.Exp, [r_PB[6], r_bf], [r_e32[0]], scale=-1.0, bias=bfneg[:, 0:1])
                act(e32[1][0:16, 0:512], e32[0][0:16, 0:512], AF.Ln, [r_e32[0]], [r_e32[1]], bias=1.0)
                if tg == 0:
                    P.op("dve", lambda e: e.tensor_tensor_scan(out=E2[0][0:16, 0:512], data0=E2[1][0:16, 0:512], data1=e32[1][0:16, 0:512],
                                                               initial=0.0, op0=ALU.mult, op1=ALU.add),
                         [r_E2[1], r_e32[1]], [r_E2[0]])
                else:
                    P.op("dve", lambda e: e.tensor_tensor_scan(out=E2[0][0:16, 0:512], data0=E2[1][0:16, 0:512], data1=e32[1][0:16, 0:512],
                                                               initial=carry[:, 0:1], op0=ALU.mult, op1=ALU.add),
                         [r_E2[1], r_e32[1], r_carry], [r_E2[0]])
                tcopy("dve", carry[:, 0:1], E2[0][0:16, 511:512], [r_E2[0]], [r_carry])
                tcopy("dve", frow[0][:, seg], E2[0][0:16, 0:512], [r_E2[0]], [r_frow])
                tt("dve", Ls32[0][0:16, :], E2[0][0:16, 0:512], frow[0][:, seg], ALU.subtract, [r_E2[0], r_frow], [r_Ls32[0]])
                tcopy("dve", frow[1][:, seg], Ls32[0][0:16, :], [r_Ls32[0]], [r_frow])
                ts("dve", frow[2][:, seg], E2[0][0:16, 0:512], -1.0, None, ALU.mult, None, [r_E2[0]], [r_frow])
                ts("dve", frow[3][:, seg], Ls32[0][0:16, :], -1.0, None, ALU.mult, None, [r_Ls32[0]], [r_frow])
            for hd in range(2):
                memset("pool", qa[hd][64:68, :], 1.0, [r_qa[hd]])
                memset("pool", ka[hd][64:68, :], 1.0, [r_ka[hd]])
            for hp in range(8):
                load_pair_weights(None, ws_kv[0], ws_kv[1], hp)
                for hd in range(2):
                    hh = 2 * hp + hd
                    tcopy("pool", qa[hd][0:64, :], bT[hd * 64:(hd + 1) * 64, hp, :], [r_R2], [r_qa[hd]])
                    dma(qa[hd][64:65, :], frow[2][hh:hh + 1, :], r_qrow[hd][0], [r_frow, r_qa[hd]], [r_qrow[hd][0]])
                    dma(qa[hd][65:66, :], frow[3][hh:hh + 1, :], r_qrow[hd][1], [r_frow, r_qa[hd]], [r_qrow[hd][1]])
                    dma(ka[hd][66:67, :], frow[0][hh:hh + 1, :], r_krow[hd][0], [r_frow, r_ka[hd]], [r_krow[hd][0]])
                    dma(ka[hd][67:68, :], frow[1][hh:hh + 1, :], r_krow[hd][1], [r_frow, r_ka[hd]], [r_krow[hd][1]])

                def sink_k1(tg, bank):
                    for hd in range(2):
                        tcopy("dve", ka[hd][0:64, tg * 512:(tg + 1) * 512], PB[bank][hd * 64:(hd + 1) * 64, :], [r_PB[bank]], [r_ka[hd]])
                proj_featmajor(Wk, r_Wk, aT, r_R1, 6, sink_k1)
                proj_v(aT, r_R1, 6, 1.0)
                attn_fox_pair(hp)
            out_proj(1, fox_post_g[0], bT, r_R2)
            ffn(1)
            for t in range(NT):
                dma(y[b, t * 128:(t + 1) * 128, :], hview[:, t, :], r_h[t], [r_h[t]], [])
        P.wait_all("sp", r_h)
        P.emit()
    return nc


_NC_CACHE = {}


def kernel(**inputs):
    x = np.ascontiguousarray(inputs["x"], dtype=np.float32)
    B, S, _ = x.shape
    NB = B // N_CORES
    key = (NB, S)
    if key not in _NC_CACHE:
        _NC_CACHE[key] = build_nc(NB, S)
    nc = _NC_CACHE[key]
    wnames = ["sb_pre_g", "sb_w_qkv", "sb_w_o", "sb_post_g", "kv_norm_g", "w_kvf", "b_f", "fox_pre_g", "fox_w_q",
              "fox_w_o", "fox_post_g", "ffn_pre_g", "w_up", "conv_w", "conv_b", "w_down", "ffn_post_g"]
    ws = {k: np.ascontiguousarray(inputs[k], dtype=np.float32) for k in wnames}
    in_maps = []
    for c in range(N_CORES):
        m = {"x": x[c * NB:(c + 1) * NB]}
        m.update(ws)
        in_maps.append(m)
    res = run_bass_kernel_spmd(nc, in_maps, core_ids=list(range(N_CORES)))
    return np.concatenate([r["y"] for r in res.results], axis=0)
```

```python
from contextlib import ExitStack
import numpy as np
import concourse.bass as bass
import concourse.mybir as mybir
from concourse.bass_utils import run_bass_kernel_spmd

F32 = mybir.dt.float32
BF16 = mybir.dt.bfloat16
AF = mybir.ActivationFunctionType
ALU = mybir.AluOpType

D = 1024
H = 16
DH = 64
FF = 2816
NCH = FF // 128
EPS = 1e-6
N_CORES = 8


class Res:
    __slots__ = ("name", "lw", "rd", "sem", "semcnt")

    def __init__(self, name, sem=None):
        self.name = name
        self.lw = None
        self.rd = {}
        self.sem = sem
        self.semcnt = 0


class Prog:
    ENGS = ("pe", "act", "dve", "pool", "sp")

    def __init__(self, nc, ctx):
        self.nc = nc
        self.ctx = ctx
        self.streams = {e: [] for e in self.ENGS}
        self.count = {e: 0 for e in self.ENGS}
        self.waited = {e: {} for e in self.ENGS}
        self.esem = {e: ctx.enter_context(nc.semaphore("es_" + e)) for e in self.ENGS}
        self.nres = 0

    def res(self, name=None, dma=False):
        self.nres += 1
        name = name or ("r%d" % self.nres)
        sem = self.ctx.enter_context(self.nc.semaphore("ds%d" % self.nres)) if dma else None
        return Res(name, sem)

    def _deps(self, reads, writes):
        deps = []
        for r in reads:
            if r.lw is not None:
                deps.append(r.lw)
        for w in writes:
            if w.lw is not None:
                deps.append(w.lw)
            deps.extend(w.rd.items())
        return deps

    def _waits_for(self, eng, deps):
        best = {}
        for (key, val) in deps:
            if key == "pe" and eng == "pe":
                continue
            if val > best.get(key, 0):
                best[key] = val
        out = []
        wd = self.waited[eng]
        for key, val in best.items():
            if wd.get(key, 0) >= val:
                continue
            wd[key] = val
            out.append((key, val))
        return out

    def _sem_of(self, key):
        if isinstance(key, str):
            return self.esem[key]
        return key.sem

    def _record(self, ev, reads, writes):
        k, v = ev
        for r in reads:
            if r.rd.get(k, 0) < v:
                r.rd[k] = v
        for w in writes:
            w.lw = ev
            w.rd = {}

    def op(self, eng, fn, reads=(), writes=()):
        waits = self._waits_for(eng, self._deps(reads, writes))
        self.count[eng] += 1
        ev = (eng, self.count[eng])
        self.streams[eng].append((waits, fn, None))
        self._record(ev, reads, writes)
        return ev

    def dma(self, eng, fn, semres, reads=(), writes=()):
        waits = self._waits_for(eng, self._deps(reads, writes))
        semres.semcnt += 1
        ev = (semres, 16 * semres.semcnt)
        self.streams[eng].append((waits, fn, semres))
        self._record(ev, reads, writes)
        return ev

    def alias(self, olds, news):
        evs = {}
        for o in olds:
            if o.lw is not None:
                k, v = o.lw
                evs[k] = max(evs.get(k, 0), v)
            for k, v in o.rd.items():
                evs[k] = max(evs.get(k, 0), v)
        for n in news:
            for k, v in evs.items():
                if n.rd.get(k, 0) < v:
                    n.rd[k] = v

    def wait_all(self, eng, resources):
        deps = []
        for r in resources:
            if r.lw is not None:
                deps.append(r.lw)
            deps.extend(r.rd.items())
        waits = self._waits_for(eng, deps)
        self.streams[eng].append((waits, None, None))

    def emit(self):
        nc = self.nc
        with nc.Block() as block:
            def run(engname):
                def body(e):
                    esem = self.esem[engname]
                    for (waits, fn, semres) in self.streams[engname]:
                        for (key, val) in waits:
                            e.wait_ge(self._sem_of(key), val)
                        if fn is None:
                            continue
                        ins = fn(e)
                        if semres is None:
                            ins.then_inc(esem, 1)
                        else:
                            ins.then_inc(semres.sem, 16)
                return body
            block.tensor(run("pe"))
            block.scalar(run("act"))
            block.vector(run("dve"))
            block.gpsimd(run("pool"))
            block.sync(run("sp"))


def build_nc(NB, S):
    NT = S // 128
    NG = S // 512
    nc = bass.Bass("TRN2", target_bir_lowering=False)
    dt_in = lambda name, shape: nc.dram_tensor(name, list(shape), F32, kind="ExternalInput").ap()
    x = dt_in("x", [NB, S, D])
    sb_pre_g = dt_in("sb_pre_g", [1, D])
    sb_w_qkv = dt_in("sb_w_qkv", [1, D, 3 * D])
    sb_w_o = dt_in("sb_w_o", [1, D, D])
    sb_post_g = dt_in("sb_post_g", [1, D])
    kv_norm_g = dt_in("kv_norm_g", [D])
    w_kvf = dt_in("w_kvf", [D, 2 * D + H])
    b_f = dt_in("b_f", [H])
    fox_pre_g = dt_in("fox_pre_g", [1, D])
    fox_w_q = dt_in("fox_w_q", [1, D, D])
    fox_w_o = dt_in("fox_w_o", [1, D, D])
    fox_post_g = dt_in("fox_post_g", [1, D])
    ffn_pre_g = dt_in("ffn_pre_g", [2, D])
    w_up = dt_in("w_up", [2, D, 2 * FF])
    conv_w = dt_in("conv_w", [2, 3, 2 * FF])
    conv_b = dt_in("conv_b", [2, 2 * FF])
    w_down = dt_in("w_down", [2, FF, D])
    ffn_post_g = dt_in("ffn_post_g", [2, D])
    y = nc.dram_tensor("y", [NB, S, D], F32, kind="ExternalOutput").ap()

    ws_qkv = nc.dram_tensor("ws_qkv", [5, 8, 128, 1024], BF16).ap()
    ws_kv = nc.dram_tensor("ws_kv", [2, 8, 128, 1024], BF16).ap()
    ws_f = nc.dram_tensor("ws_f", [128, 8 * 16], BF16).ap()
    ws_o = nc.dram_tensor("ws_o", [2, 128, 8192], BF16).ap()
    ws_up = nc.dram_tensor("ws_up", [2, NCH, 128, 2048], BF16).ap()
    ws_dn = nc.dram_tensor("ws_dn", [2, NCH, 128, 1024], BF16).ap()

    with ExitStack() as ctx:
        P = Prog(nc, ctx)
        sbt = lambda name, shape, dt: ctx.enter_context(nc.sbuf_tensor(name, list(shape), dt))
        pst = lambda name, shape, dt: ctx.enter_context(nc.psum_tensor(name, list(shape), dt))

        Hbuf = sbt("Hbuf", [128, NT * D], F32)
        hview = Hbuf[:].rearrange("p (t d) -> p t d", d=D)
        r_h = [P.res("h%d" % t, dma=True) for t in range(NT)]
        R1N = max(8 * S, 16384)
        R2N = max(8 * S, 16384)
        R3N = max(4 * S + NT * 256, 8192)
        R1 = sbt("R1", [128, R1N], BF16)
        R2 = sbt("R2", [128, R2N], BF16)
        R3 = sbt("R3", [128, R3N], BF16)
        r_R1 = P.res("R1")
        r_R2 = P.res("R2")
        aT = R1[:, 0:8 * S].rearrange("p (k s) -> p k s", s=S)
        bT = R2[:, 0:8 * S].rearrange("p (k s) -> p k s", s=S)
        qa = [R3[:, 0:S], R3[:, S:2 * S]]
        ka = [R3[:, 2 * S:3 * S], R3[:, 3 * S:4 * S]]
        vpad = R3[:, 4 * S:4 * S + NT * 256].rearrange("p (t h c) -> p t h c", h=2, c=128)
        r_qa = [P.res("qa0", dma=True), P.res("qa1", dma=True)]
        r_ka = [P.res("ka0", dma=True), P.res("ka1", dma=True)]
        r_vpad = P.res("vpad")
        r_qrow = [[P.res("qrow%d%d" % (i, j), dma=True) for j in range(2)] for i in range(2)]
        r_krow = [[P.res("krow%d%d" % (i, j), dma=True) for j in range(2)] for i in range(2)]
        r_attn_all = r_qa + r_ka + [r_vpad] + r_qrow[0] + r_qrow[1] + r_krow[0] + r_krow[1]
        WoT = R3[:, 0:8192].rearrange("p (k n) -> p k n", n=D)
        r_Wo = P.res("Wo", dma=True)
        gT = R2[:, 0:NCH * 512].rearrange("p (c t) -> p c t", t=512)
        r_gT = [P.res("gT%d" % c) for c in range(NCH)]
        wupb = [R2[:, NCH * 512 + i * 2048: NCH * 512 + (i + 1) * 2048].rearrange("p (g k j) -> p g k j", g=2, k=8) for i in range(2)]
        r_wup = [P.res("wup%d" % i, dma=True) for i in range(2)]
        NWD = 8
        wdnb = [R3[:, i * 1024:(i + 1) * 1024] for i in range(NWD)]
        r_wdn = [P.res("wdn%d" % i, dma=True) for i in range(NWD)]

        Wq = sbt("Wq", [128, 8, 128], BF16); r_Wq = P.res("Wq", dma=True)
        Wk = sbt("Wk", [128, 8, 128], BF16); r_Wk = P.res("Wk", dma=True)
        Wv = sbt("Wv", [128, 8, 128], BF16); r_Wv = P.res("Wv", dma=True)
        Wf = sbt("Wf", [128, 8, 16], BF16); r_Wf = P.res("Wf", dma=True)
        e32 = [sbt("e32_%d" % i, [128, 516], F32) for i in range(4)]; r_e32 = [P.res() for _ in range(4)]
        spb = [sbt("spb_%d" % i, [128, 512], BF16) for i in range(4)]; r_sp = [P.res() for _ in range(4)]
        E2 = [sbt("E2_%d" % i, [128, 512], F32) for i in range(2)]; r_E2 = [P.res() for _ in range(2)]
        Ab = [sbt("Ab_%d" % i, [128, 512], BF16) for i in range(3)]; r_A = [P.res() for _ in range(3)]
        Ls32 = [sbt("Ls32_%d" % i, [128, 512], F32) for i in range(1)]; r_Ls32 = [P.res() for _ in range(1)]
        rinv = sbt("rinv", [128, 512], F32); r_rinv = P.res()
        junk = sbt("junk", [128, 1024], BF16); r_junk = P.res()
        xn = sbt("xn", [128, 1024], BF16); r_xn = P.res()
        tmpf = sbt("tmpf", [128, 1024], F32); r_tmpA = P.res(); r_tmpB = P.res()
        Gpre = sbt("Gpre", [128, 1024], F32); r_Gpre = P.res("Gpre", dma=True)
        Gpost = Gpre; r_Gpost = r_Gpre
        st = sbt("stats", [128, 8], F32); r_st = P.res()
        pst_ = sbt("pstats", [128, 3 * 16], F32); r_pst = P.res()
        frowA = sbt("frowA", [80, S], BF16)
        frowB = sbt("frowB", [16, S], BF16)
        frow = [frowA[0:16, :], frowA[32:48, :], frowA[64:80, :], frowB[0:16, :]]
        r_frow = P.res()
        carry = sbt("carry", [16, 1], F32); r_carry = P.res()
        bfneg = sbt("bfneg", [16, 1], F32); r_bf = P.res("bf", dma=True)
        CW = [sbt("CW%d" % l, [128, 4, 2 * NCH], F32) for l in range(2)]; r_CW = [P.res("CW%d" % l, dma=True) for l in range(2)]
        hraw = e32; r_hraw = r_e32
        cacc = [E2[0][:, :], E2[1][:, :], tmpf[:, 0:512], tmpf[:, 512:1024]]; r_cacc = [r_E2[0], r_E2[1], r_tmpA, r_tmpB]
        gl = rinv; r_gl = r_rinv
        halo = sbt("halo", [128, 2 * NCH, 2], F32); r_halo = P.res()
        ident = sbt("ident", [128, 128], BF16); r_ident = P.res()
        mstrict = sbt("mstrict", [128, 128], BF16); r_mstrict = P.res()
        mincl = sbt("mincl", [128, 128], BF16); r_mincl = P.res()
        UI = sbt("UI", [128, 128], BF16); r_UI = P.res()
        Lc = sbt("Lc", [128, 128], BF16); r_Lc = P.res()
        PB = [pst("PB%d" % i, [128, 512], F32) for i in range(7)]; r_PB = [P.res("PB%d" % i) for i in range(7)]
        PT = pst("PT", [128, 1024], BF16); r_PT = P.res("PT")

        def mm(out, lhsT, rhs, start, stop, reads, writes):
            P.op("pe", lambda e: e.matmul(out, lhsT=lhsT, rhs=rhs, start=start, stop=stop, skip_group_check=True), reads, writes)

        def act(out, in_, func, reads, writes, **kw):
            P.op("act", lambda e: e.activation(out=out, in_=in_, func=func, **kw), reads, writes)

        def tcopy(eng, out, in_, reads, writes):
            P.op(eng, lambda e: e.tensor_copy(out=out, in_=in_), reads, writes)

        def tt(eng, out, in0, in1, op, reads, writes):
            P.op(eng, lambda e: e.tensor_tensor(out=out, in0=in0, in1=in1, op=op), reads, writes)

        def ts(eng, out, in0, s1, s2, op0, op1, reads, writes):
            if s2 is None:
                P.op(eng, lambda e: e.tensor_scalar(out=out, in0=in0, scalar1=s1, scalar2=None, op0=op0), reads, writes)
            else:
                P.op(eng, lambda e: e.tensor_scalar(out=out, in0=in0, scalar1=s1, scalar2=s2, op0=op0, op1=op1), reads, writes)

        def stt(eng, out, in0, scalar, in1, op0, op1, reads, writes):
            P.op(eng, lambda e: e.scalar_tensor_tensor(out=out, in0=in0, scalar=scalar, in1=in1, op0=op0, op1=op1), reads, writes)

        def memset(eng, ap, val, writes):
            P.op(eng, lambda e: e.memset(ap, val), (), writes)

        def dma(out, in_, semres, reads, writes, slow=False):
            if slow:
                P.dma("sp", lambda e: e.dma_start(out=out, in_=in_, allow_slow_non_contiguous=True), semres, reads, writes)
            else:
                P.dma("sp", lambda e: e.dma_start(out=out, in_=in_), semres, reads, writes)

        def aff(ap, pattern, cmp, cm, writes):
            P.op("pool", lambda e: e.affine_select(out=ap, in_=ap, pattern=pattern, compare_op=cmp, fill=0.0, base=0, channel_multiplier=cm), writes, writes)
        for (t_, r_) in ((ident, r_ident), (mstrict, r_mstrict), (mincl, r_mincl), (UI, r_UI), (Lc, r_Lc)):
            memset("pool", t_[:], 1.0, [r_])
        aff(ident[:], [[-1, 128]], ALU.is_equal, 1, [r_ident])
        aff(mstrict[:], [[1, 128]], ALU.is_gt, -1, [r_mstrict])
        aff(mincl[:], [[1, 128]], ALU.is_ge, -1, [r_mincl])
        aff(UI[:], [[-1, 128]], ALU.is_ge, 1, [r_UI])
        aff(Lc[:], [[1, 128]], ALU.is_gt, -1, [r_Lc])
        memset("pool", halo[:], 0.0, [r_halo])
        for l in range(2):
            for k in range(3):
                dma(CW[l][:, k, :], conv_w[l, k].rearrange("(c p) -> p c", p=128), r_CW[l], [], [r_CW[l]], slow=True)
            dma(CW[l][:, 3, :], conv_b[l].rearrange("(c p) -> p c", p=128), r_CW[l], [], [r_CW[l]], slow=True)
        dma(bfneg[:], b_f.rearrange("(h o) -> h o", o=1), r_bf, [], [r_bf], slow=True)
        ts("dve", bfneg[:], bfneg[:], -1.0, None, ALU.mult, None, [r_bf], [r_bf])

        NSLOT = 4 if NT * D >= 4 * 4096 else 2
        if NSLOT == 4:
            stg32 = [Hbuf[:, i * 4096:(i + 1) * 4096] for i in range(4)]
        else:
            stg32 = [R1[:, i * 8192:(i + 1) * 8192].bitcast(F32) for i in range(2)]
        stg16 = [R2[:, i * 4096:(i + 1) * 4096] for i in range(NSLOT)]
        r_s32 = [P.res("s32_%d" % i, dma=True) for i in range(NSLOT)]
        r_s16 = [P.res("s16_%d" % i) for i in range(NSLOT)]
        r_s16d = [P.res("s16d_%d" % i, dma=True) for i in range(NSLOT)]
        cast_engs = ["dve", "pool", "act"]
        ucount = [0]

        def cast_unit(srcs, E, dst, outview=None):
            i = ucount[0] % NSLOT
            eng = cast_engs[ucount[0] % 3]
            ucount[0] += 1
            for (vf, src) in srcs:
                dma(vf(stg32[i]), src, r_s32[i], [], [r_s32[i]])
            if eng == "act":
                act(stg16[i][:, 0:E], stg32[i][:, 0:E], AF.Copy, [r_s32[i]], [r_s16[i]])
            else:
                tcopy(eng, stg16[i][:, 0:E], stg32[i][:, 0:E], [r_s32[i]], [r_s16[i]])
            src16 = stg16[i][:, 0:E] if outview is None else outview(stg16[i][:, 0:E])
            dma(dst, src16, r_s16d[i], [r_s16[i]], [r_s16d[i]])

        def img4(stage, a, b, c):
            return stage[:, 0:a * b * c].rearrange("p (a b c) -> p a b c", a=a, b=b)

        def img3(stage, a, b):
            return stage[:, 0:a * b].rearrange("p (a b) -> p a b", a=a)

        def cast_cols(src2d, col0, dst_units):
            for half in range(2):
                srcs = []
                for hp4 in range(4):
                    hp = half * 4 + hp4
                    srcs.append((lambda s, hp4=hp4: img4(s, 4, 8, 128)[:, hp4],
                                 src2d[:, col0 + hp * 128: col0 + (hp + 1) * 128].rearrange("(k p) j -> p k j", p=128)))
                cast_unit(srcs, 4096, dst_units[half * 4:half * 4 + 4].rearrange("u p e -> p u e"),
                          outview=lambda v: v.rearrange("p (u e) -> p u e", u=4))
        cast_cols(sb_w_qkv[0], 0, ws_qkv[0])
        cast_cols(sb_w_qkv[0], D, ws_qkv[1])
        cast_cols(sb_w_qkv[0], 2 * D, ws_qkv[2])

        def cast_wo(src2d, dst):
            for half in range(2):
                srcs = [(lambda s: img3(s, 4, 1024),
                         src2d[half * 512:(half + 1) * 512, :].rearrange("(k p) n -> p k n", p=128))]
                cast_unit(srcs, 4096, dst[:, half * 4096:(half + 1) * 4096])
        cast_wo(sb_w_o[0], ws_o[0])

        def cast_ffn(l):
            for c0 in range(0, NCH, 2):
                srcs = []
                for gu in range(2):
                    for ci in range(2):
                        c = c0 + ci
                        srcs.append((lambda s, gu=gu, ci=ci: s[:, 0:4096].rearrange("p (c g k j) -> p c g k j", c=2, g=2, k=8)[:, ci, gu],
                                     w_up[l][:, gu * FF + c * 128: gu * FF + (c + 1) * 128].rearrange("(k p) j -> p k j", p=128)))
                cast_unit(srcs, 4096, ws_up[l, c0:c0 + 2].rearrange("c p e -> p c e"),
                          outview=lambda v: v.rearrange("p (c e) -> p c e", c=2))
            for c0 in range(0, NCH, 4):
                n = min(4, NCH - c0)
                srcs = [(lambda s, n=n: img3(s, n, 1024),
                         w_down[l][c0 * 128:(c0 + n) * 128, :].rearrange("(c p) n -> p c n", p=128))]
                cast_unit(srcs, n * 1024, ws_dn[l, c0:c0 + n].rearrange("c p e -> p c e"),
                          outview=lambda v, n=n: v.rearrange("p (c e) -> p c e", c=n))
        cast_ffn(0)
        cast_cols(w_kvf, 0, ws_kv[0])
        cast_cols(w_kvf, D, ws_kv[1])
        cast_unit([(lambda s: img3(s, 8, 16), w_kvf[:, 2 * D:2 * D + H].rearrange("(k p) j -> p k j", p=128))], 128, ws_f)
        cast_cols(fox_w_q[0], 0, ws_qkv[3])
        cast_wo(fox_w_o[0], ws_o[1])
        cast_ffn(1)
        r_scr = r_s16d
        P.alias(r_s32 + r_s16 + r_s16d, [r_R1, r_R2] + r_h)

        def prenorm(g_ap, dstT, r_dst, first_alias=None):
            dma(Gpre[:], g_ap.partition_broadcast(128), r_Gpre, [], [r_Gpre])
            for t in range(NT):
                act(junk[:], hview[:, t, :], AF.Square, [r_h[t]], [r_junk, r_pst], accum_out=pst_[:, t:t + 1])
            act(pst_[:, 16:16 + NT], pst_[:, 0:NT], AF.Ln, [r_pst], [r_pst], scale=1.0 / D, bias=EPS)
            act(pst_[:, 32:32 + NT], pst_[:, 16:16 + NT], AF.Exp, [r_pst], [r_pst], scale=-0.5)
            for t in range(NT):
                stt("dve", xn[:], hview[:, t, :], pst_[:, 32 + t:33 + t], Gpre[:], ALU.mult, ALU.mult, [r_h[t], r_pst, r_Gpre], [r_xn])
                for kc in range(8):
                    P.op("pe", lambda e, kc=kc: e.transpose(PT[:, kc * 128:(kc + 1) * 128], xn[:, kc * 128:(kc + 1) * 128], ident[:]),
                         [r_xn, r_ident], [r_PT])
                tcopy("dve", dstT[:, :, t * 128:(t + 1) * 128], PT[:].rearrange("p (k j) -> p k j", j=128), [r_PT], [r_dst])

        def postnorm_residual(t, ba, bb, g_res):
            act(junk[:, 0:512], PB[ba][:], AF.Square, [r_PB[ba]], [r_junk, r_st], accum_out=st[:, 3:4])
            act(junk[:, 512:1024], PB[bb][:], AF.Square, [r_PB[bb]], [r_junk, r_st], accum_out=st[:, 4:5])
            tt("dve", st[:, 5:6], st[:, 3:4], st[:, 4:5], ALU.add, [r_st], [r_st])
            act(st[:, 6:7], st[:, 5:6], AF.Ln, [r_st], [r_st], scale=1.0 / D, bias=EPS)
            act(st[:, 7:8], st[:, 6:7], AF.Exp, [r_st], [r_st], scale=-0.5)
            stt("dve", tmpf[:, 0:512], PB[ba][:], st[:, 7:8], Gpost[:, 0:512], ALU.mult, ALU.mult, [r_PB[ba], r_st, g_res], [r_tmpA])
            stt("dve", tmpf[:, 512:1024], PB[bb][:], st[:, 7:8], Gpost[:, 512:1024], ALU.mult, ALU.mult, [r_PB[bb], r_st, g_res], [r_tmpB])
            tt("pool", hview[:, t, :], hview[:, t, :], tmpf[:], ALU.add, [r_h[t], r_tmpA, r_tmpB], [r_h[t]])

        def out_proj(widx, g_ap, srcT, r_src):
            P.alias(r_attn_all, [r_Wo])
            dma(WoT[:, 0:4, :], ws_o[widx][:, 0:4096].rearrange("p (k n) -> p k n", n=D), r_Wo, r_scr, [r_Wo])
            dma(WoT[:, 4:8, :], ws_o[widx][:, 4096:8192].rearrange("p (k n) -> p k n", n=D), r_Wo, r_scr, [r_Wo])
            dma(Gpost[:], g_ap.partition_broadcast(128), r_Gpost, [], [r_Gpost])
            for t in range(NT):
                ba, bb = (0, 1) if t % 2 == 0 else (2, 3)
                for n, bk in ((0, ba), (1, bb)):
                    for kc in range(8):
                        mm(PB[bk][:], srcT[:, kc, t * 128:(t + 1) * 128], WoT[:, kc, n * 512:(n + 1) * 512],
                           kc == 0, kc == 7, [r_src, r_Wo], [r_PB[bk]])
                postnorm_residual(t, ba, bb, r_Gpost)
            P.alias([r_Wo], r_attn_all)

        def pipeline(n, stages):
            ns = len(stages)
            for tick in range(n + ns - 1):
                for k in range(ns - 1, -1, -1):
                    i = tick - k
                    if 0 <= i < n:
                        stages[k](i)

        def load_pair_weights(qsrc, ksrc, vsrc, hp):
            if qsrc is not None:
                dma(Wq[:], qsrc[hp].rearrange("p (k j) -> p k j", j=128), r_Wq, r_scr, [r_Wq])
            dma(Wk[:], ksrc[hp].rearrange("p (k j) -> p k j", j=128), r_Wk, r_scr, [r_Wk])
            dma(Wv[:], vsrc[hp].rearrange("p (k j) -> p k j", j=128), r_Wv, r_scr, [r_Wv])

        def proj_featmajor(Wt, r_W, srcT, r_src, bank, sink):
            for tg in range(NG):
                for kc in range(8):
                    mm(PB[bank][:], Wt[:, kc, :], srcT[:, kc, tg * 512:(tg + 1) * 512], kc == 0, kc == 7, [r_W, r_src], [r_PB[bank]])
                sink(tg, bank)

        def proj_v(srcT, r_src, bank, padval):
            memset("pool", vpad[:, :, 0, 64:128], padval, [r_vpad])
            memset("pool", vpad[:, :, 1, 0:64], padval, [r_vpad])
            for t4 in range(0, NT, 4):
                for ti in range(4):
                    t = t4 + ti
                    for kc in range(8):
                        mm(PB[bank][:, ti * 128:(ti + 1) * 128], srcT[:, kc, t * 128:(t + 1) * 128], Wv[:, kc, :], kc == 0, kc == 7,
                           [r_src, r_Wv], [r_PB[bank]])
                pv = PB[bank][:].rearrange("p (t c) -> p t c", c=128)
                tcopy("dve", vpad[:, t4:t4 + 4, 0, 0:64], pv[:, :, 0:64], [r_PB[bank]], [r_vpad])
                tcopy("dve", vpad[:, t4:t4 + 4, 1, 64:128], pv[:, :, 64:128], [r_PB[bank]], [r_vpad])

        def attn_sb_pair(hp, obank_base):
            units = []
            for g in range(NG):
                for sb in range(4 * g + 3, -1, -1):
                    for hd in range(2):
                        units.append((g, sb, hd))
            n = len(units)

            def geom(u):
                g, sb, hd = units[u]
                c0 = max(sb * 128, g * 512) - g * 512
                return g, sb, hd, c0, (sb >= 4 * g)

            def s_z(u):
                g, sb, hd, c0, diag = geom(u)
                zb = u % 2
                mm(PB[zb][:, c0:512], ka[hd][0:128, sb * 128:(sb + 1) * 128], qa[hd][0:128, g * 512 + c0:(g + 1) * 512], True, True,
                   [r_ka[hd], r_qa[hd]], [r_PB[zb]])

            def s_esp(u):
                g, sb, hd, c0, diag = geom(u)
                zb = u % 2
                eb = u % 4
                act(e32[eb][:, c0:512], PB[zb][:, c0:512], AF.Exp, [r_PB[zb]], [r_e32[eb]])
                act(spb[eb][:, c0:512], e32[eb][:, c0:512], AF.Ln, [r_e32[eb]], [r_sp[eb]], bias=1.0)
                if diag:
                    tt("pool", spb[eb][:, c0:c0 + 128], spb[eb][:, c0:c0 + 128], mstrict[:], ALU.mult, [r_sp[eb], r_mstrict], [r_sp[eb]])

            def s_x(u):
                g, sb, hd, c0, diag = geom(u)
                eb = u % 4
                xb = 2 + hd
                mm(PB[xb][:, c0:512], UI[:], spb[eb][:, c0:512], sb == 4 * g + 3, False, [r_UI, r_sp[eb]], [r_PB[xb]])

            def s_a(u):
                g, sb, hd, c0, diag = geom(u)
                eb = u % 4
                xb = 2 + hd
                ab = u % 3
                act(E2[hd][:, c0:512], PB[xb][:, c0:512], AF.Exp, [r_PB[xb]], [r_E2[hd]], scale=-1.0)
                if sb > 0:
                    mm(PB[xb][:, c0:512], Lc[:], spb[eb][:, c0:512], False, False, [r_Lc, r_sp[eb]], [r_PB[xb]])
                tt("dve", Ab[ab][:, c0:512], e32[eb][:, c0:512], E2[hd][:, c0:512], ALU.mult, [r_e32[eb], r_E2[hd]], [r_A[ab]])
                if diag:
                    tt("pool", Ab[ab][:, c0:c0 + 128], Ab[ab][:, c0:c0 + 128], mstrict[:], ALU.mult, [r_A[ab], r_mstrict], [r_A[ab]])

            def s_pv(u):
                g, sb, hd, c0, diag = geom(u)
                ab = u % 3
                ob = obank_base + (g % 2)
                firstm = (sb == 4 * g + 3 and hd == 0)
                mm(PB[ob][:, c0:512], vpad[:, sb, hd, :], Ab[ab][:, c0:512], firstm, False, [r_vpad, r_A[ab]], [r_PB[ob]])
                if sb == 0 and hd == 1:
                    tcopy("dve", bT[:, hp, g * 512:(g + 1) * 512], PB[ob][:], [r_PB[ob]], [r_R2])

            pipeline(n, [s_z, s_esp, s_x, s_a, s_pv])

        def attn_fox_pair(hp):
            units = []
            for g in range(NG):
                for sb in range(4 * g + 3, -1, -1):
                    for hd in range(2):
                        units.append((g, sb, hd))
            n = len(units)

            def geom(u):
                g, sb, hd = units[u]
                c0 = max(sb * 128, g * 512) - g * 512
                return g, sb, hd, c0, (sb >= 4 * g)

            def obank(g, hd):
                return (4 + hd) if g % 2 == 0 else (2 + hd)

            def s_z(u):
                g, sb, hd, c0, diag = geom(u)
                zb = u % 2
                mm(PB[zb][:, c0:512], ka[hd][0:128, sb * 128:(sb + 1) * 128], qa[hd][0:128, g * 512 + c0:(g + 1) * 512], True, True,
                   [r_ka[hd], r_qa[hd]] + r_qrow[hd] + r_krow[hd], [r_PB[zb]])

            def s_a(u):
                g, sb, hd, c0, diag = geom(u)
                zb = u % 2
                ab = u % 3
                act(Ab[ab][:, c0:512], PB[zb][:, c0:512], AF.Exp, [r_PB[zb]], [r_A[ab]])
                if diag:
                    tt("pool", Ab[ab][:, c0:c0 + 128], Ab[ab][:, c0:c0 + 128], mincl[:], ALU.mult, [r_A[ab], r_mincl], [r_A[ab]])

            def s_pv(u):
                g, sb, hd, c0, diag = geom(u)
                ab = u % 3
                ob = obank(g, hd)
                mm(PB[ob][:, c0:512], vpad[:, sb, hd, :], Ab[ab][:, c0:512], sb == 4 * g + 3, False, [r_vpad, r_A[ab]], [r_PB[ob]])
                if sb == 0:
                    if hd == 0:
                        P.op("dve", lambda e: e.reciprocal(out=rinv[0:64, :], in_=PB[ob][64:128, :]), [r_PB[ob]], [r_rinv])
                        tt("dve", bT[0:64, hp, g * 512:(g + 1) * 512], PB[ob][0:64, :], rinv[0:64, :], ALU.mult, [r_PB[ob], r_rinv], [r_R2])
                    else:
                        P.op("dve", lambda e: e.reciprocal(out=rinv[64:128, :], in_=PB[ob][0:64, :]), [r_PB[ob]], [r_rinv])
                        tt("dve", bT[64:128, hp, g * 512:(g + 1) * 512], PB[ob][64:128, :], rinv[64:128, :], ALU.mult, [r_PB[ob], r_rinv], [r_R2])

            pipeline(n, [s_z, s_a, s_pv])

        def ffn(l):
            prenorm(ffn_pre_g[l], aT, r_R1)
            dma(Gpost[:], ffn_post_g[l].partition_broadcast(128), r_Gpost, [], [r_Gpost])
            P.alias([r_R2], r_gT + r_wup)
            P.alias(r_attn_all, r_wdn)
            LAG, PRE = 2, 6
            for tg in range(NG):
                def load_up(c):
                    dma(wupb[c % 2][:], ws_up[l, c].rearrange("p (g k j) -> p g k j", g=2, k=8), r_wup[c % 2], r_scr, [r_wup[c % 2]])
                steps = [(half, c) for half in range(2) for c in range(NCH)]
                loaded = [0]

                def ensure_loaded(k):
                    while loaded[0] <= min(k, len(steps) - 1):
                        i = loaded[0] % NWD
                        dma(wdnb[i], ws_dn[l, steps[loaded[0]][1]], r_wdn[i], r_scr, [r_wdn[i]])
                        loaded[0] += 1

                def down_step(k):
                    ensure_loaded(k + PRE)
                    half, c = steps[k]
                    i = k % NWD
                    for t2 in range(2):
                        tl = 2 * half + t2
                        for nn in range(2):
                            bk = 2 * t2 + nn
                            mm(PB[bk][:], gT[:, c, tl * 128:(tl + 1) * 128], wdnb[i][:, nn * 512:(nn + 1) * 512], c == 0, c == NCH - 1,
                               [r_gT[c], r_wdn[i]], [r_PB[bk]])
                    if c == NCH - 1:
                        for t2 in range(2):
                            postnorm_residual(tg * 4 + 2 * half + t2, 2 * t2, 2 * t2 + 1, r_Gpost)

                def gate_mul(cc):
                    cg, cu = 2 * (cc % 2), 2 * (cc % 2) + 1
                    act(gl[:], cacc[cg], AF.Gelu_apprx_tanh, [r_cacc[cg]], [r_gl])
                    tt("pool", gT[:, cc, :], gl[:], cacc[cu], ALU.mult, [r_gl, r_cacc[cu]], [r_gT[cc]])

                load_up(0)
                ensure_loaded(PRE - 1)
                kdown = 0
                for c in range(NCH):
                    if c + 1 < NCH:
                        load_up(c + 1)
                    for gu in range(2):
                        bk = 4 + gu
                        for kc in range(8):
                            mm(PB[bk][:], wupb[c % 2][:, gu, kc, :], aT[:, kc, tg * 512:(tg + 1) * 512], kc == 0, kc == 7,
                               [r_wup[c % 2], r_R1], [r_PB[bk]])
                    if c >= LAG:
                        down_step(kdown)
                        kdown += 1
                    for gu in range(2):
                        bk = 4 + gu
                        ci = gu * NCH + c
                        hb = 2 * (c % 2) + gu
                        hr, rh = hraw[hb], r_hraw[hb]
                        ca, rca = cacc[hb], r_cacc[hb]
                        if tg == 0:
                            memset("pool", hr[:, 2:4], 0.0, [rh])
                        else:
                            tcopy("pool", hr[:, 2:4], halo[:, ci, :], [r_halo], [rh])
                        act(hr[:, 4:516], PB[bk][:], AF.Copy, [r_PB[bk]], [rh])
                        tcopy("pool", halo[:, ci, :], hr[:, 514:516], [rh], [r_halo])
                        act(ca, PB[bk][:], AF.Identity, [r_PB[bk], r_CW[l]], [rca],
                            scale=CW[l][:, 2, ci:ci + 1], bias=CW[l][:, 3, ci:ci + 1])
                        stt("dve", ca, hr[:, 3:515], CW[l][:, 1, ci:ci + 1], ca, ALU.mult, ALU.add, [rh, r_CW[l], rca], [rca])
                        stt("dve", ca, hr[:, 2:514], CW[l][:, 0, ci:ci + 1], ca, ALU.mult, ALU.add, [rh, r_CW[l], rca], [rca])
                    if c > 0:
                        gate_mul(c - 1)
                    if c == NCH - 1:
                        gate_mul(c)
                while kdown < len(steps):
                    down_step(kdown)
                    kdown += 1
            P.alias(r_gT + r_wup, [r_R2])
            P.alias(r_wdn, r_attn_all)

        for b in range(NB):
            for t in range(NT):
                dma(hview[:, t, :], x[b, t * 128:(t + 1) * 128, :], r_h[t], [], [r_h[t]])
            prenorm(sb_pre_g[0], aT, r_R1)
            for hd in range(2):
                memset("pool", qa[hd][64:128, :], 0.0, [r_qa[hd]])
                memset("pool", ka[hd][64:128, :], 0.0, [r_ka[hd]])
            for hp in range(8):
                load_pair_weights(ws_qkv[0], ws_qkv[1], ws_qkv[2], hp)

                def sink_q(tg, bank):
                    for hd in range(2):
                        ts("dve", qa[hd][0:64, tg * 512:(tg + 1) * 512], PB[bank][hd * 64:(hd + 1) * 64, :], 0.125, None, ALU.mult, None,
                           [r_PB[bank]], [r_qa[hd]])

                def sink_k(tg, bank):
                    for hd in range(2):
                        tcopy("dve", ka[hd][0:64, tg * 512:(tg + 1) * 512], PB[bank][hd * 64:(hd + 1) * 64, :], [r_PB[bank]], [r_ka[hd]])
                proj_featmajor(Wq, r_Wq, aT, r_R1, 6, sink_q)
                proj_featmajor(Wk, r_Wk, aT, r_R1, 6, sink_k)
                proj_v(aT, r_R1, 6, 0.0)
                attn_sb_pair(hp, 4)
            out_proj(0, sb_post_g[0], bT, r_R2)
            ffn(0)
            prenorm(fox_pre_g[0], aT, r_R1)
            for hp in range(8):
                dma(Wq[:], ws_qkv[3][hp].rearrange("p (k j) -> p k j", j=128), r_Wq, r_scr, [r_Wq])

                def sink_qall(tg, bank, hp=hp):
                    ts("dve", bT[:, hp, tg * 512:(tg + 1) * 512], PB[bank][:], 0.125, None, ALU.mult, None, [r_PB[bank]], [r_R2])
                proj_featmajor(Wq, r_Wq, aT, r_R1, 6, sink_qall)
            prenorm(kv_norm_g, aT, r_R1)
            dma(Wf[:], ws_f.rearrange("p (k j) -> p k j", j=16), r_Wf, r_scr, [r_Wf])
            memset("pool", E2[1][0:16, 0:512], 1.0, [r_E2[1]])
            for tg in range(NG):
                seg = slice(tg * 512, (tg + 1) * 512)
                for kc in range(8):
                    mm(PB[6][0:16, :], Wf[:, kc, :], aT[:, kc, seg], kc == 0, kc == 7, [r_Wf, r_R1], [r_PB[6]])
                act(e32[0][0:16, 0:512], PB[6][0:16, :], AF.Exp, [r_PB[6], r_bf], [r_e32[0]], scale=-1.0, bias=bfneg[:, 0:1])
                act(e32[1][0:16, 0:512], e32[0][0:16, 0:512], AF.Ln, [r_e32[0]], [r_e32[1]], bias=1.0)
                if tg == 0:
                    P.op("dve", lambda e: e.tensor_tensor_scan(out=E2[0][0:16, 0:512], data0=E2[1][0:16, 0:512], data1=e32[1][0:16, 0:512],
                                                               initial=0.0, op0=ALU.mult, op1=ALU.add),
                         [r_E2[1], r_e32[1]], [r_E2[0]])
                else:
                    P.op("dve", lambda e: e.tensor_tensor_scan(out=E2[0][0:16, 0:512], data0=E2[1][0:16, 0:512], data1=e32[1][0:16, 0:512],
                                                               initial=carry[:, 0:1], op0=ALU.mult, op1=ALU.add),
                         [r_E2[1], r_e32[1], r_carry], [r_E2[0]])
                tcopy("dve", carry[:, 0:1], E2[0][0:16, 511:512], [r_E2[0]], [r_carry])
                tcopy("dve", frow[0][:, seg], E2[0][0:16, 0:512], [r_E2[0]], [r_frow])
                tt("dve", Ls32[0][0:16, :], E2[0][0:16, 0:512], frow[0][:, seg], ALU.subtract, [r_E2[0], r_frow], [r_Ls32[0]])
                tcopy("dve", frow[1][:, seg], Ls32[0][0:16, :], [r_Ls32[0]], [r_frow])
                ts("dve", frow[2][:, seg], E2[0][0:16, 0:512], -1.0, None, ALU.mult, None, [r_E2[0]], [r_frow])
                ts("dve", frow[3][:, seg], Ls32[0][0:16, :], -1.0, None, ALU.mult, None, [r_Ls32[0]], [r_frow])
            for hd in range(2):
                memset("pool", qa[hd][64:128, :], 0.0, [r_qa[hd]])
                memset("pool", ka[hd][64:128, :], 0.0, [r_ka[hd]])
                memset("pool", qa[hd][64:68, :], 1.0, [r_qa[hd]])
                memset("pool", ka[hd][64:68, :], 1.0, [r_ka[hd]])
            for hp in range(8):
                load_pair_weights(None, ws_kv[0], ws_kv[1], hp)
                for hd in range(2):
                    hh = 2 * hp + hd
                    if hd == 0:
                        act(qa[hd][0:64, :], bT[0:64, hp, :], AF.Copy, [r_R2], [r_qa[hd]])
                    else:
                        tcopy("dve", qa[hd][0:64, :], bT[64:128, hp, :], [r_R2], [r_qa[hd]])
                    dma(qa[hd][64:65, :], frow[2][hh:hh + 1, :], r_qrow[hd][0], [r_frow, r_qa[hd]], [r_qrow[hd][0]])
                    dma(qa[hd][65:66, :], frow[3][hh:hh + 1, :], r_qrow[hd][1], [r_frow, r_qa[hd]], [r_qrow[hd][1]])
                    dma(ka[hd][66:67, :], frow[0][hh:hh + 1, :], r_krow[hd][0], [r_frow, r_ka[hd]], [r_krow[hd][0]])
                    dma(ka[hd][67:68, :], frow[1][hh:hh + 1, :], r_krow[hd][1], [r_frow, r_ka[hd]], [r_krow[hd][1]])

                def sink_k1(tg, bank):
                    for hd in range(2):
                        tcopy("dve", ka[hd][0:64, tg * 512:(tg + 1) * 512], PB[bank][hd * 64:(hd + 1) * 64, :], [r_PB[bank]], [r_ka[hd]])
                proj_featmajor(Wk, r_Wk, aT, r_R1, 6, sink_k1)
                proj_v(aT, r_R1, 6, 1.0)
                attn_fox_pair(hp)
            out_proj(1, fox_post_g[0], bT, r_R2)
            ffn(1)
            for t in range(NT):
                dma(y[b, t * 128:(t + 1) * 128, :], hview[:, t, :], r_h[t], [r_h[t]], [])
        P.wait_all("sp", r_h)
        P.emit()
    return nc


_NC_CACHE = {}


def kernel(**inputs):
    x = np.ascontiguousarray(inputs["x"], dtype=np.float32)
    B, S, _ = x.shape
    NB = B // N_CORES
    key = (NB, S)
    if key not in _NC_CACHE:
        _NC_CACHE[key] = build_nc(NB, S)
    nc = _NC_CACHE[key]
    wnames = ["sb_pre_g", "sb_w_qkv", "sb_w_o", "sb_post_g", "kv_norm_g", "w_kvf", "b_f", "fox_pre_g", "fox_w_q",
              "fox_w_o", "fox_post_g", "ffn_pre_g", "w_up", "conv_w", "conv_b", "w_down", "ffn_post_g"]
    ws = {k: np.ascontiguousarray(inputs[k], dtype=np.float32) for k in wnames}
    in_maps = []
    for c in range(N_CORES):
        m = {"x": x[c * NB:(c + 1) * NB]}
        m.update(ws)
        in_maps.append(m)
    res = run_bass_kernel_spmd(nc, in_maps, core_ids=list(range(N_CORES)))
    return np.concatenate([r["y"] for r in res.results], axis=0)
```

```python
from contextlib import ExitStack
import numpy as np
import concourse.bass as bass
import concourse.mybir as mybir
from concourse.bass_utils import run_bass_kernel_spmd

F32 = mybir.dt.float32
BF16 = mybir.dt.bfloat16
AF = mybir.ActivationFunctionType
ALU = mybir.AluOpType

D = 1024
H = 16
DH = 64
FF = 2816
NCH = FF // 128
EPS = 1e-6
N_CORES = 8


class Res:
    __slots__ = ("name", "lw", "rd", "sem", "semcnt")

    def __init__(self, name, sem=None):
        self.name = name
        self.lw = None
        self.rd = {}
        self.sem = sem
        self.semcnt = 0


class Prog:
    ENGS = ("pe", "act", "dve", "pool", "sp")

    def __init__(self, nc, ctx):
        self.nc = nc
        self.ctx = ctx
        self.streams = {e: [] for e in self.ENGS}
        self.count = {e: 0 for e in self.ENGS}
        self.waited = {e: {} for e in self.ENGS}
        self.esem = {e: ctx.enter_context(nc.semaphore("es_" + e)) for e in self.ENGS}
        self.nres = 0

    def res(self, name=None, dma=False):
        self.nres += 1
        name = name or ("r%d" % self.nres)
        sem = self.ctx.enter_context(self.nc.semaphore("ds%d" % self.nres)) if dma else None
        return Res(name, sem)

    def _deps(self, reads, writes):
        deps = []
        for r in reads:
            if r.lw is not None:
                deps.append(r.lw)
        for w in writes:
            if w.lw is not None:
                deps.append(w.lw)
            deps.extend(w.rd.items())
        return deps

    def _waits_for(self, eng, deps):
        best = {}
        for (key, val) in deps:
            if key == "pe" and eng == "pe":
                continue
            if val > best.get(key, 0):
                best[key] = val
        out = []
        wd = self.waited[eng]
        for key, val in best.items():
            if wd.get(key, 0) >= val:
                continue
            wd[key] = val
            out.append((key, val))
        return out

    def _sem_of(self, key):
        if isinstance(key, str):
            return self.esem[key]
        return key.sem

    def _record(self, ev, reads, writes):
        k, v = ev
        for r in reads:
            if r.rd.get(k, 0) < v:
                r.rd[k] = v
        for w in writes:
            w.lw = ev
            w.rd = {}

    def op(self, eng, fn, reads=(), writes=()):
        waits = self._waits_for(eng, self._deps(reads, writes))
        self.count[eng] += 1
        ev = (eng, self.count[eng])
        self.streams[eng].append((waits, fn, None))
        self._record(ev, reads, writes)
        return ev

    def dma(self, eng, fn, semres, reads=(), writes=()):
        waits = self._waits_for(eng, self._deps(reads, writes))
        semres.semcnt += 1
        ev = (semres, 16 * semres.semcnt)
        self.streams[eng].append((waits, fn, semres))
        self._record(ev, reads, writes)
        return ev

    def alias(self, olds, news):
        evs = {}
        for o in olds:
            if o.lw is not None:
                k, v = o.lw
                evs[k] = max(evs.get(k, 0), v)
            for k, v in o.rd.items():
                evs[k] = max(evs.get(k, 0), v)
        for n in news:
            for k, v in evs.items():
                if n.rd.get(k, 0) < v:
                    n.rd[k] = v

    def wait_all(self, eng, resources):
        deps = []
        for r in resources:
            if r.lw is not None:
                deps.append(r.lw)
            deps.extend(r.rd.items())
        waits = self._waits_for(eng, deps)
        self.streams[eng].append((waits, None, None))

    def emit(self):
        nc = self.nc
        with nc.Block() as block:
            def run(engname):
                def body(e):
                    esem = self.esem[engname]
                    for (waits, fn, semres) in self.streams[engname]:
                        for (key, val) in waits:
                            e.wait_ge(self._sem_of(key), val)
                        if fn is None:
                            continue
                        ins = fn(e)
                        if semres is None:
                            ins.then_inc(esem, 1)
                        else:
                            ins.then_inc(semres.sem, 16)
                return body
            block.tensor(run("pe"))
            block.scalar(run("act"))
            block.vector(run("dve"))
            block.gpsimd(run("pool"))
            block.sync(run("sp"))


def build_nc(NB, S):
    NT = S // 128
    NG = S // 512
    nc = bass.Bass("TRN2", target_bir_lowering=False)
    dt_in = lambda name, shape: nc.dram_tensor(name, list(shape), F32, kind="ExternalInput").ap()
    x = dt_in("x", [NB, S, D])
    sb_pre_g = dt_in("sb_pre_g", [1, D])
    sb_w_qkv = dt_in("sb_w_qkv", [1, D, 3 * D])
    sb_w_o = dt_in("sb_w_o", [1, D, D])
    sb_post_g = dt_in("sb_post_g", [1, D])
    kv_norm_g = dt_in("kv_norm_g", [D])
    w_kvf = dt_in("w_kvf", [D, 2 * D + H])
    b_f = dt_in("b_f", [H])
    fox_pre_g = dt_in("fox_pre_g", [1, D])
    fox_w_q = dt_in("fox_w_q", [1, D, D])
    fox_w_o = dt_in("fox_w_o", [1, D, D])
    fox_post_g = dt_in("fox_post_g", [1, D])
    ffn_pre_g = dt_in("ffn_pre_g", [2, D])
    w_up = dt_in("w_up", [2, D, 2 * FF])
    conv_w = dt_in("conv_w", [2, 3, 2 * FF])
    conv_b = dt_in("conv_b", [2, 2 * FF])
    w_down = dt_in("w_down", [2, FF, D])
    ffn_post_g = dt_in("ffn_post_g", [2, D])
    y = nc.dram_tensor("y", [NB, S, D], F32, kind="ExternalOutput").ap()

    ws_qkv = nc.dram_tensor("ws_qkv", [5, 8, 128, 1024], BF16).ap()
    ws_kv = nc.dram_tensor("ws_kv", [2, 8, 128, 1024], BF16).ap()
    ws_f = nc.dram_tensor("ws_f", [128, 8 * 16], BF16).ap()
    ws_o = nc.dram_tensor("ws_o", [2, 128, 8192], BF16).ap()
    ws_up = nc.dram_tensor("ws_up", [2, NCH, 128, 2048], BF16).ap()
    ws_dn = nc.dram_tensor("ws_dn", [2, NCH, 128, 1024], BF16).ap()

    with ExitStack() as ctx:
        P = Prog(nc, ctx)
        sbt = lambda name, shape, dt: ctx.enter_context(nc.sbuf_tensor(name, list(shape), dt))
        pst = lambda name, shape, dt: ctx.enter_context(nc.psum_tensor(name, list(shape), dt))

        Hbuf = sbt("Hbuf", [128, NT * D], F32)
        hview = Hbuf[:].rearrange("p (t d) -> p t d", d=D)
        r_h = [P.res("h%d" % t, dma=True) for t in range(NT)]
        R1N = max(8 * S, 16384)
        R2N = max(8 * S, 16384)
        R3N = max(4 * S + NT * 256, 8192)
        R1 = sbt("R1", [128, R1N], BF16)
        R2 = sbt("R2", [128, R2N], BF16)
        R3 = sbt("R3", [128, R3N], BF16)
        r_R1 = P.res("R1")
        r_R2 = P.res("R2")
        aT = R1[:, 0:8 * S].rearrange("p (k s) -> p k s", s=S)
        bT = R2[:, 0:8 * S].rearrange("p (k s) -> p k s", s=S)
        qa = [R3[:, 0:S], R3[:, S:2 * S]]
        ka = [R3[:, 2 * S:3 * S], R3[:, 3 * S:4 * S]]
        vpad = R3[:, 4 * S:4 * S + NT * 256].rearrange("p (t h c) -> p t h c", h=2, c=128)
        r_qa = [P.res("qa0", dma=True), P.res("qa1", dma=True)]
        r_ka = [P.res("ka0", dma=True), P.res("ka1", dma=True)]
        r_vpad = P.res("vpad")
        r_qrow = [[P.res("qrow%d%d" % (i, j), dma=True) for j in range(2)] for i in range(2)]
        r_krow = [[P.res("krow%d%d" % (i, j), dma=True) for j in range(2)] for i in range(2)]
        r_attn_all = r_qa + r_ka + [r_vpad] + r_qrow[0] + r_qrow[1] + r_krow[0] + r_krow[1]
        WoT = R3[:, 0:8192].rearrange("p (k n) -> p k n", n=D)
        r_Wo = P.res("Wo", dma=True)
        gT = R2[:, 0:NCH * 512].rearrange("p (c t) -> p c t", t=512)
        r_gT = [P.res("gT%d" % c) for c in range(NCH)]
        wupb = [R2[:, NCH * 512 + i * 2048: NCH * 512 + (i + 1) * 2048].rearrange("p (g k j) -> p g k j", g=2, k=8) for i in range(2)]
        r_wup = [P.res("wup%d" % i, dma=True) for i in range(2)]
        NWD = 8
        wdnb = [R3[:, i * 1024:(i + 1) * 1024] for i in range(NWD)]
        r_wdn = [P.res("wdn%d" % i, dma=True) for i in range(NWD)]

        Wq = sbt("Wq", [128, 8, 128], BF16); r_Wq = P.res("Wq", dma=True)
        Wk = sbt("Wk", [128, 8, 128], BF16); r_Wk = P.res("Wk", dma=True)
        Wv = sbt("Wv", [128, 8, 128], BF16); r_Wv = P.res("Wv", dma=True)
        Wf = sbt("Wf", [128, 8, 16], BF16); r_Wf = P.res("Wf", dma=True)
        e32 = [sbt("e32_%d" % i, [128, 516], F32) for i in range(4)]; r_e32 = [P.res() for _ in range(4)]
        spb = [sbt("spb_%d" % i, [128, 512], BF16) for i in range(4)]; r_sp = [P.res() for _ in range(4)]
        E2 = [sbt("E2_%d" % i, [128, 512], F32) for i in range(2)]; r_E2 = [P.res() for _ in range(2)]
        Ab = [sbt("Ab_%d" % i, [128, 512], BF16) for i in range(3)]; r_A = [P.res() for _ in range(3)]
        Ls32 = [sbt("Ls32_%d" % i, [128, 512], F32) for i in range(1)]; r_Ls32 = [P.res() for _ in range(1)]
        rinv = sbt("rinv", [128, 512], F32); r_rinv = P.res()
        junk = sbt("junk", [128, 1024], BF16); r_junk = P.res()
        xn = sbt("xn", [128, 1024], BF16); r_xn = P.res()
        tmpf = sbt("tmpf", [128, 1024], F32); r_tmpA = P.res(); r_tmpB = P.res()
        Gpre = sbt("Gpre", [128, 1024], F32); r_Gpre = P.res("Gpre", dma=True)
        Gpost = Gpre; r_Gpost = r_Gpre
        st = sbt("stats", [128, 8], F32); r_st = P.res()
        pst_ = sbt("pstats", [128, 3 * 16], F32); r_pst = P.res()
        frowA = sbt("frowA", [80, S], BF16)
        frowB = sbt("frowB", [16, S], BF16)
        frow = [frowA[0:16, :], frowA[32:48, :], frowA[64:80, :], frowB[0:16, :]]
        r_frow = P.res()
        carry = sbt("carry", [16, 1], F32); r_carry = P.res()
        bfneg = sbt("bfneg", [16, 1], F32); r_bf = P.res("bf", dma=True)
        CW = [sbt("CW%d" % l, [128, 4, 2 * NCH], F32) for l in range(2)]; r_CW = [P.res("CW%d" % l, dma=True) for l in range(2)]
        hraw = e32; r_hraw = r_e32
        cacc = [E2[0][:, :], E2[1][:, :], tmpf[:, 0:512], tmpf[:, 512:1024]]; r_cacc = [r_E2[0], r_E2[1], r_tmpA, r_tmpB]
        gl = rinv; r_gl = r_rinv
        halo = sbt("halo", [128, 2 * NCH, 2], F32); r_halo = P.res()
        ident = sbt("ident", [128, 128], BF16); r_ident = P.res()
        mstrict = sbt("mstrict", [128, 128], BF16); r_mstrict = P.res()
        mincl = sbt("mincl", [128, 128], BF16); r_mincl = P.res()
        UI = sbt("UI", [128, 128], BF16); r_UI = P.res()
        Lc = sbt("Lc", [128, 128], BF16); r_Lc = P.res()
        PB = [pst("PB%d" % i, [128, 512], F32) for i in range(7)]; r_PB = [P.res("PB%d" % i) for i in range(7)]
        PT = pst("PT", [128, 1024], BF16); r_PT = P.res("PT")

        def mm(out, lhsT, rhs, start, stop, reads, writes):
            P.op("pe", lambda e: e.matmul(out, lhsT=lhsT, rhs=rhs, start=start, stop=stop, skip_group_check=True), reads, writes)

        def act(out, in_, func, reads, writes, **kw):
            P.op("act", lambda e: e.activation(out=out, in_=in_, func=func, **kw), reads, writes)

        def tcopy(eng, out, in_, reads, writes):
            P.op(eng, lambda e: e.tensor_copy(out=out, in_=in_), reads, writes)

        def tt(eng, out, in0, in1, op, reads, writes):
            P.op(eng, lambda e: e.tensor_tensor(out=out, in0=in0, in1=in1, op=op), reads, writes)

        def ts(eng, out, in0, s1, s2, op0, op1, reads, writes):
            if s2 is None:
                P.op(eng, lambda e: e.tensor_scalar(out=out, in0=in0, scalar1=s1, scalar2=None, op0=op0), reads, writes)
            else:
                P.op(eng, lambda e: e.tensor_scalar(out=out, in0=in0, scalar1=s1, scalar2=s2, op0=op0, op1=op1), reads, writes)

        def stt(eng, out, in0, scalar, in1, op0, op1, reads, writes):
            P.op(eng, lambda e: e.scalar_tensor_tensor(out=out, in0=in0, scalar=scalar, in1=in1, op0=op0, op1=op1), reads, writes)

        def memset(eng, ap, val, writes):
            P.op(eng, lambda e: e.memset(ap, val), (), writes)

        def dma(out, in_, semres, reads, writes, slow=False):
            if slow:
                P.dma("sp", lambda e: e.dma_start(out=out, in_=in_, allow_slow_non_contiguous=True), semres, reads, writes)
            else:
                P.dma("sp", lambda e: e.dma_start(out=out, in_=in_), semres, reads, writes)

        def aff(ap, pattern, cmp, cm, writes):
            P.op("pool", lambda e: e.affine_select(out=ap, in_=ap, pattern=pattern, compare_op=cmp, fill=0.0, base=0, channel_multiplier=cm), writes, writes)
        for (t_, r_) in ((ident, r_ident), (mstrict, r_mstrict), (mincl, r_mincl), (UI, r_UI), (Lc, r_Lc)):
            memset("pool", t_[:], 1.0, [r_])
        aff(ident[:], [[-1, 128]], ALU.is_equal, 1, [r_ident])
        aff(mstrict[:], [[1, 128]], ALU.is_gt, -1, [r_mstrict])
        aff(mincl[:], [[1, 128]], ALU.is_ge, -1, [r_mincl])
        aff(UI[:], [[-1, 128]], ALU.is_ge, 1, [r_UI])
        aff(Lc[:], [[1, 128]], ALU.is_gt, -1, [r_Lc])
        memset("pool", halo[:], 0.0, [r_halo])
        for l in range(2):
            for k in range(3):
                dma(CW[l][:, k, :], conv_w[l, k].rearrange("(c p) -> p c", p=128), r_CW[l], [], [r_CW[l]], slow=True)
            dma(CW[l][:, 3, :], conv_b[l].rearrange("(c p) -> p c", p=128), r_CW[l], [], [r_CW[l]], slow=True)
        dma(bfneg[:], b_f.rearrange("(h o) -> h o", o=1), r_bf, [], [r_bf], slow=True)
        ts("dve", bfneg[:], bfneg[:], -1.0, None, ALU.mult, None, [r_bf], [r_bf])

        NSLOT = 4 if NT * D >= 4 * 4096 else 2
        if NSLOT == 4:
            stg32 = [Hbuf[:, i * 4096:(i + 1) * 4096] for i in range(4)]
        else:
            stg32 = [R1[:, i * 8192:(i + 1) * 8192].bitcast(F32) for i in range(2)]
        stg16 = [R2[:, i * 4096:(i + 1) * 4096] for i in range(NSLOT)]
        r_s32 = [P.res("s32_%d" % i, dma=True) for i in range(NSLOT)]
        r_s16 = [P.res("s16_%d" % i) for i in range(NSLOT)]
        r_s16d = [P.res("s16d_%d" % i, dma=True) for i in range(NSLOT)]
        cast_engs = ["dve", "pool", "act"]
        ucount = [0]

        def cast_unit(srcs, E, dst, outview=None):
            cast_list.append((srcs, E, dst, outview))

        cast_list = []

        def emit_casts():
            n = len(cast_list)
            LOOK = NSLOT - 1

            def emit_in(u):
                i = u % NSLOT
                for (vf, src) in cast_list[u][0]:
                    dma(vf(stg32[i]), src, r_s32[i], [], [r_s32[i]])

            def emit_cast_out(u):
                i = u % NSLOT
                srcs, E, dst, outview = cast_list[u]
                if u % 2 == 0:
                    act(stg16[i][:, 0:E], stg32[i][:, 0:E], AF.Copy, [r_s32[i]], [r_s16[i]])
                else:
                    tcopy("dve", stg16[i][:, 0:E], stg32[i][:, 0:E], [r_s32[i]], [r_s16[i]])
                src16 = stg16[i][:, 0:E] if outview is None else outview(stg16[i][:, 0:E])
                dma(dst, src16, r_s16d[i], [r_s16[i]], [r_s16d[i]])
            for u in range(n + LOOK):
                if u < n:
                    emit_in(u)
                if u - LOOK >= 0:
                    emit_cast_out(u - LOOK)

        def img4(stage, a, b, c):
            return stage[:, 0:a * b * c].rearrange("p (a b c) -> p a b c", a=a, b=b)

        def img3(stage, a, b):
            return stage[:, 0:a * b].rearrange("p (a b) -> p a b", a=a)

        def cast_cols(src2d, col0, dst_units):
            for half in range(2):
                srcs = []
                for hp4 in range(4):
                    hp = half * 4 + hp4
                    srcs.append((lambda s, hp4=hp4: img4(s, 4, 8, 128)[:, hp4],
                                 src2d[:, col0 + hp * 128: col0 + (hp + 1) * 128].rearrange("(k p) j -> p k j", p=128)))
                cast_unit(srcs, 4096, dst_units[half * 4:half * 4 + 4].rearrange("u p e -> p u e"),
                          outview=lambda v: v.rearrange("p (u e) -> p u e", u=4))
        cast_cols(sb_w_qkv[0], 0, ws_qkv[0])
        cast_cols(sb_w_qkv[0], D, ws_qkv[1])
        cast_cols(sb_w_qkv[0], 2 * D, ws_qkv[2])

        def cast_wo(src2d, dst):
            for half in range(2):
                srcs = [(lambda s: img3(s, 4, 1024),
                         src2d[half * 512:(half + 1) * 512, :].rearrange("(k p) n -> p k n", p=128))]
                cast_unit(srcs, 4096, dst[:, half * 4096:(half + 1) * 4096])
        cast_wo(sb_w_o[0], ws_o[0])

        def cast_ffn(l):
            for c0 in range(0, NCH, 2):
                srcs = []
                for gu in range(2):
                    for ci in range(2):
                        c = c0 + ci
                        srcs.append((lambda s, gu=gu, ci=ci: s[:, 0:4096].rearrange("p (c g k j) -> p c g k j", c=2, g=2, k=8)[:, ci, gu],
                                     w_up[l][:, gu * FF + c * 128: gu * FF + (c + 1) * 128].rearrange("(k p) j -> p k j", p=128)))
                cast_unit(srcs, 4096, ws_up[l, c0:c0 + 2].rearrange("c p e -> p c e"),
                          outview=lambda v: v.rearrange("p (c e) -> p c e", c=2))
            for c0 in range(0, NCH, 4):
                n = min(4, NCH - c0)
                srcs = [(lambda s, n=n: img3(s, n, 1024),
                         w_down[l][c0 * 128:(c0 + n) * 128, :].rearrange("(c p) n -> p c n", p=128))]
                cast_unit(srcs, n * 1024, ws_dn[l, c0:c0 + n].rearrange("c p e -> p c e"),
                          outview=lambda v, n=n: v.rearrange("p (c e) -> p c e", c=n))
        cast_ffn(0)
        cast_cols(w_kvf, 0, ws_kv[0])
        cast_cols(w_kvf, D, ws_kv[1])
        cast_unit([(lambda s: img3(s, 8, 16), w_kvf[:, 2 * D:2 * D + H].rearrange("(k p) j -> p k j", p=128))], 128, ws_f)
        cast_cols(fox_w_q[0], 0, ws_qkv[3])
        cast_wo(fox_w_o[0], ws_o[1])
        cast_ffn(1)
        emit_casts()
        r_scr = r_s16d
        P.alias(r_s32 + r_s16 + r_s16d, [r_R1, r_R2] + r_h)

        def prenorm(g_ap, dstT, r_dst, first_alias=None):
            dma(Gpre[:], g_ap.partition_broadcast(128), r_Gpre, [], [r_Gpre])
            for t in range(NT):
                act(junk[:], hview[:, t, :], AF.Square, [r_h[t]], [r_junk, r_pst], accum_out=pst_[:, t:t + 1])
            act(pst_[:, 16:16 + NT], pst_[:, 0:NT], AF.Ln, [r_pst], [r_pst], scale=1.0 / D, bias=EPS)
            act(pst_[:, 32:32 + NT], pst_[:, 16:16 + NT], AF.Exp, [r_pst], [r_pst], scale=-0.5)
            for t in range(NT):
                stt("dve", xn[:], hview[:, t, :], pst_[:, 32 + t:33 + t], Gpre[:], ALU.mult, ALU.mult, [r_h[t], r_pst, r_Gpre], [r_xn])
                for kc in range(8):
                    P.op("pe", lambda e, kc=kc: e.transpose(PT[:, kc * 128:(kc + 1) * 128], xn[:, kc * 128:(kc + 1) * 128], ident[:]),
                         [r_xn, r_ident], [r_PT])
                tcopy("dve", dstT[:, :, t * 128:(t + 1) * 128], PT[:].rearrange("p (k j) -> p k j", j=128), [r_PT], [r_dst])

        def postnorm_residual(t, ba, bb, g_res):
            act(junk[:, 0:512], PB[ba][:], AF.Square, [r_PB[ba]], [r_junk, r_st], accum_out=st[:, 3:4])
            act(junk[:, 512:1024], PB[bb][:], AF.Square, [r_PB[bb]], [r_junk, r_st], accum_out=st[:, 4:5])
            tt("dve", st[:, 5:6], st[:, 3:4], st[:, 4:5], ALU.add, [r_st], [r_st])
            act(st[:, 6:7], st[:, 5:6], AF.Ln, [r_st], [r_st], scale=1.0 / D, bias=EPS)
            act(st[:, 7:8], st[:, 6:7], AF.Exp, [r_st], [r_st], scale=-0.5)
            stt("dve", tmpf[:, 0:512], PB[ba][:], st[:, 7:8], Gpost[:, 0:512], ALU.mult, ALU.mult, [r_PB[ba], r_st, g_res], [r_tmpA])
            stt("dve", tmpf[:, 512:1024], PB[bb][:], st[:, 7:8], Gpost[:, 512:1024], ALU.mult, ALU.mult, [r_PB[bb], r_st, g_res], [r_tmpB])
            tt("pool", hview[:, t, :], hview[:, t, :], tmpf[:], ALU.add, [r_h[t], r_tmpA, r_tmpB], [r_h[t]])

        def out_proj(widx, g_ap, srcT, r_src):
            P.alias(r_attn_all, [r_Wo])
            dma(WoT[:, 0:4, :], ws_o[widx][:, 0:4096].rearrange("p (k n) -> p k n", n=D), r_Wo, r_scr, [r_Wo])
            dma(WoT[:, 4:8, :], ws_o[widx][:, 4096:8192].rearrange("p (k n) -> p k n", n=D), r_Wo, r_scr, [r_Wo])
            dma(Gpost[:], g_ap.partition_broadcast(128), r_Gpost, [], [r_Gpost])
            for t in range(NT):
                ba, bb = (0, 1) if t % 2 == 0 else (2, 3)
                for n, bk in ((0, ba), (1, bb)):
                    for kc in range(8):
                        mm(PB[bk][:], srcT[:, kc, t * 128:(t + 1) * 128], WoT[:, kc, n * 512:(n + 1) * 512],
                           kc == 0, kc == 7, [r_src, r_Wo], [r_PB[bk]])
                postnorm_residual(t, ba, bb, r_Gpost)
            P.alias([r_Wo], r_attn_all)

        def pipeline(n, stages):
            ns = len(stages)
            for tick in range(n + ns - 1):
                for k in range(ns - 1, -1, -1):
                    i = tick - k
                    if 0 <= i < n:
                        stages[k](i)

        def load_pair_weights(qsrc, ksrc, vsrc, hp):
            if qsrc is not None:
                dma(Wq[:], qsrc[hp].rearrange("p (k j) -> p k j", j=128), r_Wq, r_scr, [r_Wq])
            dma(Wk[:], ksrc[hp].rearrange("p (k j) -> p k j", j=128), r_Wk, r_scr, [r_Wk])
            dma(Wv[:], vsrc[hp].rearrange("p (k j) -> p k j", j=128), r_Wv, r_scr, [r_Wv])

        def proj_featmajor(Wt, r_W, srcT, r_src, bank, sink):
            for tg in range(NG):
                for kc in range(8):
                    mm(PB[bank][:], Wt[:, kc, :], srcT[:, kc, tg * 512:(tg + 1) * 512], kc == 0, kc == 7, [r_W, r_src], [r_PB[bank]])
                sink(tg, bank)

        def proj_v(srcT, r_src, bank, padval):
            memset("pool", vpad[:, :, 0, 64:128], padval, [r_vpad])
            memset("pool", vpad[:, :, 1, 0:64], padval, [r_vpad])
            for t4 in range(0, NT, 4):
                for ti in range(4):
                    t = t4 + ti
                    for kc in range(8):
                        mm(PB[bank][:, ti * 128:(ti + 1) * 128], srcT[:, kc, t * 128:(t + 1) * 128], Wv[:, kc, :], kc == 0, kc == 7,
                           [r_src, r_Wv], [r_PB[bank]])
                pv = PB[bank][:].rearrange("p (t c) -> p t c", c=128)
                tcopy("dve", vpad[:, t4:t4 + 4, 0, 0:64], pv[:, :, 0:64], [r_PB[bank]], [r_vpad])
                tcopy("dve", vpad[:, t4:t4 + 4, 1, 64:128], pv[:, :, 64:128], [r_PB[bank]], [r_vpad])

        def attn_sb_pair(hp, obank_base):
            units = []
            for g in range(NG):
                for sb in range(4 * g + 3, -1, -1):
                    for hd in range(2):
                        units.append((g, sb, hd))
            n = len(units)

            def geom(u):
                g, sb, hd = units[u]
                c0 = max(sb * 128, g * 512) - g * 512
                return g, sb, hd, c0, (sb >= 4 * g)

            def s_z(u):
                g, sb, hd, c0, diag = geom(u)
                zb = u % 2
                mm(PB[zb][:, c0:512], ka[hd][0:128, sb * 128:(sb + 1) * 128], qa[hd][0:128, g * 512 + c0:(g + 1) * 512], True, True,
                   [r_ka[hd], r_qa[hd]], [r_PB[zb]])

            def s_esp(u):
                g, sb, hd, c0, diag = geom(u)
                zb = u % 2
                eb = u % 4
                act(e32[eb][:, c0:512], PB[zb][:, c0:512], AF.Exp, [r_PB[zb]], [r_e32[eb]])
                act(spb[eb][:, c0:512], e32[eb][:, c0:512], AF.Ln, [r_e32[eb]], [r_sp[eb]], bias=1.0)
                if diag:
                    tt("pool", spb[eb][:, c0:c0 + 128], spb[eb][:, c0:c0 + 128], mstrict[:], ALU.mult, [r_sp[eb], r_mstrict], [r_sp[eb]])

            def s_x(u):
                g, sb, hd, c0, diag = geom(u)
                eb = u % 4
                xb = 2 + hd
                mm(PB[xb][:, c0:512], UI[:], spb[eb][:, c0:512], sb == 4 * g + 3, False, [r_UI, r_sp[eb]], [r_PB[xb]])

            def s_a(u):
                g, sb, hd, c0, diag = geom(u)
                eb = u % 4
                xb = 2 + hd
                ab = u % 3
                act(E2[hd][:, c0:512], PB[xb][:, c0:512], AF.Exp, [r_PB[xb]], [r_E2[hd]], scale=-1.0)
                if sb > 0:
                    mm(PB[xb][:, c0:512], Lc[:], spb[eb][:, c0:512], False, False, [r_Lc, r_sp[eb]], [r_PB[xb]])
                tt("dve", Ab[ab][:, c0:512], e32[eb][:, c0:512], E2[hd][:, c0:512], ALU.mult, [r_e32[eb], r_E2[hd]], [r_A[ab]])
                if diag:
                    tt("pool", Ab[ab][:, c0:c0 + 128], Ab[ab][:, c0:c0 + 128], mstrict[:], ALU.mult, [r_A[ab], r_mstrict], [r_A[ab]])

            def s_pv(u):
                g, sb, hd, c0, diag = geom(u)
                ab = u % 3
                ob = obank_base + (g % 2)
                firstm = (sb == 4 * g + 3 and hd == 0)
                mm(PB[ob][:, c0:512], vpad[:, sb, hd, :], Ab[ab][:, c0:512], firstm, False, [r_vpad, r_A[ab]], [r_PB[ob]])
                if sb == 0 and hd == 1:
                    tcopy("dve", bT[:, hp, g * 512:(g + 1) * 512], PB[ob][:], [r_PB[ob]], [r_R2])

            pipeline(n, [s_z, s_esp, s_x, s_a, s_pv])

        def attn_fox_pair(hp):
            units = []
            for g in range(NG):
                for sb in range(4 * g + 3, -1, -1):
                    for hd in range(2):
                        units.append((g, sb, hd))
            n = len(units)

            def geom(u):
                g, sb, hd = units[u]
                c0 = max(sb * 128, g * 512) - g * 512
                return g, sb, hd, c0, (sb >= 4 * g)

            def obank(g, hd):
                return (4 + hd) if g % 2 == 0 else (2 + hd)

            def s_z(u):
                g, sb, hd, c0, diag = geom(u)
                zb = u % 2
                mm(PB[zb][:, c0:512], ka[hd][0:128, sb * 128:(sb + 1) * 128], qa[hd][0:128, g * 512 + c0:(g + 1) * 512], True, True,
                   [r_ka[hd], r_qa[hd]] + r_qrow[hd] + r_krow[hd], [r_PB[zb]])

            def s_a(u):
                g, sb, hd, c0, diag = geom(u)
                zb = u % 2
                ab = u % 3
                act(Ab[ab][:, c0:512], PB[zb][:, c0:512], AF.Exp, [r_PB[zb]], [r_A[ab]])
                if diag:
                    tt("pool", Ab[ab][:, c0:c0 + 128], Ab[ab][:, c0:c0 + 128], mincl[:], ALU.mult, [r_A[ab], r_mincl], [r_A[ab]])

            def s_pv(u):
                g, sb, hd, c0, diag = geom(u)
                ab = u % 3
                ob = obank(g, hd)
                mm(PB[ob][:, c0:512], vpad[:, sb, hd, :], Ab[ab][:, c0:512], sb == 4 * g + 3, False, [r_vpad, r_A[ab]], [r_PB[ob]])
                if sb == 0:
                    if hd == 0:
                        P.op("dve", lambda e: e.reciprocal(out=rinv[0:64, :], in_=PB[ob][64:128, :]), [r_PB[ob]], [r_rinv])
                        tt("dve", bT[0:64, hp, g * 512:(g + 1) * 512], PB[ob][0:64, :], rinv[0:64, :], ALU.mult, [r_PB[ob], r_rinv], [r_R2])
                    else:
                        P.op("dve", lambda e: e.reciprocal(out=rinv[64:128, :], in_=PB[ob][0:64, :]), [r_PB[ob]], [r_rinv])
                        tt("dve", bT[64:128, hp, g * 512:(g + 1) * 512], PB[ob][64:128, :], rinv[64:128, :], ALU.mult, [r_PB[ob], r_rinv], [r_R2])

            pipeline(n, [s_z, s_a, s_pv])

        def ffn(l):
            prenorm(ffn_pre_g[l], aT, r_R1)
            dma(Gpost[:], ffn_post_g[l].partition_broadcast(128), r_Gpost, [], [r_Gpost])
            P.alias([r_R2], r_gT + r_wup)
            P.alias(r_attn_all, r_wdn)
            LAG, PRE = 2, 6
            for tg in range(NG):
                def load_up(c):
                    dma(wupb[c % 2][:], ws_up[l, c].rearrange("p (g k j) -> p g k j", g=2, k=8), r_wup[c % 2], r_scr, [r_wup[c % 2]])
                steps = [(half, c) for half in range(2) for c in range(NCH)]
                loaded = [0]

                def ensure_loaded(k):
                    while loaded[0] <= min(k, len(steps) - 1):
                        i = loaded[0] % NWD
                        dma(wdnb[i], ws_dn[l, steps[loaded[0]][1]], r_wdn[i], r_scr, [r_wdn[i]])
                        loaded[0] += 1

                def down_step(k):
                    ensure_loaded(k + PRE)
                    half, c = steps[k]
                    i = k % NWD
                    for t2 in range(2):
                        tl = 2 * half + t2
                        for nn in range(2):
                            bk = 2 * t2 + nn
                            mm(PB[bk][:], gT[:, c, tl * 128:(tl + 1) * 128], wdnb[i][:, nn * 512:(nn + 1) * 512], c == 0, c == NCH - 1,
                               [r_gT[c], r_wdn[i]], [r_PB[bk]])
                    if c == NCH - 1:
                        for t2 in range(2):
                            postnorm_residual(tg * 4 + 2 * half + t2, 2 * t2, 2 * t2 + 1, r_Gpost)

                def gate_mul(cc):
                    cg, cu = 2 * (cc % 2), 2 * (cc % 2) + 1
                    act(gl[:], cacc[cg], AF.Gelu_apprx_tanh, [r_cacc[cg]], [r_gl])
                    tt("pool", gT[:, cc, :], gl[:], cacc[cu], ALU.mult, [r_gl, r_cacc[cu]], [r_gT[cc]])

                load_up(0)
                ensure_loaded(PRE - 1)
                kdown = 0
                for c in range(NCH):
                    if c + 1 < NCH:
                        load_up(c + 1)
                    for gu in range(2):
                        bk = 4 + gu
                        for kc in range(8):
                            mm(PB[bk][:], wupb[c % 2][:, gu, kc, :], aT[:, kc, tg * 512:(tg + 1) * 512], kc == 0, kc == 7,
                               [r_wup[c % 2], r_R1], [r_PB[bk]])
                    if c >= LAG:
                        down_step(kdown)
                        kdown += 1
                    for gu in range(2):
                        bk = 4 + gu
                        ci = gu * NCH + c
                        hb = 2 * (c % 2) + gu
                        hr, rh = hraw[hb], r_hraw[hb]
                        ca, rca = cacc[hb], r_cacc[hb]
                        if tg == 0:
                            memset("pool", hr[:, 2:4], 0.0, [rh])
                        else:
                            tcopy("pool", hr[:, 2:4], halo[:, ci, :], [r_halo], [rh])
                        act(hr[:, 4:516], PB[bk][:], AF.Copy, [r_PB[bk]], [rh])
                        tcopy("pool", halo[:, ci, :], hr[:, 514:516], [rh], [r_halo])
                        act(ca, PB[bk][:], AF.Identity, [r_PB[bk], r_CW[l]], [rca],
                            scale=CW[l][:, 2, ci:ci + 1], bias=CW[l][:, 3, ci:ci + 1])
                        stt("dve", ca, hr[:, 3:515], CW[l][:, 1, ci:ci + 1], ca, ALU.mult, ALU.add, [rh, r_CW[l], rca], [rca])
                        stt("dve", ca, hr[:, 2:514], CW[l][:, 0, ci:ci + 1], ca, ALU.mult, ALU.add, [rh, r_CW[l], rca], [rca])
                    if c > 0:
                        gate_mul(c - 1)
                    if c == NCH - 1:
                        gate_mul(c)
                while kdown < len(steps):
                    down_step(kdown)
                    kdown += 1
            P.alias(r_gT + r_wup, [r_R2])
            P.alias(r_wdn, r_attn_all)

        for b in range(NB):
            for t in range(NT):
                dma(hview[:, t, :], x[b, t * 128:(t + 1) * 128, :], r_h[t], [], [r_h[t]])
            prenorm(sb_pre_g[0], aT, r_R1)
            for hd in range(2):
                memset("pool", qa[hd][64:128, :], 0.0, [r_qa[hd]])
                memset("pool", ka[hd][64:128, :], 0.0, [r_ka[hd]])
            load_pair_weights(ws_qkv[0], ws_qkv[1], ws_qkv[2], 0)
            for hp in range(8):

                def sink_q(tg, bank):
                    for hd in range(2):
                        ts("dve", qa[hd][0:64, tg * 512:(tg + 1) * 512], PB[bank][hd * 64:(hd + 1) * 64, :], 0.125, None, ALU.mult, None,
                           [r_PB[bank]], [r_qa[hd]])

                def sink_k(tg, bank):
                    for hd in range(2):
                        tcopy("dve", ka[hd][0:64, tg * 512:(tg + 1) * 512], PB[bank][hd * 64:(hd + 1) * 64, :], [r_PB[bank]], [r_ka[hd]])
                proj_featmajor(Wq, r_Wq, aT, r_R1, 6, sink_q)
                proj_featmajor(Wk, r_Wk, aT, r_R1, 6, sink_k)
                proj_v(aT, r_R1, 6, 0.0)
                if hp + 1 < 8:
                    load_pair_weights(ws_qkv[0], ws_qkv[1], ws_qkv[2], hp + 1)
                attn_sb_pair(hp, 4)
            out_proj(0, sb_post_g[0], bT, r_R2)
            ffn(0)
            prenorm(fox_pre_g[0], aT, r_R1)
            for hp in range(8):
                dma(Wq[:], ws_qkv[3][hp].rearrange("p (k j) -> p k j", j=128), r_Wq, r_scr, [r_Wq])

                def sink_qall(tg, bank, hp=hp):
                    ts("dve", bT[:, hp, tg * 512:(tg + 1) * 512], PB[bank][:], 0.125, None, ALU.mult, None, [r_PB[bank]], [r_R2])
                proj_featmajor(Wq, r_Wq, aT, r_R1, 6, sink_qall)
            prenorm(kv_norm_g, aT, r_R1)
            dma(Wf[:], ws_f.rearrange("p (k j) -> p k j", j=16), r_Wf, r_scr, [r_Wf])
            memset("pool", E2[1][0:16, 0:512], 1.0, [r_E2[1]])
            for tg in range(NG):
                seg = slice(tg * 512, (tg + 1) * 512)
                for kc in range(8):
                    mm(PB[6][0:16, :], Wf[:, kc, :], aT[:, kc, seg], kc == 0, kc == 7, [r_Wf, r_R1], [r_PB[6]])
                act(e32[0][0:16, 0:512], PB[6][0:16, :], AF.Exp, [r_PB[6], r_bf], [r_e32[0]], scale=-1.0, bias=bfneg[:, 0:1])
                act(e32[1][0:16, 0:512], e32[0][0:16, 0:512], AF.Ln, [r_e32[0]], [r_e32[1]], bias=1.0)
                if tg == 0:
                    P.op("dve", lambda e: e.tensor_tensor_scan(out=E2[0][0:16, 0:512], data0=E2[1][0:16, 0:512], data1=e32[1][0:16, 0:512],
                                                               initial=0.0, op0=ALU.mult, op1=ALU.add),
                         [r_E2[1], r_e32[1]], [r_E2[0]])
                else:
                    P.op("dve", lambda e: e.tensor_tensor_scan(out=E2[0][0:16, 0:512], data0=E2[1][0:16, 0:512], data1=e32[1][0:16, 0:512],
                                                               initial=carry[:, 0:1], op0=ALU.mult, op1=ALU.add),
                         [r_E2[1], r_e32[1], r_carry], [r_E2[0]])
                tcopy("dve", carry[:, 0:1], E2[0][0:16, 511:512], [r_E2[0]], [r_carry])
                tcopy("dve", frow[0][:, seg], E2[0][0:16, 0:512], [r_E2[0]], [r_frow])
                tt("dve", Ls32[0][0:16, :], E2[0][0:16, 0:512], frow[0][:, seg], ALU.subtract, [r_E2[0], r_frow], [r_Ls32[0]])
                tcopy("dve", frow[1][:, seg], Ls32[0][0:16, :], [r_Ls32[0]], [r_frow])
                ts("dve", frow[2][:, seg], E2[0][0:16, 0:512], -1.0, None, ALU.mult, None, [r_E2[0]], [r_frow])
                ts("dve", frow[3][:, seg], Ls32[0][0:16, :], -1.0, None, ALU.mult, None, [r_Ls32[0]], [r_frow])
            for hd in range(2):
                memset("pool", qa[hd][64:128, :], 0.0, [r_qa[hd]])
                memset("pool", ka[hd][64:128, :], 0.0, [r_ka[hd]])
                memset("pool", qa[hd][64:68, :], 1.0, [r_qa[hd]])
                memset("pool", ka[hd][64:68, :], 1.0, [r_ka[hd]])
            load_pair_weights(None, ws_kv[0], ws_kv[1], 0)
            for hp in range(8):
                for hd in range(2):
                    hh = 2 * hp + hd
                    if hd == 0:
                        act(qa[hd][0:64, :], bT[0:64, hp, :], AF.Copy, [r_R2], [r_qa[hd]])
                    else:
                        tcopy("dve", qa[hd][0:64, :], bT[64:128, hp, :], [r_R2], [r_qa[hd]])
                    dma(qa[hd][64:65, :], frow[2][hh:hh + 1, :], r_qrow[hd][0], [r_frow, r_qa[hd]], [r_qrow[hd][0]])
                    dma(qa[hd][65:66, :], frow[3][hh:hh + 1, :], r_qrow[hd][1], [r_frow, r_qa[hd]], [r_qrow[hd][1]])
                    dma(ka[hd][66:67, :], frow[0][hh:hh + 1, :], r_krow[hd][0], [r_frow, r_ka[hd]], [r_krow[hd][0]])
                    dma(ka[hd][67:68, :], frow[1][hh:hh + 1, :], r_krow[hd][1], [r_frow, r_ka[hd]], [r_krow[hd][1]])

                def sink_k1(tg, bank):
                    for hd in range(2):
                        tcopy("dve", ka[hd][0:64, tg * 512:(tg + 1) * 512], PB[bank][hd * 64:(hd + 1) * 64, :], [r_PB[bank]], [r_ka[hd]])
                proj_featmajor(Wk, r_Wk, aT, r_R1, 6, sink_k1)
                proj_v(aT, r_R1, 6, 1.0)
                if hp + 1 < 8:
                    load_pair_weights(None, ws_kv[0], ws_kv[1], hp + 1)
                attn_fox_pair(hp)
            out_proj(1, fox_post_g[0], bT, r_R2)
            ffn(1)
            for t in range(NT):
                dma(y[b, t * 128:(t + 1) * 128, :], hview[:, t, :], r_h[t], [r_h[t]], [])
        P.wait_all("sp", r_h)
        P.emit()
    return nc


_NC_CACHE = {}


def kernel(**inputs):
    x = np.ascontiguousarray(inputs["x"], dtype=np.float32)
    B, S, _ = x.shape
    NB = B // N_CORES
    key = (NB, S)
    if key not in _NC_CACHE:
        _NC_CACHE[key] = build_nc(NB, S)
    nc = _NC_CACHE[key]
    wnames = ["sb_pre_g", "sb_w_qkv", "sb_w_o", "sb_post_g", "kv_norm_g", "w_kvf", "b_f", "fox_pre_g", "fox_w_q",
              "fox_w_o", "fox_post_g", "ffn_pre_g", "w_up", "conv_w", "conv_b", "w_down", "ffn_post_g"]
    ws = {k: np.ascontiguousarray(inputs[k], dtype=np.float32) for k in wnames}
    in_maps = []
    for c in range(N_CORES):
        m = {"x": x[c * NB:(c + 1) * NB]}
        m.update(ws)
        in_maps.append(m)
    res = run_bass_kernel_spmd(nc, in_maps, core_ids=list(range(N_CORES)))
    return np.concatenate([r["y"] for r in res.results], axis=0)
```

```python
from contextlib import ExitStack
import numpy as np
import concourse.bass as bass
import concourse.mybir as mybir
from concourse.bass_utils import run_bass_kernel_spmd

F32 = mybir.dt.float32
BF16 = mybir.dt.bfloat16
AF = mybir.ActivationFunctionType
ALU = mybir.AluOpType

D = 1024
H = 16
DH = 64
FF = 2816
NCH = FF // 128
EPS = 1e-6
N_CORES = 8


class Res:
    __slots__ = ("name", "lw", "rd", "sem", "semcnt")

    def __init__(self, name, sem=None):
        self.name = name
        self.lw = None
        self.rd = {}
        self.sem = sem
        self.semcnt = 0


class Prog:
    ENGS = ("pe", "act", "dve", "pool", "sp")

    def __init__(self, nc, ctx):
        self.nc = nc
        self.ctx = ctx
        self.streams = {e: [] for e in self.ENGS}
        self.count = {e: 0 for e in self.ENGS}
        self.waited = {e: {} for e in self.ENGS}
        self.esem = {e: ctx.enter_context(nc.semaphore("es_" + e)) for e in self.ENGS}
        self.nres = 0

    def res(self, name=None, dma=False):
        self.nres += 1
        name = name or ("r%d" % self.nres)
        sem = self.ctx.enter_context(self.nc.semaphore("ds%d" % self.nres)) if dma else None
        return Res(name, sem)

    def _deps(self, reads, writes):
        deps = []
        for r in reads:
            if r.lw is not None:
                deps.append(r.lw)
        for w in writes:
            if w.lw is not None:
                deps.append(w.lw)
            deps.extend(w.rd.items())
        return deps

    def _waits_for(self, eng, deps):
        best = {}
        for (key, val) in deps:
            if key == "pe" and eng == "pe":
                continue
            if val > best.get(key, 0):
                best[key] = val
        out = []
        wd = self.waited[eng]
        for key, val in best.items():
            if wd.get(key, 0) >= val:
                continue
            wd[key] = val
            out.append((key, val))
        return out

    def _sem_of(self, key):
        if isinstance(key, str):
            return self.esem[key]
        return key.sem

    def _record(self, ev, reads, writes):
        k, v = ev
        for r in reads:
            if r.rd.get(k, 0) < v:
                r.rd[k] = v
        for w in writes:
            w.lw = ev
            w.rd = {}

    def op(self, eng, fn, reads=(), writes=()):
        waits = self._waits_for(eng, self._deps(reads, writes))
        self.count[eng] += 1
        ev = (eng, self.count[eng])
        self.streams[eng].append((waits, fn, None))
        self._record(ev, reads, writes)
        return ev

    def dma(self, eng, fn, semres, reads=(), writes=()):
        waits = self._waits_for(eng, self._deps(reads, writes))
        semres.semcnt += 1
        ev = (semres, 16 * semres.semcnt)
        self.streams[eng].append((waits, fn, semres))
        self._record(ev, reads, writes)
        return ev

    def alias(self, olds, news):
        evs = {}
        for o in olds:
            if o.lw is not None:
                k, v = o.lw
                evs[k] = max(evs.get(k, 0), v)
            for k, v in o.rd.items():
                evs[k] = max(evs.get(k, 0), v)
        for n in news:
            for k, v in evs.items():
                if n.rd.get(k, 0) < v:
                    n.rd[k] = v

    def wait_all(self, eng, resources):
        deps = []
        for r in resources:
            if r.lw is not None:
                deps.append(r.lw)
            deps.extend(r.rd.items())
        waits = self._waits_for(eng, deps)
        self.streams[eng].append((waits, None, None))

    def emit(self):
        nc = self.nc
        with nc.Block() as block:
            def run(engname):
                def body(e):
                    esem = self.esem[engname]
                    for (waits, fn, semres) in self.streams[engname]:
                        for (key, val) in waits:
                            e.wait_ge(self._sem_of(key), val)
                        if fn is None:
                            continue
                        ins = fn(e)
                        if semres is None:
                            ins.then_inc(esem, 1)
                        else:
                            ins.then_inc(semres.sem, 16)
                return body
            block.tensor(run("pe"))
            block.scalar(run("act"))
            block.vector(run("dve"))
            block.gpsimd(run("pool"))
            block.sync(run("sp"))


def build_nc(NB, S):
    NT = S // 128
    NG = S // 512
    nc = bass.Bass("TRN2", target_bir_lowering=False)
    dt_in = lambda name, shape: nc.dram_tensor(name, list(shape), F32, kind="ExternalInput").ap()
    x = dt_in("x", [NB, S, D])
    sb_pre_g = dt_in("sb_pre_g", [1, D])
    sb_w_qkv = dt_in("sb_w_qkv", [1, D, 3 * D])
    sb_w_o = dt_in("sb_w_o", [1, D, D])
    sb_post_g = dt_in("sb_post_g", [1, D])
    kv_norm_g = dt_in("kv_norm_g", [D])
    w_kvf = dt_in("w_kvf", [D, 2 * D + H])
    b_f = dt_in("b_f", [H])
    fox_pre_g = dt_in("fox_pre_g", [1, D])
    fox_w_q = dt_in("fox_w_q", [1, D, D])
    fox_w_o = dt_in("fox_w_o", [1, D, D])
    fox_post_g = dt_in("fox_post_g", [1, D])
    ffn_pre_g = dt_in("ffn_pre_g", [2, D])
    w_up = dt_in("w_up", [2, D, 2 * FF])
    conv_w = dt_in("conv_w", [2, 3, 2 * FF])
    conv_b = dt_in("conv_b", [2, 2 * FF])
    w_down = dt_in("w_down", [2, FF, D])
    ffn_post_g = dt_in("ffn_post_g", [2, D])
    y = nc.dram_tensor("y", [NB, S, D], F32, kind="ExternalOutput").ap()

    ws_qkv = nc.dram_tensor("ws_qkv", [5, 8, 128, 1024], BF16).ap()
    ws_kv = nc.dram_tensor("ws_kv", [2, 8, 128, 1024], BF16).ap()
    ws_f = nc.dram_tensor("ws_f", [128, 8 * 16], BF16).ap()
    ws_o = nc.dram_tensor("ws_o", [2, 128, 8192], BF16).ap()
    ws_up = nc.dram_tensor("ws_up", [2, NCH, 128, 2048], BF16).ap()
    ws_dn = nc.dram_tensor("ws_dn", [2, NCH, 128, 1024], BF16).ap()

    with ExitStack() as ctx:
        P = Prog(nc, ctx)
        sbt = lambda name, shape, dt: ctx.enter_context(nc.sbuf_tensor(name, list(shape), dt))
        pst = lambda name, shape, dt: ctx.enter_context(nc.psum_tensor(name, list(shape), dt))

        Hbuf = sbt("Hbuf", [128, NT * D], F32)
        hview = Hbuf[:].rearrange("p (t d) -> p t d", d=D)
        r_h = [P.res("h%d" % t, dma=True) for t in range(NT)]
        R1N = max(8 * S, 16384)
        R2N = max(8 * S, 16384)
        R3N = max(4 * S + NT * 256, 8192)
        R1 = sbt("R1", [128, R1N], BF16)
        R2 = sbt("R2", [128, R2N], BF16)
        R3 = sbt("R3", [128, R3N], BF16)
        r_R1 = P.res("R1")
        r_R2 = P.res("R2")
        aT = R1[:, 0:8 * S].rearrange("p (k s) -> p k s", s=S)
        bT = R2[:, 0:8 * S].rearrange("p (k s) -> p k s", s=S)
        qa = [R3[:, 0:S], R3[:, S:2 * S]]
        ka = [R3[:, 2 * S:3 * S], R3[:, 3 * S:4 * S]]
        vpad = R3[:, 4 * S:4 * S + NT * 256].rearrange("p (t h c) -> p t h c", h=2, c=128)
        r_qa = [P.res("qa0", dma=True), P.res("qa1", dma=True)]
        r_ka = [P.res("ka0", dma=True), P.res("ka1", dma=True)]
        r_vpad = P.res("vpad")
        r_qrow = [[P.res("qrow%d%d" % (i, j), dma=True) for j in range(2)] for i in range(2)]
        r_krow = [[P.res("krow%d%d" % (i, j), dma=True) for j in range(2)] for i in range(2)]
        r_attn_all = r_qa + r_ka + [r_vpad] + r_qrow[0] + r_qrow[1] + r_krow[0] + r_krow[1]
        WoT = R3[:, 0:8192].rearrange("p (k n) -> p k n", n=D)
        r_Wo = P.res("Wo", dma=True)
        gT = R2[:, 0:NCH * 512].rearrange("p (c t) -> p c t", t=512)
        r_gT = [P.res("gT%d" % c) for c in range(NCH)]
        wupb = [R2[:, NCH * 512 + i * 2048: NCH * 512 + (i + 1) * 2048].rearrange("p (g k j) -> p g k j", g=2, k=8) for i in range(2)]
        r_wup = [P.res("wup%d" % i, dma=True) for i in range(2)]
        NWD = 8
        wdnb = [R3[:, i * 1024:(i + 1) * 1024] for i in range(NWD)]
        r_wdn = [P.res("wdn%d" % i, dma=True) for i in range(NWD)]

        Wq = sbt("Wq", [128, 8, 128], BF16); r_Wq = P.res("Wq", dma=True)
        Wk = sbt("Wk", [128, 8, 128], BF16); r_Wk = P.res("Wk", dma=True)
        Wv = sbt("Wv", [128, 8, 128], BF16); r_Wv = P.res("Wv", dma=True)
        Wf = sbt("Wf", [128, 8, 16], BF16); r_Wf = P.res("Wf", dma=True)
        e32 = [sbt("e32_%d" % i, [128, 516], F32) for i in range(4)]; r_e32 = [P.res() for _ in range(4)]
        spb = [sbt("spb_%d" % i, [128, 512], BF16) for i in range(4)]; r_sp = [P.res() for _ in range(4)]
        E2 = [sbt("E2_%d" % i, [128, 512], F32) for i in range(2)]; r_E2 = [P.res() for _ in range(2)]
        Ab = [sbt("Ab_%d" % i, [128, 512], BF16) for i in range(3)]; r_A = [P.res() for _ in range(3)]
        Ls32 = [sbt("Ls32_%d" % i, [128, 512], F32) for i in range(1)]; r_Ls32 = [P.res() for _ in range(1)]
        rinv = sbt("rinv", [128, 512], F32); r_rinv = P.res()
        junk = sbt("junk", [128, 1024], BF16); r_junk = P.res()
        xn = sbt("xn", [128, 1024], BF16); r_xn = P.res()
        tmpf = sbt("tmpf", [128, 1024], F32); r_tmpA = P.res(); r_tmpB = P.res()
        Gpre = sbt("Gpre", [128, 1024], F32); r_Gpre = P.res("Gpre", dma=True)
        Gpost = Gpre; r_Gpost = r_Gpre
        st = sbt("stats", [128, 8], F32); r_st = P.res()
        pst_ = sbt("pstats", [128, 3 * 16], F32); r_pst = P.res()
        frowA = sbt("frowA", [80, S], BF16)
        frowB = sbt("frowB", [16, S], BF16)
        frow = [frowA[0:16, :], frowA[32:48, :], frowA[64:80, :], frowB[0:16, :]]
        r_frow = P.res()
        carry = sbt("carry", [16, 1], F32); r_carry = P.res()
        bfneg = sbt("bfneg", [16, 1], F32); r_bf = P.res("bf", dma=True)
        CW = [sbt("CW%d" % l, [128, 4, 2 * NCH], F32) for l in range(2)]; r_CW = [P.res("CW%d" % l, dma=True) for l in range(2)]
        hraw = e32; r_hraw = r_e32
        cacc = [E2[0][:, :], E2[1][:, :], tmpf[:, 0:512], tmpf[:, 512:1024]]; r_cacc = [r_E2[0], r_E2[1], r_tmpA, r_tmpB]
        gl = rinv; r_gl = r_rinv
        halo = sbt("halo", [128, 2 * NCH, 2], F32); r_halo = P.res()
        ident = sbt("ident", [128, 128], BF16); r_ident = P.res()
        mstrict = sbt("mstrict", [128, 128], BF16); r_mstrict = P.res()
        mincl = sbt("mincl", [128, 128], BF16); r_mincl = P.res()
        UI = sbt("UI", [128, 128], BF16); r_UI = P.res()
        Lc = sbt("Lc", [128, 128], BF16); r_Lc = P.res()
        PB = [pst("PB%d" % i, [128, 512], F32) for i in range(7)]; r_PB = [P.res("PB%d" % i) for i in range(7)]
        PT = pst("PT", [128, 1024], BF16); r_PT = P.res("PT")

        def mm(out, lhsT, rhs, start, stop, reads, writes):
            P.op("pe", lambda e: e.matmul(out, lhsT=lhsT, rhs=rhs, start=start, stop=stop, skip_group_check=True), reads, writes)

        def act(out, in_, func, reads, writes, **kw):
            P.op("act", lambda e: e.activation(out=out, in_=in_, func=func, **kw), reads, writes)

        def tcopy(eng, out, in_, reads, writes):
            P.op(eng, lambda e: e.tensor_copy(out=out, in_=in_), reads, writes)

        def tt(eng, out, in0, in1, op, reads, writes):
            P.op(eng, lambda e: e.tensor_tensor(out=out, in0=in0, in1=in1, op=op), reads, writes)

        def ts(eng, out, in0, s1, s2, op0, op1, reads, writes):
            if s2 is None:
                P.op(eng, lambda e: e.tensor_scalar(out=out, in0=in0, scalar1=s1, scalar2=None, op0=op0), reads, writes)
            else:
                P.op(eng, lambda e: e.tensor_scalar(out=out, in0=in0, scalar1=s1, scalar2=s2, op0=op0, op1=op1), reads, writes)

        def stt(eng, out, in0, scalar, in1, op0, op1, reads, writes):
            P.op(eng, lambda e: e.scalar_tensor_tensor(out=out, in0=in0, scalar=scalar, in1=in1, op0=op0, op1=op1), reads, writes)

        def memset(eng, ap, val, writes):
            P.op(eng, lambda e: e.memset(ap, val), (), writes)

        def dma(out, in_, semres, reads, writes, slow=False):
            if slow:
                P.dma("sp", lambda e: e.dma_start(out=out, in_=in_, allow_slow_non_contiguous=True), semres, reads, writes)
            else:
                P.dma("sp", lambda e: e.dma_start(out=out, in_=in_), semres, reads, writes)

        def aff(ap, pattern, cmp, cm, writes):
            P.op("pool", lambda e: e.affine_select(out=ap, in_=ap, pattern=pattern, compare_op=cmp, fill=0.0, base=0, channel_multiplier=cm), writes, writes)
        for (t_, r_) in ((ident, r_ident), (mstrict, r_mstrict), (mincl, r_mincl), (UI, r_UI), (Lc, r_Lc)):
            memset("pool", t_[:], 1.0, [r_])
        aff(ident[:], [[-1, 128]], ALU.is_equal, 1, [r_ident])
        aff(mstrict[:], [[1, 128]], ALU.is_gt, -1, [r_mstrict])
        aff(mincl[:], [[1, 128]], ALU.is_ge, -1, [r_mincl])
        aff(UI[:], [[-1, 128]], ALU.is_ge, 1, [r_UI])
        aff(Lc[:], [[1, 128]], ALU.is_gt, -1, [r_Lc])
        memset("pool", halo[:], 0.0, [r_halo])
        for l in range(2):
            for k in range(3):
                dma(CW[l][:, k, :], conv_w[l, k].rearrange("(c p) -> p c", p=128), r_CW[l], [], [r_CW[l]], slow=True)
            dma(CW[l][:, 3, :], conv_b[l].rearrange("(c p) -> p c", p=128), r_CW[l], [], [r_CW[l]], slow=True)
        dma(bfneg[:], b_f.rearrange("(h o) -> h o", o=1), r_bf, [], [r_bf], slow=True)
        ts("dve", bfneg[:], bfneg[:], -1.0, None, ALU.mult, None, [r_bf], [r_bf])

        NSLOT = 4 if NT * D >= 4 * 4096 else 2
        if NSLOT == 4:
            stg32 = [Hbuf[:, i * 4096:(i + 1) * 4096] for i in range(4)]
        else:
            stg32 = [R1[:, i * 8192:(i + 1) * 8192].bitcast(F32) for i in range(2)]
        stg16 = [R2[:, i * 4096:(i + 1) * 4096] for i in range(NSLOT)]
        r_s32 = [P.res("s32_%d" % i, dma=True) for i in range(NSLOT)]
        r_s16 = [P.res("s16_%d" % i) for i in range(NSLOT)]
        r_s16d = [P.res("s16d_%d" % i, dma=True) for i in range(NSLOT)]
        cast_engs = ["dve", "pool", "act"]
        ucount = [0]

        def cast_unit(srcs, E, dst, outview=None):
            cast_list.append((srcs, E, dst, outview))

        cast_list = []

        def emit_casts():
            n = len(cast_list)
            LOOK = NSLOT - 1

            def emit_in(u):
                i = u % NSLOT
                for (vf, src) in cast_list[u][0]:
                    dma(vf(stg32[i]), src, r_s32[i], [], [r_s32[i]])

            def emit_cast_out(u):
                i = u % NSLOT
                srcs, E, dst, outview = cast_list[u]
                if u % 2 == 0:
                    act(stg16[i][:, 0:E], stg32[i][:, 0:E], AF.Copy, [r_s32[i]], [r_s16[i]])
                else:
                    tcopy("dve", stg16[i][:, 0:E], stg32[i][:, 0:E], [r_s32[i]], [r_s16[i]])
                src16 = stg16[i][:, 0:E] if outview is None else outview(stg16[i][:, 0:E])
                dma(dst, src16, r_s16d[i], [r_s16[i]], [r_s16d[i]])
            for u in range(n + LOOK):
                if u < n:
                    emit_in(u)
                if u - LOOK >= 0:
                    emit_cast_out(u - LOOK)

        def img4(stage, a, b, c):
            return stage[:, 0:a * b * c].rearrange("p (a b c) -> p a b c", a=a, b=b)

        def img3(stage, a, b):
            return stage[:, 0:a * b].rearrange("p (a b) -> p a b", a=a)

        def cast_cols(src2d, col0, dst_units):
            for half in range(2):
                srcs = []
                for hp4 in range(4):
                    hp = half * 4 + hp4
                    srcs.append((lambda s, hp4=hp4: img4(s, 4, 8, 128)[:, hp4],
                                 src2d[:, col0 + hp * 128: col0 + (hp + 1) * 128].rearrange("(k p) j -> p k j", p=128)))
                cast_unit(srcs, 4096, dst_units[half * 4:half * 4 + 4].rearrange("u p e -> p u e"),
                          outview=lambda v: v.rearrange("p (u e) -> p u e", u=4))
        cast_cols(sb_w_qkv[0], 0, ws_qkv[0])
        cast_cols(sb_w_qkv[0], D, ws_qkv[1])
        cast_cols(sb_w_qkv[0], 2 * D, ws_qkv[2])

        def cast_wo(src2d, dst):
            for half in range(2):
                srcs = [(lambda s: img3(s, 4, 1024),
                         src2d[half * 512:(half + 1) * 512, :].rearrange("(k p) n -> p k n", p=128))]
                cast_unit(srcs, 4096, dst[:, half * 4096:(half + 1) * 4096])
        cast_wo(sb_w_o[0], ws_o[0])

        def cast_ffn(l):
            for c0 in range(0, NCH, 2):
                srcs = []
                for gu in range(2):
                    for ci in range(2):
                        c = c0 + ci
                        srcs.append((lambda s, gu=gu, ci=ci: s[:, 0:4096].rearrange("p (c g k j) -> p c g k j", c=2, g=2, k=8)[:, ci, gu],
                                     w_up[l][:, gu * FF + c * 128: gu * FF + (c + 1) * 128].rearrange("(k p) j -> p k j", p=128)))
                cast_unit(srcs, 4096, ws_up[l, c0:c0 + 2].rearrange("c p e -> p c e"),
                          outview=lambda v: v.rearrange("p (c e) -> p c e", c=2))
            for c0 in range(0, NCH, 4):
                n = min(4, NCH - c0)
                srcs = [(lambda s, n=n: img3(s, n, 1024),
                         w_down[l][c0 * 128:(c0 + n) * 128, :].rearrange("(c p) n -> p c n", p=128))]
                cast_unit(srcs, n * 1024, ws_dn[l, c0:c0 + n].rearrange("c p e -> p c e"),
                          outview=lambda v, n=n: v.rearrange("p (c e) -> p c e", c=n))
        cast_ffn(0)
        cast_cols(w_kvf, 0, ws_kv[0])
        cast_cols(w_kvf, D, ws_kv[1])
        cast_unit([(lambda s: img3(s, 8, 16), w_kvf[:, 2 * D:2 * D + H].rearrange("(k p) j -> p k j", p=128))], 128, ws_f)
        cast_cols(fox_w_q[0], 0, ws_qkv[3])
        cast_wo(fox_w_o[0], ws_o[1])
        cast_ffn(1)
        emit_casts()
        r_scr = r_s16d
        P.alias(r_s32 + r_s16 + r_s16d, [r_R1, r_R2] + r_h)

        def prenorm(g_ap, dstT, r_dst, first_alias=None):
            dma(Gpre[:], g_ap.partition_broadcast(128), r_Gpre, [], [r_Gpre])
            for t in range(NT):
                act(junk[:], hview[:, t, :], AF.Square, [r_h[t]], [r_junk, r_pst], accum_out=pst_[:, t:t + 1])
            act(pst_[:, 16:16 + NT], pst_[:, 0:NT], AF.Ln, [r_pst], [r_pst], scale=1.0 / D, bias=EPS)
            act(pst_[:, 32:32 + NT], pst_[:, 16:16 + NT], AF.Exp, [r_pst], [r_pst], scale=-0.5)
            for t in range(NT):
                stt("dve", xn[:], hview[:, t, :], pst_[:, 32 + t:33 + t], Gpre[:], ALU.mult, ALU.mult, [r_h[t], r_pst, r_Gpre], [r_xn])
                for kc in range(8):
                    P.op("pe", lambda e, kc=kc: e.transpose(PT[:, kc * 128:(kc + 1) * 128], xn[:, kc * 128:(kc + 1) * 128], ident[:]),
                         [r_xn, r_ident], [r_PT])
                tcopy("dve", dstT[:, :, t * 128:(t + 1) * 128], PT[:].rearrange("p (k j) -> p k j", j=128), [r_PT], [r_dst])

        def postnorm_residual(t, ba, bb, g_res):
            act(junk[:, 0:512], PB[ba][:], AF.Square, [r_PB[ba]], [r_junk, r_st], accum_out=st[:, 3:4])
            act(junk[:, 512:1024], PB[bb][:], AF.Square, [r_PB[bb]], [r_junk, r_st], accum_out=st[:, 4:5])
            tt("dve", st[:, 5:6], st[:, 3:4], st[:, 4:5], ALU.add, [r_st], [r_st])
            act(st[:, 6:7], st[:, 5:6], AF.Ln, [r_st], [r_st], scale=1.0 / D, bias=EPS)
            act(st[:, 7:8], st[:, 6:7], AF.Exp, [r_st], [r_st], scale=-0.5)
            stt("dve", tmpf[:, 0:512], PB[ba][:], st[:, 7:8], Gpost[:, 0:512], ALU.mult, ALU.mult, [r_PB[ba], r_st, g_res], [r_tmpA])
            stt("dve", tmpf[:, 512:1024], PB[bb][:], st[:, 7:8], Gpost[:, 512:1024], ALU.mult, ALU.mult, [r_PB[bb], r_st, g_res], [r_tmpB])
            tt("pool", hview[:, t, :], hview[:, t, :], tmpf[:], ALU.add, [r_h[t], r_tmpA, r_tmpB], [r_h[t]])

        def out_proj(widx, g_ap, srcT, r_src):
            P.alias(r_attn_all, [r_Wo])
            dma(WoT[:, 0:4, :], ws_o[widx][:, 0:4096].rearrange("p (k n) -> p k n", n=D), r_Wo, r_scr, [r_Wo])
            dma(WoT[:, 4:8, :], ws_o[widx][:, 4096:8192].rearrange("p (k n) -> p k n", n=D), r_Wo, r_scr, [r_Wo])
            dma(Gpost[:], g_ap.partition_broadcast(128), r_Gpost, [], [r_Gpost])
            for t in range(NT):
                ba, bb = (0, 1) if t % 2 == 0 else (2, 3)
                for n, bk in ((0, ba), (1, bb)):
                    for kc in range(8):
                        mm(PB[bk][:], srcT[:, kc, t * 128:(t + 1) * 128], WoT[:, kc, n * 512:(n + 1) * 512],
                           kc == 0, kc == 7, [r_src, r_Wo], [r_PB[bk]])
                postnorm_residual(t, ba, bb, r_Gpost)
            P.alias([r_Wo], r_attn_all)

        def pipeline(n, stages):
            ns = len(stages)
            for tick in range(n + ns - 1):
                for k in range(ns - 1, -1, -1):
                    i = tick - k
                    if 0 <= i < n:
                        stages[k](i)

        def load_pair_weights(qsrc, ksrc, vsrc, hp):
            if qsrc is not None:
                dma(Wq[:], qsrc[hp].rearrange("p (k j) -> p k j", j=128), r_Wq, r_scr, [r_Wq])
            dma(Wk[:], ksrc[hp].rearrange("p (k j) -> p k j", j=128), r_Wk, r_scr, [r_Wk])
            dma(Wv[:], vsrc[hp].rearrange("p (k j) -> p k j", j=128), r_Wv, r_scr, [r_Wv])

        pbank = [0]

        def next_pbank():
            pbank[0] += 1
            return (6, 0, 1)[pbank[0] % 3]

        def proj_featmajor(Wt, r_W, srcT, r_src, bank, sink):
            for tg in range(NG):
                bank = next_pbank()
                for kc in range(8):
                    mm(PB[bank][:], Wt[:, kc, :], srcT[:, kc, tg * 512:(tg + 1) * 512], kc == 0, kc == 7, [r_W, r_src], [r_PB[bank]])
                sink(tg, bank)

        def proj_v(srcT, r_src, bank, padval):
            memset("pool", vpad[:, :, 0, 64:128], padval, [r_vpad])
            memset("pool", vpad[:, :, 1, 0:64], padval, [r_vpad])
            for t4 in range(0, NT, 4):
                bank = next_pbank()
                for ti in range(4):
                    t = t4 + ti
                    for kc in range(8):
                        mm(PB[bank][:, ti * 128:(ti + 1) * 128], srcT[:, kc, t * 128:(t + 1) * 128], Wv[:, kc, :], kc == 0, kc == 7,
                           [r_src, r_Wv], [r_PB[bank]])
                pv = PB[bank][:].rearrange("p (t c) -> p t c", c=128)
                tcopy("dve", vpad[:, t4:t4 + 4, 0, 0:64], pv[:, :, 0:64], [r_PB[bank]], [r_vpad])
                tcopy("dve", vpad[:, t4:t4 + 4, 1, 64:128], pv[:, :, 64:128], [r_PB[bank]], [r_vpad])

        def attn_sb_pair(hp, obank_base):
            units = []
            for g in range(NG):
                for sb in range(4 * g + 3, -1, -1):
                    for hd in range(2):
                        units.append((g, sb, hd))
            n = len(units)

            def geom(u):
                g, sb, hd = units[u]
                c0 = max(sb * 128, g * 512) - g * 512
                return g, sb, hd, c0, (sb >= 4 * g)

            def s_z(u):
                g, sb, hd, c0, diag = geom(u)
                zb = u % 2
                mm(PB[zb][:, c0:512], ka[hd][0:128, sb * 128:(sb + 1) * 128], qa[hd][0:128, g * 512 + c0:(g + 1) * 512], True, True,
                   [r_ka[hd], r_qa[hd]], [r_PB[zb]])

            def s_esp(u):
                g, sb, hd, c0, diag = geom(u)
                zb = u % 2
                eb = u % 4
                act(e32[eb][:, c0:512], PB[zb][:, c0:512], AF.Exp, [r_PB[zb]], [r_e32[eb]])
                act(spb[eb][:, c0:512], e32[eb][:, c0:512], AF.Ln, [r_e32[eb]], [r_sp[eb]], bias=1.0)
                if diag:
                    tt("pool", spb[eb][:, c0:c0 + 128], spb[eb][:, c0:c0 + 128], mstrict[:], ALU.mult, [r_sp[eb], r_mstrict], [r_sp[eb]])

            def s_x(u):
                g, sb, hd, c0, diag = geom(u)
                eb = u % 4
                xb = 2 + hd
                mm(PB[xb][:, c0:512], UI[:], spb[eb][:, c0:512], sb == 4 * g + 3, False, [r_UI, r_sp[eb]], [r_PB[xb]])

            def s_a(u):
                g, sb, hd, c0, diag = geom(u)
                eb = u % 4
                xb = 2 + hd
                ab = u % 3
                act(E2[hd][:, c0:512], PB[xb][:, c0:512], AF.Exp, [r_PB[xb]], [r_E2[hd]], scale=-1.0)
                if sb > 0:
                    mm(PB[xb][:, c0:512], Lc[:], spb[eb][:, c0:512], False, False, [r_Lc, r_sp[eb]], [r_PB[xb]])
                tt("dve", Ab[ab][:, c0:512], e32[eb][:, c0:512], E2[hd][:, c0:512], ALU.mult, [r_e32[eb], r_E2[hd]], [r_A[ab]])
                if diag:
                    tt("pool", Ab[ab][:, c0:c0 + 128], Ab[ab][:, c0:c0 + 128], mstrict[:], ALU.mult, [r_A[ab], r_mstrict], [r_A[ab]])

            def s_pv(u):
                g, sb, hd, c0, diag = geom(u)
                ab = u % 3
                ob = obank_base + (g % 2)
                firstm = (sb == 4 * g + 3 and hd == 0)
                mm(PB[ob][:, c0:512], vpad[:, sb, hd, :], Ab[ab][:, c0:512], firstm, False, [r_vpad, r_A[ab]], [r_PB[ob]])
                if sb == 0 and hd == 1:
                    tcopy("dve", bT[:, hp, g * 512:(g + 1) * 512], PB[ob][:], [r_PB[ob]], [r_R2])

            pipeline(n, [s_z, s_esp, s_x, s_a, s_pv])

        def attn_fox_pair(hp):
            units = []
            for g in range(NG):
                for sb in range(4 * g + 3, -1, -1):
                    for hd in range(2):
                        units.append((g, sb, hd))
            n = len(units)

            def geom(u):
                g, sb, hd = units[u]
                c0 = max(sb * 128, g * 512) - g * 512
                return g, sb, hd, c0, (sb >= 4 * g)

            def obank(g, hd):
                return (4 + hd) if g % 2 == 0 else (2 + hd)

            def s_z(u):
                g, sb, hd, c0, diag = geom(u)
                zb = u % 2
                mm(PB[zb][:, c0:512], ka[hd][0:128, sb * 128:(sb + 1) * 128], qa[hd][0:128, g * 512 + c0:(g + 1) * 512], True, True,
                   [r_ka[hd], r_qa[hd]] + r_qrow[hd] + r_krow[hd], [r_PB[zb]])

            def s_a(u):
                g, sb, hd, c0, diag = geom(u)
                zb = u % 2
                ab = u % 3
                act(Ab[ab][:, c0:512], PB[zb][:, c0:512], AF.Exp, [r_PB[zb]], [r_A[ab]])
                if diag:
                    tt("pool", Ab[ab][:, c0:c0 + 128], Ab[ab][:, c0:c0 + 128], mincl[:], ALU.mult, [r_A[ab], r_mincl], [r_A[ab]])

            def s_pv(u):
                g, sb, hd, c0, diag = geom(u)
                ab = u % 3
                ob = obank(g, hd)
                mm(PB[ob][:, c0:512], vpad[:, sb, hd, :], Ab[ab][:, c0:512], sb == 4 * g + 3, False, [r_vpad, r_A[ab]], [r_PB[ob]])
                if sb == 0:
                    if hd == 0:
                        P.op("dve", lambda e: e.reciprocal(out=rinv[0:64, :], in_=PB[ob][64:128, :]), [r_PB[ob]], [r_rinv])
                        tt("dve", bT[0:64, hp, g * 512:(g + 1) * 512], PB[ob][0:64, :], rinv[0:64, :], ALU.mult, [r_PB[ob], r_rinv], [r_R2])
                    else:
                        P.op("dve", lambda e: e.reciprocal(out=rinv[64:128, :], in_=PB[ob][0:64, :]), [r_PB[ob]], [r_rinv])
                        tt("dve", bT[64:128, hp, g * 512:(g + 1) * 512], PB[ob][64:128, :], rinv[64:128, :], ALU.mult, [r_PB[ob], r_rinv], [r_R2])

            pipeline(n, [s_z, s_a, s_pv])

        def ffn(l, after_tile=None):
            prenorm(ffn_pre_g[l], aT, r_R1)
            dma(Gpost[:], ffn_post_g[l].partition_broadcast(128), r_Gpost, [], [r_Gpost])
            P.alias([r_R2], r_gT + r_wup)
            P.alias(r_attn_all, r_wdn)
            LAG, PRE = 2, 6
            for tg in range(NG):
                def load_up(c):
                    dma(wupb[c % 2][:], ws_up[l, c].rearrange("p (g k j) -> p g k j", g=2, k=8), r_wup[c % 2], r_scr, [r_wup[c % 2]])
                steps = [(half, c) for half in range(2) for c in range(NCH)]
                loaded = [0]

                def ensure_loaded(k):
                    while loaded[0] <= min(k, len(steps) - 1):
                        i = loaded[0] % NWD
                        dma(wdnb[i], ws_dn[l, steps[loaded[0]][1]], r_wdn[i], r_scr, [r_wdn[i]])
                        loaded[0] += 1

                def down_step(k):
                    ensure_loaded(k + PRE)
                    half, c = steps[k]
                    i = k % NWD
                    for t2 in range(2):
                        tl = 2 * half + t2
                        for nn in range(2):
                            bk = 2 * t2 + nn
                            mm(PB[bk][:], gT[:, c, tl * 128:(tl + 1) * 128], wdnb[i][:, nn * 512:(nn + 1) * 512], c == 0, c == NCH - 1,
                               [r_gT[c], r_wdn[i]], [r_PB[bk]])
                    if c == NCH - 1:
                        for t2 in range(2):
                            postnorm_residual(tg * 4 + 2 * half + t2, 2 * t2, 2 * t2 + 1, r_Gpost)
                            if after_tile is not None:
                                after_tile(tg * 4 + 2 * half + t2)

                def gate_mul(cc):
                    cg, cu = 2 * (cc % 2), 2 * (cc % 2) + 1
                    act(gl[:], cacc[cg], AF.Gelu_apprx_tanh, [r_cacc[cg]], [r_gl])
                    tt("pool", gT[:, cc, :], gl[:], cacc[cu], ALU.mult, [r_gl, r_cacc[cu]], [r_gT[cc]])

                load_up(0)
                ensure_loaded(PRE - 1)
                kdown = 0
                for c in range(NCH):
                    if c + 1 < NCH:
                        load_up(c + 1)
                    for gu in range(2):
                        bk = 4 + gu
                        for kc in range(8):
                            mm(PB[bk][:], wupb[c % 2][:, gu, kc, :], aT[:, kc, tg * 512:(tg + 1) * 512], kc == 0, kc == 7,
                               [r_wup[c % 2], r_R1], [r_PB[bk]])
                    if c >= LAG:
                        down_step(kdown)
                        kdown += 1
                    for gu in range(2):
                        bk = 4 + gu
                        ci = gu * NCH + c
                        hb = 2 * (c % 2) + gu
                        hr, rh = hraw[hb], r_hraw[hb]
                        ca, rca = cacc[hb], r_cacc[hb]
                        if tg == 0:
                            memset("pool", hr[:, 2:4], 0.0, [rh])
                        else:
                            tcopy("pool", hr[:, 2:4], halo[:, ci, :], [r_halo], [rh])
                        act(hr[:, 4:516], PB[bk][:], AF.Copy, [r_PB[bk]], [rh])
                        tcopy("pool", halo[:, ci, :], hr[:, 514:516], [rh], [r_halo])
                        act(ca, PB[bk][:], AF.Identity, [r_PB[bk], r_CW[l]], [rca],
                            scale=CW[l][:, 2, ci:ci + 1], bias=CW[l][:, 3, ci:ci + 1])
                        stt("dve", ca, hr[:, 3:515], CW[l][:, 1, ci:ci + 1], ca, ALU.mult, ALU.add, [rh, r_CW[l], rca], [rca])
                        stt("dve", ca, hr[:, 2:514], CW[l][:, 0, ci:ci + 1], ca, ALU.mult, ALU.add, [rh, r_CW[l], rca], [rca])
                    if c > 0:
                        gate_mul(c - 1)
                    if c == NCH - 1:
                        gate_mul(c)
                while kdown < len(steps):
                    down_step(kdown)
                    kdown += 1
            P.alias(r_gT + r_wup, [r_R2])
            P.alias(r_wdn, r_attn_all)

        for t in range(NT):
            dma(hview[:, t, :], x[0, t * 128:(t + 1) * 128, :], r_h[t], [], [r_h[t]])
        for b in range(NB):
            prenorm(sb_pre_g[0], aT, r_R1)
            for hd in range(2):
                memset("pool", qa[hd][64:128, :], 0.0, [r_qa[hd]])
                memset("pool", ka[hd][64:128, :], 0.0, [r_ka[hd]])
            load_pair_weights(ws_qkv[0], ws_qkv[1], ws_qkv[2], 0)
            for hp in range(8):

                def sink_q(tg, bank):
                    for hd in range(2):
                        ts("dve", qa[hd][0:64, tg * 512:(tg + 1) * 512], PB[bank][hd * 64:(hd + 1) * 64, :], 0.125, None, ALU.mult, None,
                           [r_PB[bank]], [r_qa[hd]])

                def sink_k(tg, bank):
                    for hd in range(2):
                        tcopy("dve", ka[hd][0:64, tg * 512:(tg + 1) * 512], PB[bank][hd * 64:(hd + 1) * 64, :], [r_PB[bank]], [r_ka[hd]])
                proj_featmajor(Wq, r_Wq, aT, r_R1, 6, sink_q)
                proj_featmajor(Wk, r_Wk, aT, r_R1, 6, sink_k)
                proj_v(aT, r_R1, 6, 0.0)
                if hp + 1 < 8:
                    load_pair_weights(ws_qkv[0], ws_qkv[1], ws_qkv[2], hp + 1)
                attn_sb_pair(hp, 4)
            out_proj(0, sb_post_g[0], bT, r_R2)
            ffn(0)
            prenorm(fox_pre_g[0], aT, r_R1)
            for hp in range(8):
                dma(Wq[:], ws_qkv[3][hp].rearrange("p (k j) -> p k j", j=128), r_Wq, r_scr, [r_Wq])

                def sink_qall(tg, bank, hp=hp):
                    ts("dve", bT[:, hp, tg * 512:(tg + 1) * 512], PB[bank][:], 0.125, None, ALU.mult, None, [r_PB[bank]], [r_R2])
                proj_featmajor(Wq, r_Wq, aT, r_R1, 6, sink_qall)
            prenorm(kv_norm_g, aT, r_R1)
            dma(Wf[:], ws_f.rearrange("p (k j) -> p k j", j=16), r_Wf, r_scr, [r_Wf])
            memset("pool", E2[1][0:16, 0:512], 1.0, [r_E2[1]])
            for tg in range(NG):
                seg = slice(tg * 512, (tg + 1) * 512)
                for kc in range(8):
                    mm(PB[6][0:16, :], Wf[:, kc, :], aT[:, kc, seg], kc == 0, kc == 7, [r_Wf, r_R1], [r_PB[6]])
                act(e32[0][0:16, 0:512], PB[6][0:16, :], AF.Exp, [r_PB[6], r_bf], [r_e32[0]], scale=-1.0, bias=bfneg[:, 0:1])
                act(e32[1][0:16, 0:512], e32[0][0:16, 0:512], AF.Ln, [r_e32[0]], [r_e32[1]], bias=1.0)
                if tg == 0:
                    P.op("dve", lambda e: e.tensor_tensor_scan(out=E2[0][0:16, 0:512], data0=E2[1][0:16, 0:512], data1=e32[1][0:16, 0:512],
                                                               initial=0.0, op0=ALU.mult, op1=ALU.add),
                         [r_E2[1], r_e32[1]], [r_E2[0]])
                else:
                    P.op("dve", lambda e: e.tensor_tensor_scan(out=E2[0][0:16, 0:512], data0=E2[1][0:16, 0:512], data1=e32[1][0:16, 0:512],
                                                               initial=carry[:, 0:1], op0=ALU.mult, op1=ALU.add),
                         [r_E2[1], r_e32[1], r_carry], [r_E2[0]])
                tcopy("dve", carry[:, 0:1], E2[0][0:16, 511:512], [r_E2[0]], [r_carry])
                tcopy("dve", frow[0][:, seg], E2[0][0:16, 0:512], [r_E2[0]], [r_frow])
                tt("dve", Ls32[0][0:16, :], E2[0][0:16, 0:512], frow[0][:, seg], ALU.subtract, [r_E2[0], r_frow], [r_Ls32[0]])
                tcopy("dve", frow[1][:, seg], Ls32[0][0:16, :], [r_Ls32[0]], [r_frow])
                ts("dve", frow[2][:, seg], E2[0][0:16, 0:512], -1.0, None, ALU.mult, None, [r_E2[0]], [r_frow])
                ts("dve", frow[3][:, seg], Ls32[0][0:16, :], -1.0, None, ALU.mult, None, [r_Ls32[0]], [r_frow])
            for hd in range(2):
                memset("pool", qa[hd][64:128, :], 0.0, [r_qa[hd]])
                memset("pool", ka[hd][64:128, :], 0.0, [r_ka[hd]])
                memset("pool", qa[hd][64:68, :], 1.0, [r_qa[hd]])
                memset("pool", ka[hd][64:68, :], 1.0, [r_ka[hd]])
            load_pair_weights(None, ws_kv[0], ws_kv[1], 0)
            for hp in range(8):
                for hd in range(2):
                    hh = 2 * hp + hd
                    if hd == 0:
                        act(qa[hd][0:64, :], bT[0:64, hp, :], AF.Copy, [r_R2], [r_qa[hd]])
                    else:
                        tcopy("dve", qa[hd][0:64, :], bT[64:128, hp, :], [r_R2], [r_qa[hd]])
                    dma(qa[hd][64:65, :], frow[2][hh:hh + 1, :], r_qrow[hd][0], [r_frow, r_qa[hd]], [r_qrow[hd][0]])
                    dma(qa[hd][65:66, :], frow[3][hh:hh + 1, :], r_qrow[hd][1], [r_frow, r_qa[hd]], [r_qrow[hd][1]])
                    dma(ka[hd][66:67, :], frow[0][hh:hh + 1, :], r_krow[hd][0], [r_frow, r_ka[hd]], [r_krow[hd][0]])
                    dma(ka[hd][67:68, :], frow[1][hh:hh + 1, :], r_krow[hd][1], [r_frow, r_ka[hd]], [r_krow[hd][1]])

                def sink_k1(tg, bank):
                    for hd in range(2):
                        tcopy("dve", ka[hd][0:64, tg * 512:(tg + 1) * 512], PB[bank][hd * 64:(hd + 1) * 64, :], [r_PB[bank]], [r_ka[hd]])
                proj_featmajor(Wk, r_Wk, aT, r_R1, 6, sink_k1)
                proj_v(aT, r_R1, 6, 1.0)
                if hp + 1 < 8:
                    load_pair_weights(None, ws_kv[0], ws_kv[1], hp + 1)
                attn_fox_pair(hp)
            out_proj(1, fox_post_g[0], bT, r_R2)
            def stream_io(t, b=b):
                dma(y[b, t * 128:(t + 1) * 128, :], hview[:, t, :], r_h[t], [r_h[t]], [])
                if b + 1 < NB:
                    dma(hview[:, t, :], x[b + 1, t * 128:(t + 1) * 128, :], r_h[t], [], [r_h[t]])
            ffn(1, after_tile=stream_io)
        P.wait_all("sp", r_h)
        P.emit()
    return nc


_NC_CACHE = {}


def kernel(**inputs):
    x = np.ascontiguousarray(inputs["x"], dtype=np.float32)
    B, S, _ = x.shape
    NB = B // N_CORES
    key = (NB, S)
    if key not in _NC_CACHE:
        _NC_CACHE[key] = build_nc(NB, S)
    nc = _NC_CACHE[key]
    wnames = ["sb_pre_g", "sb_w_qkv", "sb_w_o", "sb_post_g", "kv_norm_g", "w_kvf", "b_f", "fox_pre_g", "fox_w_q",
              "fox_w_o", "fox_post_g", "ffn_pre_g", "w_up", "conv_w", "conv_b", "w_down", "ffn_post_g"]
    ws = {k: np.ascontiguousarray(inputs[k], dtype=np.float32) for k in wnames}
    in_maps = []
    for c in range(N_CORES):
        m = {"x": x[c * NB:(c + 1) * NB]}
        m.update(ws)
        in_maps.append(m)
    res = run_bass_kernel_spmd(nc, in_maps, core_ids=list(range(N_CORES)))
    return np.concatenate([r["y"] for r in res.results], axis=0)
```

```python
from contextlib import ExitStack
import numpy as np
import concourse.bass as bass
import concourse.mybir as mybir
from concourse.bass_utils import run_bass_kernel_spmd

F32 = mybir.dt.float32
BF16 = mybir.dt.bfloat16
AF = mybir.ActivationFunctionType
ALU = mybir.AluOpType

D = 1024
H = 16
DH = 64
FF = 2816
NCH = FF // 128
EPS = 1e-6
N_CORES = 8


class Res:
    __slots__ = ("name", "lw", "rd", "sem", "semcnt")

    def __init__(self, name, sem=None):
        self.name = name
        self.lw = None
        self.rd = {}
        self.sem = sem
        self.semcnt = 0


class Prog:
    ENGS = ("pe", "act", "dve", "pool", "sp")

    def __init__(self, nc, ctx):
        self.nc = nc
        self.ctx = ctx
        self.streams = {e: [] for e in self.ENGS}
        self.count = {e: 0 for e in self.ENGS}
        self.waited = {e: {} for e in self.ENGS}
        self.esem = {e: ctx.enter_context(nc.semaphore("es_" + e)) for e in self.ENGS}
        self.nres = 0

    def res(self, name=None, dma=False):
        self.nres += 1
        name = name or ("r%d" % self.nres)
        sem = self.ctx.enter_context(self.nc.semaphore("ds%d" % self.nres)) if dma else None
        return Res(name, sem)

    def _deps(self, reads, writes):
        deps = []
        for r in reads:
            if r.lw is not None:
                deps.append(r.lw)
        for w in writes:
            if w.lw is not None:
                deps.append(w.lw)
            deps.extend(w.rd.items())
        return deps

    def _waits_for(self, eng, deps):
        best = {}
        for (key, val) in deps:
            if key == "pe" and eng == "pe":
                continue
            if val > best.get(key, 0):
                best[key] = val
        out = []
        wd = self.waited[eng]
        for key, val in best.items():
            if wd.get(key, 0) >= val:
                continue
            wd[key] = val
            out.append((key, val))
        return out

    def _sem_of(self, key):
        if isinstance(key, str):
            return self.esem[key]
        return key.sem

    def _record(self, ev, reads, writes):
        k, v = ev
        for r in reads:
            if r.rd.get(k, 0) < v:
                r.rd[k] = v
        for w in writes:
            w.lw = ev
            w.rd = {}

    def op(self, eng, fn, reads=(), writes=()):
        waits = self._waits_for(eng, self._deps(reads, writes))
        self.count[eng] += 1
        ev = (eng, self.count[eng])
        self.streams[eng].append((waits, fn, None))
        self._record(ev, reads, writes)
        return ev

    def dma(self, eng, fn, semres, reads=(), writes=()):
        waits = self._waits_for(eng, self._deps(reads, writes))
        semres.semcnt += 1
        ev = (semres, 16 * semres.semcnt)
        self.streams[eng].append((waits, fn, semres))
        self._record(ev, reads, writes)
        return ev

    def alias(self, olds, news):
        evs = {}
        for o in olds:
            if o.lw is not None:
                k, v = o.lw
                evs[k] = max(evs.get(k, 0), v)
            for k, v in o.rd.items():
                evs[k] = max(evs.get(k, 0), v)
        for n in news:
            for k, v in evs.items():
                if n.rd.get(k, 0) < v:
                    n.rd[k] = v

    def wait_all(self, eng, resources):
        deps = []
        for r in resources:
            if r.lw is not None:
                deps.append(r.lw)
            deps.extend(r.rd.items())
        waits = self._waits_for(eng, deps)
        self.streams[eng].append((waits, None, None))

    def emit(self):
        nc = self.nc
        with nc.Block() as block:
            def run(engname):
                def body(e):
                    esem = self.esem[engname]
                    for (waits, fn, semres) in self.streams[engname]:
                        for (key, val) in waits:
                            e.wait_ge(self._sem_of(key), val)
                        if fn is None:
                            continue
                        ins = fn(e)
                        if semres is None:
                            ins.then_inc(esem, 1)
                        else:
                            ins.then_inc(semres.sem, 16)
                return body
            block.tensor(run("pe"))
            block.scalar(run("act"))
            block.vector(run("dve"))
            block.gpsimd(run("pool"))
            block.sync(run("sp"))


def build_nc(NB, S):
    NT = S // 128
    NG = S // 512
    nc = bass.Bass("TRN2", target_bir_lowering=False)
    dt_in = lambda name, shape: nc.dram_tensor(name, list(shape), F32, kind="ExternalInput").ap()
    x = dt_in("x", [NB, S, D])
    sb_pre_g = dt_in("sb_pre_g", [1, D])
    sb_w_qkv = dt_in("sb_w_qkv", [1, D, 3 * D])
    sb_w_o = dt_in("sb_w_o", [1, D, D])
    sb_post_g = dt_in("sb_post_g", [1, D])
    kv_norm_g = dt_in("kv_norm_g", [D])
    w_kvf = dt_in("w_kvf", [D, 2 * D + H])
    b_f = dt_in("b_f", [H])
    fox_pre_g = dt_in("fox_pre_g", [1, D])
    fox_w_q = dt_in("fox_w_q", [1, D, D])
    fox_w_o = dt_in("fox_w_o", [1, D, D])
    fox_post_g = dt_in("fox_post_g", [1, D])
    ffn_pre_g = dt_in("ffn_pre_g", [2, D])
    w_up = dt_in("w_up", [2, D, 2 * FF])
    conv_w = dt_in("conv_w", [2, 3, 2 * FF])
    conv_b = dt_in("conv_b", [2, 2 * FF])
    w_down = dt_in("w_down", [2, FF, D])
    ffn_post_g = dt_in("ffn_post_g", [2, D])
    y = nc.dram_tensor("y", [NB, S, D], F32, kind="ExternalOutput").ap()

    ws_qkv = nc.dram_tensor("ws_qkv", [5, 8, 128, 1024], BF16).ap()
    ws_kv = nc.dram_tensor("ws_kv", [2, 8, 128, 1024], BF16).ap()
    ws_f = nc.dram_tensor("ws_f", [128, 8 * 16], BF16).ap()
    ws_o = nc.dram_tensor("ws_o", [2, 128, 8192], BF16).ap()
    ws_up = nc.dram_tensor("ws_up", [2, NCH, 128, 2048], BF16).ap()
    ws_dn = nc.dram_tensor("ws_dn", [2, NCH, 128, 1024], BF16).ap()

    with ExitStack() as ctx:
        P = Prog(nc, ctx)
        sbt = lambda name, shape, dt: ctx.enter_context(nc.sbuf_tensor(name, list(shape), dt))
        pst = lambda name, shape, dt: ctx.enter_context(nc.psum_tensor(name, list(shape), dt))

        Hbuf = sbt("Hbuf", [128, NT * D], F32)
        hview = Hbuf[:].rearrange("p (t d) -> p t d", d=D)
        r_h = [P.res("h%d" % t, dma=True) for t in range(NT)]
        R1N = max(8 * S, 16384)
        R2N = max(8 * S, 16384)
        R3N = max(4 * S + NT * 256, 8192)
        R1 = sbt("R1", [128, R1N], BF16)
        R2 = sbt("R2", [128, R2N], BF16)
        R3 = sbt("R3", [128, R3N], BF16)
        r_R1 = P.res("R1")
        r_R2 = P.res("R2")
        aT = R1[:, 0:8 * S].rearrange("p (k s) -> p k s", s=S)
        bT = R2[:, 0:8 * S].rearrange("p (k s) -> p k s", s=S)
        qa = [R3[:, 0:S], R3[:, S:2 * S]]
        ka = [R3[:, 2 * S:3 * S], R3[:, 3 * S:4 * S]]
        vpad = R3[:, 4 * S:4 * S + NT * 256].rearrange("p (t h c) -> p t h c", h=2, c=128)
        r_qa = [P.res("qa0", dma=True), P.res("qa1", dma=True)]
        r_ka = [P.res("ka0", dma=True), P.res("ka1", dma=True)]
        r_vpad = P.res("vpad")
        r_qrow = [[P.res("qrow%d%d" % (i, j), dma=True) for j in range(2)] for i in range(2)]
        r_krow = [[P.res("krow%d%d" % (i, j), dma=True) for j in range(2)] for i in range(2)]
        r_attn_all = r_qa + r_ka + [r_vpad] + r_qrow[0] + r_qrow[1] + r_krow[0] + r_krow[1]
        WoT = R3[:, 0:8192].rearrange("p (k n) -> p k n", n=D)
        r_Wo = P.res("Wo", dma=True)
        gT = R2[:, 0:NCH * 512].rearrange("p (c t) -> p c t", t=512)
        r_gT = [P.res("gT%d" % c) for c in range(NCH)]
        wupb = [R2[:, NCH * 512 + i * 2048: NCH * 512 + (i + 1) * 2048].rearrange("p (g k j) -> p g k j", g=2, k=8) for i in range(2)]
        r_wup = [P.res("wup%d" % i, dma=True) for i in range(2)]
        NWD = 8
        wdnb = [R3[:, i * 1024:(i + 1) * 1024] for i in range(NWD)]
        r_wdn = [P.res("wdn%d" % i, dma=True) for i in range(NWD)]

        Wq = sbt("Wq", [128, 8, 128], BF16); r_Wq = P.res("Wq", dma=True)
        Wk = sbt("Wk", [128, 8, 128], BF16); r_Wk = P.res("Wk", dma=True)
        Wv = sbt("Wv", [128, 8, 128], BF16); r_Wv = P.res("Wv", dma=True)
        Wf = sbt("Wf", [128, 8, 16], BF16); r_Wf = P.res("Wf", dma=True)
        e32 = [sbt("e32_%d" % i, [128, 516], F32) for i in range(4)]; r_e32 = [P.res() for _ in range(4)]
        spb = [sbt("spb_%d" % i, [128, 512], BF16) for i in range(4)]; r_sp = [P.res() for _ in range(4)]
        E2 = [sbt("E2_%d" % i, [128, 512], F32) for i in range(2)]; r_E2 = [P.res() for _ in range(2)]
        Ab = [sbt("Ab_%d" % i, [128, 512], BF16) for i in range(3)]; r_A = [P.res() for _ in range(3)]
        Ls32 = [sbt("Ls32_%d" % i, [128, 512], F32) for i in range(1)]; r_Ls32 = [P.res() for _ in range(1)]
        rinv = sbt("rinv", [128, 512], F32); r_rinv = P.res()
        junk = sbt("junk", [128, 1024], BF16); r_junk = P.res()
        xn = sbt("xn", [128, 1024], BF16); r_xn = P.res()
        tmpf = sbt("tmpf", [128, 1024], F32); r_tmpA = P.res(); r_tmpB = P.res()
        Gpre = sbt("Gpre", [128, 1024], F32); r_Gpre = P.res("Gpre", dma=True)
        Gpost = Gpre; r_Gpost = r_Gpre
        st = sbt("stats", [128, 8], F32); r_st = P.res()
        pst_ = sbt("pstats", [128, 3 * 16], F32); r_pst = P.res()
        frowA = sbt("frowA", [80, S], BF16)
        frowB = sbt("frowB", [16, S], BF16)
        frow = [frowA[0:16, :], frowA[32:48, :], frowA[64:80, :], frowB[0:16, :]]
        r_frow = P.res()
        carry = sbt("carry", [16, 1], F32); r_carry = P.res()
        bfneg = sbt("bfneg", [16, 1], F32); r_bf = P.res("bf", dma=True)
        CW = [sbt("CW%d" % l, [128, 4, 2 * NCH], F32) for l in range(2)]; r_CW = [P.res("CW%d" % l, dma=True) for l in range(2)]
        hraw = e32; r_hraw = r_e32
        cacc = [E2[0][:, :], E2[1][:, :], tmpf[:, 0:512], tmpf[:, 512:1024]]; r_cacc = [r_E2[0], r_E2[1], r_tmpA, r_tmpB]
        gl = rinv; r_gl = r_rinv
        halo = sbt("halo", [128, 2 * NCH, 2], F32); r_halo = P.res()
        ident = sbt("ident", [128, 128], BF16); r_ident = P.res()
        mstrict = sbt("mstrict", [128, 128], BF16); r_mstrict = P.res()
        mincl = sbt("mincl", [128, 128], BF16); r_mincl = P.res()
        UI = sbt("UI", [128, 128], BF16); r_UI = P.res()
        Lc = sbt("Lc", [128, 128], BF16); r_Lc = P.res()
        PB = [pst("PB%d" % i, [128, 512], F32) for i in range(7)]; r_PB = [P.res("PB%d" % i) for i in range(7)]
        PT = pst("PT", [128, 1024], BF16); r_PT = P.res("PT")

        def mm(out, lhsT, rhs, start, stop, reads, writes):
            P.op("pe", lambda e: e.matmul(out, lhsT=lhsT, rhs=rhs, start=start, stop=stop, skip_group_check=True), reads, writes)

        def act(out, in_, func, reads, writes, **kw):
            P.op("act", lambda e: e.activation(out=out, in_=in_, func=func, **kw), reads, writes)

        def tcopy(eng, out, in_, reads, writes):
            P.op(eng, lambda e: e.tensor_copy(out=out, in_=in_), reads, writes)

        def tt(eng, out, in0, in1, op, reads, writes):
            P.op(eng, lambda e: e.tensor_tensor(out=out, in0=in0, in1=in1, op=op), reads, writes)

        def ts(eng, out, in0, s1, s2, op0, op1, reads, writes):
            if s2 is None:
                P.op(eng, lambda e: e.tensor_scalar(out=out, in0=in0, scalar1=s1, scalar2=None, op0=op0), reads, writes)
            else:
                P.op(eng, lambda e: e.tensor_scalar(out=out, in0=in0, scalar1=s1, scalar2=s2, op0=op0, op1=op1), reads, writes)

        def stt(eng, out, in0, scalar, in1, op0, op1, reads, writes):
            P.op(eng, lambda e: e.scalar_tensor_tensor(out=out, in0=in0, scalar=scalar, in1=in1, op0=op0, op1=op1), reads, writes)

        def memset(eng, ap, val, writes):
            P.op(eng, lambda e: e.memset(ap, val), (), writes)

        def dma(out, in_, semres, reads, writes, slow=False):
            if slow:
                P.dma("sp", lambda e: e.dma_start(out=out, in_=in_, allow_slow_non_contiguous=True), semres, reads, writes)
            else:
                P.dma("sp", lambda e: e.dma_start(out=out, in_=in_), semres, reads, writes)

        def aff(ap, pattern, cmp, cm, writes):
            P.op("pool", lambda e: e.affine_select(out=ap, in_=ap, pattern=pattern, compare_op=cmp, fill=0.0, base=0, channel_multiplier=cm), writes, writes)
        for (t_, r_) in ((ident, r_ident), (mstrict, r_mstrict), (mincl, r_mincl), (UI, r_UI), (Lc, r_Lc)):
            memset("pool", t_[:], 1.0, [r_])
        aff(ident[:], [[-1, 128]], ALU.is_equal, 1, [r_ident])
        aff(mstrict[:], [[1, 128]], ALU.is_gt, -1, [r_mstrict])
        aff(mincl[:], [[1, 128]], ALU.is_ge, -1, [r_mincl])
        aff(UI[:], [[-1, 128]], ALU.is_ge, 1, [r_UI])
        aff(Lc[:], [[1, 128]], ALU.is_gt, -1, [r_Lc])
        memset("pool", halo[:], 0.0, [r_halo])
        for l in range(2):
            for k in range(3):
                dma(CW[l][:, k, :], conv_w[l, k].rearrange("(c p) -> p c", p=128), r_CW[l], [], [r_CW[l]], slow=True)
            dma(CW[l][:, 3, :], conv_b[l].rearrange("(c p) -> p c", p=128), r_CW[l], [], [r_CW[l]], slow=True)
        dma(bfneg[:], b_f.rearrange("(h o) -> h o", o=1), r_bf, [], [r_bf], slow=True)
        ts("dve", bfneg[:], bfneg[:], -1.0, None, ALU.mult, None, [r_bf], [r_bf])

        NSLOT = 4 if NT * D >= 4 * 4096 else 2
        if NSLOT == 4:
            stg32 = [Hbuf[:, i * 4096:(i + 1) * 4096] for i in range(4)]
        else:
            stg32 = [R1[:, i * 8192:(i + 1) * 8192].bitcast(F32) for i in range(2)]
        stg16 = [R2[:, i * 4096:(i + 1) * 4096] for i in range(NSLOT)]
        r_s32 = [P.res("s32_%d" % i, dma=True) for i in range(NSLOT)]
        r_s16 = [P.res("s16_%d" % i) for i in range(NSLOT)]
        r_s16d = [P.res("s16d_%d" % i, dma=True) for i in range(NSLOT)]
        cast_engs = ["dve", "pool", "act"]
        ucount = [0]

        def cast_unit(srcs, E, dst, outview=None):
            cast_list.append((srcs, E, dst, outview))

        cast_list = []

        def emit_casts():
            n = len(cast_list)
            LOOK = NSLOT - 1

            def emit_in(u):
                i = u % NSLOT
                for (vf, src) in cast_list[u][0]:
                    dma(vf(stg32[i]), src, r_s32[i], [], [r_s32[i]])

            def emit_cast_out(u):
                i = u % NSLOT
                srcs, E, dst, outview = cast_list[u]
                if u % 2 == 0:
                    act(stg16[i][:, 0:E], stg32[i][:, 0:E], AF.Copy, [r_s32[i]], [r_s16[i]])
                else:
                    tcopy("dve", stg16[i][:, 0:E], stg32[i][:, 0:E], [r_s32[i]], [r_s16[i]])
                src16 = stg16[i][:, 0:E] if outview is None else outview(stg16[i][:, 0:E])
                dma(dst, src16, r_s16d[i], [r_s16[i]], [r_s16d[i]])
            for u in range(n + LOOK):
                if u < n:
                    emit_in(u)
                if u - LOOK >= 0:
                    emit_cast_out(u - LOOK)

        def img4(stage, a, b, c):
            return stage[:, 0:a * b * c].rearrange("p (a b c) -> p a b c", a=a, b=b)

        def img3(stage, a, b):
            return stage[:, 0:a * b].rearrange("p (a b) -> p a b", a=a)

        def cast_cols(src2d, col0, dst_units):
            for half in range(2):
                srcs = []
                for hp4 in range(4):
                    hp = half * 4 + hp4
                    srcs.append((lambda s, hp4=hp4: img4(s, 4, 8, 128)[:, hp4],
                                 src2d[:, col0 + hp * 128: col0 + (hp + 1) * 128].rearrange("(k p) j -> p k j", p=128)))
                cast_unit(srcs, 4096, dst_units[half * 4:half * 4 + 4].rearrange("u p e -> p u e"),
                          outview=lambda v: v.rearrange("p (u e) -> p u e", u=4))
        cast_cols(sb_w_qkv[0], 0, ws_qkv[0])
        cast_cols(sb_w_qkv[0], D, ws_qkv[1])
        cast_cols(sb_w_qkv[0], 2 * D, ws_qkv[2])

        def cast_wo(src2d, dst):
            for half in range(2):
                srcs = [(lambda s: img3(s, 4, 1024),
                         src2d[half * 512:(half + 1) * 512, :].rearrange("(k p) n -> p k n", p=128))]
                cast_unit(srcs, 4096, dst[:, half * 4096:(half + 1) * 4096])
        cast_wo(sb_w_o[0], ws_o[0])

        def cast_ffn(l):
            for c0 in range(0, NCH, 2):
                srcs = []
                for gu in range(2):
                    for ci in range(2):
                        c = c0 + ci
                        srcs.append((lambda s, gu=gu, ci=ci: s[:, 0:4096].rearrange("p (c g k j) -> p c g k j", c=2, g=2, k=8)[:, ci, gu],
                                     w_up[l][:, gu * FF + c * 128: gu * FF + (c + 1) * 128].rearrange("(k p) j -> p k j", p=128)))
                cast_unit(srcs, 4096, ws_up[l, c0:c0 + 2].rearrange("c p e -> p c e"),
                          outview=lambda v: v.rearrange("p (c e) -> p c e", c=2))
            for c0 in range(0, NCH, 4):
                n = min(4, NCH - c0)
                srcs = [(lambda s, n=n: img3(s, n, 1024),
                         w_down[l][c0 * 128:(c0 + n) * 128, :].rearrange("(c p) n -> p c n", p=128))]
                cast_unit(srcs, n * 1024, ws_dn[l, c0:c0 + n].rearrange("c p e -> p c e"),
                          outview=lambda v, n=n: v.rearrange("p (c e) -> p c e", c=n))
        cast_ffn(0)
        cast_cols(w_kvf, 0, ws_kv[0])
        cast_cols(w_kvf, D, ws_kv[1])
        cast_unit([(lambda s: img3(s, 8, 16), w_kvf[:, 2 * D:2 * D + H].rearrange("(k p) j -> p k j", p=128))], 128, ws_f)
        cast_cols(fox_w_q[0], 0, ws_qkv[3])
        cast_wo(fox_w_o[0], ws_o[1])
        cast_ffn(1)
        emit_casts()
        r_scr = r_s16d
        P.alias(r_s32 + r_s16 + r_s16d, [r_R1, r_R2] + r_h)

        def prenorm(g_ap, dstT, r_dst, first_alias=None):
            dma(Gpre[:], g_ap.partition_broadcast(128), r_Gpre, [], [r_Gpre])
            for t in range(NT):
                act(junk[:], hview[:, t, :], AF.Square, [r_h[t]], [r_junk, r_pst], accum_out=pst_[:, t:t + 1])
            act(pst_[:, 16:16 + NT], pst_[:, 0:NT], AF.Ln, [r_pst], [r_pst], scale=1.0 / D, bias=EPS)
            act(pst_[:, 32:32 + NT], pst_[:, 16:16 + NT], AF.Exp, [r_pst], [r_pst], scale=-0.5)
            for t in range(NT):
                stt("dve", xn[:], hview[:, t, :], pst_[:, 32 + t:33 + t], Gpre[:], ALU.mult, ALU.mult, [r_h[t], r_pst, r_Gpre], [r_xn])
                for kc in range(8):
                    P.op("pe", lambda e, kc=kc: e.transpose(PT[:, kc * 128:(kc + 1) * 128], xn[:, kc * 128:(kc + 1) * 128], ident[:]),
                         [r_xn, r_ident], [r_PT])
                tcopy("dve", dstT[:, :, t * 128:(t + 1) * 128], PT[:].rearrange("p (k j) -> p k j", j=128), [r_PT], [r_dst])

        def postnorm_residual(t, ba, bb, g_res):
            act(junk[:, 0:512], PB[ba][:], AF.Square, [r_PB[ba]], [r_junk, r_st], accum_out=st[:, 3:4])
            act(junk[:, 512:1024], PB[bb][:], AF.Square, [r_PB[bb]], [r_junk, r_st], accum_out=st[:, 4:5])
            tt("dve", st[:, 5:6], st[:, 3:4], st[:, 4:5], ALU.add, [r_st], [r_st])
            act(st[:, 6:7], st[:, 5:6], AF.Ln, [r_st], [r_st], scale=1.0 / D, bias=EPS)
            act(st[:, 7:8], st[:, 6:7], AF.Exp, [r_st], [r_st], scale=-0.5)
            stt("dve", tmpf[:, 0:512], PB[ba][:], st[:, 7:8], Gpost[:, 0:512], ALU.mult, ALU.mult, [r_PB[ba], r_st, g_res], [r_tmpA])
            stt("dve", tmpf[:, 512:1024], PB[bb][:], st[:, 7:8], Gpost[:, 512:1024], ALU.mult, ALU.mult, [r_PB[bb], r_st, g_res], [r_tmpB])
            tt("pool", hview[:, t, :], hview[:, t, :], tmpf[:], ALU.add, [r_h[t], r_tmpA, r_tmpB], [r_h[t]])

        def out_proj(widx, g_ap, srcT, r_src):
            P.alias(r_attn_all, [r_Wo])
            dma(WoT[:, 0:4, :], ws_o[widx][:, 0:4096].rearrange("p (k n) -> p k n", n=D), r_Wo, r_scr, [r_Wo])
            dma(WoT[:, 4:8, :], ws_o[widx][:, 4096:8192].rearrange("p (k n) -> p k n", n=D), r_Wo, r_scr, [r_Wo])
            dma(Gpost[:], g_ap.partition_broadcast(128), r_Gpost, [], [r_Gpost])
            for t in range(NT):
                ba, bb = (0, 1) if t % 2 == 0 else (2, 3)
                for n, bk in ((0, ba), (1, bb)):
                    for kc in range(8):
                        mm(PB[bk][:], srcT[:, kc, t * 128:(t + 1) * 128], WoT[:, kc, n * 512:(n + 1) * 512],
                           kc == 0, kc == 7, [r_src, r_Wo], [r_PB[bk]])
                postnorm_residual(t, ba, bb, r_Gpost)
            P.alias([r_Wo], r_attn_all)

        def pipeline(n, stages):
            ns = len(stages)
            for tick in range(n + ns - 1):
                for k in range(ns - 1, -1, -1):
                    i = tick - k
                    if 0 <= i < n:
                        stages[k](i)

        def load_pair_weights(qsrc, ksrc, vsrc, hp):
            if qsrc is not None:
                dma(Wq[:], qsrc[hp].rearrange("p (k j) -> p k j", j=128), r_Wq, r_scr, [r_Wq])
            dma(Wk[:], ksrc[hp].rearrange("p (k j) -> p k j", j=128), r_Wk, r_scr, [r_Wk])
            dma(Wv[:], vsrc[hp].rearrange("p (k j) -> p k j", j=128), r_Wv, r_scr, [r_Wv])

        pbank = [0]

        def next_pbank():
            pbank[0] += 1
            return (6, 0, 1)[pbank[0] % 3]

        def proj_featmajor(Wt, r_W, srcT, r_src, bank, sink):
            for tg in range(NG):
                bank = next_pbank()
                for kc in range(8):
                    mm(PB[bank][:], Wt[:, kc, :], srcT[:, kc, tg * 512:(tg + 1) * 512], kc == 0, kc == 7, [r_W, r_src], [r_PB[bank]])
                sink(tg, bank)

        def proj_v(srcT, r_src, bank, padval):
            memset("pool", vpad[:, :, 0, 64:128], padval, [r_vpad])
            memset("pool", vpad[:, :, 1, 0:64], padval, [r_vpad])
            for t4 in range(0, NT, 4):
                bank = next_pbank()
                for ti in range(4):
                    t = t4 + ti
                    for kc in range(8):
                        mm(PB[bank][:, ti * 128:(ti + 1) * 128], srcT[:, kc, t * 128:(t + 1) * 128], Wv[:, kc, :], kc == 0, kc == 7,
                           [r_src, r_Wv], [r_PB[bank]])
                pv = PB[bank][:].rearrange("p (t c) -> p t c", c=128)
                tcopy("dve", vpad[:, t4:t4 + 4, 0, 0:64], pv[:, :, 0:64], [r_PB[bank]], [r_vpad])
                tcopy("dve", vpad[:, t4:t4 + 4, 1, 64:128], pv[:, :, 64:128], [r_PB[bank]], [r_vpad])

        def attn_sb_pair(hp, obank_base):
            units = []
            for g in range(NG):
                for sb in range(4 * g + 3, -1, -1):
                    for hd in range(2):
                        units.append((g, sb, hd))
            n = len(units)

            def geom(u):
                g, sb, hd = units[u]
                c0 = max(sb * 128, g * 512) - g * 512
                return g, sb, hd, c0, (sb >= 4 * g)

            def s_z(u):
                g, sb, hd, c0, diag = geom(u)
                zb = u % 2
                mm(PB[zb][:, c0:512], ka[hd][0:128, sb * 128:(sb + 1) * 128], qa[hd][0:128, g * 512 + c0:(g + 1) * 512], True, True,
                   [r_ka[hd], r_qa[hd]], [r_PB[zb]])

            def s_e(u):
                g, sb, hd, c0, diag = geom(u)
                zb = u % 2
                eb = u % 4
                act(e32[eb][:, c0:512], PB[zb][:, c0:512], AF.Exp, [r_PB[zb]], [r_e32[eb]])

            def s_esp(u):
                g, sb, hd, c0, diag = geom(u)
                zb = u % 2
                eb = u % 4
                act(spb[eb][:, c0:512], e32[eb][:, c0:512], AF.Ln, [r_e32[eb]], [r_sp[eb]], bias=1.0)
                if diag:
                    tt("pool", spb[eb][:, c0:c0 + 128], spb[eb][:, c0:c0 + 128], mstrict[:], ALU.mult, [r_sp[eb], r_mstrict], [r_sp[eb]])

            def s_x(u):
                g, sb, hd, c0, diag = geom(u)
                eb = u % 4
                xb = 2 + hd
                mm(PB[xb][:, c0:512], UI[:], spb[eb][:, c0:512], sb == 4 * g + 3, False, [r_UI, r_sp[eb]], [r_PB[xb]])

            def s_a(u):
                g, sb, hd, c0, diag = geom(u)
                eb = u % 4
                xb = 2 + hd
                ab = u % 3
                act(E2[hd][:, c0:512], PB[xb][:, c0:512], AF.Exp, [r_PB[xb]], [r_E2[hd]], scale=-1.0)
                if sb > 0:
                    mm(PB[xb][:, c0:512], Lc[:], spb[eb][:, c0:512], False, False, [r_Lc, r_sp[eb]], [r_PB[xb]])
                tt("dve", Ab[ab][:, c0:512], e32[eb][:, c0:512], E2[hd][:, c0:512], ALU.mult, [r_e32[eb], r_E2[hd]], [r_A[ab]])
                if diag:
                    tt("pool", Ab[ab][:, c0:c0 + 128], Ab[ab][:, c0:c0 + 128], mstrict[:], ALU.mult, [r_A[ab], r_mstrict], [r_A[ab]])

            def s_pv(u):
                g, sb, hd, c0, diag = geom(u)
                ab = u % 3
                ob = obank_base + (g % 2)
                firstm = (sb == 4 * g + 3 and hd == 0)
                mm(PB[ob][:, c0:512], vpad[:, sb, hd, :], Ab[ab][:, c0:512], firstm, False, [r_vpad, r_A[ab]], [r_PB[ob]])
                if sb == 0 and hd == 1:
                    tcopy("dve", bT[:, hp, g * 512:(g + 1) * 512], PB[ob][:], [r_PB[ob]], [r_R2])

            pipeline(n, [s_z, s_e, s_esp, s_x, s_a, s_pv])

        def attn_fox_pair(hp):
            units = []
            for g in range(NG):
                for sb in range(4 * g + 3, -1, -1):
                    for hd in range(2):
                        units.append((g, sb, hd))
            n = len(units)

            def geom(u):
                g, sb, hd = units[u]
                c0 = max(sb * 128, g * 512) - g * 512
                return g, sb, hd, c0, (sb >= 4 * g)

            def obank(g, hd):
                return (4 + hd) if g % 2 == 0 else (2 + hd)

            def s_z(u):
                g, sb, hd, c0, diag = geom(u)
                zb = u % 2
                mm(PB[zb][:, c0:512], ka[hd][0:128, sb * 128:(sb + 1) * 128], qa[hd][0:128, g * 512 + c0:(g + 1) * 512], True, True,
                   [r_ka[hd], r_qa[hd]] + r_qrow[hd] + r_krow[hd], [r_PB[zb]])

            def s_a(u):
                g, sb, hd, c0, diag = geom(u)
                zb = u % 2
                ab = u % 3
                act(Ab[ab][:, c0:512], PB[zb][:, c0:512], AF.Exp, [r_PB[zb]], [r_A[ab]])
                if diag:
                    tt("pool", Ab[ab][:, c0:c0 + 128], Ab[ab][:, c0:c0 + 128], mincl[:], ALU.mult, [r_A[ab], r_mincl], [r_A[ab]])

            def s_pv(u):
                g, sb, hd, c0, diag = geom(u)
                ab = u % 3
                ob = obank(g, hd)
                mm(PB[ob][:, c0:512], vpad[:, sb, hd, :], Ab[ab][:, c0:512], sb == 4 * g + 3, False, [r_vpad, r_A[ab]], [r_PB[ob]])
                if sb == 0:
                    if hd == 0:
                        P.op("dve", lambda e: e.reciprocal(out=rinv[0:64, :], in_=PB[ob][64:128, :]), [r_PB[ob]], [r_rinv])
                        tt("dve", bT[0:64, hp, g * 512:(g + 1) * 512], PB[ob][0:64, :], rinv[0:64, :], ALU.mult, [r_PB[ob], r_rinv], [r_R2])
                    else:
                        P.op("dve", lambda e: e.reciprocal(out=rinv[64:128, :], in_=PB[ob][0:64, :]), [r_PB[ob]], [r_rinv])
                        tt("dve", bT[64:128, hp, g * 512:(g + 1) * 512], PB[ob][64:128, :], rinv[64:128, :], ALU.mult, [r_PB[ob], r_rinv], [r_R2])

            pipeline(n, [s_z, s_a, s_pv])

        def ffn(l, after_tile=None):
            prenorm(ffn_pre_g[l], aT, r_R1)
            dma(Gpost[:], ffn_post_g[l].partition_broadcast(128), r_Gpost, [], [r_Gpost])
            P.alias([r_R2], r_gT + r_wup)
            P.alias(r_attn_all, r_wdn)
            LAG, PRE = 2, 6
            for tg in range(NG):
                def load_up(c):
                    dma(wupb[c % 2][:], ws_up[l, c].rearrange("p (g k j) -> p g k j", g=2, k=8), r_wup[c % 2], r_scr, [r_wup[c % 2]])
                steps = [(half, c) for half in range(2) for c in range(NCH)]
                loaded = [0]

                def ensure_loaded(k):
                    while loaded[0] <= min(k, len(steps) - 1):
                        i = loaded[0] % NWD
                        dma(wdnb[i], ws_dn[l, steps[loaded[0]][1]], r_wdn[i], r_scr, [r_wdn[i]])
                        loaded[0] += 1

                def down_step(k):
                    ensure_loaded(k + PRE)
                    half, c = steps[k]
                    i = k % NWD
                    for t2 in range(2):
                        tl = 2 * half + t2
                        for nn in range(2):
                            bk = 2 * t2 + nn
                            mm(PB[bk][:], gT[:, c, tl * 128:(tl + 1) * 128], wdnb[i][:, nn * 512:(nn + 1) * 512], c == 0, c == NCH - 1,
                               [r_gT[c], r_wdn[i]], [r_PB[bk]])
                    if c == NCH - 1:
                        for t2 in range(2):
                            postnorm_residual(tg * 4 + 2 * half + t2, 2 * t2, 2 * t2 + 1, r_Gpost)
                            if after_tile is not None:
                                after_tile(tg * 4 + 2 * half + t2)

                def gate_mul(cc):
                    cg, cu = 2 * (cc % 2), 2 * (cc % 2) + 1
                    act(gl[:], cacc[cg], AF.Gelu_apprx_tanh, [r_cacc[cg]], [r_gl])
                    tt("pool", gT[:, cc, :], gl[:], cacc[cu], ALU.mult, [r_gl, r_cacc[cu]], [r_gT[cc]])

                load_up(0)
                ensure_loaded(PRE - 1)
                kdown = 0
                for c in range(NCH):
                    if c + 1 < NCH:
                        load_up(c + 1)
                    for gu in range(2):
                        bk = 4 + gu
                        for kc in range(8):
                            mm(PB[bk][:], wupb[c % 2][:, gu, kc, :], aT[:, kc, tg * 512:(tg + 1) * 512], kc == 0, kc == 7,
                               [r_wup[c % 2], r_R1], [r_PB[bk]])
                    if c >= LAG:
                        down_step(kdown)
                        kdown += 1
                    for gu in range(2):
                        bk = 4 + gu
                        ci = gu * NCH + c
                        hb = 2 * (c % 2) + gu
                        hr, rh = hraw[hb], r_hraw[hb]
                        ca, rca = cacc[hb], r_cacc[hb]
                        if tg == 0:
                            memset("pool", hr[:, 2:4], 0.0, [rh])
                        else:
                            tcopy("pool", hr[:, 2:4], halo[:, ci, :], [r_halo], [rh])
                        act(hr[:, 4:516], PB[bk][:], AF.Copy, [r_PB[bk]], [rh])
                        tcopy("pool", halo[:, ci, :], hr[:, 514:516], [rh], [r_halo])
                        act(ca, PB[bk][:], AF.Identity, [r_PB[bk], r_CW[l]], [rca],
                            scale=CW[l][:, 2, ci:ci + 1], bias=CW[l][:, 3, ci:ci + 1])
                    for tap in (1, 0):
                        for gu in range(2):
                            ci = gu * NCH + c
                            hb = 2 * (c % 2) + gu
                            hr, rh = hraw[hb], r_hraw[hb]
                            ca, rca = cacc[hb], r_cacc[hb]
                            stt("dve", ca, hr[:, 2 + tap:514 + tap], CW[l][:, tap, ci:ci + 1], ca, ALU.mult, ALU.add, [rh, r_CW[l], rca], [rca])
                    if c > 0:
                        gate_mul(c - 1)
                    if c == NCH - 1:
                        gate_mul(c)
                while kdown < len(steps):
                    down_step(kdown)
                    kdown += 1
            P.alias(r_gT + r_wup, [r_R2])
            P.alias(r_wdn, r_attn_all)

        for t in range(NT):
            dma(hview[:, t, :], x[0, t * 128:(t + 1) * 128, :], r_h[t], [], [r_h[t]])
        for b in range(NB):
            prenorm(sb_pre_g[0], aT, r_R1)
            for hd in range(2):
                memset("pool", qa[hd][64:128, :], 0.0, [r_qa[hd]])
                memset("pool", ka[hd][64:128, :], 0.0, [r_ka[hd]])
            load_pair_weights(ws_qkv[0], ws_qkv[1], ws_qkv[2], 0)
            for hp in range(8):

                def sink_q(tg, bank):
                    for hd in range(2):
                        ts("dve", qa[hd][0:64, tg * 512:(tg + 1) * 512], PB[bank][hd * 64:(hd + 1) * 64, :], 0.125, None, ALU.mult, None,
                           [r_PB[bank]], [r_qa[hd]])

                def sink_k(tg, bank):
                    for hd in range(2):
                        tcopy("dve", ka[hd][0:64, tg * 512:(tg + 1) * 512], PB[bank][hd * 64:(hd + 1) * 64, :], [r_PB[bank]], [r_ka[hd]])
                proj_featmajor(Wq, r_Wq, aT, r_R1, 6, sink_q)
                proj_featmajor(Wk, r_Wk, aT, r_R1, 6, sink_k)
                proj_v(aT, r_R1, 6, 0.0)
                if hp + 1 < 8:
                    load_pair_weights(ws_qkv[0], ws_qkv[1], ws_qkv[2], hp + 1)
                attn_sb_pair(hp, 4)
            out_proj(0, sb_post_g[0], bT, r_R2)
            ffn(0)
            prenorm(fox_pre_g[0], aT, r_R1)
            for hp in range(8):
                dma(Wq[:], ws_qkv[3][hp].rearrange("p (k j) -> p k j", j=128), r_Wq, r_scr, [r_Wq])

                def sink_qall(tg, bank, hp=hp):
                    ts("dve", bT[:, hp, tg * 512:(tg + 1) * 512], PB[bank][:], 0.125, None, ALU.mult, None, [r_PB[bank]], [r_R2])
                proj_featmajor(Wq, r_Wq, aT, r_R1, 6, sink_qall)
            prenorm(kv_norm_g, aT, r_R1)
            dma(Wf[:], ws_f.rearrange("p (k j) -> p k j", j=16), r_Wf, r_scr, [r_Wf])
            memset("pool", E2[1][0:16, 0:512], 1.0, [r_E2[1]])
            for tg in range(NG):
                seg = slice(tg * 512, (tg + 1) * 512)
                for kc in range(8):
                    mm(PB[6][0:16, :], Wf[:, kc, :], aT[:, kc, seg], kc == 0, kc == 7, [r_Wf, r_R1], [r_PB[6]])
                act(e32[0][0:16, 0:512], PB[6][0:16, :], AF.Exp, [r_PB[6], r_bf], [r_e32[0]], scale=-1.0, bias=bfneg[:, 0:1])
                act(e32[1][0:16, 0:512], e32[0][0:16, 0:512], AF.Ln, [r_e32[0]], [r_e32[1]], bias=1.0)
                if tg == 0:
                    P.op("dve", lambda e: e.tensor_tensor_scan(out=E2[0][0:16, 0:512], data0=E2[1][0:16, 0:512], data1=e32[1][0:16, 0:512],
                                                               initial=0.0, op0=ALU.mult, op1=ALU.add),
                         [r_E2[1], r_e32[1]], [r_E2[0]])
                else:
                    P.op("dve", lambda e: e.tensor_tensor_scan(out=E2[0][0:16, 0:512], data0=E2[1][0:16, 0:512], data1=e32[1][0:16, 0:512],
                                                               initial=carry[:, 0:1], op0=ALU.mult, op1=ALU.add),
                         [r_E2[1], r_e32[1], r_carry], [r_E2[0]])
                tcopy("dve", carry[:, 0:1], E2[0][0:16, 511:512], [r_E2[0]], [r_carry])
                tcopy("dve", frow[0][:, seg], E2[0][0:16, 0:512], [r_E2[0]], [r_frow])
                tt("dve", Ls32[0][0:16, :], E2[0][0:16, 0:512], frow[0][:, seg], ALU.subtract, [r_E2[0], r_frow], [r_Ls32[0]])
                tcopy("dve", frow[1][:, seg], Ls32[0][0:16, :], [r_Ls32[0]], [r_frow])
                ts("dve", frow[2][:, seg], E2[0][0:16, 0:512], -1.0, None, ALU.mult, None, [r_E2[0]], [r_frow])
                ts("dve", frow[3][:, seg], Ls32[0][0:16, :], -1.0, None, ALU.mult, None, [r_Ls32[0]], [r_frow])
            for hd in range(2):
                memset("pool", qa[hd][64:128, :], 0.0, [r_qa[hd]])
                memset("pool", ka[hd][64:128, :], 0.0, [r_ka[hd]])
                memset("pool", qa[hd][64:68, :], 1.0, [r_qa[hd]])
                memset("pool", ka[hd][64:68, :], 1.0, [r_ka[hd]])
            load_pair_weights(None, ws_kv[0], ws_kv[1], 0)
            for hp in range(8):
                for hd in range(2):
                    hh = 2 * hp + hd
                    if hd == 0:
                        act(qa[hd][0:64, :], bT[0:64, hp, :], AF.Copy, [r_R2], [r_qa[hd]])
                    else:
                        tcopy("dve", qa[hd][0:64, :], bT[64:128, hp, :], [r_R2], [r_qa[hd]])
                    dma(qa[hd][64:65, :], frow[2][hh:hh + 1, :], r_qrow[hd][0], [r_frow, r_qa[hd]], [r_qrow[hd][0]])
                    dma(qa[hd][65:66, :], frow[3][hh:hh + 1, :], r_qrow[hd][1], [r_frow, r_qa[hd]], [r_qrow[hd][1]])
                    dma(ka[hd][66:67, :], frow[0][hh:hh + 1, :], r_krow[hd][0], [r_frow, r_ka[hd]], [r_krow[hd][0]])
                    dma(ka[hd][67:68, :], frow[1][hh:hh + 1, :], r_krow[hd][1], [r_frow, r_ka[hd]], [r_krow[hd][1]])

                def sink_k1(tg, bank):
                    for hd in range(2):
                        tcopy("dve", ka[hd][0:64, tg * 512:(tg + 1) * 512], PB[bank][hd * 64:(hd + 1) * 64, :], [r_PB[bank]], [r_ka[hd]])
                proj_featmajor(Wk, r_Wk, aT, r_R1, 6, sink_k1)
                proj_v(aT, r_R1, 6, 1.0)
                if hp + 1 < 8:
                    load_pair_weights(None, ws_kv[0], ws_kv[1], hp + 1)
                attn_fox_pair(hp)
            out_proj(1, fox_post_g[0], bT, r_R2)
            def stream_io(t, b=b):
                dma(y[b, t * 128:(t + 1) * 128, :], hview[:, t, :], r_h[t], [r_h[t]], [])
                if b + 1 < NB:
                    dma(hview[:, t, :], x[b + 1, t * 128:(t + 1) * 128, :], r_h[t], [], [r_h[t]])
            ffn(1, after_tile=stream_io)
        P.wait_all("sp", r_h)
        P.emit()
    return nc


_NC_CACHE = {}


def kernel(**inputs):
    x = np.ascontiguousarray(inputs["x"], dtype=np.float32)
    B, S, _ = x.shape
    NB = B // N_CORES
    key = (NB, S)
    if key not in _NC_CACHE:
        _NC_CACHE[key] = build_nc(NB, S)
    nc = _NC_CACHE[key]
    wnames = ["sb_pre_g", "sb_w_qkv", "sb_w_o", "sb_post_g", "kv_norm_g", "w_kvf", "b_f", "fox_pre_g", "fox_w_q",
              "fox_w_o", "fox_post_g", "ffn_pre_g", "w_up", "conv_w", "conv_b", "w_down", "ffn_post_g"]
    ws = {k: np.ascontiguousarray(inputs[k], dtype=np.float32) for k in wnames}
    in_maps = []
    for c in range(N_CORES):
        m = {"x": x[c * NB:(c + 1) * NB]}
        m.update(ws)
        in_maps.append(m)
    res = run_bass_kernel_spmd(nc, in_maps, core_ids=list(range(N_CORES)))
    return np.concatenate([r["y"] for r in res.results], axis=0)
```

```python
from contextlib import ExitStack
import numpy as np
import concourse.bass as bass
import concourse.mybir as mybir
from concourse.bass_utils import run_bass_kernel_spmd

F32 = mybir.dt.float32
BF16 = mybir.dt.bfloat16
AF = mybir.ActivationFunctionType
ALU = mybir.AluOpType

D = 1024
H = 16
DH = 64
FF = 2816
NCH = FF // 128
EPS = 1e-6
N_CORES = 8


class Res:
    __slots__ = ("name", "lw", "rd", "sem", "semcnt")

    def __init__(self, name, sem=None):
        self.name = name
        self.lw = None
        self.rd = {}
        self.sem = sem
        self.semcnt = 0


class Prog:
    ENGS = ("pe", "act", "dve", "pool", "sp")

    def __init__(self, nc, ctx):
        self.nc = nc
        self.ctx = ctx
        self.streams = {e: [] for e in self.ENGS}
        self.count = {e: 0 for e in self.ENGS}
        self.waited = {e: {} for e in self.ENGS}
        self.esem = {e: ctx.enter_context(nc.semaphore("es_" + e)) for e in self.ENGS}
        self.nres = 0

    def res(self, name=None, dma=False):
        self.nres += 1
        name = name or ("r%d" % self.nres)
        sem = self.ctx.enter_context(self.nc.semaphore("ds%d" % self.nres)) if dma else None
        return Res(name, sem)

    def _deps(self, reads, writes):
        deps = []
        for r in reads:
            if r.lw is not None:
                deps.append(r.lw)
        for w in writes:
            if w.lw is not None:
                deps.append(w.lw)
            deps.extend(w.rd.items())
        return deps

    def _waits_for(self, eng, deps):
        best = {}
        for (key, val) in deps:
            if key == "pe" and eng == "pe":
                continue
            if val > best.get(key, 0):
                best[key] = val
        out = []
        wd = self.waited[eng]
        for key, val in best.items():
            if wd.get(key, 0) >= val:
                continue
            wd[key] = val
            out.append((key, val))
        return out

    def _sem_of(self, key):
        if isinstance(key, str):
            return self.esem[key]
        return key.sem

    def _record(self, ev, reads, writes):
        k, v = ev
        for r in reads:
            if r.rd.get(k, 0) < v:
                r.rd[k] = v
        for w in writes:
            w.lw = ev
            w.rd = {}

    def op(self, eng, fn, reads=(), writes=()):
        waits = self._waits_for(eng, self._deps(reads, writes))
        self.count[eng] += 1
        ev = (eng, self.count[eng])
        self.streams[eng].append((waits, fn, None))
        self._record(ev, reads, writes)
        return ev

    def dma(self, eng, fn, semres, reads=(), writes=()):
        waits = self._waits_for(eng, self._deps(reads, writes))
        semres.semcnt += 1
        ev = (semres, 16 * semres.semcnt)
        self.streams[eng].append((waits, fn, semres))
        self._record(ev, reads, writes)
        return ev

    def alias(self, olds, news):
        evs = {}
        for o in olds:
            if o.lw is not None:
                k, v = o.lw
                evs[k] = max(evs.get(k, 0), v)
            for k, v in o.rd.items():
                evs[k] = max(evs.get(k, 0), v)
        for n in news:
            for k, v in evs.items():
                if n.rd.get(k, 0) < v:
                    n.rd[k] = v

    def wait_all(self, eng, resources):
        deps = []
        for r in resources:
            if r.lw is not None:
                deps.append(r.lw)
            deps.extend(r.rd.items())
        waits = self._waits_for(eng, deps)
        self.streams[eng].append((waits, None, None))

    def emit(self):
        nc = self.nc
        with nc.Block() as block:
            def run(engname):
                def body(e):
                    esem = self.esem[engname]
                    for (waits, fn, semres) in self.streams[engname]:
                        for (key, val) in waits:
                            e.wait_ge(self._sem_of(key), val)
                        if fn is None:
                            continue
                        ins = fn(e)
                        if semres is None:
                            ins.then_inc(esem, 1)
                        else:
                            ins.then_inc(semres.sem, 16)
                return body
            block.tensor(run("pe"))
            block.scalar(run("act"))
            block.vector(run("dve"))
            block.gpsimd(run("pool"))
            block.sync(run("sp"))


def build_nc(NB, S):
    NT = S // 128
    NG = S // 512
    nc = bass.Bass("TRN2", target_bir_lowering=False)
    dt_in = lambda name, shape: nc.dram_tensor(name, list(shape), F32, kind="ExternalInput").ap()
    x = dt_in("x", [NB, S, D])
    sb_pre_g = dt_in("sb_pre_g", [1, D])
    sb_w_qkv = dt_in("sb_w_qkv", [1, D, 3 * D])
    sb_w_o = dt_in("sb_w_o", [1, D, D])
    sb_post_g = dt_in("sb_post_g", [1, D])
    kv_norm_g = dt_in("kv_norm_g", [D])
    w_kvf = dt_in("w_kvf", [D, 2 * D + H])
    b_f = dt_in("b_f", [H])
    fox_pre_g = dt_in("fox_pre_g", [1, D])
    fox_w_q = dt_in("fox_w_q", [1, D, D])
    fox_w_o = dt_in("fox_w_o", [1, D, D])
    fox_post_g = dt_in("fox_post_g", [1, D])
    ffn_pre_g = dt_in("ffn_pre_g", [2, D])
    w_up = dt_in("w_up", [2, D, 2 * FF])
    conv_w = dt_in("conv_w", [2, 3, 2 * FF])
    conv_b = dt_in("conv_b", [2, 2 * FF])
    w_down = dt_in("w_down", [2, FF, D])
    ffn_post_g = dt_in("ffn_post_g", [2, D])
    y = nc.dram_tensor("y", [NB, S, D], F32, kind="ExternalOutput").ap()

    ws_qkv = nc.dram_tensor("ws_qkv", [5, 8, 128, 1024], BF16).ap()
    ws_kv = nc.dram_tensor("ws_kv", [2, 8, 128, 1024], BF16).ap()
    ws_f = nc.dram_tensor("ws_f", [128, 8 * 16], BF16).ap()
    ws_o = nc.dram_tensor("ws_o", [2, 128, 8192], BF16).ap()
    ws_up = nc.dram_tensor("ws_up", [2, NCH, 128, 2048], BF16).ap()
    ws_dn = nc.dram_tensor("ws_dn", [2, NCH, 128, 1024], BF16).ap()

    with ExitStack() as ctx:
        P = Prog(nc, ctx)
        sbt = lambda name, shape, dt: ctx.enter_context(nc.sbuf_tensor(name, list(shape), dt))
        pst = lambda name, shape, dt: ctx.enter_context(nc.psum_tensor(name, list(shape), dt))

        Hbuf = sbt("Hbuf", [128, NT * D], F32)
        hview = Hbuf[:].rearrange("p (t d) -> p t d", d=D)
        r_h = [P.res("h%d" % t, dma=True) for t in range(NT)]
        R1N = max(8 * S, 16384)
        R2N = max(8 * S, 16384)
        R3N = max(4 * S + NT * 256, 8192)
        R1 = sbt("R1", [128, R1N], BF16)
        R2 = sbt("R2", [128, R2N], BF16)
        R3 = sbt("R3", [128, R3N], BF16)
        r_R1 = P.res("R1")
        r_R2 = P.res("R2")
        aT = R1[:, 0:8 * S].rearrange("p (k s) -> p k s", s=S)
        bT = R2[:, 0:8 * S].rearrange("p (k s) -> p k s", s=S)
        qa = [R3[:, 0:S], R3[:, S:2 * S]]
        ka = [R3[:, 2 * S:3 * S], R3[:, 3 * S:4 * S]]
        vpad = R3[:, 4 * S:4 * S + NT * 256].rearrange("p (t h c) -> p t h c", h=2, c=128)
        r_qa = [P.res("qa0", dma=True), P.res("qa1", dma=True)]
        r_ka = [P.res("ka0", dma=True), P.res("ka1", dma=True)]
        r_vpad = P.res("vpad")
        r_qrow = [[P.res("qrow%d%d" % (i, j), dma=True) for j in range(2)] for i in range(2)]
        r_krow = [[P.res("krow%d%d" % (i, j), dma=True) for j in range(2)] for i in range(2)]
        r_attn_all = r_qa + r_ka + [r_vpad] + r_qrow[0] + r_qrow[1] + r_krow[0] + r_krow[1]
        WoT = R3[:, 0:8192].rearrange("p (k n) -> p k n", n=D)
        r_Wo = P.res("Wo", dma=True)
        gT = R2[:, 0:NCH * 512].rearrange("p (c t) -> p c t", t=512)
        r_gT = [P.res("gT%d" % c) for c in range(NCH)]
        wupb = [R2[:, NCH * 512 + i * 2048: NCH * 512 + (i + 1) * 2048].rearrange("p (g k j) -> p g k j", g=2, k=8) for i in range(2)]
        r_wup = [P.res("wup%d" % i, dma=True) for i in range(2)]
        NWD = 8
        wdnb = [R3[:, i * 1024:(i + 1) * 1024] for i in range(NWD)]
        r_wdn = [P.res("wdn%d" % i, dma=True) for i in range(NWD)]

        Wq = sbt("Wq", [128, 8, 128], BF16); r_Wq = P.res("Wq", dma=True)
        Wk = sbt("Wk", [128, 8, 128], BF16); r_Wk = P.res("Wk", dma=True)
        Wv = sbt("Wv", [128, 8, 128], BF16); r_Wv = P.res("Wv", dma=True)
        Wf = sbt("Wf", [128, 8, 16], BF16); r_Wf = P.res("Wf", dma=True)
        e32 = [sbt("e32_%d" % i, [128, 516], F32) for i in range(4)]; r_e32 = [P.res() for _ in range(4)]
        spb = [sbt("spb_%d" % i, [128, 512], BF16) for i in range(4)]; r_sp = [P.res() for _ in range(4)]
        E2 = [sbt("E2_%d" % i, [128, 512], F32) for i in range(2)]; r_E2 = [P.res() for _ in range(2)]
        Ab = [sbt("Ab_%d" % i, [128, 512], BF16) for i in range(3)]; r_A = [P.res() for _ in range(3)]
        Ls32 = [sbt("Ls32_%d" % i, [128, 512], F32) for i in range(1)]; r_Ls32 = [P.res() for _ in range(1)]
        rinv = sbt("rinv", [128, 512], F32); r_rinv = P.res()
        junk = sbt("junk", [128, 1024], BF16); r_junk = P.res()
        xn = sbt("xn", [128, 1024], BF16); r_xn = P.res()
        tmpf = sbt("tmpf", [128, 1024], F32); r_tmpA = P.res(); r_tmpB = P.res()
        Gpre = sbt("Gpre", [128, 1024], F32); r_Gpre = P.res("Gpre", dma=True)
        Gpost = Gpre; r_Gpost = r_Gpre
        st = sbt("stats", [128, 8], F32); r_st = P.res()
        pst_ = sbt("pstats", [128, 3 * 16], F32); r_pst = P.res()
        frowA = sbt("frowA", [80, S], BF16)
        frowB = sbt("frowB", [16, S], BF16)
        frow = [frowA[0:16, :], frowA[32:48, :], frowA[64:80, :], frowB[0:16, :]]
        r_frow = P.res()
        carry = sbt("carry", [16, 1], F32); r_carry = P.res()
        bfneg = sbt("bfneg", [16, 1], F32); r_bf = P.res("bf", dma=True)
        CW = [sbt("CW%d" % l, [128, 4, 2 * NCH], F32) for l in range(2)]; r_CW = [P.res("CW%d" % l, dma=True) for l in range(2)]
        hraw = e32; r_hraw = r_e32
        cacc = [E2[0][:, :], E2[1][:, :], tmpf[:, 0:512], tmpf[:, 512:1024]]; r_cacc = [r_E2[0], r_E2[1], r_tmpA, r_tmpB]
        gl = rinv; r_gl = r_rinv
        halo = sbt("halo", [128, 2 * NCH, 2], F32); r_halo = P.res()
        ident = sbt("ident", [128, 128], BF16); r_ident = P.res()
        mstrict = sbt("mstrict", [128, 128], BF16); r_mstrict = P.res()
        mincl = sbt("mincl", [128, 128], BF16); r_mincl = P.res()
        UI = sbt("UI", [128, 128], BF16); r_UI = P.res()
        Lc = sbt("Lc", [128, 128], BF16); r_Lc = P.res()
        PB = [pst("PB%d" % i, [128, 512], F32) for i in range(7)]; r_PB = [P.res("PB%d" % i) for i in range(7)]
        PT = pst("PT", [128, 1024], BF16); r_PT = P.res("PT")

        def mm(out, lhsT, rhs, start, stop, reads, writes):
            P.op("pe", lambda e: e.matmul(out, lhsT=lhsT, rhs=rhs, start=start, stop=stop, skip_group_check=True), reads, writes)

        def act(out, in_, func, reads, writes, **kw):
            P.op("act", lambda e: e.activation(out=out, in_=in_, func=func, **kw), reads, writes)

        def tcopy(eng, out, in_, reads, writes):
            P.op(eng, lambda e: e.tensor_copy(out=out, in_=in_), reads, writes)

        def tt(eng, out, in0, in1, op, reads, writes):
            P.op(eng, lambda e: e.tensor_tensor(out=out, in0=in0, in1=in1, op=op), reads, writes)

        def ts(eng, out, in0, s1, s2, op0, op1, reads, writes):
            if s2 is None:
                P.op(eng, lambda e: e.tensor_scalar(out=out, in0=in0, scalar1=s1, scalar2=None, op0=op0), reads, writes)
            else:
                P.op(eng, lambda e: e.tensor_scalar(out=out, in0=in0, scalar1=s1, scalar2=s2, op0=op0, op1=op1), reads, writes)

        def stt(eng, out, in0, scalar, in1, op0, op1, reads, writes):
            P.op(eng, lambda e: e.scalar_tensor_tensor(out=out, in0=in0, scalar=scalar, in1=in1, op0=op0, op1=op1), reads, writes)

        def memset(eng, ap, val, writes):
            P.op(eng, lambda e: e.memset(ap, val), (), writes)

        def dma(out, in_, semres, reads, writes, slow=False):
            if slow:
                P.dma("sp", lambda e: e.dma_start(out=out, in_=in_, allow_slow_non_contiguous=True), semres, reads, writes)
            else:
                P.dma("sp", lambda e: e.dma_start(out=out, in_=in_), semres, reads, writes)

        def aff(ap, pattern, cmp, cm, writes):
            P.op("pool", lambda e: e.affine_select(out=ap, in_=ap, pattern=pattern, compare_op=cmp, fill=0.0, base=0, channel_multiplier=cm), writes, writes)
        for (t_, r_) in ((ident, r_ident), (mstrict, r_mstrict), (mincl, r_mincl), (UI, r_UI), (Lc, r_Lc)):
            memset("pool", t_[:], 1.0, [r_])
        aff(ident[:], [[-1, 128]], ALU.is_equal, 1, [r_ident])
        aff(mstrict[:], [[1, 128]], ALU.is_gt, -1, [r_mstrict])
        aff(mincl[:], [[1, 128]], ALU.is_ge, -1, [r_mincl])
        aff(UI[:], [[-1, 128]], ALU.is_ge, 1, [r_UI])
        aff(Lc[:], [[1, 128]], ALU.is_gt, -1, [r_Lc])
        memset("pool", halo[:], 0.0, [r_halo])
        for l in range(2):
            for k in range(3):
                dma(CW[l][:, k, :], conv_w[l, k].rearrange("(c p) -> p c", p=128), r_CW[l], [], [r_CW[l]], slow=True)
            dma(CW[l][:, 3, :], conv_b[l].rearrange("(c p) -> p c", p=128), r_CW[l], [], [r_CW[l]], slow=True)
        dma(bfneg[:], b_f.rearrange("(h o) -> h o", o=1), r_bf, [], [r_bf], slow=True)
        ts("dve", bfneg[:], bfneg[:], -1.0, None, ALU.mult, None, [r_bf], [r_bf])

        NSLOT = 4 if NT * D >= 4 * 4096 else 2
        if NSLOT == 4:
            stg32 = [Hbuf[:, i * 4096:(i + 1) * 4096] for i in range(4)]
        else:
            stg32 = [R1[:, i * 8192:(i + 1) * 8192].bitcast(F32) for i in range(2)]
        stg16 = [R2[:, i * 4096:(i + 1) * 4096] for i in range(NSLOT)]
        r_s32 = [P.res("s32_%d" % i, dma=True) for i in range(NSLOT)]
        r_s16 = [P.res("s16_%d" % i) for i in range(NSLOT)]
        r_s16d = [P.res("s16d_%d" % i, dma=True) for i in range(NSLOT)]
        cast_engs = ["dve", "pool", "act"]
        ucount = [0]

        def cast_unit(srcs, E, dst, outview=None):
            cast_list.append((srcs, E, dst, outview))

        cast_list = []

        def emit_casts():
            n = len(cast_list)
            LOOK = NSLOT - 1

            def emit_in(u):
                i = u % NSLOT
                for (vf, src) in cast_list[u][0]:
                    dma(vf(stg32[i]), src, r_s32[i], [], [r_s32[i]])

            def emit_cast_out(u):
                i = u % NSLOT
                srcs, E, dst, outview = cast_list[u]
                if u % 2 == 0:
                    act(stg16[i][:, 0:E], stg32[i][:, 0:E], AF.Copy, [r_s32[i]], [r_s16[i]])
                else:
                    tcopy("dve", stg16[i][:, 0:E], stg32[i][:, 0:E], [r_s32[i]], [r_s16[i]])
                src16 = stg16[i][:, 0:E] if outview is None else outview(stg16[i][:, 0:E])
                dma(dst, src16, r_s16d[i], [r_s16[i]], [r_s16d[i]])
            for u in range(n + LOOK):
                if u < n:
                    emit_in(u)
                if u - LOOK >= 0:
                    emit_cast_out(u - LOOK)

        def img4(stage, a, b, c):
            return stage[:, 0:a * b * c].rearrange("p (a b c) -> p a b c", a=a, b=b)

        def img3(stage, a, b):
            return stage[:, 0:a * b].rearrange("p (a b) -> p a b", a=a)

        def cast_cols(src2d, col0, dst_units):
            for half in range(2):
                srcs = []
                for hp4 in range(4):
                    hp = half * 4 + hp4
                    srcs.append((lambda s, hp4=hp4: img4(s, 4, 8, 128)[:, hp4],
                                 src2d[:, col0 + hp * 128: col0 + (hp + 1) * 128].rearrange("(k p) j -> p k j", p=128)))
                cast_unit(srcs, 4096, dst_units[half * 4:half * 4 + 4].rearrange("u p e -> p u e"),
                          outview=lambda v: v.rearrange("p (u e) -> p u e", u=4))
        cast_cols(sb_w_qkv[0], 0, ws_qkv[0])
        cast_cols(sb_w_qkv[0], D, ws_qkv[1])
        cast_cols(sb_w_qkv[0], 2 * D, ws_qkv[2])

        def cast_wo(src2d, dst):
            for half in range(2):
                srcs = [(lambda s: img3(s, 4, 1024),
                         src2d[half * 512:(half + 1) * 512, :].rearrange("(k p) n -> p k n", p=128))]
                cast_unit(srcs, 4096, dst[:, half * 4096:(half + 1) * 4096])
        cast_wo(sb_w_o[0], ws_o[0])

        def cast_ffn(l):
            for c0 in range(0, NCH, 2):
                srcs = []
                for gu in range(2):
                    for ci in range(2):
                        c = c0 + ci
                        srcs.append((lambda s, gu=gu, ci=ci: s[:, 0:4096].rearrange("p (c g k j) -> p c g k j", c=2, g=2, k=8)[:, ci, gu],
                                     w_up[l][:, gu * FF + c * 128: gu * FF + (c + 1) * 128].rearrange("(k p) j -> p k j", p=128)))
                cast_unit(srcs, 4096, ws_up[l, c0:c0 + 2].rearrange("c p e -> p c e"),
                          outview=lambda v: v.rearrange("p (c e) -> p c e", c=2))
            for c0 in range(0, NCH, 4):
                n = min(4, NCH - c0)
                srcs = [(lambda s, n=n: img3(s, n, 1024),
                         w_down[l][c0 * 128:(c0 + n) * 128, :].rearrange("(c p) n -> p c n", p=128))]
                cast_unit(srcs, n * 1024, ws_dn[l, c0:c0 + n].rearrange("c p e -> p c e"),
                          outview=lambda v, n=n: v.rearrange("p (c e) -> p c e", c=n))
        cast_ffn(0)
        cast_cols(w_kvf, 0, ws_kv[0])
        cast_cols(w_kvf, D, ws_kv[1])
        cast_unit([(lambda s: img3(s, 8, 16), w_kvf[:, 2 * D:2 * D + H].rearrange("(k p) j -> p k j", p=128))], 128, ws_f)
        cast_cols(fox_w_q[0], 0, ws_qkv[3])
        cast_wo(fox_w_o[0], ws_o[1])
        cast_ffn(1)
        emit_casts()
        r_scr = r_s16d
        P.alias(r_s32 + r_s16 + r_s16d, [r_R1, r_R2] + r_h)

        def prenorm(g_ap, dstT, r_dst, first_alias=None):
            dma(Gpre[:], g_ap.partition_broadcast(128), r_Gpre, [], [r_Gpre])
            for t in range(NT):
                act(junk[:], hview[:, t, :], AF.Square, [r_h[t]], [r_junk, r_pst], accum_out=pst_[:, t:t + 1])
            act(pst_[:, 16:16 + NT], pst_[:, 0:NT], AF.Ln, [r_pst], [r_pst], scale=1.0 / D, bias=EPS)
            act(pst_[:, 32:32 + NT], pst_[:, 16:16 + NT], AF.Exp, [r_pst], [r_pst], scale=-0.5)
            for t in range(NT):
                stt("dve", xn[:], hview[:, t, :], pst_[:, 32 + t:33 + t], Gpre[:], ALU.mult, ALU.mult, [r_h[t], r_pst, r_Gpre], [r_xn])
                for kc in range(8):
                    P.op("pe", lambda e, kc=kc: e.transpose(PT[:, kc * 128:(kc + 1) * 128], xn[:, kc * 128:(kc + 1) * 128], ident[:]),
                         [r_xn, r_ident], [r_PT])
                tcopy("dve", dstT[:, :, t * 128:(t + 1) * 128], PT[:].rearrange("p (k j) -> p k j", j=128), [r_PT], [r_dst])

        def postnorm_residual(t, ba, bb, g_res):
            act(junk[:, 0:512], PB[ba][:], AF.Square, [r_PB[ba]], [r_junk, r_st], accum_out=st[:, 3:4])
            act(junk[:, 512:1024], PB[bb][:], AF.Square, [r_PB[bb]], [r_junk, r_st], accum_out=st[:, 4:5])
            tt("dve", st[:, 5:6], st[:, 3:4], st[:, 4:5], ALU.add, [r_st], [r_st])
            act(st[:, 6:7], st[:, 5:6], AF.Ln, [r_st], [r_st], scale=1.0 / D, bias=EPS)
            act(st[:, 7:8], st[:, 6:7], AF.Exp, [r_st], [r_st], scale=-0.5)
            stt("dve", tmpf[:, 0:512], PB[ba][:], st[:, 7:8], Gpost[:, 0:512], ALU.mult, ALU.mult, [r_PB[ba], r_st, g_res], [r_tmpA])
            stt("dve", tmpf[:, 512:1024], PB[bb][:], st[:, 7:8], Gpost[:, 512:1024], ALU.mult, ALU.mult, [r_PB[bb], r_st, g_res], [r_tmpB])
            tt("pool", hview[:, t, :], hview[:, t, :], tmpf[:], ALU.add, [r_h[t], r_tmpA, r_tmpB], [r_h[t]])

        def out_proj(widx, g_ap, srcT, r_src):
            P.alias(r_attn_all, [r_Wo])
            dma(WoT[:, 0:4, :], ws_o[widx][:, 0:4096].rearrange("p (k n) -> p k n", n=D), r_Wo, r_scr, [r_Wo])
            dma(WoT[:, 4:8, :], ws_o[widx][:, 4096:8192].rearrange("p (k n) -> p k n", n=D), r_Wo, r_scr, [r_Wo])
            dma(Gpost[:], g_ap.partition_broadcast(128), r_Gpost, [], [r_Gpost])
            for t in range(NT):
                ba, bb = (0, 1) if t % 2 == 0 else (2, 3)
                for n, bk in ((0, ba), (1, bb)):
                    for kc in range(8):
                        mm(PB[bk][:], srcT[:, kc, t * 128:(t + 1) * 128], WoT[:, kc, n * 512:(n + 1) * 512],
                           kc == 0, kc == 7, [r_src, r_Wo], [r_PB[bk]])
                postnorm_residual(t, ba, bb, r_Gpost)
            P.alias([r_Wo], r_attn_all)

        def pipeline(n, stages, forward=False):
            ns = len(stages)
            for tick in range(n + ns - 1):
                for k in (range(ns) if forward else range(ns - 1, -1, -1)):
                    i = tick - k
                    if 0 <= i < n:
                        stages[k](i)

        def load_pair_weights(qsrc, ksrc, vsrc, hp):
            if qsrc is not None:
                dma(Wq[:], qsrc[hp].rearrange("p (k j) -> p k j", j=128), r_Wq, r_scr, [r_Wq])
            dma(Wk[:], ksrc[hp].rearrange("p (k j) -> p k j", j=128), r_Wk, r_scr, [r_Wk])
            dma(Wv[:], vsrc[hp].rearrange("p (k j) -> p k j", j=128), r_Wv, r_scr, [r_Wv])

        pbank = [0]

        def next_pbank():
            pbank[0] += 1
            return (6, 0, 1)[pbank[0] % 3]

        def proj_featmajor(Wt, r_W, srcT, r_src, bank, sink):
            for tg in range(NG):
                bank = next_pbank()
                for kc in range(8):
                    mm(PB[bank][:], Wt[:, kc, :], srcT[:, kc, tg * 512:(tg + 1) * 512], kc == 0, kc == 7, [r_W, r_src], [r_PB[bank]])
                sink(tg, bank)

        def proj_v(srcT, r_src, bank, padval):
            memset("pool", vpad[:, :, 0, 64:128], padval, [r_vpad])
            memset("pool", vpad[:, :, 1, 0:64], padval, [r_vpad])
            for t4 in range(0, NT, 4):
                bank = next_pbank()
                for ti in range(4):
                    t = t4 + ti
                    for kc in range(8):
                        mm(PB[bank][:, ti * 128:(ti + 1) * 128], srcT[:, kc, t * 128:(t + 1) * 128], Wv[:, kc, :], kc == 0, kc == 7,
                           [r_src, r_Wv], [r_PB[bank]])
                pv = PB[bank][:].rearrange("p (t c) -> p t c", c=128)
                tcopy("dve", vpad[:, t4:t4 + 4, 0, 0:64], pv[:, :, 0:64], [r_PB[bank]], [r_vpad])
                tcopy("dve", vpad[:, t4:t4 + 4, 1, 64:128], pv[:, :, 64:128], [r_PB[bank]], [r_vpad])

        def attn_sb_pair(hp, obank_base):
            units = []
            for g in range(NG):
                for sb in range(4 * g + 3, -1, -1):
                    for hd in range(2):
                        units.append((g, sb, hd))
            n = len(units)

            def geom(u):
                g, sb, hd = units[u]
                c0 = max(sb * 128, g * 512) - g * 512
                return g, sb, hd, c0, (sb >= 4 * g)

            def s_z(u):
                g, sb, hd, c0, diag = geom(u)
                zb = (0, 1, 6)[u % 3]
                mm(PB[zb][:, c0:512], ka[hd][0:128, sb * 128:(sb + 1) * 128], qa[hd][0:128, g * 512 + c0:(g + 1) * 512], True, True,
                   [r_ka[hd], r_qa[hd]], [r_PB[zb]])

            def s_e(u):
                g, sb, hd, c0, diag = geom(u)
                zb = (0, 1, 6)[u % 3]
                eb = u % 4
                act(e32[eb][:, c0:512], PB[zb][:, c0:512], AF.Exp, [r_PB[zb]], [r_e32[eb]])

            def s_esp(u):
                g, sb, hd, c0, diag = geom(u)
                zb = (0, 1, 6)[u % 3]
                eb = u % 4
                act(spb[eb][:, c0:512], e32[eb][:, c0:512], AF.Ln, [r_e32[eb]], [r_sp[eb]], bias=1.0)
                if diag:
                    tt("pool", spb[eb][:, c0:c0 + 128], spb[eb][:, c0:c0 + 128], mstrict[:], ALU.mult, [r_sp[eb], r_mstrict], [r_sp[eb]])

            def s_x(u):
                g, sb, hd, c0, diag = geom(u)
                eb = u % 4
                xb = 2 + hd
                mm(PB[xb][:, c0:512], UI[:], spb[eb][:, c0:512], sb == 4 * g + 3, False, [r_UI, r_sp[eb]], [r_PB[xb]])

            def s_a(u):
                g, sb, hd, c0, diag = geom(u)
                eb = u % 4
                xb = 2 + hd
                ab = u % 3
                act(E2[hd][:, c0:512], PB[xb][:, c0:512], AF.Exp, [r_PB[xb]], [r_E2[hd]], scale=-1.0)
                if sb > 0:
                    mm(PB[xb][:, c0:512], Lc[:], spb[eb][:, c0:512], False, False, [r_Lc, r_sp[eb]], [r_PB[xb]])
                tt("dve", Ab[ab][:, c0:512], e32[eb][:, c0:512], E2[hd][:, c0:512], ALU.mult, [r_e32[eb], r_E2[hd]], [r_A[ab]])
                if diag:
                    tt("pool", Ab[ab][:, c0:c0 + 128], Ab[ab][:, c0:c0 + 128], mstrict[:], ALU.mult, [r_A[ab], r_mstrict], [r_A[ab]])

            def s_pv(u):
                g, sb, hd, c0, diag = geom(u)
                ab = u % 3
                ob = obank_base + (g % 2)
                firstm = (sb == 4 * g + 3 and hd == 0)
                mm(PB[ob][:, c0:512], vpad[:, sb, hd, :], Ab[ab][:, c0:512], firstm, False, [r_vpad, r_A[ab]], [r_PB[ob]])
                if sb == 0 and hd == 1:
                    tcopy("dve", bT[:, hp, g * 512:(g + 1) * 512], PB[ob][:], [r_PB[ob]], [r_R2])

            pipeline(n, [s_z, s_e, s_esp, s_x, s_a, s_pv])

        def attn_fox_pair(hp):
            units = []
            for g in range(NG):
                for sb in range(4 * g + 3, -1, -1):
                    for hd in range(2):
                        units.append((g, sb, hd))
            n = len(units)

            def geom(u):
                g, sb, hd = units[u]
                c0 = max(sb * 128, g * 512) - g * 512
                return g, sb, hd, c0, (sb >= 4 * g)

            def obank(g, hd):
                return (4 + hd) if g % 2 == 0 else (2 + hd)

            def s_z(u):
                g, sb, hd, c0, diag = geom(u)
                zb = (0, 1, 6)[u % 3]
                mm(PB[zb][:, c0:512], ka[hd][0:128, sb * 128:(sb + 1) * 128], qa[hd][0:128, g * 512 + c0:(g + 1) * 512], True, True,
                   [r_ka[hd], r_qa[hd]] + r_qrow[hd] + r_krow[hd], [r_PB[zb]])

            def s_a(u):
                g, sb, hd, c0, diag = geom(u)
                zb = (0, 1, 6)[u % 3]
                ab = u % 3
                act(Ab[ab][:, c0:512], PB[zb][:, c0:512], AF.Exp, [r_PB[zb]], [r_A[ab]])
                if diag:
                    tt("pool", Ab[ab][:, c0:c0 + 128], Ab[ab][:, c0:c0 + 128], mincl[:], ALU.mult, [r_A[ab], r_mincl], [r_A[ab]])

            def s_pv(u):
                g, sb, hd, c0, diag = geom(u)
                ab = u % 3
                ob = obank(g, hd)
                mm(PB[ob][:, c0:512], vpad[:, sb, hd, :], Ab[ab][:, c0:512], sb == 4 * g + 3, False, [r_vpad, r_A[ab]], [r_PB[ob]])
                if sb == 0:
                    if hd == 0:
                        P.op("dve", lambda e: e.reciprocal(out=rinv[0:64, :], in_=PB[ob][64:128, :]), [r_PB[ob]], [r_rinv])
                        tt("dve", bT[0:64, hp, g * 512:(g + 1) * 512], PB[ob][0:64, :], rinv[0:64, :], ALU.mult, [r_PB[ob], r_rinv], [r_R2])
                    else:
                        P.op("dve", lambda e: e.reciprocal(out=rinv[64:128, :], in_=PB[ob][0:64, :]), [r_PB[ob]], [r_rinv])
                        tt("dve", bT[64:128, hp, g * 512:(g + 1) * 512], PB[ob][64:128, :], rinv[64:128, :], ALU.mult, [r_PB[ob], r_rinv], [r_R2])

            pipeline(n, [s_z, s_a, s_pv], forward=True)

        def ffn(l, after_tile=None):
            prenorm(ffn_pre_g[l], aT, r_R1)
            dma(Gpost[:], ffn_post_g[l].partition_broadcast(128), r_Gpost, [], [r_Gpost])
            P.alias([r_R2], r_gT + r_wup)
            P.alias(r_attn_all, r_wdn)
            LAG, PRE = 2, 6
            for tg in range(NG):
                def load_up(c):
                    dma(wupb[c % 2][:], ws_up[l, c].rearrange("p (g k j) -> p g k j", g=2, k=8), r_wup[c % 2], r_scr, [r_wup[c % 2]])
                steps = [(half, c) for half in range(2) for c in range(NCH)]
                loaded = [0]

                def ensure_loaded(k):
                    while loaded[0] <= min(k, len(steps) - 1):
                        i = loaded[0] % NWD
                        dma(wdnb[i], ws_dn[l, steps[loaded[0]][1]], r_wdn[i], r_scr, [r_wdn[i]])
                        loaded[0] += 1

                def down_step(k):
                    ensure_loaded(k + PRE)
                    half, c = steps[k]
                    i = k % NWD
                    for t2 in range(2):
                        tl = 2 * half + t2
                        for nn in range(2):
                            bk = 2 * t2 + nn
                            mm(PB[bk][:], gT[:, c, tl * 128:(tl + 1) * 128], wdnb[i][:, nn * 512:(nn + 1) * 512], c == 0, c == NCH - 1,
                               [r_gT[c], r_wdn[i]], [r_PB[bk]])
                    if c == NCH - 1:
                        for t2 in range(2):
                            postnorm_residual(tg * 4 + 2 * half + t2, 2 * t2, 2 * t2 + 1, r_Gpost)
                            if after_tile is not None:
                                after_tile(tg * 4 + 2 * half + t2)

                def gate_mul(cc):
                    cg, cu = 2 * (cc % 2), 2 * (cc % 2) + 1
                    act(gl[:], cacc[cg], AF.Gelu_apprx_tanh, [r_cacc[cg]], [r_gl])
                    tt("pool", gT[:, cc, :], gl[:], cacc[cu], ALU.mult, [r_gl, r_cacc[cu]], [r_gT[cc]])

                load_up(0)
                ensure_loaded(PRE - 1)
                kdown = 0
                for c in range(NCH):
                    if c + 1 < NCH:
                        load_up(c + 1)
                    for gu in range(2):
                        bk = 4 + gu
                        for kc in range(8):
                            mm(PB[bk][:], wupb[c % 2][:, gu, kc, :], aT[:, kc, tg * 512:(tg + 1) * 512], kc == 0, kc == 7,
                               [r_wup[c % 2], r_R1], [r_PB[bk]])
                    if c >= LAG:
                        down_step(kdown)
                        kdown += 1
                    for gu in range(2):
                        bk = 4 + gu
                        ci = gu * NCH + c
                        hb = 2 * (c % 2) + gu
                        hr, rh = hraw[hb], r_hraw[hb]
                        ca, rca = cacc[hb], r_cacc[hb]
                        if tg == 0:
                            memset("pool", hr[:, 2:4], 0.0, [rh])
                        else:
                            tcopy("pool", hr[:, 2:4], halo[:, ci, :], [r_halo], [rh])
                        act(hr[:, 4:516], PB[bk][:], AF.Copy, [r_PB[bk]], [rh])
                        tcopy("pool", halo[:, ci, :], hr[:, 514:516], [rh], [r_halo])
                        act(ca, PB[bk][:], AF.Identity, [r_PB[bk], r_CW[l]], [rca],
                            scale=CW[l][:, 2, ci:ci + 1], bias=CW[l][:, 3, ci:ci + 1])
                    for tap in (1, 0):
                        for gu in range(2):
                            ci = gu * NCH + c
                            hb = 2 * (c % 2) + gu
                            hr, rh = hraw[hb], r_hraw[hb]
                            ca, rca = cacc[hb], r_cacc[hb]
                            stt("dve", ca, hr[:, 2 + tap:514 + tap], CW[l][:, tap, ci:ci + 1], ca, ALU.mult, ALU.add, [rh, r_CW[l], rca], [rca])
                    if c > 0:
                        gate_mul(c - 1)
                    if c == NCH - 1:
                        gate_mul(c)
                while kdown < len(steps):
                    down_step(kdown)
                    kdown += 1
            P.alias(r_gT + r_wup, [r_R2])
            P.alias(r_wdn, r_attn_all)

        for t in range(NT):
            dma(hview[:, t, :], x[0, t * 128:(t + 1) * 128, :], r_h[t], [], [r_h[t]])
        for b in range(NB):
            prenorm(sb_pre_g[0], aT, r_R1)
            for hd in range(2):
                memset("pool", qa[hd][64:128, :], 0.0, [r_qa[hd]])
                memset("pool", ka[hd][64:128, :], 0.0, [r_ka[hd]])
            load_pair_weights(ws_qkv[0], ws_qkv[1], ws_qkv[2], 0)
            for hp in range(8):

                def sink_q(tg, bank):
                    for hd in range(2):
                        ts("dve", qa[hd][0:64, tg * 512:(tg + 1) * 512], PB[bank][hd * 64:(hd + 1) * 64, :], 0.125, None, ALU.mult, None,
                           [r_PB[bank]], [r_qa[hd]])

                def sink_k(tg, bank):
                    for hd in range(2):
                        tcopy("dve", ka[hd][0:64, tg * 512:(tg + 1) * 512], PB[bank][hd * 64:(hd + 1) * 64, :], [r_PB[bank]], [r_ka[hd]])
                proj_featmajor(Wq, r_Wq, aT, r_R1, 6, sink_q)
                proj_featmajor(Wk, r_Wk, aT, r_R1, 6, sink_k)
                proj_v(aT, r_R1, 6, 0.0)
                if hp + 1 < 8:
                    load_pair_weights(ws_qkv[0], ws_qkv[1], ws_qkv[2], hp + 1)
                attn_sb_pair(hp, 4)
            out_proj(0, sb_post_g[0], bT, r_R2)
            ffn(0)
            prenorm(fox_pre_g[0], aT, r_R1)
            for hp in range(8):
                dma(Wq[:], ws_qkv[3][hp].rearrange("p (k j) -> p k j", j=128), r_Wq, r_scr, [r_Wq])

                def sink_qall(tg, bank, hp=hp):
                    ts("dve", bT[:, hp, tg * 512:(tg + 1) * 512], PB[bank][:], 0.125, None, ALU.mult, None, [r_PB[bank]], [r_R2])
                proj_featmajor(Wq, r_Wq, aT, r_R1, 6, sink_qall)
            prenorm(kv_norm_g, aT, r_R1)
            dma(Wf[:], ws_f.rearrange("p (k j) -> p k j", j=16), r_Wf, r_scr, [r_Wf])
            memset("pool", E2[1][0:16, 0:512], 1.0, [r_E2[1]])
            for tg in range(NG):
                seg = slice(tg * 512, (tg + 1) * 512)
                for kc in range(8):
                    mm(PB[6][0:16, :], Wf[:, kc, :], aT[:, kc, seg], kc == 0, kc == 7, [r_Wf, r_R1], [r_PB[6]])
                act(e32[0][0:16, 0:512], PB[6][0:16, :], AF.Exp, [r_PB[6], r_bf], [r_e32[0]], scale=-1.0, bias=bfneg[:, 0:1])
                act(e32[1][0:16, 0:512], e32[0][0:16, 0:512], AF.Ln, [r_e32[0]], [r_e32[1]], bias=1.0)
                if tg == 0:
                    P.op("dve", lambda e: e.tensor_tensor_scan(out=E2[0][0:16, 0:512], data0=E2[1][0:16, 0:512], data1=e32[1][0:16, 0:512],
                                                               initial=0.0, op0=ALU.mult, op1=ALU.add),
                         [r_E2[1], r_e32[1]], [r_E2[0]])
                else:
                    P.op("dve", lambda e: e.tensor_tensor_scan(out=E2[0][0:16, 0:512], data0=E2[1][0:16, 0:512], data1=e32[1][0:16, 0:512],
                                                               initial=carry[:, 0:1], op0=ALU.mult, op1=ALU.add),
                         [r_E2[1], r_e32[1], r_carry], [r_E2[0]])
                tcopy("dve", carry[:, 0:1], E2[0][0:16, 511:512], [r_E2[0]], [r_carry])
                tcopy("dve", frow[0][:, seg], E2[0][0:16, 0:512], [r_E2[0]], [r_frow])
                tt("dve", Ls32[0][0:16, :], E2[0][0:16, 0:512], frow[0][:, seg], ALU.subtract, [r_E2[0], r_frow], [r_Ls32[0]])
                tcopy("dve", frow[1][:, seg], Ls32[0][0:16, :], [r_Ls32[0]], [r_frow])
                ts("dve", frow[2][:, seg], E2[0][0:16, 0:512], -1.0, None, ALU.mult, None, [r_E2[0]], [r_frow])
                ts("dve", frow[3][:, seg], Ls32[0][0:16, :], -1.0, None, ALU.mult, None, [r_Ls32[0]], [r_frow])
            for hd in range(2):
                memset("pool", qa[hd][64:128, :], 0.0, [r_qa[hd]])
                memset("pool", ka[hd][64:128, :], 0.0, [r_ka[hd]])
                memset("pool", qa[hd][64:68, :], 1.0, [r_qa[hd]])
                memset("pool", ka[hd][64:68, :], 1.0, [r_ka[hd]])
            load_pair_weights(None, ws_kv[0], ws_kv[1], 0)
            for hp in range(8):
                for hd in range(2):
                    hh = 2 * hp + hd
                    if hd == 0:
                        act(qa[hd][0:64, :], bT[0:64, hp, :], AF.Copy, [r_R2], [r_qa[hd]])
                    else:
                        tcopy("dve", qa[hd][0:64, :], bT[64:128, hp, :], [r_R2], [r_qa[hd]])
                    dma(qa[hd][64:65, :], frow[2][hh:hh + 1, :], r_qrow[hd][0], [r_frow, r_qa[hd]], [r_qrow[hd][0]])
                    dma(qa[hd][65:66, :], frow[3][hh:hh + 1, :], r_qrow[hd][1], [r_frow, r_qa[hd]], [r_qrow[hd][1]])
                    dma(ka[hd][66:67, :], frow[0][hh:hh + 1, :], r_krow[hd][0], [r_frow, r_ka[hd]], [r_krow[hd][0]])
                    dma(ka[hd][67:68, :], frow[1][hh:hh + 1, :], r_krow[hd][1], [r_frow, r_ka[hd]], [r_krow[hd][1]])

                def sink_k1(tg, bank):
                    for hd in range(2):
                        tcopy("dve", ka[hd][0:64, tg * 512:(tg + 1) * 512], PB[bank][hd * 64:(hd + 1) * 64, :], [r_PB[bank]], [r_ka[hd]])
                proj_featmajor(Wk, r_Wk, aT, r_R1, 6, sink_k1)
                proj_v(aT, r_R1, 6, 1.0)
                if hp + 1 < 8:
                    load_pair_weights(None, ws_kv[0], ws_kv[1], hp + 1)
                attn_fox_pair(hp)
            out_proj(1, fox_post_g[0], bT, r_R2)
            def stream_io(t, b=b):
                dma(y[b, t * 128:(t + 1) * 128, :], hview[:, t, :], r_h[t], [r_h[t]], [])
                if b + 1 < NB:
                    dma(hview[:, t, :], x[b + 1, t * 128:(t + 1) * 128, :], r_h[t], [], [r_h[t]])
            ffn(1, after_tile=stream_io)
        P.wait_all("sp", r_h)
        P.emit()
    return nc


_NC_CACHE = {}


def kernel(**inputs):
    x = np.ascontiguousarray(inputs["x"], dtype=np.float32)
    B, S, _ = x.shape
    NB = B // N_CORES
    key = (NB, S)
    if key not in _NC_CACHE:
        _NC_CACHE[key] = build_nc(NB, S)
    nc = _NC_CACHE[key]
    wnames = ["sb_pre_g", "sb_w_qkv", "sb_w_o", "sb_post_g", "kv_norm_g", "w_kvf", "b_f", "fox_pre_g", "fox_w_q",
              "fox_w_o", "fox_post_g", "ffn_pre_g", "w_up", "conv_w", "conv_b", "w_down", "ffn_post_g"]
    ws = {k: np.ascontiguousarray(inputs[k], dtype=np.float32) for k in wnames}
    in_maps = []
    for c in range(N_CORES):
        m = {"x": x[c * NB:(c + 1) * NB]}
        m.update(ws)
        in_maps.append(m)
    res = run_bass_kernel_spmd(nc, in_maps, core_ids=list(range(N_CORES)))
    return np.concatenate([r["y"] for r in res.results], axis=0)
```

```python
from contextlib import ExitStack
import numpy as np
import concourse.bass as bass
import concourse.mybir as mybir
from concourse.bass_utils import run_bass_kernel_spmd

F32 = mybir.dt.float32
BF16 = mybir.dt.bfloat16
AF = mybir.ActivationFunctionType
ALU = mybir.AluOpType

D = 1024
H = 16
DH = 64
FF = 2816
NCH = FF // 128
EPS = 1e-6
N_CORES = 8


class Res:
    __slots__ = ("name", "lw", "rd", "sem", "semcnt")

    def __init__(self, name, sem=None):
        self.name = name
        self.lw = None
        self.rd = {}
        self.sem = sem
        self.semcnt = 0


class Prog:
    ENGS = ("pe", "act", "dve", "pool", "sp")

    def __init__(self, nc, ctx):
        self.nc = nc
        self.ctx = ctx
        self.streams = {e: [] for e in self.ENGS}
        self.count = {e: 0 for e in self.ENGS}
        self.waited = {e: {} for e in self.ENGS}
        self.esem = {e: ctx.enter_context(nc.semaphore("es_" + e)) for e in self.ENGS}
        self.nres = 0

    def res(self, name=None, dma=False):
        self.nres += 1
        name = name or ("r%d" % self.nres)
        sem = self.ctx.enter_context(self.nc.semaphore("ds%d" % self.nres)) if dma else None
        return Res(name, sem)

    def _deps(self, reads, writes):
        deps = []
        for r in reads:
            if r.lw is not None:
                deps.append(r.lw)
        for w in writes:
            if w.lw is not None:
                deps.append(w.lw)
            deps.extend(w.rd.items())
        return deps

    def _waits_for(self, eng, deps):
        best = {}
        for (key, val) in deps:
            if key == "pe" and eng == "pe":
                continue
            if val > best.get(key, 0):
                best[key] = val
        out = []
        wd = self.waited[eng]
        for key, val in best.items():
            if wd.get(key, 0) >= val:
                continue
            wd[key] = val
            out.append((key, val))
        return out

    def _sem_of(self, key):
        if isinstance(key, str):
            return self.esem[key]
        return key.sem

    def _record(self, ev, reads, writes):
        k, v = ev
        for r in reads:
            if r.rd.get(k, 0) < v:
                r.rd[k] = v
        for w in writes:
            w.lw = ev
            w.rd = {}

    def op(self, eng, fn, reads=(), writes=()):
        waits = self._waits_for(eng, self._deps(reads, writes))
        self.count[eng] += 1
        ev = (eng, self.count[eng])
        self.streams[eng].append((waits, fn, None))
        self._record(ev, reads, writes)
        return ev

    def dma(self, eng, fn, semres, reads=(), writes=()):
        waits = self._waits_for(eng, self._deps(reads, writes))
        semres.semcnt += 1
        ev = (semres, 16 * semres.semcnt)
        self.streams[eng].append((waits, fn, semres))
        self._record(ev, reads, writes)
        return ev

    def alias(self, olds, news):
        evs = {}
        for o in olds:
            if o.lw is not None:
                k, v = o.lw
                evs[k] = max(evs.get(k, 0), v)
            for k, v in o.rd.items():
                evs[k] = max(evs.get(k, 0), v)
        for n in news:
            for k, v in evs.items():
                if n.rd.get(k, 0) < v:
                    n.rd[k] = v

    def wait_all(self, eng, resources):
        deps = []
        for r in resources:
            if r.lw is not None:
                deps.append(r.lw)
            deps.extend(r.rd.items())
        waits = self._waits_for(eng, deps)
        self.streams[eng].append((waits, None, None))

    def emit(self):
        nc = self.nc
        with nc.Block() as block:
            def run(engname):
                def body(e):
                    esem = self.esem[engname]
                    for (waits, fn, semres) in self.streams[engname]:
                        for (key, val) in waits:
                            e.wait_ge(self._sem_of(key), val)
                        if fn is None:
                            continue
                        ins = fn(e)
                        if semres is None:
                            ins.then_inc(esem, 1)
                        else:
                            ins.then_inc(semres.sem, 16)
                return body
            block.tensor(run("pe"))
            block.scalar(run("act"))
            block.vector(run("dve"))
            block.gpsimd(run("pool"))
            block.sync(run("sp"))


def build_nc(NB, S):
    NT = S // 128
    NG = S // 512
    nc = bass.Bass("TRN2", target_bir_lowering=False)
    dt_in = lambda name, shape: nc.dram_tensor(name, list(shape), F32, kind="ExternalInput").ap()
    x = dt_in("x", [NB, S, D])
    sb_pre_g = dt_in("sb_pre_g", [1, D])
    sb_w_qkv = dt_in("sb_w_qkv", [1, D, 3 * D])
    sb_w_o = dt_in("sb_w_o", [1, D, D])
    sb_post_g = dt_in("sb_post_g", [1, D])
    kv_norm_g = dt_in("kv_norm_g", [D])
    w_kvf = dt_in("w_kvf", [D, 2 * D + H])
    b_f = dt_in("b_f", [H])
    fox_pre_g = dt_in("fox_pre_g", [1, D])
    fox_w_q = dt_in("fox_w_q", [1, D, D])
    fox_w_o = dt_in("fox_w_o", [1, D, D])
    fox_post_g = dt_in("fox_post_g", [1, D])
    ffn_pre_g = dt_in("ffn_pre_g", [2, D])
    w_up = dt_in("w_up", [2, D, 2 * FF])
    conv_w = dt_in("conv_w", [2, 3, 2 * FF])
    conv_b = dt_in("conv_b", [2, 2 * FF])
    w_down = dt_in("w_down", [2, FF, D])
    ffn_post_g = dt_in("ffn_post_g", [2, D])
    y = nc.dram_tensor("y", [NB, S, D], F32, kind="ExternalOutput").ap()

    ws_qkv = nc.dram_tensor("ws_qkv", [5, 8, 128, 1024], BF16).ap()
    ws_kv = nc.dram_tensor("ws_kv", [2, 8, 128, 1024], BF16).ap()
    ws_f = nc.dram_tensor("ws_f", [128, 8 * 16], BF16).ap()
    ws_o = nc.dram_tensor("ws_o", [2, 128, 8192], BF16).ap()
    ws_up = nc.dram_tensor("ws_up", [2, NCH, 128, 2048], BF16).ap()
    ws_dn = nc.dram_tensor("ws_dn", [2, NCH, 128, 1024], BF16).ap()

    with ExitStack() as ctx:
        P = Prog(nc, ctx)
        sbt = lambda name, shape, dt: ctx.enter_context(nc.sbuf_tensor(name, list(shape), dt))
        pst = lambda name, shape, dt: ctx.enter_context(nc.psum_tensor(name, list(shape), dt))

        Hbuf = sbt("Hbuf", [128, NT * D], F32)
        hview = Hbuf[:].rearrange("p (t d) -> p t d", d=D)
        r_h = [P.res("h%d" % t, dma=True) for t in range(NT)]
        R1N = max(8 * S, 16384)
        R2N = max(8 * S, 16384)
        R3N = max(4 * S + NT * 256, 8192)
        R1 = sbt("R1", [128, R1N], BF16)
        R2 = sbt("R2", [128, R2N], BF16)
        R3 = sbt("R3", [128, R3N], BF16)
        r_R1 = P.res("R1")
        r_R2 = P.res("R2")
        aT = R1[:, 0:8 * S].rearrange("p (k s) -> p k s", s=S)
        bT = R2[:, 0:8 * S].rearrange("p (k s) -> p k s", s=S)
        qa = [R3[:, 0:S], R3[:, S:2 * S]]
        ka = [R3[:, 2 * S:3 * S], R3[:, 3 * S:4 * S]]
        vpad = R3[:, 4 * S:4 * S + NT * 256].rearrange("p (t h c) -> p t h c", h=2, c=128)
        r_qa = [P.res("qa0", dma=True), P.res("qa1", dma=True)]
        r_ka = [P.res("ka0", dma=True), P.res("ka1", dma=True)]
        r_vpad = P.res("vpad")
        r_qrow = [[P.res("qrow%d%d" % (i, j), dma=True) for j in range(2)] for i in range(2)]
        r_krow = [[P.res("krow%d%d" % (i, j), dma=True) for j in range(2)] for i in range(2)]
        r_attn_all = r_qa + r_ka + [r_vpad] + r_qrow[0] + r_qrow[1] + r_krow[0] + r_krow[1]
        WoT = R3[:, 0:8192].rearrange("p (k n) -> p k n", n=D)
        r_Wo = P.res("Wo", dma=True)
        gT = R2[:, 0:NCH * 512].rearrange("p (c t) -> p c t", t=512)
        r_gT = [P.res("gT%d" % c) for c in range(NCH)]
        wupb = [R2[:, NCH * 512 + i * 2048: NCH * 512 + (i + 1) * 2048].rearrange("p (g k j) -> p g k j", g=2, k=8) for i in range(2)]
        r_wup = [P.res("wup%d" % i, dma=True) for i in range(2)]
        NWD = 8
        wdnb = [R3[:, i * 1024:(i + 1) * 1024] for i in range(NWD)]
        r_wdn = [P.res("wdn%d" % i, dma=True) for i in range(NWD)]

        Wq = sbt("Wq", [128, 8, 128], BF16); r_Wq = P.res("Wq", dma=True)
        Wk = sbt("Wk", [128, 8, 128], BF16); r_Wk = P.res("Wk", dma=True)
        Wv = sbt("Wv", [128, 8, 128], BF16); r_Wv = P.res("Wv", dma=True)
        Wf = sbt("Wf", [128, 8, 16], BF16); r_Wf = P.res("Wf", dma=True)
        e32 = [sbt("e32_%d" % i, [128, 516], F32) for i in range(4)]; r_e32 = [P.res() for _ in range(4)]
        spb = [sbt("spb_%d" % i, [128, 512], BF16) for i in range(4)]; r_sp = [P.res() for _ in range(4)]
        E2 = [sbt("E2_%d" % i, [128, 512], F32) for i in range(2)]; r_E2 = [P.res() for _ in range(2)]
        Ab = [sbt("Ab_%d" % i, [128, 512], BF16) for i in range(3)]; r_A = [P.res() for _ in range(3)]
        Ls32 = [sbt("Ls32_%d" % i, [128, 512], F32) for i in range(1)]; r_Ls32 = [P.res() for _ in range(1)]
        rinv = sbt("rinv", [128, 512], F32); r_rinv = P.res()
        junk = sbt("junk", [128, 1024], BF16); r_junk = P.res()
        xn = sbt("xn", [128, 1024], BF16); r_xn = P.res()
        tmpf = sbt("tmpf", [128, 1024], F32); r_tmpA = P.res(); r_tmpB = P.res()
        Gpre = sbt("Gpre", [128, 1024], F32); r_Gpre = P.res("Gpre", dma=True)
        Gpost = Gpre; r_Gpost = r_Gpre
        st = sbt("stats", [128, 8], F32); r_st = P.res()
        pst_ = sbt("pstats", [128, 3 * 16], F32); r_pst = P.res()
        frowA = sbt("frowA", [80, S], BF16)
        frowB = sbt("frowB", [16, S], BF16)
        frow = [frowA[0:16, :], frowA[32:48, :], frowA[64:80, :], frowB[0:16, :]]
        r_frow = P.res()
        carry = sbt("carry", [16, 1], F32); r_carry = P.res()
        bfneg = sbt("bfneg", [16, 1], F32); r_bf = P.res("bf", dma=True)
        CW = [sbt("CW%d" % l, [128, 4, 2 * NCH], F32) for l in range(2)]; r_CW = [P.res("CW%d" % l, dma=True) for l in range(2)]
        hraw = e32; r_hraw = r_e32
        cacc = [E2[0][:, :], E2[1][:, :], tmpf[:, 0:512], tmpf[:, 512:1024]]; r_cacc = [r_E2[0], r_E2[1], r_tmpA, r_tmpB]
        gl = rinv; r_gl = r_rinv
        halo = sbt("halo", [128, 2 * NCH, 2], F32); r_halo = P.res()
        ident = sbt("ident", [128, 128], BF16); r_ident = P.res()
        mstrict = sbt("mstrict", [128, 128], BF16); r_mstrict = P.res()
        mincl = sbt("mincl", [128, 128], BF16); r_mincl = P.res()
        UI = sbt("UI", [128, 128], BF16); r_UI = P.res()
        Lc = sbt("Lc", [128, 128], BF16); r_Lc = P.res()
        PB = [pst("PB%d" % i, [128, 512], F32) for i in range(7)]; r_PB = [P.res("PB%d" % i) for i in range(7)]
        PT = pst("PT", [128, 1024], BF16); r_PT = P.res("PT")
        PBx = [PB[i][:] for i in range(7)] + [PT[:].bitcast(F32)]
        r_PBx = r_PB + [r_PT]

        def mm(out, lhsT, rhs, start, stop, reads, writes):
            P.op("pe", lambda e: e.matmul(out, lhsT=lhsT, rhs=rhs, start=start, stop=stop, skip_group_check=True), reads, writes)

        def act(out, in_, func, reads, writes, **kw):
            P.op("act", lambda e: e.activation(out=out, in_=in_, func=func, **kw), reads, writes)

        def tcopy(eng, out, in_, reads, writes):
            P.op(eng, lambda e: e.tensor_copy(out=out, in_=in_), reads, writes)

        def tt(eng, out, in0, in1, op, reads, writes):
            P.op(eng, lambda e: e.tensor_tensor(out=out, in0=in0, in1=in1, op=op), reads, writes)

        def ts(eng, out, in0, s1, s2, op0, op1, reads, writes):
            if s2 is None:
                P.op(eng, lambda e: e.tensor_scalar(out=out, in0=in0, scalar1=s1, scalar2=None, op0=op0), reads, writes)
            else:
                P.op(eng, lambda e: e.tensor_scalar(out=out, in0=in0, scalar1=s1, scalar2=s2, op0=op0, op1=op1), reads, writes)

        def stt(eng, out, in0, scalar, in1, op0, op1, reads, writes):
            P.op(eng, lambda e: e.scalar_tensor_tensor(out=out, in0=in0, scalar=scalar, in1=in1, op0=op0, op1=op1), reads, writes)

        def memset(eng, ap, val, writes):
            P.op(eng, lambda e: e.memset(ap, val), (), writes)

        def dma(out, in_, semres, reads, writes, slow=False):
            if slow:
                P.dma("sp", lambda e: e.dma_start(out=out, in_=in_, allow_slow_non_contiguous=True), semres, reads, writes)
            else:
                P.dma("sp", lambda e: e.dma_start(out=out, in_=in_), semres, reads, writes)

        def aff(ap, pattern, cmp, cm, writes):
            P.op("pool", lambda e: e.affine_select(out=ap, in_=ap, pattern=pattern, compare_op=cmp, fill=0.0, base=0, channel_multiplier=cm), writes, writes)
        for (t_, r_) in ((ident, r_ident), (mstrict, r_mstrict), (mincl, r_mincl), (UI, r_UI), (Lc, r_Lc)):
            memset("pool", t_[:], 1.0, [r_])
        aff(ident[:], [[-1, 128]], ALU.is_equal, 1, [r_ident])
        aff(mstrict[:], [[1, 128]], ALU.is_gt, -1, [r_mstrict])
        aff(mincl[:], [[1, 128]], ALU.is_ge, -1, [r_mincl])
        aff(UI[:], [[-1, 128]], ALU.is_ge, 1, [r_UI])
        aff(Lc[:], [[1, 128]], ALU.is_gt, -1, [r_Lc])
        memset("pool", halo[:], 0.0, [r_halo])
        for l in range(2):
            for k in range(3):
                dma(CW[l][:, k, :], conv_w[l, k].rearrange("(c p) -> p c", p=128), r_CW[l], [], [r_CW[l]], slow=True)
            dma(CW[l][:, 3, :], conv_b[l].rearrange("(c p) -> p c", p=128), r_CW[l], [], [r_CW[l]], slow=True)
        dma(bfneg[:], b_f.rearrange("(h o) -> h o", o=1), r_bf, [], [r_bf], slow=True)
        ts("dve", bfneg[:], bfneg[:], -1.0, None, ALU.mult, None, [r_bf], [r_bf])

        NSLOT = 4 if NT * D >= 4 * 4096 else 2
        if NSLOT == 4:
            stg32 = [Hbuf[:, i * 4096:(i + 1) * 4096] for i in range(4)]
        else:
            stg32 = [R1[:, i * 8192:(i + 1) * 8192].bitcast(F32) for i in range(2)]
        stg16 = [R2[:, i * 4096:(i + 1) * 4096] for i in range(NSLOT)]
        r_s32 = [P.res("s32_%d" % i, dma=True) for i in range(NSLOT)]
        r_s16 = [P.res("s16_%d" % i) for i in range(NSLOT)]
        r_s16d = [P.res("s16d_%d" % i, dma=True) for i in range(NSLOT)]
        cast_engs = ["dve", "pool", "act"]
        ucount = [0]

        def cast_unit(srcs, E, dst, outview=None):
            cast_list.append((srcs, E, dst, outview))

        cast_list = []

        def emit_casts():
            n = len(cast_list)
            LOOK = NSLOT - 1

            def emit_in(u):
                i = u % NSLOT
                for (vf, src) in cast_list[u][0]:
                    dma(vf(stg32[i]), src, r_s32[i], [], [r_s32[i]])

            def emit_cast_out(u):
                i = u % NSLOT
                srcs, E, dst, outview = cast_list[u]
                if u % 2 == 0:
                    act(stg16[i][:, 0:E], stg32[i][:, 0:E], AF.Copy, [r_s32[i]], [r_s16[i]])
                else:
                    tcopy("dve", stg16[i][:, 0:E], stg32[i][:, 0:E], [r_s32[i]], [r_s16[i]])
                src16 = stg16[i][:, 0:E] if outview is None else outview(stg16[i][:, 0:E])
                dma(dst, src16, r_s16d[i], [r_s16[i]], [r_s16d[i]])
            for u in range(n + LOOK):
                if u < n:
                    emit_in(u)
                if u - LOOK >= 0:
                    emit_cast_out(u - LOOK)

        def img4(stage, a, b, c):
            return stage[:, 0:a * b * c].rearrange("p (a b c) -> p a b c", a=a, b=b)

        def img3(stage, a, b):
            return stage[:, 0:a * b].rearrange("p (a b) -> p a b", a=a)

        def cast_cols(src2d, col0, dst_units):
            for half in range(2):
                srcs = []
                for hp4 in range(4):
                    hp = half * 4 + hp4
                    srcs.append((lambda s, hp4=hp4: img4(s, 4, 8, 128)[:, hp4],
                                 src2d[:, col0 + hp * 128: col0 + (hp + 1) * 128].rearrange("(k p) j -> p k j", p=128)))
                cast_unit(srcs, 4096, dst_units[half * 4:half * 4 + 4].rearrange("u p e -> p u e"),
                          outview=lambda v: v.rearrange("p (u e) -> p u e", u=4))
        cast_cols(sb_w_qkv[0], 0, ws_qkv[0])
        cast_cols(sb_w_qkv[0], D, ws_qkv[1])
        cast_cols(sb_w_qkv[0], 2 * D, ws_qkv[2])

        def cast_wo(src2d, dst):
            for half in range(2):
                srcs = [(lambda s: img3(s, 4, 1024),
                         src2d[half * 512:(half + 1) * 512, :].rearrange("(k p) n -> p k n", p=128))]
                cast_unit(srcs, 4096, dst[:, half * 4096:(half + 1) * 4096])
        cast_wo(sb_w_o[0], ws_o[0])

        def cast_ffn(l):
            for c0 in range(0, NCH, 2):
                srcs = []
                for gu in range(2):
                    for ci in range(2):
                        c = c0 + ci
                        srcs.append((lambda s, gu=gu, ci=ci: s[:, 0:4096].rearrange("p (c g k j) -> p c g k j", c=2, g=2, k=8)[:, ci, gu],
                                     w_up[l][:, gu * FF + c * 128: gu * FF + (c + 1) * 128].rearrange("(k p) j -> p k j", p=128)))
                cast_unit(srcs, 4096, ws_up[l, c0:c0 + 2].rearrange("c p e -> p c e"),
                          outview=lambda v: v.rearrange("p (c e) -> p c e", c=2))
            for c0 in range(0, NCH, 4):
                n = min(4, NCH - c0)
                srcs = [(lambda s, n=n: img3(s, n, 1024),
                         w_down[l][c0 * 128:(c0 + n) * 128, :].rearrange("(c p) n -> p c n", p=128))]
                cast_unit(srcs, n * 1024, ws_dn[l, c0:c0 + n].rearrange("c p e -> p c e"),
                          outview=lambda v, n=n: v.rearrange("p (c e) -> p c e", c=n))
        cast_ffn(0)
        cast_cols(w_kvf, 0, ws_kv[0])
        cast_cols(w_kvf, D, ws_kv[1])
        cast_unit([(lambda s: img3(s, 8, 16), w_kvf[:, 2 * D:2 * D + H].rearrange("(k p) j -> p k j", p=128))], 128, ws_f)
        cast_cols(fox_w_q[0], 0, ws_qkv[3])
        cast_wo(fox_w_o[0], ws_o[1])
        cast_ffn(1)
        emit_casts()
        r_scr = r_s16d
        P.alias(r_s32 + r_s16 + r_s16d, [r_R1, r_R2] + r_h)

        def prenorm(g_ap, dstT, r_dst, first_alias=None):
            dma(Gpre[:], g_ap.partition_broadcast(128), r_Gpre, [], [r_Gpre])
            for t in range(NT):
                act(junk[:], hview[:, t, :], AF.Square, [r_h[t]], [r_junk, r_pst], accum_out=pst_[:, t:t + 1])
            act(pst_[:, 16:16 + NT], pst_[:, 0:NT], AF.Ln, [r_pst], [r_pst], scale=1.0 / D, bias=EPS)
            act(pst_[:, 32:32 + NT], pst_[:, 16:16 + NT], AF.Exp, [r_pst], [r_pst], scale=-0.5)
            for t in range(NT):
                stt("dve", xn[:], hview[:, t, :], pst_[:, 32 + t:33 + t], Gpre[:], ALU.mult, ALU.mult, [r_h[t], r_pst, r_Gpre], [r_xn])
                for kc in range(8):
                    P.op("pe", lambda e, kc=kc: e.transpose(PT[:, kc * 128:(kc + 1) * 128], xn[:, kc * 128:(kc + 1) * 128], ident[:]),
                         [r_xn, r_ident], [r_PT])
                tcopy("dve", dstT[:, :, t * 128:(t + 1) * 128], PT[:].rearrange("p (k j) -> p k j", j=128), [r_PT], [r_dst])

        def postnorm_residual(t, ba, bb, g_res):
            act(junk[:, 0:512], PBx[ba], AF.Square, [r_PBx[ba]], [r_junk, r_st], accum_out=st[:, 3:4])
            act(junk[:, 512:1024], PBx[bb], AF.Square, [r_PBx[bb]], [r_junk, r_st], accum_out=st[:, 4:5])
            tt("dve", st[:, 5:6], st[:, 3:4], st[:, 4:5], ALU.add, [r_st], [r_st])
            act(st[:, 6:7], st[:, 5:6], AF.Ln, [r_st], [r_st], scale=1.0 / D, bias=EPS)
            act(st[:, 7:8], st[:, 6:7], AF.Exp, [r_st], [r_st], scale=-0.5)
            stt("dve", tmpf[:, 0:512], PBx[ba], st[:, 7:8], Gpost[:, 0:512], ALU.mult, ALU.mult, [r_PBx[ba], r_st, g_res], [r_tmpA])
            stt("dve", tmpf[:, 512:1024], PBx[bb], st[:, 7:8], Gpost[:, 512:1024], ALU.mult, ALU.mult, [r_PBx[bb], r_st, g_res], [r_tmpB])
            tt("pool", hview[:, t, :], hview[:, t, :], tmpf[:], ALU.add, [r_h[t], r_tmpA, r_tmpB], [r_h[t]])

        def out_proj(widx, g_ap, srcT, r_src):
            P.alias(r_attn_all, [r_Wo])
            dma(WoT[:, 0:4, :], ws_o[widx][:, 0:4096].rearrange("p (k n) -> p k n", n=D), r_Wo, r_scr, [r_Wo])
            dma(WoT[:, 4:8, :], ws_o[widx][:, 4096:8192].rearrange("p (k n) -> p k n", n=D), r_Wo, r_scr, [r_Wo])
            dma(Gpost[:], g_ap.partition_broadcast(128), r_Gpost, [], [r_Gpost])
            for t in range(NT):
                ba, bb = (0, 1) if t % 2 == 0 else (2, 3)
                for n, bk in ((0, ba), (1, bb)):
                    for kc in range(8):
                        mm(PB[bk][:], srcT[:, kc, t * 128:(t + 1) * 128], WoT[:, kc, n * 512:(n + 1) * 512],
                           kc == 0, kc == 7, [r_src, r_Wo], [r_PB[bk]])
                postnorm_residual(t, ba, bb, r_Gpost)
            P.alias([r_Wo], r_attn_all)

        def pipeline(n, stages, forward=False):
            ns = len(stages)
            for tick in range(n + ns - 1):
                for k in (range(ns) if forward else range(ns - 1, -1, -1)):
                    i = tick - k
                    if 0 <= i < n:
                        stages[k](i)

        def load_pair_weights(qsrc, ksrc, vsrc, hp):
            if qsrc is not None:
                dma(Wq[:], qsrc[hp].rearrange("p (k j) -> p k j", j=128), r_Wq, r_scr, [r_Wq])
            dma(Wk[:], ksrc[hp].rearrange("p (k j) -> p k j", j=128), r_Wk, r_scr, [r_Wk])
            dma(Wv[:], vsrc[hp].rearrange("p (k j) -> p k j", j=128), r_Wv, r_scr, [r_Wv])

        pbank = [0]

        def next_pbank():
            pbank[0] += 1
            return (6, 0, 1)[pbank[0] % 3]

        def proj_featmajor(Wt, r_W, srcT, r_src, bank, sink):
            for tg in range(NG):
                bank = next_pbank()
                for kc in range(8):
                    mm(PB[bank][:], Wt[:, kc, :], srcT[:, kc, tg * 512:(tg + 1) * 512], kc == 0, kc == 7, [r_W, r_src], [r_PB[bank]])
                sink(tg, bank)

        def proj_v(srcT, r_src, bank, padval):
            memset("pool", vpad[:, :, 0, 64:128], padval, [r_vpad])
            memset("pool", vpad[:, :, 1, 0:64], padval, [r_vpad])
            for t4 in range(0, NT, 4):
                bank = next_pbank()
                for ti in range(4):
                    t = t4 + ti
                    for kc in range(8):
                        mm(PB[bank][:, ti * 128:(ti + 1) * 128], srcT[:, kc, t * 128:(t + 1) * 128], Wv[:, kc, :], kc == 0, kc == 7,
                           [r_src, r_Wv], [r_PB[bank]])
                pv = PB[bank][:].rearrange("p (t c) -> p t c", c=128)
                tcopy("dve", vpad[:, t4:t4 + 4, 0, 0:64], pv[:, :, 0:64], [r_PB[bank]], [r_vpad])
                tcopy("dve", vpad[:, t4:t4 + 4, 1, 64:128], pv[:, :, 64:128], [r_PB[bank]], [r_vpad])

        def attn_sb_pair(hp, obank_base):
            units = []
            for g in range(NG):
                for sb in range(4 * g + 3, -1, -1):
                    for hd in range(2):
                        units.append((g, sb, hd))
            n = len(units)

            def geom(u):
                g, sb, hd = units[u]
                c0 = max(sb * 128, g * 512) - g * 512
                return g, sb, hd, c0, (sb >= 4 * g)

            def s_z(u):
                g, sb, hd, c0, diag = geom(u)
                zb = (0, 1, 6)[u % 3]
                mm(PB[zb][:, c0:512], ka[hd][0:128, sb * 128:(sb + 1) * 128], qa[hd][0:128, g * 512 + c0:(g + 1) * 512], True, True,
                   [r_ka[hd], r_qa[hd]], [r_PB[zb]])

            def s_e(u):
                g, sb, hd, c0, diag = geom(u)
                zb = (0, 1, 6)[u % 3]
                eb = u % 4
                act(e32[eb][:, c0:512], PB[zb][:, c0:512], AF.Exp, [r_PB[zb]], [r_e32[eb]])

            def s_esp(u):
                g, sb, hd, c0, diag = geom(u)
                zb = (0, 1, 6)[u % 3]
                eb = u % 4
                act(spb[eb][:, c0:512], e32[eb][:, c0:512], AF.Ln, [r_e32[eb]], [r_sp[eb]], bias=1.0)
                if diag:
                    tt("pool", spb[eb][:, c0:c0 + 128], spb[eb][:, c0:c0 + 128], mstrict[:], ALU.mult, [r_sp[eb], r_mstrict], [r_sp[eb]])

            def s_x(u):
                g, sb, hd, c0, diag = geom(u)
                eb = u % 4
                xb = 2 + hd
                mm(PB[xb][:, c0:512], UI[:], spb[eb][:, c0:512], sb == 4 * g + 3, False, [r_UI, r_sp[eb]], [r_PB[xb]])

            def s_a(u):
                g, sb, hd, c0, diag = geom(u)
                eb = u % 4
                xb = 2 + hd
                ab = u % 3
                act(E2[hd][:, c0:512], PB[xb][:, c0:512], AF.Exp, [r_PB[xb]], [r_E2[hd]], scale=-1.0)
                if sb > 0:
                    mm(PB[xb][:, c0:512], Lc[:], spb[eb][:, c0:512], False, False, [r_Lc, r_sp[eb]], [r_PB[xb]])
                tt("dve", Ab[ab][:, c0:512], e32[eb][:, c0:512], E2[hd][:, c0:512], ALU.mult, [r_e32[eb], r_E2[hd]], [r_A[ab]])
                if diag:
                    tt("pool", Ab[ab][:, c0:c0 + 128], Ab[ab][:, c0:c0 + 128], mstrict[:], ALU.mult, [r_A[ab], r_mstrict], [r_A[ab]])

            def s_pv(u):
                g, sb, hd, c0, diag = geom(u)
                ab = u % 3
                ob = obank_base + (g % 2)
                firstm = (sb == 4 * g + 3 and hd == 0)
                mm(PB[ob][:, c0:512], vpad[:, sb, hd, :], Ab[ab][:, c0:512], firstm, False, [r_vpad, r_A[ab]], [r_PB[ob]])
                if sb == 0 and hd == 1:
                    tcopy("dve", bT[:, hp, g * 512:(g + 1) * 512], PB[ob][:], [r_PB[ob]], [r_R2])

            pipeline(n, [s_z, s_e, s_esp, s_x, s_a, s_pv])

        def attn_fox_pair(hp):
            units = []
            for g in range(NG):
                for sb in range(4 * g + 3, -1, -1):
                    for hd in range(2):
                        units.append((g, sb, hd))
            n = len(units)

            def geom(u):
                g, sb, hd = units[u]
                c0 = max(sb * 128, g * 512) - g * 512
                return g, sb, hd, c0, (sb >= 4 * g)

            def obank(g, hd):
                return (4 + hd) if g % 2 == 0 else (2 + hd)

            def s_z(u):
                g, sb, hd, c0, diag = geom(u)
                zb = (0, 1, 6)[u % 3]
                mm(PB[zb][:, c0:512], ka[hd][0:128, sb * 128:(sb + 1) * 128], qa[hd][0:128, g * 512 + c0:(g + 1) * 512], True, True,
                   [r_ka[hd], r_qa[hd]] + r_qrow[hd] + r_krow[hd], [r_PB[zb]])

            def s_a(u):
                g, sb, hd, c0, diag = geom(u)
                zb = (0, 1, 6)[u % 3]
                ab = u % 3
                act(Ab[ab][:, c0:512], PB[zb][:, c0:512], AF.Exp, [r_PB[zb]], [r_A[ab]])
                if diag:
                    tt("pool", Ab[ab][:, c0:c0 + 128], Ab[ab][:, c0:c0 + 128], mincl[:], ALU.mult, [r_A[ab], r_mincl], [r_A[ab]])

            def s_pv(u):
                g, sb, hd, c0, diag = geom(u)
                ab = u % 3
                ob = obank(g, hd)
                mm(PB[ob][:, c0:512], vpad[:, sb, hd, :], Ab[ab][:, c0:512], sb == 4 * g + 3, False, [r_vpad, r_A[ab]], [r_PB[ob]])
                if sb == 0:
                    if hd == 0:
                        P.op("dve", lambda e: e.reciprocal(out=rinv[0:64, :], in_=PB[ob][64:128, :]), [r_PB[ob]], [r_rinv])
                        tt("dve", bT[0:64, hp, g * 512:(g + 1) * 512], PB[ob][0:64, :], rinv[0:64, :], ALU.mult, [r_PB[ob], r_rinv], [r_R2])
                    else:
                        P.op("dve", lambda e: e.reciprocal(out=rinv[64:128, :], in_=PB[ob][0:64, :]), [r_PB[ob]], [r_rinv])
                        tt("dve", bT[64:128, hp, g * 512:(g + 1) * 512], PB[ob][64:128, :], rinv[64:128, :], ALU.mult, [r_PB[ob], r_rinv], [r_R2])

            pipeline(n, [s_z, s_a, s_pv], forward=True)

        def ffn(l, after_tile=None):
            prenorm(ffn_pre_g[l], aT, r_R1)
            dma(Gpost[:], ffn_post_g[l].partition_broadcast(128), r_Gpost, [], [r_Gpost])
            P.alias([r_R2], r_gT + r_wup)
            P.alias(r_attn_all, r_wdn)
            LAG, PRE = 2, 6
            for tg in range(NG):
                def load_up(c):
                    dma(wupb[c % 2][:], ws_up[l, c].rearrange("p (g k j) -> p g k j", g=2, k=8), r_wup[c % 2], r_scr, [r_wup[c % 2]])
                steps = [(half, c) for half in range(2) for c in range(NCH)]
                loaded = [0]
                if tg % 2 == 0:
                    upb, hb0, hb1 = (4, 5), (0, 1, 2, 3), (4, 5, 6, 7)
                else:
                    upb, hb0, hb1 = (0, 1), (4, 5, 6, 7), (0, 1, 2, 3)

                def ensure_loaded(k):
                    while loaded[0] <= min(k, len(steps) - 1):
                        i = loaded[0] % NWD
                        dma(wdnb[i], ws_dn[l, steps[loaded[0]][1]], r_wdn[i], r_scr, [r_wdn[i]])
                        loaded[0] += 1

                def down_step(k):
                    ensure_loaded(k + PRE)
                    half, c = steps[k]
                    i = k % NWD
                    dbanks = hb0 if half == 0 else hb1
                    for t2 in range(2):
                        tl = 2 * half + t2
                        for nn in range(2):
                            bk = dbanks[2 * t2 + nn]
                            mm(PBx[bk], gT[:, c, tl * 128:(tl + 1) * 128], wdnb[i][:, nn * 512:(nn + 1) * 512], c == 0, c == NCH - 1,
                               [r_gT[c], r_wdn[i]], [r_PBx[bk]])
                    if c == NCH - 1:
                        for t2 in range(2):
                            postnorm_residual(tg * 4 + 2 * half + t2, dbanks[2 * t2], dbanks[2 * t2 + 1], r_Gpost)
                            if after_tile is not None:
                                after_tile(tg * 4 + 2 * half + t2)

                def gate_mul(cc):
                    cg, cu = 2 * (cc % 2), 2 * (cc % 2) + 1
                    act(gl[:], cacc[cg], AF.Gelu_apprx_tanh, [r_cacc[cg]], [r_gl])
                    tt("pool", gT[:, cc, :], gl[:], cacc[cu], ALU.mult, [r_gl, r_cacc[cu]], [r_gT[cc]])

                load_up(0)
                ensure_loaded(PRE - 1)
                kdown = 0
                for c in range(NCH):
                    if c + 1 < NCH:
                        load_up(c + 1)
                    for gu in range(2):
                        bk = upb[gu]
                        for kc in range(8):
                            mm(PB[bk][:], wupb[c % 2][:, gu, kc, :], aT[:, kc, tg * 512:(tg + 1) * 512], kc == 0, kc == 7,
                               [r_wup[c % 2], r_R1], [r_PB[bk]])
                    if c >= LAG:
                        down_step(kdown)
                        kdown += 1
                    for gu in range(2):
                        bk = upb[gu]
                        ci = gu * NCH + c
                        hb = 2 * (c % 2) + gu
                        hr, rh = hraw[hb], r_hraw[hb]
                        ca, rca = cacc[hb], r_cacc[hb]
                        if tg == 0:
                            memset("pool", hr[:, 2:4], 0.0, [rh])
                        else:
                            tcopy("pool", hr[:, 2:4], halo[:, ci, :], [r_halo], [rh])
                        act(hr[:, 4:516], PB[bk][:], AF.Copy, [r_PB[bk]], [rh])
                        tcopy("pool", halo[:, ci, :], hr[:, 514:516], [rh], [r_halo])
                        act(ca, PB[bk][:], AF.Identity, [r_PB[bk], r_CW[l]], [rca],
                            scale=CW[l][:, 2, ci:ci + 1], bias=CW[l][:, 3, ci:ci + 1])
                    for tap in (1, 0):
                        for gu in range(2):
                            ci = gu * NCH + c
                            hb = 2 * (c % 2) + gu
                            hr, rh = hraw[hb], r_hraw[hb]
                            ca, rca = cacc[hb], r_cacc[hb]
                            stt("dve", ca, hr[:, 2 + tap:514 + tap], CW[l][:, tap, ci:ci + 1], ca, ALU.mult, ALU.add, [rh, r_CW[l], rca], [rca])
                    if c > 0:
                        gate_mul(c - 1)
                    if c == NCH - 1:
                        gate_mul(c)
                while kdown < len(steps):
                    down_step(kdown)
                    kdown += 1
            P.alias(r_gT + r_wup, [r_R2])
            P.alias(r_wdn, r_attn_all)

        for t in range(NT):
            dma(hview[:, t, :], x[0, t * 128:(t + 1) * 128, :], r_h[t], [], [r_h[t]])
        for b in range(NB):
            prenorm(sb_pre_g[0], aT, r_R1)
            for hd in range(2):
                memset("pool", qa[hd][64:128, :], 0.0, [r_qa[hd]])
                memset("pool", ka[hd][64:128, :], 0.0, [r_ka[hd]])
            load_pair_weights(ws_qkv[0], ws_qkv[1], ws_qkv[2], 0)
            for hp in range(8):

                def sink_q(tg, bank):
                    for hd in range(2):
                        ts("dve", qa[hd][0:64, tg * 512:(tg + 1) * 512], PB[bank][hd * 64:(hd + 1) * 64, :], 0.125, None, ALU.mult, None,
                           [r_PB[bank]], [r_qa[hd]])

                def sink_k(tg, bank):
                    for hd in range(2):
                        tcopy("dve", ka[hd][0:64, tg * 512:(tg + 1) * 512], PB[bank][hd * 64:(hd + 1) * 64, :], [r_PB[bank]], [r_ka[hd]])
                proj_featmajor(Wq, r_Wq, aT, r_R1, 6, sink_q)
                proj_featmajor(Wk, r_Wk, aT, r_R1, 6, sink_k)
                proj_v(aT, r_R1, 6, 0.0)
                if hp + 1 < 8:
                    load_pair_weights(ws_qkv[0], ws_qkv[1], ws_qkv[2], hp + 1)
                attn_sb_pair(hp, 4)
            out_proj(0, sb_post_g[0], bT, r_R2)
            ffn(0)
            prenorm(fox_pre_g[0], aT, r_R1)
            for hp in range(8):
                dma(Wq[:], ws_qkv[3][hp].rearrange("p (k j) -> p k j", j=128), r_Wq, r_scr, [r_Wq])

                def sink_qall(tg, bank, hp=hp):
                    ts("dve", bT[:, hp, tg * 512:(tg + 1) * 512], PB[bank][:], 0.125, None, ALU.mult, None, [r_PB[bank]], [r_R2])
                proj_featmajor(Wq, r_Wq, aT, r_R1, 6, sink_qall)
            prenorm(kv_norm_g, aT, r_R1)
            dma(Wf[:], ws_f.rearrange("p (k j) -> p k j", j=16), r_Wf, r_scr, [r_Wf])
            memset("pool", E2[1][0:16, 0:512], 1.0, [r_E2[1]])
            for tg in range(NG):
                seg = slice(tg * 512, (tg + 1) * 512)
                for kc in range(8):
                    mm(PB[6][0:16, :], Wf[:, kc, :], aT[:, kc, seg], kc == 0, kc == 7, [r_Wf, r_R1], [r_PB[6]])
                act(e32[0][0:16, 0:512], PB[6][0:16, :], AF.Exp, [r_PB[6], r_bf], [r_e32[0]], scale=-1.0, bias=bfneg[:, 0:1])
                act(e32[1][0:16, 0:512], e32[0][0:16, 0:512], AF.Ln, [r_e32[0]], [r_e32[1]], bias=1.0)
                if tg == 0:
                    P.op("dve", lambda e: e.tensor_tensor_scan(out=E2[0][0:16, 0:512], data0=E2[1][0:16, 0:512], data1=e32[1][0:16, 0:512],
                                                               initial=0.0, op0=ALU.mult, op1=ALU.add),
                         [r_E2[1], r_e32[1]], [r_E2[0]])
                else:
                    P.op("dve", lambda e: e.tensor_tensor_scan(out=E2[0][0:16, 0:512], data0=E2[1][0:16, 0:512], data1=e32[1][0:16, 0:512],
                                                               initial=carry[:, 0:1], op0=ALU.mult, op1=ALU.add),
                         [r_E2[1], r_e32[1], r_carry], [r_E2[0]])
                tcopy("dve", carry[:, 0:1], E2[0][0:16, 511:512], [r_E2[0]], [r_carry])
                tcopy("dve", frow[0][:, seg], E2[0][0:16, 0:512], [r_E2[0]], [r_frow])
                tt("dve", Ls32[0][0:16, :], E2[0][0:16, 0:512], frow[0][:, seg], ALU.subtract, [r_E2[0], r_frow], [r_Ls32[0]])
                tcopy("dve", frow[1][:, seg], Ls32[0][0:16, :], [r_Ls32[0]], [r_frow])
                ts("dve", frow[2][:, seg], E2[0][0:16, 0:512], -1.0, None, ALU.mult, None, [r_E2[0]], [r_frow])
                ts("dve", frow[3][:, seg], Ls32[0][0:16, :], -1.0, None, ALU.mult, None, [r_Ls32[0]], [r_frow])
            for hd in range(2):
                memset("pool", qa[hd][64:128, :], 0.0, [r_qa[hd]])
                memset("pool", ka[hd][64:128, :], 0.0, [r_ka[hd]])
                memset("pool", qa[hd][64:68, :], 1.0, [r_qa[hd]])
                memset("pool", ka[hd][64:68, :], 1.0, [r_ka[hd]])
            load_pair_weights(None, ws_kv[0], ws_kv[1], 0)
            for hp in range(8):
                for hd in range(2):
                    hh = 2 * hp + hd
                    if hd == 0:
                        act(qa[hd][0:64, :], bT[0:64, hp, :], AF.Copy, [r_R2], [r_qa[hd]])
                    else:
                        tcopy("dve", qa[hd][0:64, :], bT[64:128, hp, :], [r_R2], [r_qa[hd]])
                    dma(qa[hd][64:65, :], frow[2][hh:hh + 1, :], r_qrow[hd][0], [r_frow, r_qa[hd]], [r_qrow[hd][0]])
                    dma(qa[hd][65:66, :], frow[3][hh:hh + 1, :], r_qrow[hd][1], [r_frow, r_qa[hd]], [r_qrow[hd][1]])
                    dma(ka[hd][66:67, :], frow[0][hh:hh + 1, :], r_krow[hd][0], [r_frow, r_ka[hd]], [r_krow[hd][0]])
                    dma(ka[hd][67:68, :], frow[1][hh:hh + 1, :], r_krow[hd][1], [r_frow, r_ka[hd]], [r_krow[hd][1]])

                def sink_k1(tg, bank):
                    for hd in range(2):
                        tcopy("dve", ka[hd][0:64, tg * 512:(tg + 1) * 512], PB[bank][hd * 64:(hd + 1) * 64, :], [r_PB[bank]], [r_ka[hd]])
                proj_featmajor(Wk, r_Wk, aT, r_R1, 6, sink_k1)
                proj_v(aT, r_R1, 6, 1.0)
                if hp + 1 < 8:
                    load_pair_weights(None, ws_kv[0], ws_kv[1], hp + 1)
                attn_fox_pair(hp)
            out_proj(1, fox_post_g[0], bT, r_R2)
            def stream_io(t, b=b):
                dma(y[b, t * 128:(t + 1) * 128, :], hview[:, t, :], r_h[t], [r_h[t]], [])
                if b + 1 < NB:
                    dma(hview[:, t, :], x[b + 1, t * 128:(t + 1) * 128, :], r_h[t], [], [r_h[t]])
            ffn(1, after_tile=stream_io)
        P.wait_all("sp", r_h)
        P.emit()
    return nc


_NC_CACHE = {}


def kernel(**inputs):
    x = np.ascontiguousarray(inputs["x"], dtype=np.float32)
    B, S, _ = x.shape
    NB = B // N_CORES
    key = (NB, S)
    if key not in _NC_CACHE:
        _NC_CACHE[key] = build_nc(NB, S)
    nc = _NC_CACHE[key]
    wnames = ["sb_pre_g", "sb_w_qkv", "sb_w_o", "sb_post_g", "kv_norm_g", "w_kvf", "b_f", "fox_pre_g", "fox_w_q",
              "fox_w_o", "fox_post_g", "ffn_pre_g", "w_up", "conv_w", "conv_b", "w_down", "ffn_post_g"]
    ws = {k: np.ascontiguousarray(inputs[k], dtype=np.float32) for k in wnames}
    in_maps = []
    for c in range(N_CORES):
        m = {"x": x[c * NB:(c + 1) * NB]}
        m.update(ws)
        in_maps.append(m)
    res = run_bass_kernel_spmd(nc, in_maps, core_ids=list(range(N_CORES)))
    return np.concatenate([r["y"] for r in res.results], axis=0)
```

```python
from contextlib import ExitStack
import numpy as np
import concourse.bass as bass
import concourse.mybir as mybir
from concourse.bass_utils import run_bass_kernel_spmd

F32 = mybir.dt.float32
BF16 = mybir.dt.bfloat16
AF = mybir.ActivationFunctionType
ALU = mybir.AluOpType

D = 1024
H = 16
DH = 64
FF = 2816
NCH = FF // 128
EPS = 1e-6
N_CORES = 8


class Res:
    __slots__ = ("name", "lw", "rd", "sem", "semcnt")

    def __init__(self, name, sem=None):
        self.name = name
        self.lw = None
        self.rd = {}
        self.sem = sem
        self.semcnt = 0


class Prog:
    ENGS = ("pe", "act", "dve", "pool", "sp")

    def __init__(self, nc, ctx):
        self.nc = nc
        self.ctx = ctx
        self.streams = {e: [] for e in self.ENGS}
        self.count = {e: 0 for e in self.ENGS}
        self.waited = {e: {} for e in self.ENGS}
        self.esem = {e: ctx.enter_context(nc.semaphore("es_" + e)) for e in self.ENGS}
        self.nres = 0

    def res(self, name=None, dma=False):
        self.nres += 1
        name = name or ("r%d" % self.nres)
        sem = self.ctx.enter_context(self.nc.semaphore("ds%d" % self.nres)) if dma else None
        return Res(name, sem)

    def _deps(self, reads, writes):
        deps = []
        for r in reads:
            if r.lw is not None:
                deps.append(r.lw)
        for w in writes:
            if w.lw is not None:
                deps.append(w.lw)
            deps.extend(w.rd.items())
        return deps

    def _waits_for(self, eng, deps):
        best = {}
        for (key, val) in deps:
            if key == "pe" and eng == "pe":
                continue
            if val > best.get(key, 0):
                best[key] = val
        out = []
        wd = self.waited[eng]
        for key, val in best.items():
            if wd.get(key, 0) >= val:
                continue
            wd[key] = val
            out.append((key, val))
        return out

    def _sem_of(self, key):
        if isinstance(key, str):
            return self.esem[key]
        return key.sem

    def _record(self, ev, reads, writes):
        k, v = ev
        for r in reads:
            if r.rd.get(k, 0) < v:
                r.rd[k] = v
        for w in writes:
            w.lw = ev
            w.rd = {}

    def op(self, eng, fn, reads=(), writes=()):
        waits = self._waits_for(eng, self._deps(reads, writes))
        self.count[eng] += 1
        ev = (eng, self.count[eng])
        self.streams[eng].append((waits, fn, None))
        self._record(ev, reads, writes)
        return ev

    def dma(self, eng, fn, semres, reads=(), writes=()):
        waits = self._waits_for(eng, self._deps(reads, writes))
        semres.semcnt += 1
        ev = (semres, 16 * semres.semcnt)
        self.streams[eng].append((waits, fn, semres))
        self._record(ev, reads, writes)
        return ev

    def alias(self, olds, news):
        evs = {}
        for o in olds:
            if o.lw is not None:
                k, v = o.lw
                evs[k] = max(evs.get(k, 0), v)
            for k, v in o.rd.items():
                evs[k] = max(evs.get(k, 0), v)
        for n in news:
            for k, v in evs.items():
                if n.rd.get(k, 0) < v:
                    n.rd[k] = v

    def wait_all(self, eng, resources):
        deps = []
        for r in resources:
            if r.lw is not None:
                deps.append(r.lw)
            deps.extend(r.rd.items())
        waits = self._waits_for(eng, deps)
        self.streams[eng].append((waits, None, None))

    def emit(self):
        nc = self.nc
        with nc.Block() as block:
            def run(engname):
                def body(e):
                    esem = self.esem[engname]
                    for (waits, fn, semres) in self.streams[engname]:
                        for (key, val) in waits:
                            e.wait_ge(self._sem_of(key), val)
                        if fn is None:
                            continue
                        ins = fn(e)
                        if semres is None:
                            ins.then_inc(esem, 1)
                        else:
                            ins.then_inc(semres.sem, 16)
                return body
            block.tensor(run("pe"))
            block.scalar(run("act"))
            block.vector(run("dve"))
            block.gpsimd(run("pool"))
            block.sync(run("sp"))


def build_nc(NB, S):
    NT = S // 128
    NG = S // 512
    nc = bass.Bass("TRN2", target_bir_lowering=False)
    dt_in = lambda name, shape: nc.dram_tensor(name, list(shape), F32, kind="ExternalInput").ap()
    x = dt_in("x", [NB, S, D])
    sb_pre_g = dt_in("sb_pre_g", [1, D])
    sb_w_qkv = dt_in("sb_w_qkv", [1, D, 3 * D])
    sb_w_o = dt_in("sb_w_o", [1, D, D])
    sb_post_g = dt_in("sb_post_g", [1, D])
    kv_norm_g = dt_in("kv_norm_g", [D])
    w_kvf = dt_in("w_kvf", [D, 2 * D + H])
    b_f = dt_in("b_f", [H])
    fox_pre_g = dt_in("fox_pre_g", [1, D])
    fox_w_q = dt_in("fox_w_q", [1, D, D])
    fox_w_o = dt_in("fox_w_o", [1, D, D])
    fox_post_g = dt_in("fox_post_g", [1, D])
    ffn_pre_g = dt_in("ffn_pre_g", [2, D])
    w_up = dt_in("w_up", [2, D, 2 * FF])
    conv_w = dt_in("conv_w", [2, 3, 2 * FF])
    conv_b = dt_in("conv_b", [2, 2 * FF])
    w_down = dt_in("w_down", [2, FF, D])
    ffn_post_g = dt_in("ffn_post_g", [2, D])
    y = nc.dram_tensor("y", [NB, S, D], F32, kind="ExternalOutput").ap()

    ws_qkv = nc.dram_tensor("ws_qkv", [5, 8, 128, 1024], BF16).ap()
    ws_kv = nc.dram_tensor("ws_kv", [2, 8, 128, 1024], BF16).ap()
    ws_f = nc.dram_tensor("ws_f", [128, 8 * 16], BF16).ap()
    ws_o = nc.dram_tensor("ws_o", [2, 128, 8192], BF16).ap()
    ws_up = nc.dram_tensor("ws_up", [2, NCH, 128, 2048], BF16).ap()
    ws_dn = nc.dram_tensor("ws_dn", [2, NCH, 128, 1024], BF16).ap()

    with ExitStack() as ctx:
        P = Prog(nc, ctx)
        sbt = lambda name, shape, dt: ctx.enter_context(nc.sbuf_tensor(name, list(shape), dt))
        pst = lambda name, shape, dt: ctx.enter_context(nc.psum_tensor(name, list(shape), dt))

        Hbuf = sbt("Hbuf", [128, NT * D], F32)
        hview = Hbuf[:].rearrange("p (t d) -> p t d", d=D)
        r_h = [P.res("h%d" % t, dma=True) for t in range(NT)]
        R1N = max(8 * S, 16384)
        R2N = max(8 * S, 16384)
        R3N = max(4 * S + NT * 256, 8192)
        R1 = sbt("R1", [128, R1N], BF16)
        R2 = sbt("R2", [128, R2N], BF16)
        R3 = sbt("R3", [128, R3N], BF16)
        r_R1 = P.res("R1")
        r_R2 = P.res("R2")
        aT = R1[:, 0:8 * S].rearrange("p (k s) -> p k s", s=S)
        bT = R2[:, 0:8 * S].rearrange("p (k s) -> p k s", s=S)
        qa = [R3[:, 0:S], R3[:, S:2 * S]]
        ka = [R3[:, 2 * S:3 * S], R3[:, 3 * S:4 * S]]
        vpad = R3[:, 4 * S:4 * S + NT * 256].rearrange("p (t h c) -> p t h c", h=2, c=128)
        r_qa = [P.res("qa0", dma=True), P.res("qa1", dma=True)]
        r_ka = [P.res("ka0", dma=True), P.res("ka1", dma=True)]
        r_vpad = P.res("vpad")
        r_qrow = [[P.res("qrow%d%d" % (i, j), dma=True) for j in range(2)] for i in range(2)]
        r_krow = [[P.res("krow%d%d" % (i, j), dma=True) for j in range(2)] for i in range(2)]
        r_attn_all = r_qa + r_ka + [r_vpad] + r_qrow[0] + r_qrow[1] + r_krow[0] + r_krow[1]
        WoT = R3[:, 0:8192].rearrange("p (k n) -> p k n", n=D)
        r_Wo = P.res("Wo", dma=True)
        gT = R2[:, 0:NCH * 512].rearrange("p (c t) -> p c t", t=512)
        r_gT = [P.res("gT%d" % c) for c in range(NCH)]
        wupb = [R2[:, NCH * 512 + i * 2048: NCH * 512 + (i + 1) * 2048].rearrange("p (g k j) -> p g k j", g=2, k=8) for i in range(2)]
        r_wup = [P.res("wup%d" % i, dma=True) for i in range(2)]
        NWD = min(12, R3N // 1024)
        wdnb = [R3[:, i * 1024:(i + 1) * 1024] for i in range(NWD)]
        r_wdn = [P.res("wdn%d" % i, dma=True) for i in range(NWD)]

        Wq = sbt("Wq", [128, 8, 128], BF16); r_Wq = P.res("Wq", dma=True)
        Wk = sbt("Wk", [128, 8, 128], BF16); r_Wk = P.res("Wk", dma=True)
        Wv = sbt("Wv", [128, 8, 128], BF16); r_Wv = P.res("Wv", dma=True)
        Wf = sbt("Wf", [128, 8, 16], BF16); r_Wf = P.res("Wf", dma=True)
        e32 = [sbt("e32_%d" % i, [128, 516], F32) for i in range(4)]; r_e32 = [P.res() for _ in range(4)]
        spb = [sbt("spb_%d" % i, [128, 512], BF16) for i in range(4)]; r_sp = [P.res() for _ in range(4)]
        E2 = [sbt("E2_%d" % i, [128, 512], F32) for i in range(2)]; r_E2 = [P.res() for _ in range(2)]
        Ab = [sbt("Ab_%d" % i, [128, 512], BF16) for i in range(3)]; r_A = [P.res() for _ in range(3)]
        Ls32 = [sbt("Ls32_%d" % i, [128, 512], F32) for i in range(1)]; r_Ls32 = [P.res() for _ in range(1)]
        rinv = sbt("rinv", [128, 512], F32); r_rinv = P.res()
        junk = sbt("junk", [128, 1024], BF16); r_junk = P.res()
        xn = sbt("xn", [128, 1024], BF16); r_xn = P.res()
        tmpf = sbt("tmpf", [128, 1024], F32); r_tmpA = P.res(); r_tmpB = P.res()
        Gpre = sbt("Gpre", [128, 1024], F32); r_Gpre = P.res("Gpre", dma=True)
        Gpost = Gpre; r_Gpost = r_Gpre
        st = sbt("stats", [128, 8], F32); r_st = P.res()
        pst_ = sbt("pstats", [128, 3 * 16], F32); r_pst = P.res()
        frowA = sbt("frowA", [80, S], BF16)
        frowB = sbt("frowB", [16, S], BF16)
        frow = [frowA[0:16, :], frowA[32:48, :], frowA[64:80, :], frowB[0:16, :]]
        r_frow = P.res()
        carry = sbt("carry", [16, 1], F32); r_carry = P.res()
        bfneg = sbt("bfneg", [16, 1], F32); r_bf = P.res("bf", dma=True)
        CW = [sbt("CW%d" % l, [128, 4, 2 * NCH], F32) for l in range(2)]; r_CW = [P.res("CW%d" % l, dma=True) for l in range(2)]
        hraw = e32; r_hraw = r_e32
        cacc = [E2[0][:, :], E2[1][:, :], tmpf[:, 0:512], tmpf[:, 512:1024]]; r_cacc = [r_E2[0], r_E2[1], r_tmpA, r_tmpB]
        gl = rinv; r_gl = r_rinv
        halo = sbt("halo", [128, 2 * NCH, 2], F32); r_halo = P.res()
        ident = sbt("ident", [128, 128], BF16); r_ident = P.res()
        mstrict = sbt("mstrict", [128, 128], BF16); r_mstrict = P.res()
        mincl = sbt("mincl", [128, 128], BF16); r_mincl = P.res()
        UI = sbt("UI", [128, 128], BF16); r_UI = P.res()
        Lc = sbt("Lc", [128, 128], BF16); r_Lc = P.res()
        PB = [pst("PB%d" % i, [128, 512], F32) for i in range(7)]; r_PB = [P.res("PB%d" % i) for i in range(7)]
        PT = pst("PT", [128, 1024], BF16); r_PT = P.res("PT")
        PBx = [PB[i][:] for i in range(7)] + [PT[:].bitcast(F32)]
        r_PBx = r_PB + [r_PT]

        def mm(out, lhsT, rhs, start, stop, reads, writes):
            P.op("pe", lambda e: e.matmul(out, lhsT=lhsT, rhs=rhs, start=start, stop=stop, skip_group_check=True), reads, writes)

        def act(out, in_, func, reads, writes, **kw):
            P.op("act", lambda e: e.activation(out=out, in_=in_, func=func, **kw), reads, writes)

        def tcopy(eng, out, in_, reads, writes):
            P.op(eng, lambda e: e.tensor_copy(out=out, in_=in_), reads, writes)

        def tt(eng, out, in0, in1, op, reads, writes):
            P.op(eng, lambda e: e.tensor_tensor(out=out, in0=in0, in1=in1, op=op), reads, writes)

        def ts(eng, out, in0, s1, s2, op0, op1, reads, writes):
            if s2 is None:
                P.op(eng, lambda e: e.tensor_scalar(out=out, in0=in0, scalar1=s1, scalar2=None, op0=op0), reads, writes)
            else:
                P.op(eng, lambda e: e.tensor_scalar(out=out, in0=in0, scalar1=s1, scalar2=s2, op0=op0, op1=op1), reads, writes)

        def stt(eng, out, in0, scalar, in1, op0, op1, reads, writes):
            P.op(eng, lambda e: e.scalar_tensor_tensor(out=out, in0=in0, scalar=scalar, in1=in1, op0=op0, op1=op1), reads, writes)

        def memset(eng, ap, val, writes):
            P.op(eng, lambda e: e.memset(ap, val), (), writes)

        def dma(out, in_, semres, reads, writes, slow=False):
            if slow:
                P.dma("sp", lambda e: e.dma_start(out=out, in_=in_, allow_slow_non_contiguous=True), semres, reads, writes)
            else:
                P.dma("sp", lambda e: e.dma_start(out=out, in_=in_), semres, reads, writes)

        def aff(ap, pattern, cmp, cm, writes):
            P.op("pool", lambda e: e.affine_select(out=ap, in_=ap, pattern=pattern, compare_op=cmp, fill=0.0, base=0, channel_multiplier=cm), writes, writes)
        for (t_, r_) in ((ident, r_ident), (mstrict, r_mstrict), (mincl, r_mincl), (UI, r_UI), (Lc, r_Lc)):
            memset("pool", t_[:], 1.0, [r_])
        aff(ident[:], [[-1, 128]], ALU.is_equal, 1, [r_ident])
        aff(mstrict[:], [[1, 128]], ALU.is_gt, -1, [r_mstrict])
        aff(mincl[:], [[1, 128]], ALU.is_ge, -1, [r_mincl])
        aff(UI[:], [[-1, 128]], ALU.is_ge, 1, [r_UI])
        aff(Lc[:], [[1, 128]], ALU.is_gt, -1, [r_Lc])
        memset("pool", halo[:], 0.0, [r_halo])
        for l in range(2):
            for k in range(3):
                dma(CW[l][:, k, :], conv_w[l, k].rearrange("(c p) -> p c", p=128), r_CW[l], [], [r_CW[l]], slow=True)
            dma(CW[l][:, 3, :], conv_b[l].rearrange("(c p) -> p c", p=128), r_CW[l], [], [r_CW[l]], slow=True)
        dma(bfneg[:], b_f.rearrange("(h o) -> h o", o=1), r_bf, [], [r_bf], slow=True)
        ts("dve", bfneg[:], bfneg[:], -1.0, None, ALU.mult, None, [r_bf], [r_bf])

        NSLOT = 4 if NT * D >= 4 * 4096 else 2
        if NSLOT == 4:
            stg32 = [Hbuf[:, i * 4096:(i + 1) * 4096] for i in range(4)]
        else:
            stg32 = [R1[:, i * 8192:(i + 1) * 8192].bitcast(F32) for i in range(2)]
        stg16 = [R2[:, i * 4096:(i + 1) * 4096] for i in range(NSLOT)]
        r_s32 = [P.res("s32_%d" % i, dma=True) for i in range(NSLOT)]
        r_s16 = [P.res("s16_%d" % i) for i in range(NSLOT)]
        r_s16d = [P.res("s16d_%d" % i, dma=True) for i in range(NSLOT)]
        cast_engs = ["dve", "pool", "act"]
        ucount = [0]

        def cast_unit(srcs, E, dst, outview=None):
            cast_list.append((srcs, E, dst, outview))

        cast_list = []

        def emit_casts():
            n = len(cast_list)
            LOOK = NSLOT - 1

            def emit_in(u):
                i = u % NSLOT
                for (vf, src) in cast_list[u][0]:
                    dma(vf(stg32[i]), src, r_s32[i], [], [r_s32[i]])

            def emit_cast_out(u):
                i = u % NSLOT
                srcs, E, dst, outview = cast_list[u]
                if u % 2 == 0:
                    act(stg16[i][:, 0:E], stg32[i][:, 0:E], AF.Copy, [r_s32[i]], [r_s16[i]])
                else:
                    tcopy("dve", stg16[i][:, 0:E], stg32[i][:, 0:E], [r_s32[i]], [r_s16[i]])
                src16 = stg16[i][:, 0:E] if outview is None else outview(stg16[i][:, 0:E])
                dma(dst, src16, r_s16d[i], [r_s16[i]], [r_s16d[i]])
            for u in range(n + LOOK):
                if u < n:
                    emit_in(u)
                if u - LOOK >= 0:
                    emit_cast_out(u - LOOK)

        def img4(stage, a, b, c):
            return stage[:, 0:a * b * c].rearrange("p (a b c) -> p a b c", a=a, b=b)

        def img3(stage, a, b):
            return stage[:, 0:a * b].rearrange("p (a b) -> p a b", a=a)

        def cast_cols(src2d, col0, dst_units):
            for half in range(2):
                srcs = []
                for hp4 in range(4):
                    hp = half * 4 + hp4
                    srcs.append((lambda s, hp4=hp4: img4(s, 4, 8, 128)[:, hp4],
                                 src2d[:, col0 + hp * 128: col0 + (hp + 1) * 128].rearrange("(k p) j -> p k j", p=128)))
                cast_unit(srcs, 4096, dst_units[half * 4:half * 4 + 4].rearrange("u p e -> p u e"),
                          outview=lambda v: v.rearrange("p (u e) -> p u e", u=4))
        cast_cols(sb_w_qkv[0], 0, ws_qkv[0])
        cast_cols(sb_w_qkv[0], D, ws_qkv[1])
        cast_cols(sb_w_qkv[0], 2 * D, ws_qkv[2])

        def cast_wo(src2d, dst):
            for half in range(2):
                srcs = [(lambda s: img3(s, 4, 1024),
                         src2d[half * 512:(half + 1) * 512, :].rearrange("(k p) n -> p k n", p=128))]
                cast_unit(srcs, 4096, dst[:, half * 4096:(half + 1) * 4096])
        cast_wo(sb_w_o[0], ws_o[0])

        def cast_ffn(l):
            for c0 in range(0, NCH, 2):
                srcs = []
                for gu in range(2):
                    for ci in range(2):
                        c = c0 + ci
                        srcs.append((lambda s, gu=gu, ci=ci: s[:, 0:4096].rearrange("p (c g k j) -> p c g k j", c=2, g=2, k=8)[:, ci, gu],
                                     w_up[l][:, gu * FF + c * 128: gu * FF + (c + 1) * 128].rearrange("(k p) j -> p k j", p=128)))
                cast_unit(srcs, 4096, ws_up[l, c0:c0 + 2].rearrange("c p e -> p c e"),
                          outview=lambda v: v.rearrange("p (c e) -> p c e", c=2))
            for c0 in range(0, NCH, 4):
                n = min(4, NCH - c0)
                srcs = [(lambda s, n=n: img3(s, n, 1024),
                         w_down[l][c0 * 128:(c0 + n) * 128, :].rearrange("(c p) n -> p c n", p=128))]
                cast_unit(srcs, n * 1024, ws_dn[l, c0:c0 + n].rearrange("c p e -> p c e"),
                          outview=lambda v, n=n: v.rearrange("p (c e) -> p c e", c=n))
        cast_ffn(0)
        cast_cols(w_kvf, 0, ws_kv[0])
        cast_cols(w_kvf, D, ws_kv[1])
        cast_unit([(lambda s: img3(s, 8, 16), w_kvf[:, 2 * D:2 * D + H].rearrange("(k p) j -> p k j", p=128))], 128, ws_f)
        cast_cols(fox_w_q[0], 0, ws_qkv[3])
        cast_wo(fox_w_o[0], ws_o[1])
        cast_ffn(1)
        emit_casts()
        r_scr = r_s16d
        P.alias(r_s32 + r_s16 + r_s16d, [r_R1, r_R2] + r_h)

        def prenorm(g_ap, dstT, r_dst, first_alias=None):
            dma(Gpre[:], g_ap.partition_broadcast(128), r_Gpre, [], [r_Gpre])
            for t in range(NT):
                act(junk[:], hview[:, t, :], AF.Square, [r_h[t]], [r_junk, r_pst], accum_out=pst_[:, t:t + 1])
            act(pst_[:, 16:16 + NT], pst_[:, 0:NT], AF.Ln, [r_pst], [r_pst], scale=1.0 / D, bias=EPS)
            act(pst_[:, 32:32 + NT], pst_[:, 16:16 + NT], AF.Exp, [r_pst], [r_pst], scale=-0.5)
            for t in range(NT):
                stt("dve", xn[:], hview[:, t, :], pst_[:, 32 + t:33 + t], Gpre[:], ALU.mult, ALU.mult, [r_h[t], r_pst, r_Gpre], [r_xn])
                for kc in range(8):
                    P.op("pe", lambda e, kc=kc: e.transpose(PT[:, kc * 128:(kc + 1) * 128], xn[:, kc * 128:(kc + 1) * 128], ident[:]),
                         [r_xn, r_ident], [r_PT])
                tcopy("dve", dstT[:, :, t * 128:(t + 1) * 128], PT[:].rearrange("p (k j) -> p k j", j=128), [r_PT], [r_dst])

        def postnorm_residual(t, ba, bb, g_res):
            act(junk[:, 0:512], PBx[ba], AF.Square, [r_PBx[ba]], [r_junk, r_st], accum_out=st[:, 3:4])
            act(junk[:, 512:1024], PBx[bb], AF.Square, [r_PBx[bb]], [r_junk, r_st], accum_out=st[:, 4:5])
            tt("dve", st[:, 5:6], st[:, 3:4], st[:, 4:5], ALU.add, [r_st], [r_st])
            act(st[:, 6:7], st[:, 5:6], AF.Ln, [r_st], [r_st], scale=1.0 / D, bias=EPS)
            act(st[:, 7:8], st[:, 6:7], AF.Exp, [r_st], [r_st], scale=-0.5)
            stt("dve", tmpf[:, 0:512], PBx[ba], st[:, 7:8], Gpost[:, 0:512], ALU.mult, ALU.mult, [r_PBx[ba], r_st, g_res], [r_tmpA])
            stt("dve", tmpf[:, 512:1024], PBx[bb], st[:, 7:8], Gpost[:, 512:1024], ALU.mult, ALU.mult, [r_PBx[bb], r_st, g_res], [r_tmpB])
            tt("pool", hview[:, t, :], hview[:, t, :], tmpf[:], ALU.add, [r_h[t], r_tmpA, r_tmpB], [r_h[t]])

        def out_proj(widx, g_ap, srcT, r_src):
            P.alias(r_attn_all, [r_Wo])
            dma(WoT[:, 0:4, :], ws_o[widx][:, 0:4096].rearrange("p (k n) -> p k n", n=D), r_Wo, r_scr, [r_Wo])
            dma(WoT[:, 4:8, :], ws_o[widx][:, 4096:8192].rearrange("p (k n) -> p k n", n=D), r_Wo, r_scr, [r_Wo])
            dma(Gpost[:], g_ap.partition_broadcast(128), r_Gpost, [], [r_Gpost])
            for t in range(NT):
                ba, bb = (0, 1) if t % 2 == 0 else (2, 3)
                for n, bk in ((0, ba), (1, bb)):
                    for kc in range(8):
                        mm(PB[bk][:], srcT[:, kc, t * 128:(t + 1) * 128], WoT[:, kc, n * 512:(n + 1) * 512],
                           kc == 0, kc == 7, [r_src, r_Wo], [r_PB[bk]])
                postnorm_residual(t, ba, bb, r_Gpost)
            P.alias([r_Wo], r_attn_all)

        def pipeline(n, stages, forward=False):
            ns = len(stages)
            for tick in range(n + ns - 1):
                for k in (range(ns) if forward else range(ns - 1, -1, -1)):
                    i = tick - k
                    if 0 <= i < n:
                        stages[k](i)

        def load_pair_weights(qsrc, ksrc, vsrc, hp):
            if qsrc is not None:
                dma(Wq[:], qsrc[hp].rearrange("p (k j) -> p k j", j=128), r_Wq, r_scr, [r_Wq])
            dma(Wk[:], ksrc[hp].rearrange("p (k j) -> p k j", j=128), r_Wk, r_scr, [r_Wk])
            dma(Wv[:], vsrc[hp].rearrange("p (k j) -> p k j", j=128), r_Wv, r_scr, [r_Wv])

        pbank = [0]

        def next_pbank():
            pbank[0] += 1
            return (6, 0, 1)[pbank[0] % 3]

        def proj_featmajor(Wt, r_W, srcT, r_src, bank, sink):
            for tg in range(NG):
                bank = next_pbank()
                for kc in range(8):
                    mm(PB[bank][:], Wt[:, kc, :], srcT[:, kc, tg * 512:(tg + 1) * 512], kc == 0, kc == 7, [r_W, r_src], [r_PB[bank]])
                sink(tg, bank)

        def proj_v(srcT, r_src, bank, padval):
            memset("pool", vpad[:, :, 0, 64:128], padval, [r_vpad])
            memset("pool", vpad[:, :, 1, 0:64], padval, [r_vpad])
            for t4 in range(0, NT, 4):
                bank = next_pbank()
                for ti in range(4):
                    t = t4 + ti
                    for kc in range(8):
                        mm(PB[bank][:, ti * 128:(ti + 1) * 128], srcT[:, kc, t * 128:(t + 1) * 128], Wv[:, kc, :], kc == 0, kc == 7,
                           [r_src, r_Wv], [r_PB[bank]])
                pv = PB[bank][:].rearrange("p (t c) -> p t c", c=128)
                tcopy("dve", vpad[:, t4:t4 + 4, 0, 0:64], pv[:, :, 0:64], [r_PB[bank]], [r_vpad])
                tcopy("dve", vpad[:, t4:t4 + 4, 1, 64:128], pv[:, :, 64:128], [r_PB[bank]], [r_vpad])

        def attn_sb_pair(hp, obank_base):
            units = []
            for g in range(NG):
                for sb in range(4 * g + 3, -1, -1):
                    for hd in range(2):
                        units.append((g, sb, hd))
            n = len(units)

            def geom(u):
                g, sb, hd = units[u]
                c0 = max(sb * 128, g * 512) - g * 512
                return g, sb, hd, c0, (sb >= 4 * g)

            def s_z(u):
                g, sb, hd, c0, diag = geom(u)
                zb = (0, 1, 6)[u % 3]
                mm(PB[zb][:, c0:512], ka[hd][0:128, sb * 128:(sb + 1) * 128], qa[hd][0:128, g * 512 + c0:(g + 1) * 512], True, True,
                   [r_ka[hd], r_qa[hd]], [r_PB[zb]])

            def s_e(u):
                g, sb, hd, c0, diag = geom(u)
                zb = (0, 1, 6)[u % 3]
                eb = u % 4
                act(e32[eb][:, c0:512], PB[zb][:, c0:512], AF.Exp, [r_PB[zb]], [r_e32[eb]])

            def s_esp(u):
                g, sb, hd, c0, diag = geom(u)
                zb = (0, 1, 6)[u % 3]
                eb = u % 4
                act(spb[eb][:, c0:512], e32[eb][:, c0:512], AF.Ln, [r_e32[eb]], [r_sp[eb]], bias=1.0)
                if diag:
                    tt("pool", spb[eb][:, c0:c0 + 128], spb[eb][:, c0:c0 + 128], mstrict[:], ALU.mult, [r_sp[eb], r_mstrict], [r_sp[eb]])

            def s_x(u):
                g, sb, hd, c0, diag = geom(u)
                eb = u % 4
                xb = 2 + hd
                mm(PB[xb][:, c0:512], UI[:], spb[eb][:, c0:512], sb == 4 * g + 3, False, [r_UI, r_sp[eb]], [r_PB[xb]])

            def s_a(u):
                g, sb, hd, c0, diag = geom(u)
                eb = u % 4
                xb = 2 + hd
                ab = u % 3
                act(E2[hd][:, c0:512], PB[xb][:, c0:512], AF.Exp, [r_PB[xb]], [r_E2[hd]], scale=-1.0)
                if sb > 0:
                    mm(PB[xb][:, c0:512], Lc[:], spb[eb][:, c0:512], False, False, [r_Lc, r_sp[eb]], [r_PB[xb]])
                tt("dve", Ab[ab][:, c0:512], e32[eb][:, c0:512], E2[hd][:, c0:512], ALU.mult, [r_e32[eb], r_E2[hd]], [r_A[ab]])
                if diag:
                    tt("pool", Ab[ab][:, c0:c0 + 128], Ab[ab][:, c0:c0 + 128], mstrict[:], ALU.mult, [r_A[ab], r_mstrict], [r_A[ab]])

            def s_pv(u):
                g, sb, hd, c0, diag = geom(u)
                ab = u % 3
                ob = obank_base + (g % 2)
                firstm = (sb == 4 * g + 3 and hd == 0)
                mm(PB[ob][:, c0:512], vpad[:, sb, hd, :], Ab[ab][:, c0:512], firstm, False, [r_vpad, r_A[ab]], [r_PB[ob]])
                if sb == 0 and hd == 1:
                    tcopy("dve", bT[:, hp, g * 512:(g + 1) * 512], PB[ob][:], [r_PB[ob]], [r_R2])

            pipeline(n, [s_z, s_e, s_esp, s_x, s_a, s_pv])

        def attn_fox_pair(hp):
            units = []
            for g in range(NG):
                for sb in range(4 * g + 3, -1, -1):
                    for hd in range(2):
                        units.append((g, sb, hd))
            n = len(units)

            def geom(u):
                g, sb, hd = units[u]
                c0 = max(sb * 128, g * 512) - g * 512
                return g, sb, hd, c0, (sb >= 4 * g)

            def obank(g, hd):
                return (4 + hd) if g % 2 == 0 else (2 + hd)

            def s_z(u):
                g, sb, hd, c0, diag = geom(u)
                zb = (0, 1, 6)[u % 3]
                mm(PB[zb][:, c0:512], ka[hd][0:128, sb * 128:(sb + 1) * 128], qa[hd][0:128, g * 512 + c0:(g + 1) * 512], True, True,
                   [r_ka[hd], r_qa[hd]] + r_qrow[hd] + r_krow[hd], [r_PB[zb]])

            def s_a(u):
                g, sb, hd, c0, diag = geom(u)
                zb = (0, 1, 6)[u % 3]
                ab = u % 3
                act(Ab[ab][:, c0:512], PB[zb][:, c0:512], AF.Exp, [r_PB[zb]], [r_A[ab]])
                if diag:
                    tt("pool", Ab[ab][:, c0:c0 + 128], Ab[ab][:, c0:c0 + 128], mincl[:], ALU.mult, [r_A[ab], r_mincl], [r_A[ab]])

            def s_pv(u):
                g, sb, hd, c0, diag = geom(u)
                ab = u % 3
                ob = obank(g, hd)
                mm(PB[ob][:, c0:512], vpad[:, sb, hd, :], Ab[ab][:, c0:512], sb == 4 * g + 3, False, [r_vpad, r_A[ab]], [r_PB[ob]])
                if sb == 0:
                    if hd == 0:
                        P.op("dve", lambda e: e.reciprocal(out=rinv[0:64, :], in_=PB[ob][64:128, :]), [r_PB[ob]], [r_rinv])
                        tt("dve", bT[0:64, hp, g * 512:(g + 1) * 512], PB[ob][0:64, :], rinv[0:64, :], ALU.mult, [r_PB[ob], r_rinv], [r_R2])
                    else:
                        P.op("dve", lambda e: e.reciprocal(out=rinv[64:128, :], in_=PB[ob][0:64, :]), [r_PB[ob]], [r_rinv])
                        tt("dve", bT[64:128, hp, g * 512:(g + 1) * 512], PB[ob][64:128, :], rinv[64:128, :], ALU.mult, [r_PB[ob], r_rinv], [r_R2])

            pipeline(n, [s_z, s_a, s_pv], forward=True)

        def ffn(l, after_tile=None):
            prenorm(ffn_pre_g[l], aT, r_R1)
            dma(Gpost[:], ffn_post_g[l].partition_broadcast(128), r_Gpost, [], [r_Gpost])
            P.alias([r_R2], r_gT + r_wup)
            P.alias(r_attn_all, r_wdn)
            LAG, PRE = 2, NWD - 2
            for tg in range(NG):
                def load_up(c):
                    dma(wupb[c % 2][:], ws_up[l, c].rearrange("p (g k j) -> p g k j", g=2, k=8), r_wup[c % 2], r_scr, [r_wup[c % 2]])
                steps = [(half, c) for half in range(2) for c in range(NCH)]
                loaded = [0]
                if tg % 2 == 0:
                    upb, hb0, hb1 = (4, 5), (0, 1, 2, 3), (4, 5, 6, 7)
                else:
                    upb, hb0, hb1 = (0, 1), (4, 5, 6, 7), (0, 1, 2, 3)

                def ensure_loaded(k):
                    while loaded[0] <= min(k, len(steps) - 1):
                        i = loaded[0] % NWD
                        dma(wdnb[i], ws_dn[l, steps[loaded[0]][1]], r_wdn[i], r_scr, [r_wdn[i]])
                        loaded[0] += 1

                def down_step(k):
                    ensure_loaded(k + PRE)
                    half, c = steps[k]
                    i = k % NWD
                    dbanks = hb0 if half == 0 else hb1
                    for t2 in range(2):
                        tl = 2 * half + t2
                        for nn in range(2):
                            bk = dbanks[2 * t2 + nn]
                            mm(PBx[bk], gT[:, c, tl * 128:(tl + 1) * 128], wdnb[i][:, nn * 512:(nn + 1) * 512], c == 0, c == NCH - 1,
                               [r_gT[c], r_wdn[i]], [r_PBx[bk]])
                    if c == NCH - 1:
                        for t2 in range(2):
                            postnorm_residual(tg * 4 + 2 * half + t2, dbanks[2 * t2], dbanks[2 * t2 + 1], r_Gpost)
                            if after_tile is not None:
                                after_tile(tg * 4 + 2 * half + t2)

                def gate_mul(cc):
                    cg, cu = 2 * (cc % 2), 2 * (cc % 2) + 1
                    act(gl[:], cacc[cg], AF.Gelu_apprx_tanh, [r_cacc[cg]], [r_gl])
                    tt("pool", gT[:, cc, :], gl[:], cacc[cu], ALU.mult, [r_gl, r_cacc[cu]], [r_gT[cc]])

                if tg == 0:
                    load_up(0)
                ensure_loaded(PRE - 1)
                kdown = 0
                for c in range(NCH):
                    if c + 1 < NCH:
                        load_up(c + 1)
                    for gu in range(2):
                        bk = upb[gu]
                        for kc in range(8):
                            mm(PB[bk][:], wupb[c % 2][:, gu, kc, :], aT[:, kc, tg * 512:(tg + 1) * 512], kc == 0, kc == 7,
                               [r_wup[c % 2], r_R1], [r_PB[bk]])
                    if c >= LAG:
                        down_step(kdown)
                        kdown += 1
                    for gu in range(2):
                        bk = upb[gu]
                        ci = gu * NCH + c
                        hb = 2 * (c % 2) + gu
                        hr, rh = hraw[hb], r_hraw[hb]
                        ca, rca = cacc[hb], r_cacc[hb]
                        if tg == 0:
                            memset("pool", hr[:, 2:4], 0.0, [rh])
                        else:
                            tcopy("pool", hr[:, 2:4], halo[:, ci, :], [r_halo], [rh])
                        act(hr[:, 4:516], PB[bk][:], AF.Copy, [r_PB[bk]], [rh])
                        tcopy("pool", halo[:, ci, :], hr[:, 514:516], [rh], [r_halo])
                        act(ca, PB[bk][:], AF.Identity, [r_PB[bk], r_CW[l]], [rca],
                            scale=CW[l][:, 2, ci:ci + 1], bias=CW[l][:, 3, ci:ci + 1])
                    for tap in (1, 0):
                        for gu in range(2):
                            ci = gu * NCH + c
                            hb = 2 * (c % 2) + gu
                            hr, rh = hraw[hb], r_hraw[hb]
                            ca, rca = cacc[hb], r_cacc[hb]
                            stt("dve", ca, hr[:, 2 + tap:514 + tap], CW[l][:, tap, ci:ci + 1], ca, ALU.mult, ALU.add, [rh, r_CW[l], rca], [rca])
                    if c > 0:
                        gate_mul(c - 1)
                    if c == NCH - 1:
                        gate_mul(c)
                if tg + 1 < NG:
                    load_up(0)
                while kdown < len(steps):
                    down_step(kdown)
                    kdown += 1
            P.alias(r_gT + r_wup, [r_R2])
            P.alias(r_wdn, r_attn_all)

        for t in range(NT):
            dma(hview[:, t, :], x[0, t * 128:(t + 1) * 128, :], r_h[t], [], [r_h[t]])
        for b in range(NB):
            prenorm(sb_pre_g[0], aT, r_R1)
            for hd in range(2):
                memset("pool", qa[hd][64:128, :], 0.0, [r_qa[hd]])
                memset("pool", ka[hd][64:128, :], 0.0, [r_ka[hd]])
            load_pair_weights(ws_qkv[0], ws_qkv[1], ws_qkv[2], 0)
            for hp in range(8):

                def sink_q(tg, bank):
                    for hd in range(2):
                        ts("dve", qa[hd][0:64, tg * 512:(tg + 1) * 512], PB[bank][hd * 64:(hd + 1) * 64, :], 0.125, None, ALU.mult, None,
                           [r_PB[bank]], [r_qa[hd]])

                def sink_k(tg, bank):
                    for hd in range(2):
                        tcopy("dve", ka[hd][0:64, tg * 512:(tg + 1) * 512], PB[bank][hd * 64:(hd + 1) * 64, :], [r_PB[bank]], [r_ka[hd]])
                proj_featmajor(Wq, r_Wq, aT, r_R1, 6, sink_q)
                proj_featmajor(Wk, r_Wk, aT, r_R1, 6, sink_k)
                proj_v(aT, r_R1, 6, 0.0)
                if hp + 1 < 8:
                    load_pair_weights(ws_qkv[0], ws_qkv[1], ws_qkv[2], hp + 1)
                attn_sb_pair(hp, 4)
            out_proj(0, sb_post_g[0], bT, r_R2)
            ffn(0)
            prenorm(fox_pre_g[0], aT, r_R1)
            for hp in range(8):
                dma(Wq[:], ws_qkv[3][hp].rearrange("p (k j) -> p k j", j=128), r_Wq, r_scr, [r_Wq])

                def sink_qall(tg, bank, hp=hp):
                    ts("dve", bT[:, hp, tg * 512:(tg + 1) * 512], PB[bank][:], 0.125, None, ALU.mult, None, [r_PB[bank]], [r_R2])
                proj_featmajor(Wq, r_Wq, aT, r_R1, 6, sink_qall)
            prenorm(kv_norm_g, aT, r_R1)
            dma(Wf[:], ws_f.rearrange("p (k j) -> p k j", j=16), r_Wf, r_scr, [r_Wf])
            memset("pool", E2[1][0:16, 0:512], 1.0, [r_E2[1]])
            for tg in range(NG):
                seg = slice(tg * 512, (tg + 1) * 512)
                for kc in range(8):
                    mm(PB[6][0:16, :], Wf[:, kc, :], aT[:, kc, seg], kc == 0, kc == 7, [r_Wf, r_R1], [r_PB[6]])
                act(e32[0][0:16, 0:512], PB[6][0:16, :], AF.Exp, [r_PB[6], r_bf], [r_e32[0]], scale=-1.0, bias=bfneg[:, 0:1])
                act(e32[1][0:16, 0:512], e32[0][0:16, 0:512], AF.Ln, [r_e32[0]], [r_e32[1]], bias=1.0)
                if tg == 0:
                    P.op("dve", lambda e: e.tensor_tensor_scan(out=E2[0][0:16, 0:512], data0=E2[1][0:16, 0:512], data1=e32[1][0:16, 0:512],
                                                               initial=0.0, op0=ALU.mult, op1=ALU.add),
                         [r_E2[1], r_e32[1]], [r_E2[0]])
                else:
                    P.op("dve", lambda e: e.tensor_tensor_scan(out=E2[0][0:16, 0:512], data0=E2[1][0:16, 0:512], data1=e32[1][0:16, 0:512],
                                                               initial=carry[:, 0:1], op0=ALU.mult, op1=ALU.add),
                         [r_E2[1], r_e32[1], r_carry], [r_E2[0]])
                tcopy("dve", carry[:, 0:1], E2[0][0:16, 511:512], [r_E2[0]], [r_carry])
                tcopy("dve", frow[0][:, seg], E2[0][0:16, 0:512], [r_E2[0]], [r_frow])
                tt("dve", Ls32[0][0:16, :], E2[0][0:16, 0:512], frow[0][:, seg], ALU.subtract, [r_E2[0], r_frow], [r_Ls32[0]])
                tcopy("dve", frow[1][:, seg], Ls32[0][0:16, :], [r_Ls32[0]], [r_frow])
                ts("dve", frow[2][:, seg], E2[0][0:16, 0:512], -1.0, None, ALU.mult, None, [r_E2[0]], [r_frow])
                ts("dve", frow[3][:, seg], Ls32[0][0:16, :], -1.0, None, ALU.mult, None, [r_Ls32[0]], [r_frow])
            for hd in range(2):
                memset("pool", qa[hd][64:128, :], 0.0, [r_qa[hd]])
                memset("pool", ka[hd][64:128, :], 0.0, [r_ka[hd]])
                memset("pool", qa[hd][64:68, :], 1.0, [r_qa[hd]])
                memset("pool", ka[hd][64:68, :], 1.0, [r_ka[hd]])
            load_pair_weights(None, ws_kv[0], ws_kv[1], 0)
            for hp in range(8):
                for hd in range(2):
                    hh = 2 * hp + hd
                    if hd == 0:
                        act(qa[hd][0:64, :], bT[0:64, hp, :], AF.Copy, [r_R2], [r_qa[hd]])
                    else:
                        tcopy("dve", qa[hd][0:64, :], bT[64:128, hp, :], [r_R2], [r_qa[hd]])
                    dma(qa[hd][64:65, :], frow[2][hh:hh + 1, :], r_qrow[hd][0], [r_frow, r_qa[hd]], [r_qrow[hd][0]])
                    dma(qa[hd][65:66, :], frow[3][hh:hh + 1, :], r_qrow[hd][1], [r_frow, r_qa[hd]], [r_qrow[hd][1]])
                    dma(ka[hd][66:67, :], frow[0][hh:hh + 1, :], r_krow[hd][0], [r_frow, r_ka[hd]], [r_krow[hd][0]])
                    dma(ka[hd][67:68, :], frow[1][hh:hh + 1, :], r_krow[hd][1], [r_frow, r_ka[hd]], [r_krow[hd][1]])

                def sink_k1(tg, bank):
                    for hd in range(2):
                        tcopy("dve", ka[hd][0:64, tg * 512:(tg + 1) * 512], PB[bank][hd * 64:(hd + 1) * 64, :], [r_PB[bank]], [r_ka[hd]])
                proj_featmajor(Wk, r_Wk, aT, r_R1, 6, sink_k1)
                proj_v(aT, r_R1, 6, 1.0)
                if hp + 1 < 8:
                    load_pair_weights(None, ws_kv[0], ws_kv[1], hp + 1)
                attn_fox_pair(hp)
            out_proj(1, fox_post_g[0], bT, r_R2)
            def stream_io(t, b=b):
                dma(y[b, t * 128:(t + 1) * 128, :], hview[:, t, :], r_h[t], [r_h[t]], [])
                if b + 1 < NB:
                    dma(hview[:, t, :], x[b + 1, t * 128:(t + 1) * 128, :], r_h[t], [], [r_h[t]])
            ffn(1, after_tile=stream_io)
        P.wait_all("sp", r_h)
        P.emit()
    return nc


_NC_CACHE = {}


def kernel(**inputs):
    x = np.ascontiguousarray(inputs["x"], dtype=np.float32)
    B, S, _ = x.shape
    NB = B // N_CORES
    key = (NB, S)
    if key not in _NC_CACHE:
        _NC_CACHE[key] = build_nc(NB, S)
    nc = _NC_CACHE[key]
    wnames = ["sb_pre_g", "sb_w_qkv", "sb_w_o", "sb_post_g", "kv_norm_g", "w_kvf", "b_f", "fox_pre_g", "fox_w_q",
              "fox_w_o", "fox_post_g", "ffn_pre_g", "w_up", "conv_w", "conv_b", "w_down", "ffn_post_g"]
    ws = {k: np.ascontiguousarray(inputs[k], dtype=np.float32) for k in wnames}
    in_maps = []
    for c in range(N_CORES):
        m = {"x": x[c * NB:(c + 1) * NB]}
        m.update(ws)
        in_maps.append(m)
    res = run_bass_kernel_spmd(nc, in_maps, core_ids=list(range(N_CORES)))
    return np.concatenate([r["y"] for r in res.results], axis=0)
```

```python
from contextlib import ExitStack
import numpy as np
import concourse.bass as bass
import concourse.mybir as mybir
from concourse.bass_utils import run_bass_kernel_spmd

F32 = mybir.dt.float32
BF16 = mybir.dt.bfloat16
AF = mybir.ActivationFunctionType
ALU = mybir.AluOpType

D = 1024
H = 16
DH = 64
FF = 2816
NCH = FF // 128
EPS = 1e-6
N_CORES = 8


class Res:
    __slots__ = ("name", "lw", "rd", "sem", "semcnt")

    def __init__(self, name, sem=None):
        self.name = name
        self.lw = None
        self.rd = {}
        self.sem = sem
        self.semcnt = 0


class Prog:
    ENGS = ("pe", "act", "dve", "pool", "sp")

    def __init__(self, nc, ctx):
        self.nc = nc
        self.ctx = ctx
        self.streams = {e: [] for e in self.ENGS}
        self.count = {e: 0 for e in self.ENGS}
        self.waited = {e: {} for e in self.ENGS}
        self.esem = {e: ctx.enter_context(nc.semaphore("es_" + e)) for e in self.ENGS}
        self.nres = 0

    def res(self, name=None, dma=False):
        self.nres += 1
        name = name or ("r%d" % self.nres)
        sem = self.ctx.enter_context(self.nc.semaphore("ds%d" % self.nres)) if dma else None
        return Res(name, sem)

    def _deps(self, reads, writes):
        deps = []
        for r in reads:
            if r.lw is not None:
                deps.append(r.lw)
        for w in writes:
            if w.lw is not None:
                deps.append(w.lw)
            deps.extend(w.rd.items())
        return deps

    def _waits_for(self, eng, deps):
        best = {}
        for (key, val) in deps:
            if key == "pe" and eng == "pe":
                continue
            if val > best.get(key, 0):
                best[key] = val
        out = []
        wd = self.waited[eng]
        for key, val in best.items():
            if wd.get(key, 0) >= val:
                continue
            wd[key] = val
            out.append((key, val))
        return out

    def _sem_of(self, key):
        if isinstance(key, str):
            return self.esem[key]
        return key.sem

    def _record(self, ev, reads, writes):
        k, v = ev
        for r in reads:
            if r.rd.get(k, 0) < v:
                r.rd[k] = v
        for w in writes:
            w.lw = ev
            w.rd = {}

    def op(self, eng, fn, reads=(), writes=()):
        waits = self._waits_for(eng, self._deps(reads, writes))
        self.count[eng] += 1
        ev = (eng, self.count[eng])
        self.streams[eng].append((waits, fn, None))
        self._record(ev, reads, writes)
        return ev

    def dma(self, eng, fn, semres, reads=(), writes=()):
        waits = self._waits_for(eng, self._deps(reads, writes))
        semres.semcnt += 1
        ev = (semres, 16 * semres.semcnt)
        self.streams[eng].append((waits, fn, semres))
        self._record(ev, reads, writes)
        return ev

    def alias(self, olds, news):
        evs = {}
        for o in olds:
            if o.lw is not None:
                k, v = o.lw
                evs[k] = max(evs.get(k, 0), v)
            for k, v in o.rd.items():
                evs[k] = max(evs.get(k, 0), v)
        for n in news:
            for k, v in evs.items():
                if n.rd.get(k, 0) < v:
                    n.rd[k] = v

    def wait_all(self, eng, resources):
        deps = []
        for r in resources:
            if r.lw is not None:
                deps.append(r.lw)
            deps.extend(r.rd.items())
        waits = self._waits_for(eng, deps)
        self.streams[eng].append((waits, None, None))

    def emit(self):
        nc = self.nc
        with nc.Block() as block:
            def run(engname):
                def body(e):
                    esem = self.esem[engname]
                    for (waits, fn, semres) in self.streams[engname]:
                        for (key, val) in waits:
                            e.wait_ge(self._sem_of(key), val)
                        if fn is None:
                            continue
                        ins = fn(e)
                        if semres is None:
                            ins.then_inc(esem, 1)
                        else:
                            ins.then_inc(semres.sem, 16)
                return body
            block.tensor(run("pe"))
            block.scalar(run("act"))
            block.vector(run("dve"))
            block.gpsimd(run("pool"))
            block.sync(run("sp"))


def build_nc(NB, S):
    NT = S // 128
    NG = S // 512
    nc = bass.Bass("TRN2", target_bir_lowering=False)
    dt_in = lambda name, shape: nc.dram_tensor(name, list(shape), F32, kind="ExternalInput").ap()
    x = dt_in("x", [NB, S, D])
    sb_pre_g = dt_in("sb_pre_g", [1, D])
    sb_w_qkv = dt_in("sb_w_qkv", [1, D, 3 * D])
    sb_w_o = dt_in("sb_w_o", [1, D, D])
    sb_post_g = dt_in("sb_post_g", [1, D])
    kv_norm_g = dt_in("kv_norm_g", [D])
    w_kvf = dt_in("w_kvf", [D, 2 * D + H])
    b_f = dt_in("b_f", [H])
    fox_pre_g = dt_in("fox_pre_g", [1, D])
    fox_w_q = dt_in("fox_w_q", [1, D, D])
    fox_w_o = dt_in("fox_w_o", [1, D, D])
    fox_post_g = dt_in("fox_post_g", [1, D])
    ffn_pre_g = dt_in("ffn_pre_g", [2, D])
    w_up = dt_in("w_up", [2, D, 2 * FF])
    conv_w = dt_in("conv_w", [2, 3, 2 * FF])
    conv_b = dt_in("conv_b", [2, 2 * FF])
    w_down = dt_in("w_down", [2, FF, D])
    ffn_post_g = dt_in("ffn_post_g", [2, D])
    y = nc.dram_tensor("y", [NB, S, D], F32, kind="ExternalOutput").ap()

    ws_qkv = nc.dram_tensor("ws_qkv", [5, 8, 128, 1024], BF16).ap()
    ws_kv = nc.dram_tensor("ws_kv", [2, 8, 128, 1024], BF16).ap()
    ws_f = nc.dram_tensor("ws_f", [128, 8 * 16], BF16).ap()
    ws_o = nc.dram_tensor("ws_o", [2, 128, 8192], BF16).ap()
    ws_up = nc.dram_tensor("ws_up", [2, NCH, 128, 2048], BF16).ap()
    ws_dn = nc.dram_tensor("ws_dn", [2, NCH, 128, 1024], BF16).ap()

    with ExitStack() as ctx:
        P = Prog(nc, ctx)
        sbt = lambda name, shape, dt: ctx.enter_context(nc.sbuf_tensor(name, list(shape), dt))
        pst = lambda name, shape, dt: ctx.enter_context(nc.psum_tensor(name, list(shape), dt))

        Hbuf = sbt("Hbuf", [128, NT * D], F32)
        hview = Hbuf[:].rearrange("p (t d) -> p t d", d=D)
        r_h = [P.res("h%d" % t, dma=True) for t in range(NT)]
        R1N = max(8 * S, 16384)
        R2N = max(8 * S, 16384)
        R3N = max(4 * S + NT * 256, 8192)
        R1 = sbt("R1", [128, R1N], BF16)
        R2 = sbt("R2", [128, R2N], BF16)
        R3 = sbt("R3", [128, R3N], BF16)
        r_R1 = P.res("R1")
        r_R2 = P.res("R2")
        aT = R1[:, 0:8 * S].rearrange("p (k s) -> p k s", s=S)
        bT = R2[:, 0:8 * S].rearrange("p (k s) -> p k s", s=S)
        qa = [R3[:, 0:S], R3[:, S:2 * S]]
        ka = [R3[:, 2 * S:3 * S], R3[:, 3 * S:4 * S]]
        vpad = R3[:, 4 * S:4 * S + NT * 256].rearrange("p (t h c) -> p t h c", h=2, c=128)
        r_qa = [P.res("qa0", dma=True), P.res("qa1", dma=True)]
        r_ka = [P.res("ka0", dma=True), P.res("ka1", dma=True)]
        r_vpad = P.res("vpad")
        r_qrow = [[P.res("qrow%d%d" % (i, j), dma=True) for j in range(2)] for i in range(2)]
        r_krow = [[P.res("krow%d%d" % (i, j), dma=True) for j in range(2)] for i in range(2)]
        r_attn_all = r_qa + r_ka + [r_vpad] + r_qrow[0] + r_qrow[1] + r_krow[0] + r_krow[1]
        WoT = R3[:, 0:8192].rearrange("p (k n) -> p k n", n=D)
        r_Wo = P.res("Wo", dma=True)
        gT = R2[:, 0:NCH * 512].rearrange("p (c t) -> p c t", t=512)
        r_gT = [P.res("gT%d" % c) for c in range(NCH)]
        wupb = [R2[:, NCH * 512 + i * 2048: NCH * 512 + (i + 1) * 2048].rearrange("p (g k j) -> p g k j", g=2, k=8) for i in range(2)]
        r_wup = [P.res("wup%d" % i, dma=True) for i in range(2)]
        NWD = min(12, R3N // 1024)
        wdnb = [R3[:, i * 1024:(i + 1) * 1024] for i in range(NWD)]
        r_wdn = [P.res("wdn%d" % i, dma=True) for i in range(NWD)]

        Wq = sbt("Wq", [128, 8, 128], BF16); r_Wq = P.res("Wq", dma=True)
        Wk = sbt("Wk", [128, 8, 128], BF16); r_Wk = P.res("Wk", dma=True)
        Wv = sbt("Wv", [128, 8, 128], BF16); r_Wv = P.res("Wv", dma=True)
        Wf = sbt("Wf", [128, 8, 16], BF16); r_Wf = P.res("Wf", dma=True)
        e32 = [sbt("e32_%d" % i, [128, 516], F32) for i in range(4)]; r_e32 = [P.res() for _ in range(4)]
        spb = [sbt("spb_%d" % i, [128, 512], BF16) for i in range(4)]; r_sp = [P.res() for _ in range(4)]
        E2 = [sbt("E2_%d" % i, [128, 512], F32) for i in range(2)]; r_E2 = [P.res() for _ in range(2)]
        Ab = [sbt("Ab_%d" % i, [128, 512], BF16) for i in range(3)]; r_A = [P.res() for _ in range(3)]
        Ls32 = [sbt("Ls32_%d" % i, [128, 512], F32) for i in range(1)]; r_Ls32 = [P.res() for _ in range(1)]
        rinv = sbt("rinv", [128, 512], F32); r_rinv = P.res()
        junk = sbt("junk", [128, 1024], BF16); r_junk = P.res()
        xn = sbt("xn", [128, 1024], BF16); r_xn = P.res()
        tmpf = sbt("tmpf", [128, 1024], F32); r_tmpA = P.res(); r_tmpB = P.res()
        Gpre = sbt("Gpre", [128, 1024], F32); r_Gpre = P.res("Gpre", dma=True)
        Gpost = Gpre; r_Gpost = r_Gpre
        st = sbt("stats", [128, 8], F32); r_st = P.res()
        pst_ = sbt("pstats", [128, 3 * 16], F32); r_pst = P.res()
        frowA = sbt("frowA", [80, S], BF16)
        frowB = sbt("frowB", [16, S], BF16)
        frow = [frowA[0:16, :], frowA[32:48, :], frowA[64:80, :], frowB[0:16, :]]
        r_frow = P.res()
        carry = sbt("carry", [16, 1], F32); r_carry = P.res()
        bfneg = sbt("bfneg", [16, 1], F32); r_bf = P.res("bf", dma=True)
        CW = [sbt("CW%d" % l, [128, 4, 2 * NCH], F32) for l in range(2)]; r_CW = [P.res("CW%d" % l, dma=True) for l in range(2)]
        hraw = e32; r_hraw = r_e32
        cacc = [E2[0][:, :], E2[1][:, :], tmpf[:, 0:512], tmpf[:, 512:1024]]; r_cacc = [r_E2[0], r_E2[1], r_tmpA, r_tmpB]
        gl = rinv; r_gl = r_rinv
        halo = sbt("halo", [128, 2 * NCH, 2], F32); r_halo = P.res()
        r_hh = [P.res() for _ in range(4)]
        r_haloc = [P.res() for _ in range(2 * NCH)]
        ident = sbt("ident", [128, 128], BF16); r_ident = P.res()
        mstrict = sbt("mstrict", [128, 128], BF16); r_mstrict = P.res()
        mincl = sbt("mincl", [128, 128], BF16); r_mincl = P.res()
        UI = sbt("UI", [128, 128], BF16); r_UI = P.res()
        Lc = sbt("Lc", [128, 128], BF16); r_Lc = P.res()
        PB = [pst("PB%d" % i, [128, 512], F32) for i in range(7)]; r_PB = [P.res("PB%d" % i) for i in range(7)]
        PT = pst("PT", [128, 1024], BF16); r_PT = P.res("PT")
        PBx = [PB[i][:] for i in range(7)] + [PT[:].bitcast(F32)]
        r_PBx = r_PB + [r_PT]

        def mm(out, lhsT, rhs, start, stop, reads, writes):
            P.op("pe", lambda e: e.matmul(out, lhsT=lhsT, rhs=rhs, start=start, stop=stop, skip_group_check=True), reads, writes)

        def act(out, in_, func, reads, writes, **kw):
            P.op("act", lambda e: e.activation(out=out, in_=in_, func=func, **kw), reads, writes)

        def tcopy(eng, out, in_, reads, writes):
            P.op(eng, lambda e: e.tensor_copy(out=out, in_=in_), reads, writes)

        def tt(eng, out, in0, in1, op, reads, writes):
            P.op(eng, lambda e: e.tensor_tensor(out=out, in0=in0, in1=in1, op=op), reads, writes)

        def ts(eng, out, in0, s1, s2, op0, op1, reads, writes):
            if s2 is None:
                P.op(eng, lambda e: e.tensor_scalar(out=out, in0=in0, scalar1=s1, scalar2=None, op0=op0), reads, writes)
            else:
                P.op(eng, lambda e: e.tensor_scalar(out=out, in0=in0, scalar1=s1, scalar2=s2, op0=op0, op1=op1), reads, writes)

        def stt(eng, out, in0, scalar, in1, op0, op1, reads, writes):
            P.op(eng, lambda e: e.scalar_tensor_tensor(out=out, in0=in0, scalar=scalar, in1=in1, op0=op0, op1=op1), reads, writes)

        def memset(eng, ap, val, writes):
            P.op(eng, lambda e: e.memset(ap, val), (), writes)

        def dma(out, in_, semres, reads, writes, slow=False):
            if slow:
                P.dma("sp", lambda e: e.dma_start(out=out, in_=in_, allow_slow_non_contiguous=True), semres, reads, writes)
            else:
                P.dma("sp", lambda e: e.dma_start(out=out, in_=in_), semres, reads, writes)

        def aff(ap, pattern, cmp, cm, writes):
            P.op("pool", lambda e: e.affine_select(out=ap, in_=ap, pattern=pattern, compare_op=cmp, fill=0.0, base=0, channel_multiplier=cm), writes, writes)
        for (t_, r_) in ((ident, r_ident), (mstrict, r_mstrict), (mincl, r_mincl), (UI, r_UI), (Lc, r_Lc)):
            memset("pool", t_[:], 1.0, [r_])
        aff(ident[:], [[-1, 128]], ALU.is_equal, 1, [r_ident])
        aff(mstrict[:], [[1, 128]], ALU.is_gt, -1, [r_mstrict])
        aff(mincl[:], [[1, 128]], ALU.is_ge, -1, [r_mincl])
        aff(UI[:], [[-1, 128]], ALU.is_ge, 1, [r_UI])
        aff(Lc[:], [[1, 128]], ALU.is_gt, -1, [r_Lc])
        memset("pool", halo[:], 0.0, [r_halo] + r_haloc)
        for l in range(2):
            for k in range(3):
                dma(CW[l][:, k, :], conv_w[l, k].rearrange("(c p) -> p c", p=128), r_CW[l], [], [r_CW[l]], slow=True)
            dma(CW[l][:, 3, :], conv_b[l].rearrange("(c p) -> p c", p=128), r_CW[l], [], [r_CW[l]], slow=True)
        dma(bfneg[:], b_f.rearrange("(h o) -> h o", o=1), r_bf, [], [r_bf], slow=True)
        ts("dve", bfneg[:], bfneg[:], -1.0, None, ALU.mult, None, [r_bf], [r_bf])

        NSLOT = 4 if NT * D >= 4 * 4096 else 2
        if NSLOT == 4:
            stg32 = [Hbuf[:, i * 4096:(i + 1) * 4096] for i in range(4)]
        else:
            stg32 = [R1[:, i * 8192:(i + 1) * 8192].bitcast(F32) for i in range(2)]
        stg16 = [R2[:, i * 4096:(i + 1) * 4096] for i in range(NSLOT)]
        r_s32 = [P.res("s32_%d" % i, dma=True) for i in range(NSLOT)]
        r_s16 = [P.res("s16_%d" % i) for i in range(NSLOT)]
        r_s16d = [P.res("s16d_%d" % i, dma=True) for i in range(NSLOT)]
        cast_engs = ["dve", "pool", "act"]
        ucount = [0]

        def cast_unit(srcs, E, dst, outview=None):
            cast_list.append((srcs, E, dst, outview))

        cast_list = []

        def emit_casts():
            n = len(cast_list)
            LOOK = NSLOT - 1

            def emit_in(u):
                i = u % NSLOT
                for (vf, src) in cast_list[u][0]:
                    dma(vf(stg32[i]), src, r_s32[i], [], [r_s32[i]])

            def emit_cast_out(u):
                i = u % NSLOT
                srcs, E, dst, outview = cast_list[u]
                if u % 2 == 0:
                    act(stg16[i][:, 0:E], stg32[i][:, 0:E], AF.Copy, [r_s32[i]], [r_s16[i]])
                else:
                    tcopy("dve", stg16[i][:, 0:E], stg32[i][:, 0:E], [r_s32[i]], [r_s16[i]])
                src16 = stg16[i][:, 0:E] if outview is None else outview(stg16[i][:, 0:E])
                dma(dst, src16, r_s16d[i], [r_s16[i]], [r_s16d[i]])
            for u in range(n + LOOK):
                if u < n:
                    emit_in(u)
                if u - LOOK >= 0:
                    emit_cast_out(u - LOOK)

        def img4(stage, a, b, c):
            return stage[:, 0:a * b * c].rearrange("p (a b c) -> p a b c", a=a, b=b)

        def img3(stage, a, b):
            return stage[:, 0:a * b].rearrange("p (a b) -> p a b", a=a)

        def cast_cols(src2d, col0, dst_units):
            for half in range(2):
                srcs = []
                for hp4 in range(4):
                    hp = half * 4 + hp4
                    srcs.append((lambda s, hp4=hp4: img4(s, 4, 8, 128)[:, hp4],
                                 src2d[:, col0 + hp * 128: col0 + (hp + 1) * 128].rearrange("(k p) j -> p k j", p=128)))
                cast_unit(srcs, 4096, dst_units[half * 4:half * 4 + 4].rearrange("u p e -> p u e"),
                          outview=lambda v: v.rearrange("p (u e) -> p u e", u=4))
        cast_cols(sb_w_qkv[0], 0, ws_qkv[0])
        cast_cols(sb_w_qkv[0], D, ws_qkv[1])
        cast_cols(sb_w_qkv[0], 2 * D, ws_qkv[2])

        def cast_wo(src2d, dst):
            for half in range(2):
                srcs = [(lambda s: img3(s, 4, 1024),
                         src2d[half * 512:(half + 1) * 512, :].rearrange("(k p) n -> p k n", p=128))]
                cast_unit(srcs, 4096, dst[:, half * 4096:(half + 1) * 4096])
        cast_wo(sb_w_o[0], ws_o[0])

        def cast_ffn(l):
            for c0 in range(0, NCH, 2):
                srcs = []
                for gu in range(2):
                    for ci in range(2):
                        c = c0 + ci
                        srcs.append((lambda s, gu=gu, ci=ci: s[:, 0:4096].rearrange("p (c g k j) -> p c g k j", c=2, g=2, k=8)[:, ci, gu],
                                     w_up[l][:, gu * FF + c * 128: gu * FF + (c + 1) * 128].rearrange("(k p) j -> p k j", p=128)))
                cast_unit(srcs, 4096, ws_up[l, c0:c0 + 2].rearrange("c p e -> p c e"),
                          outview=lambda v: v.rearrange("p (c e) -> p c e", c=2))
            for c0 in range(0, NCH, 4):
                n = min(4, NCH - c0)
                srcs = [(lambda s, n=n: img3(s, n, 1024),
                         w_down[l][c0 * 128:(c0 + n) * 128, :].rearrange("(c p) n -> p c n", p=128))]
                cast_unit(srcs, n * 1024, ws_dn[l, c0:c0 + n].rearrange("c p e -> p c e"),
                          outview=lambda v, n=n: v.rearrange("p (c e) -> p c e", c=n))
        cast_ffn(0)
        cast_cols(w_kvf, 0, ws_kv[0])
        cast_cols(w_kvf, D, ws_kv[1])
        cast_unit([(lambda s: img3(s, 8, 16), w_kvf[:, 2 * D:2 * D + H].rearrange("(k p) j -> p k j", p=128))], 128, ws_f)
        cast_cols(fox_w_q[0], 0, ws_qkv[3])
        cast_wo(fox_w_o[0], ws_o[1])
        cast_ffn(1)
        emit_casts()
        r_scr = r_s16d
        P.alias(r_s32 + r_s16 + r_s16d, [r_R1, r_R2] + r_h)

        def prenorm(g_ap, dstT, r_dst, first_alias=None):
            dma(Gpre[:], g_ap.partition_broadcast(128), r_Gpre, [], [r_Gpre])
            for t in range(NT):
                act(junk[:], hview[:, t, :], AF.Square, [r_h[t]], [r_junk, r_pst], accum_out=pst_[:, t:t + 1])
            act(pst_[:, 16:16 + NT], pst_[:, 0:NT], AF.Ln, [r_pst], [r_pst], scale=1.0 / D, bias=EPS)
            act(pst_[:, 32:32 + NT], pst_[:, 16:16 + NT], AF.Exp, [r_pst], [r_pst], scale=-0.5)
            for t in range(NT):
                stt("dve", xn[:], hview[:, t, :], pst_[:, 32 + t:33 + t], Gpre[:], ALU.mult, ALU.mult, [r_h[t], r_pst, r_Gpre], [r_xn])
                for kc in range(8):
                    P.op("pe", lambda e, kc=kc: e.transpose(PT[:, kc * 128:(kc + 1) * 128], xn[:, kc * 128:(kc + 1) * 128], ident[:]),
                         [r_xn, r_ident], [r_PT])
                tcopy("dve", dstT[:, :, t * 128:(t + 1) * 128], PT[:].rearrange("p (k j) -> p k j", j=128), [r_PT], [r_dst])

        def postnorm_residual(t, ba, bb, g_res):
            act(junk[:, 0:512], PBx[ba], AF.Square, [r_PBx[ba]], [r_junk, r_st], accum_out=st[:, 3:4])
            act(junk[:, 512:1024], PBx[bb], AF.Square, [r_PBx[bb]], [r_junk, r_st], accum_out=st[:, 4:5])
            tt("dve", st[:, 5:6], st[:, 3:4], st[:, 4:5], ALU.add, [r_st], [r_st])
            act(st[:, 6:7], st[:, 5:6], AF.Ln, [r_st], [r_st], scale=1.0 / D, bias=EPS)
            act(st[:, 7:8], st[:, 6:7], AF.Exp, [r_st], [r_st], scale=-0.5)
            stt("dve", tmpf[:, 0:512], PBx[ba], st[:, 7:8], Gpost[:, 0:512], ALU.mult, ALU.mult, [r_PBx[ba], r_st, g_res], [r_tmpA])
            stt("dve", tmpf[:, 512:1024], PBx[bb], st[:, 7:8], Gpost[:, 512:1024], ALU.mult, ALU.mult, [r_PBx[bb], r_st, g_res], [r_tmpB])
            tt("pool", hview[:, t, :], hview[:, t, :], tmpf[:], ALU.add, [r_h[t], r_tmpA, r_tmpB], [r_h[t]])

        def out_proj(widx, g_ap, srcT, r_src):
            P.alias(r_attn_all, [r_Wo])
            dma(WoT[:, 0:4, :], ws_o[widx][:, 0:4096].rearrange("p (k n) -> p k n", n=D), r_Wo, r_scr, [r_Wo])
            dma(WoT[:, 4:8, :], ws_o[widx][:, 4096:8192].rearrange("p (k n) -> p k n", n=D), r_Wo, r_scr, [r_Wo])
            dma(Gpost[:], g_ap.partition_broadcast(128), r_Gpost, [], [r_Gpost])
            for t in range(NT):
                ba, bb = (0, 1) if t % 2 == 0 else (2, 3)
                for n, bk in ((0, ba), (1, bb)):
                    for kc in range(8):
                        mm(PB[bk][:], srcT[:, kc, t * 128:(t + 1) * 128], WoT[:, kc, n * 512:(n + 1) * 512],
                           kc == 0, kc == 7, [r_src, r_Wo], [r_PB[bk]])
                postnorm_residual(t, ba, bb, r_Gpost)
            P.alias([r_Wo], r_attn_all)

        def pipeline(n, stages, forward=False):
            ns = len(stages)
            for tick in range(n + ns - 1):
                for k in (range(ns) if forward else range(ns - 1, -1, -1)):
                    i = tick - k
                    if 0 <= i < n:
                        stages[k](i)

        def load_pair_weights(qsrc, ksrc, vsrc, hp):
            if qsrc is not None:
                dma(Wq[:], qsrc[hp].rearrange("p (k j) -> p k j", j=128), r_Wq, r_scr, [r_Wq])
            dma(Wk[:], ksrc[hp].rearrange("p (k j) -> p k j", j=128), r_Wk, r_scr, [r_Wk])
            dma(Wv[:], vsrc[hp].rearrange("p (k j) -> p k j", j=128), r_Wv, r_scr, [r_Wv])

        pbank = [0]

        def next_pbank():
            pbank[0] += 1
            return (6, 0, 1)[pbank[0] % 3]

        def proj_featmajor(Wt, r_W, srcT, r_src, bank, sink):
            for tg in range(NG):
                bank = next_pbank()
                for kc in range(8):
                    mm(PB[bank][:], Wt[:, kc, :], srcT[:, kc, tg * 512:(tg + 1) * 512], kc == 0, kc == 7, [r_W, r_src], [r_PB[bank]])
                sink(tg, bank)

        def proj_v(srcT, r_src, bank, padval):
            memset("pool", vpad[:, :, 0, 64:128], padval, [r_vpad])
            memset("pool", vpad[:, :, 1, 0:64], padval, [r_vpad])
            for t4 in range(0, NT, 4):
                bank = next_pbank()
                for ti in range(4):
                    t = t4 + ti
                    for kc in range(8):
                        mm(PB[bank][:, ti * 128:(ti + 1) * 128], srcT[:, kc, t * 128:(t + 1) * 128], Wv[:, kc, :], kc == 0, kc == 7,
                           [r_src, r_Wv], [r_PB[bank]])
                pv = PB[bank][:].rearrange("p (t c) -> p t c", c=128)
                tcopy("dve", vpad[:, t4:t4 + 4, 0, 0:64], pv[:, :, 0:64], [r_PB[bank]], [r_vpad])
                tcopy("dve", vpad[:, t4:t4 + 4, 1, 64:128], pv[:, :, 64:128], [r_PB[bank]], [r_vpad])

        def attn_sb_pair(hp, obank_base):
            units = []
            for g in range(NG):
                for sb in range(4 * g + 3, -1, -1):
                    for hd in range(2):
                        units.append((g, sb, hd))
            n = len(units)

            def geom(u):
                g, sb, hd = units[u]
                c0 = max(sb * 128, g * 512) - g * 512
                return g, sb, hd, c0, (sb >= 4 * g)

            def s_z(u):
                g, sb, hd, c0, diag = geom(u)
                zb = (0, 1, 6)[u % 3]
                mm(PB[zb][:, c0:512], ka[hd][0:128, sb * 128:(sb + 1) * 128], qa[hd][0:128, g * 512 + c0:(g + 1) * 512], True, True,
                   [r_ka[hd], r_qa[hd]], [r_PB[zb]])

            def s_e(u):
                g, sb, hd, c0, diag = geom(u)
                zb = (0, 1, 6)[u % 3]
                eb = u % 4
                act(e32[eb][:, c0:512], PB[zb][:, c0:512], AF.Exp, [r_PB[zb]], [r_e32[eb]])

            def s_esp(u):
                g, sb, hd, c0, diag = geom(u)
                zb = (0, 1, 6)[u % 3]
                eb = u % 4
                act(spb[eb][:, c0:512], e32[eb][:, c0:512], AF.Ln, [r_e32[eb]], [r_sp[eb]], bias=1.0)
                if diag:
                    tt("pool", spb[eb][:, c0:c0 + 128], spb[eb][:, c0:c0 + 128], mstrict[:], ALU.mult, [r_sp[eb], r_mstrict], [r_sp[eb]])

            def s_x(u):
                g, sb, hd, c0, diag = geom(u)
                eb = u % 4
                xb = 2 + hd
                mm(PB[xb][:, c0:512], UI[:], spb[eb][:, c0:512], sb == 4 * g + 3, False, [r_UI, r_sp[eb]], [r_PB[xb]])

            def s_a(u):
                g, sb, hd, c0, diag = geom(u)
                eb = u % 4
                xb = 2 + hd
                ab = u % 3
                act(E2[hd][:, c0:512], PB[xb][:, c0:512], AF.Exp, [r_PB[xb]], [r_E2[hd]], scale=-1.0)
                if sb > 0:
                    mm(PB[xb][:, c0:512], Lc[:], spb[eb][:, c0:512], False, False, [r_Lc, r_sp[eb]], [r_PB[xb]])
                tt("dve", Ab[ab][:, c0:512], e32[eb][:, c0:512], E2[hd][:, c0:512], ALU.mult, [r_e32[eb], r_E2[hd]], [r_A[ab]])
                if diag:
                    tt("pool", Ab[ab][:, c0:c0 + 128], Ab[ab][:, c0:c0 + 128], mstrict[:], ALU.mult, [r_A[ab], r_mstrict], [r_A[ab]])

            def s_pv(u):
                g, sb, hd, c0, diag = geom(u)
                ab = u % 3
                ob = obank_base + (g % 2)
                firstm = (sb == 4 * g + 3 and hd == 0)
                mm(PB[ob][:, c0:512], vpad[:, sb, hd, :], Ab[ab][:, c0:512], firstm, False, [r_vpad, r_A[ab]], [r_PB[ob]])
                if sb == 0 and hd == 1:
                    tcopy("dve", bT[:, hp, g * 512:(g + 1) * 512], PB[ob][:], [r_PB[ob]], [r_R2])

            pipeline(n, [s_z, s_e, s_esp, s_x, s_a, s_pv])

        def attn_fox_pair(hp):
            units = []
            for g in range(NG):
                for sb in range(4 * g + 3, -1, -1):
                    for hd in range(2):
                        units.append((g, sb, hd))
            n = len(units)

            def geom(u):
                g, sb, hd = units[u]
                c0 = max(sb * 128, g * 512) - g * 512
                return g, sb, hd, c0, (sb >= 4 * g)

            def obank(g, hd):
                return (4 + hd) if g % 2 == 0 else (2 + hd)

            def s_z(u):
                g, sb, hd, c0, diag = geom(u)
                zb = (0, 1, 6)[u % 3]
                mm(PB[zb][:, c0:512], ka[hd][0:128, sb * 128:(sb + 1) * 128], qa[hd][0:128, g * 512 + c0:(g + 1) * 512], True, True,
                   [r_ka[hd], r_qa[hd]] + r_qrow[hd] + r_krow[hd], [r_PB[zb]])

            def s_a(u):
                g, sb, hd, c0, diag = geom(u)
                zb = (0, 1, 6)[u % 3]
                ab = u % 3
                act(Ab[ab][:, c0:512], PB[zb][:, c0:512], AF.Exp, [r_PB[zb]], [r_A[ab]])
                if diag:
                    tt("pool", Ab[ab][:, c0:c0 + 128], Ab[ab][:, c0:c0 + 128], mincl[:], ALU.mult, [r_A[ab], r_mincl], [r_A[ab]])

            def s_pv(u):
                g, sb, hd, c0, diag = geom(u)
                ab = u % 3
                ob = obank(g, hd)
                mm(PB[ob][:, c0:512], vpad[:, sb, hd, :], Ab[ab][:, c0:512], sb == 4 * g + 3, False, [r_vpad, r_A[ab]], [r_PB[ob]])
                if sb == 0:
                    if hd == 0:
                        P.op("dve", lambda e: e.reciprocal(out=rinv[0:64, :], in_=PB[ob][64:128, :]), [r_PB[ob]], [r_rinv])
                        tt("dve", bT[0:64, hp, g * 512:(g + 1) * 512], PB[ob][0:64, :], rinv[0:64, :], ALU.mult, [r_PB[ob], r_rinv], [r_R2])
                    else:
                        P.op("dve", lambda e: e.reciprocal(out=rinv[64:128, :], in_=PB[ob][0:64, :]), [r_PB[ob]], [r_rinv])
                        tt("dve", bT[64:128, hp, g * 512:(g + 1) * 512], PB[ob][64:128, :], rinv[64:128, :], ALU.mult, [r_PB[ob], r_rinv], [r_R2])

            pipeline(n, [s_z, s_a, s_pv], forward=True)

        def ffn(l, after_tile=None):
            prenorm(ffn_pre_g[l], aT, r_R1)
            dma(Gpost[:], ffn_post_g[l].partition_broadcast(128), r_Gpost, [], [r_Gpost])
            P.alias([r_R2], r_gT + r_wup)
            P.alias(r_attn_all, r_wdn)
            P.alias(r_e32, r_hh)
            LAG, PRE = 2, NWD - 2
            for tg in range(NG):
                def load_up(c):
                    dma(wupb[c % 2][:], ws_up[l, c].rearrange("p (g k j) -> p g k j", g=2, k=8), r_wup[c % 2], r_scr, [r_wup[c % 2]])
                steps = [(half, c) for half in range(2) for c in range(NCH)]
                loaded = [0]
                if tg % 2 == 0:
                    upb, hb0, hb1 = (4, 5), (0, 1, 2, 3), (4, 5, 6, 7)
                else:
                    upb, hb0, hb1 = (0, 1), (4, 5, 6, 7), (0, 1, 2, 3)

                def ensure_loaded(k):
                    while loaded[0] <= min(k, len(steps) - 1):
                        i = loaded[0] % NWD
                        dma(wdnb[i], ws_dn[l, steps[loaded[0]][1]], r_wdn[i], r_scr, [r_wdn[i]])
                        loaded[0] += 1

                def down_step(k):
                    ensure_loaded(k + PRE)
                    half, c = steps[k]
                    i = k % NWD
                    dbanks = hb0 if half == 0 else hb1
                    for t2 in range(2):
                        tl = 2 * half + t2
                        for nn in range(2):
                            bk = dbanks[2 * t2 + nn]
                            mm(PBx[bk], gT[:, c, tl * 128:(tl + 1) * 128], wdnb[i][:, nn * 512:(nn + 1) * 512], c == 0, c == NCH - 1,
                               [r_gT[c], r_wdn[i]], [r_PBx[bk]])
                    if c == NCH - 1:
                        for t2 in range(2):
                            postnorm_residual(tg * 4 + 2 * half + t2, dbanks[2 * t2], dbanks[2 * t2 + 1], r_Gpost)
                            if after_tile is not None:
                                after_tile(tg * 4 + 2 * half + t2)

                def gate_mul(cc):
                    cg, cu = 2 * (cc % 2), 2 * (cc % 2) + 1
                    act(gl[:], cacc[cg], AF.Gelu_apprx_tanh, [r_cacc[cg]], [r_gl])
                    tt("pool", gT[:, cc, :], gl[:], cacc[cu], ALU.mult, [r_gl, r_cacc[cu]], [r_gT[cc]])

                if tg == 0:
                    load_up(0)
                ensure_loaded(PRE - 1)
                kdown = 0
                for c in range(NCH):
                    if c + 1 < NCH:
                        load_up(c + 1)
                    for gu in range(2):
                        bk = upb[gu]
                        for kc in range(8):
                            mm(PB[bk][:], wupb[c % 2][:, gu, kc, :], aT[:, kc, tg * 512:(tg + 1) * 512], kc == 0, kc == 7,
                               [r_wup[c % 2], r_R1], [r_PB[bk]])
                    if c >= LAG:
                        down_step(kdown)
                        kdown += 1
                    for gu in range(2):
                        bk = upb[gu]
                        ci = gu * NCH + c
                        hb = 2 * (c % 2) + gu
                        hr, rh = hraw[hb], r_hraw[hb]
                        ca, rca = cacc[hb], r_cacc[hb]
                        if tg == 0:
                            memset("pool", hr[:, 2:4], 0.0, [r_hh[hb]])
                        else:
                            tcopy("pool", hr[:, 2:4], halo[:, ci, :], [r_haloc[ci]], [r_hh[hb]])
                        act(hr[:, 4:516], PB[bk][:], AF.Copy, [r_PB[bk]], [rh])
                        tcopy("pool", halo[:, ci, :], hr[:, 514:516], [rh], [r_haloc[ci]])
                        act(ca, PB[bk][:], AF.Identity, [r_PB[bk], r_CW[l]], [rca],
                            scale=CW[l][:, 2, ci:ci + 1], bias=CW[l][:, 3, ci:ci + 1])
                    for tap in (1, 0):
                        for gu in range(2):
                            ci = gu * NCH + c
                            hb = 2 * (c % 2) + gu
                            hr, rh = hraw[hb], r_hraw[hb]
                            ca, rca = cacc[hb], r_cacc[hb]
                            stt("dve", ca, hr[:, 2 + tap:514 + tap], CW[l][:, tap, ci:ci + 1], ca, ALU.mult, ALU.add,
                                [rh, r_hh[hb], r_CW[l], rca], [rca])
                    if c > 0:
                        gate_mul(c - 1)
                    if c == NCH - 1:
                        gate_mul(c)
                if tg + 1 < NG:
                    load_up(0)
                while kdown < len(steps):
                    down_step(kdown)
                    kdown += 1
            P.alias(r_gT + r_wup, [r_R2])
            P.alias(r_wdn, r_attn_all)
            P.alias(r_hh, r_e32)

        for t in range(NT):
            dma(hview[:, t, :], x[0, t * 128:(t + 1) * 128, :], r_h[t], [], [r_h[t]])
        for b in range(NB):
            prenorm(sb_pre_g[0], aT, r_R1)
            for hd in range(2):
                memset("pool", qa[hd][64:128, :], 0.0, [r_qa[hd]])
                memset("pool", ka[hd][64:128, :], 0.0, [r_ka[hd]])
            load_pair_weights(ws_qkv[0], ws_qkv[1], ws_qkv[2], 0)
            for hp in range(8):

                def sink_q(tg, bank):
                    for hd in range(2):
                        ts("dve", qa[hd][0:64, tg * 512:(tg + 1) * 512], PB[bank][hd * 64:(hd + 1) * 64, :], 0.125, None, ALU.mult, None,
                           [r_PB[bank]], [r_qa[hd]])

                def sink_k(tg, bank):
                    for hd in range(2):
                        tcopy("dve", ka[hd][0:64, tg * 512:(tg + 1) * 512], PB[bank][hd * 64:(hd + 1) * 64, :], [r_PB[bank]], [r_ka[hd]])
                proj_featmajor(Wq, r_Wq, aT, r_R1, 6, sink_q)
                proj_featmajor(Wk, r_Wk, aT, r_R1, 6, sink_k)
                proj_v(aT, r_R1, 6, 0.0)
                if hp + 1 < 8:
                    load_pair_weights(ws_qkv[0], ws_qkv[1], ws_qkv[2], hp + 1)
                attn_sb_pair(hp, 4)
            out_proj(0, sb_post_g[0], bT, r_R2)
            ffn(0)
            prenorm(fox_pre_g[0], aT, r_R1)
            for hp in range(8):
                dma(Wq[:], ws_qkv[3][hp].rearrange("p (k j) -> p k j", j=128), r_Wq, r_scr, [r_Wq])

                def sink_qall(tg, bank, hp=hp):
                    ts("dve", bT[:, hp, tg * 512:(tg + 1) * 512], PB[bank][:], 0.125, None, ALU.mult, None, [r_PB[bank]], [r_R2])
                proj_featmajor(Wq, r_Wq, aT, r_R1, 6, sink_qall)
            prenorm(kv_norm_g, aT, r_R1)
            dma(Wf[:], ws_f.rearrange("p (k j) -> p k j", j=16), r_Wf, r_scr, [r_Wf])
            memset("pool", E2[1][0:16, 0:512], 1.0, [r_E2[1]])
            for tg in range(NG):
                seg = slice(tg * 512, (tg + 1) * 512)
                for kc in range(8):
                    mm(PB[6][0:16, :], Wf[:, kc, :], aT[:, kc, seg], kc == 0, kc == 7, [r_Wf, r_R1], [r_PB[6]])
                act(e32[0][0:16, 0:512], PB[6][0:16, :], AF.Exp, [r_PB[6], r_bf], [r_e32[0]], scale=-1.0, bias=bfneg[:, 0:1])
                act(e32[1][0:16, 0:512], e32[0][0:16, 0:512], AF.Ln, [r_e32[0]], [r_e32[1]], bias=1.0)
                if tg == 0:
                    P.op("dve", lambda e: e.tensor_tensor_scan(out=E2[0][0:16, 0:512], data0=E2[1][0:16, 0:512], data1=e32[1][0:16, 0:512],
                                                               initial=0.0, op0=ALU.mult, op1=ALU.add),
                         [r_E2[1], r_e32[1]], [r_E2[0]])
                else:
                    P.op("dve", lambda e: e.tensor_tensor_scan(out=E2[0][0:16, 0:512], data0=E2[1][0:16, 0:512], data1=e32[1][0:16, 0:512],
                                                               initial=carry[:, 0:1], op0=ALU.mult, op1=ALU.add),
                         [r_E2[1], r_e32[1], r_carry], [r_E2[0]])
                tcopy("dve", carry[:, 0:1], E2[0][0:16, 511:512], [r_E2[0]], [r_carry])
                tcopy("dve", frow[0][:, seg], E2[0][0:16, 0:512], [r_E2[0]], [r_frow])
                tt("dve", Ls32[0][0:16, :], E2[0][0:16, 0:512], frow[0][:, seg], ALU.subtract, [r_E2[0], r_frow], [r_Ls32[0]])
                tcopy("dve", frow[1][:, seg], Ls32[0][0:16, :], [r_Ls32[0]], [r_frow])
                ts("dve", frow[2][:, seg], E2[0][0:16, 0:512], -1.0, None, ALU.mult, None, [r_E2[0]], [r_frow])
                ts("dve", frow[3][:, seg], Ls32[0][0:16, :], -1.0, None, ALU.mult, None, [r_Ls32[0]], [r_frow])
            for hd in range(2):
                memset("pool", qa[hd][64:128, :], 0.0, [r_qa[hd]])
                memset("pool", ka[hd][64:128, :], 0.0, [r_ka[hd]])
                memset("pool", qa[hd][64:68, :], 1.0, [r_qa[hd]])
                memset("pool", ka[hd][64:68, :], 1.0, [r_ka[hd]])
            load_pair_weights(None, ws_kv[0], ws_kv[1], 0)
            for hp in range(8):
                for hd in range(2):
                    hh = 2 * hp + hd
                    if hd == 0:
                        act(qa[hd][0:64, :], bT[0:64, hp, :], AF.Copy, [r_R2], [r_qa[hd]])
                    else:
                        tcopy("dve", qa[hd][0:64, :], bT[64:128, hp, :], [r_R2], [r_qa[hd]])
                    dma(qa[hd][64:65, :], frow[2][hh:hh + 1, :], r_qrow[hd][0], [r_frow, r_qa[hd]], [r_qrow[hd][0]])
                    dma(qa[hd][65:66, :], frow[3][hh:hh + 1, :], r_qrow[hd][1], [r_frow, r_qa[hd]], [r_qrow[hd][1]])
                    dma(ka[hd][66:67, :], frow[0][hh:hh + 1, :], r_krow[hd][0], [r_frow, r_ka[hd]], [r_krow[hd][0]])
                    dma(ka[hd][67:68, :], frow[1][hh:hh + 1, :], r_krow[hd][1], [r_frow, r_ka[hd]], [r_krow[hd][1]])

                def sink_k1(tg, bank):
                    for hd in range(2):
                        tcopy("dve", ka[hd][0:64, tg * 512:(tg + 1) * 512], PB[bank][hd * 64:(hd + 1) * 64, :], [r_PB[bank]], [r_ka[hd]])
                proj_featmajor(Wk, r_Wk, aT, r_R1, 6, sink_k1)
                proj_v(aT, r_R1, 6, 1.0)
                if hp + 1 < 8:
                    load_pair_weights(None, ws_kv[0], ws_kv[1], hp + 1)
                attn_fox_pair(hp)
            out_proj(1, fox_post_g[0], bT, r_R2)
            def stream_io(t, b=b):
                dma(y[b, t * 128:(t + 1) * 128, :], hview[:, t, :], r_h[t], [r_h[t]], [])
                if b + 1 < NB:
                    dma(hview[:, t, :], x[b + 1, t * 128:(t + 1) * 128, :], r_h[t], [], [r_h[t]])
            ffn(1, after_tile=stream_io)
        P.wait_all("sp", r_h)
        P.emit()
    return nc


_NC_CACHE = {}


def kernel(**inputs):
    x = np.ascontiguousarray(inputs["x"], dtype=np.float32)
    B, S, _ = x.shape
    NB = B // N_CORES
    key = (NB, S)
    if key not in _NC_CACHE:
        _NC_CACHE[key] = build_nc(NB, S)
    nc = _NC_CACHE[key]
    wnames = ["sb_pre_g", "sb_w_qkv", "sb_w_o", "sb_post_g", "kv_norm_g", "w_kvf", "b_f", "fox_pre_g", "fox_w_q",
              "fox_w_o", "fox_post_g", "ffn_pre_g", "w_up", "conv_w", "conv_b", "w_down", "ffn_post_g"]
    ws = {k: np.ascontiguousarray(inputs[k], dtype=np.float32) for k in wnames}
    in_maps = []
    for c in range(N_CORES):
        m = {"x": x[c * NB:(c + 1) * NB]}
        m.update(ws)
        in_maps.append(m)
    res = run_bass_kernel_spmd(nc, in_maps, core_ids=list(range(N_CORES)))
    return np.concatenate([r["y"] for r in res.results], axis=0)
```

```python
from contextlib import ExitStack
import numpy as np
import concourse.bass as bass
import concourse.mybir as mybir
from concourse.bass_utils import run_bass_kernel_spmd

F32 = mybir.dt.float32
BF16 = mybir.dt.bfloat16
AF = mybir.ActivationFunctionType
ALU = mybir.AluOpType

D = 1024
H = 16
DH = 64
FF = 2816
NCH = FF // 128
EPS = 1e-6
N_CORES = 8


class Res:
    __slots__ = ("name", "lw", "rd", "sem", "semcnt")

    def __init__(self, name, sem=None):
        self.name = name
        self.lw = None
        self.rd = {}
        self.sem = sem
        self.semcnt = 0


class Prog:
    ENGS = ("pe", "act", "dve", "pool", "sp")

    def __init__(self, nc, ctx):
        self.nc = nc
        self.ctx = ctx
        self.streams = {e: [] for e in self.ENGS}
        self.count = {e: 0 for e in self.ENGS}
        self.waited = {e: {} for e in self.ENGS}
        self.esem = {e: ctx.enter_context(nc.semaphore("es_" + e)) for e in self.ENGS}
        self.nres = 0

    def res(self, name=None, dma=False):
        self.nres += 1
        name = name or ("r%d" % self.nres)
        sem = self.ctx.enter_context(self.nc.semaphore("ds%d" % self.nres)) if dma else None
        return Res(name, sem)

    def _deps(self, reads, writes):
        deps = []
        for r in reads:
            if r.lw is not None:
                deps.append(r.lw)
        for w in writes:
            if w.lw is not None:
                deps.append(w.lw)
            deps.extend(w.rd.items())
        return deps

    def _waits_for(self, eng, deps):
        best = {}
        for (key, val) in deps:
            if key == "pe" and eng == "pe":
                continue
            if val > best.get(key, 0):
                best[key] = val
        out = []
        wd = self.waited[eng]
        for key, val in best.items():
            if wd.get(key, 0) >= val:
                continue
            wd[key] = val
            out.append((key, val))
        return out

    def _sem_of(self, key):
        if isinstance(key, str):
            return self.esem[key]
        return key.sem

    def _record(self, ev, reads, writes):
        k, v = ev
        for r in reads:
            if r.rd.get(k, 0) < v:
                r.rd[k] = v
        for w in writes:
            w.lw = ev
            w.rd = {}

    def op(self, eng, fn, reads=(), writes=()):
        waits = self._waits_for(eng, self._deps(reads, writes))
        self.count[eng] += 1
        ev = (eng, self.count[eng])
        self.streams[eng].append((waits, fn, None))
        self._record(ev, reads, writes)
        return ev

    def dma(self, eng, fn, semres, reads=(), writes=()):
        waits = self._waits_for(eng, self._deps(reads, writes))
        semres.semcnt += 1
        ev = (semres, 16 * semres.semcnt)
        self.streams[eng].append((waits, fn, semres))
        self._record(ev, reads, writes)
        return ev

    def alias(self, olds, news):
        evs = {}
        for o in olds:
            if o.lw is not None:
                k, v = o.lw
                evs[k] = max(evs.get(k, 0), v)
            for k, v in o.rd.items():
                evs[k] = max(evs.get(k, 0), v)
        for n in news:
            for k, v in evs.items():
                if n.rd.get(k, 0) < v:
                    n.rd[k] = v

    def wait_all(self, eng, resources):
        deps = []
        for r in resources:
            if r.lw is not None:
                deps.append(r.lw)
            deps.extend(r.rd.items())
        waits = self._waits_for(eng, deps)
        self.streams[eng].append((waits, None, None))

    def emit(self):
        nc = self.nc
        with nc.Block() as block:
            def run(engname):
                def body(e):
                    esem = self.esem[engname]
                    for (waits, fn, semres) in self.streams[engname]:
                        for (key, val) in waits:
                            e.wait_ge(self._sem_of(key), val)
                        if fn is None:
                            continue
                        ins = fn(e)
                        if semres is None:
                            ins.then_inc(esem, 1)
                        else:
                            ins.then_inc(semres.sem, 16)
                return body
            block.tensor(run("pe"))
            block.scalar(run("act"))
            block.vector(run("dve"))
            block.gpsimd(run("pool"))
            block.sync(run("sp"))


def build_nc(NB, S):
    NT = S // 128
    NG = S // 512
    nc = bass.Bass("TRN2", target_bir_lowering=False)
    dt_in = lambda name, shape: nc.dram_tensor(name, list(shape), F32, kind="ExternalInput").ap()
    x = dt_in("x", [NB, S, D])
    sb_pre_g = dt_in("sb_pre_g", [1, D])
    sb_w_qkv = dt_in("sb_w_qkv", [1, D, 3 * D])
    sb_w_o = dt_in("sb_w_o", [1, D, D])
    sb_post_g = dt_in("sb_post_g", [1, D])
    kv_norm_g = dt_in("kv_norm_g", [D])
    w_kvf = dt_in("w_kvf", [D, 2 * D + H])
    b_f = dt_in("b_f", [H])
    fox_pre_g = dt_in("fox_pre_g", [1, D])
    fox_w_q = dt_in("fox_w_q", [1, D, D])
    fox_w_o = dt_in("fox_w_o", [1, D, D])
    fox_post_g = dt_in("fox_post_g", [1, D])
    ffn_pre_g = dt_in("ffn_pre_g", [2, D])
    w_up = dt_in("w_up", [2, D, 2 * FF])
    conv_w = dt_in("conv_w", [2, 3, 2 * FF])
    conv_b = dt_in("conv_b", [2, 2 * FF])
    w_down = dt_in("w_down", [2, FF, D])
    ffn_post_g = dt_in("ffn_post_g", [2, D])
    y = nc.dram_tensor("y", [NB, S, D], F32, kind="ExternalOutput").ap()

    ws_qkv = nc.dram_tensor("ws_qkv", [5, 8, 128, 1024], BF16).ap()
    ws_kv = nc.dram_tensor("ws_kv", [2, 8, 128, 1024], BF16).ap()
    ws_f = nc.dram_tensor("ws_f", [128, 8 * 16], BF16).ap()
    ws_o = nc.dram_tensor("ws_o", [2, 128, 8192], BF16).ap()
    ws_up = nc.dram_tensor("ws_up", [2, NCH, 128, 2048], BF16).ap()
    ws_dn = nc.dram_tensor("ws_dn", [2, NCH, 128, 1024], BF16).ap()

    with ExitStack() as ctx:
        P = Prog(nc, ctx)
        sbt = lambda name, shape, dt: ctx.enter_context(nc.sbuf_tensor(name, list(shape), dt))
        pst = lambda name, shape, dt: ctx.enter_context(nc.psum_tensor(name, list(shape), dt))

        Hbuf = sbt("Hbuf", [128, NT * D], F32)
        hview = Hbuf[:].rearrange("p (t d) -> p t d", d=D)
        r_h = [P.res("h%d" % t, dma=True) for t in range(NT)]
        R1N = max(8 * S, 16384)
        R2N = max(8 * S, 16384)
        R3N = max(4 * S + NT * 256, 8192)
        R1 = sbt("R1", [128, R1N], BF16)
        R2 = sbt("R2", [128, R2N], BF16)
        R3 = sbt("R3", [128, R3N], BF16)
        r_R1 = P.res("R1")
        r_R2 = P.res("R2")
        aT = R1[:, 0:8 * S].rearrange("p (k s) -> p k s", s=S)
        bT = R2[:, 0:8 * S].rearrange("p (k s) -> p k s", s=S)
        qa = [R3[:, 0:S], R3[:, S:2 * S]]
        ka = [R3[:, 2 * S:3 * S], R3[:, 3 * S:4 * S]]
        vpad = R3[:, 4 * S:4 * S + NT * 256].rearrange("p (t h c) -> p t h c", h=2, c=128)
        r_qa = [P.res("qa0", dma=True), P.res("qa1", dma=True)]
        r_ka = [P.res("ka0", dma=True), P.res("ka1", dma=True)]
        r_vpad = P.res("vpad")
        r_qrow = [[P.res("qrow%d%d" % (i, j), dma=True) for j in range(2)] for i in range(2)]
        r_krow = [[P.res("krow%d%d" % (i, j), dma=True) for j in range(2)] for i in range(2)]
        r_attn_all = r_qa + r_ka + [r_vpad] + r_qrow[0] + r_qrow[1] + r_krow[0] + r_krow[1]
        WoT = R3[:, 0:8192].rearrange("p (k n) -> p k n", n=D)
        r_Wo = P.res("Wo", dma=True)
        gT = R2[:, 0:NCH * 512].rearrange("p (c t) -> p c t", t=512)
        r_gT = [P.res("gT%d" % c) for c in range(NCH)]
        wupb = [R2[:, NCH * 512 + i * 2048: NCH * 512 + (i + 1) * 2048].rearrange("p (g k j) -> p g k j", g=2, k=8) for i in range(2)]
        r_wup = [P.res("wup%d" % i, dma=True) for i in range(2)]
        NWD = min(12, R3N // 1024)
        wdnb = [R3[:, i * 1024:(i + 1) * 1024] for i in range(NWD)]
        r_wdn = [P.res("wdn%d" % i, dma=True) for i in range(NWD)]

        Wq = sbt("Wq", [128, 8, 128], BF16); r_Wq = P.res("Wq", dma=True)
        Wk = sbt("Wk", [128, 8, 128], BF16); r_Wk = P.res("Wk", dma=True)
        Wv = sbt("Wv", [128, 8, 128], BF16); r_Wv = P.res("Wv", dma=True)
        Wf = sbt("Wf", [128, 8, 16], BF16); r_Wf = P.res("Wf", dma=True)
        e32 = [sbt("e32_%d" % i, [128, 516], F32) for i in range(4)]; r_e32 = [P.res() for _ in range(4)]
        spb = [sbt("spb_%d" % i, [128, 512], BF16) for i in range(4)]; r_sp = [P.res() for _ in range(4)]
        E2 = [sbt("E2_%d" % i, [128, 512], F32) for i in range(2)]; r_E2 = [P.res() for _ in range(2)]
        Ab = [sbt("Ab_%d" % i, [128, 512], BF16) for i in range(3)]; r_A = [P.res() for _ in range(3)]
        Ls32 = [sbt("Ls32_%d" % i, [128, 512], F32) for i in range(1)]; r_Ls32 = [P.res() for _ in range(1)]
        rinv = sbt("rinv", [128, 512], F32); r_rinv = P.res()
        junk = sbt("junk", [128, 1024], BF16); r_junk = P.res()
        xn = sbt("xn", [128, 1024], BF16); r_xn = P.res()
        tmpf = sbt("tmpf", [128, 1024], F32); r_tmpA = P.res(); r_tmpB = P.res()
        Gpre = sbt("Gpre", [128, 1024], F32); r_Gpre = P.res("Gpre", dma=True)
        Gpost = Gpre; r_Gpost = r_Gpre
        st = sbt("stats", [128, 8], F32); r_st = P.res()
        pst_ = sbt("pstats", [128, 3 * 16], F32); r_pst = P.res()
        frowA = sbt("frowA", [80, S], BF16)
        frowB = sbt("frowB", [16, S], BF16)
        frow = [frowA[0:16, :], frowA[32:48, :], frowA[64:80, :], frowB[0:16, :]]
        r_frow = P.res()
        carry = sbt("carry", [16, 1], F32); r_carry = P.res()
        bfneg = sbt("bfneg", [16, 1], F32); r_bf = P.res("bf", dma=True)
        CW = [sbt("CW%d" % l, [128, 4, 2 * NCH], F32) for l in range(2)]; r_CW = [P.res("CW%d" % l, dma=True) for l in range(2)]
        hraw = e32; r_hraw = r_e32
        cacc = [E2[0][:, :], E2[1][:, :], tmpf[:, 0:512], tmpf[:, 512:1024]]; r_cacc = [r_E2[0], r_E2[1], r_tmpA, r_tmpB]
        gl = rinv; r_gl = r_rinv
        halo = sbt("halo", [128, 2 * NCH, 2], F32); r_halo = P.res()
        r_hh = [P.res() for _ in range(4)]
        r_haloc = [P.res() for _ in range(2 * NCH)]
        ident = sbt("ident", [128, 128], BF16); r_ident = P.res()
        mstrict = sbt("mstrict", [128, 128], BF16); r_mstrict = P.res()
        mincl = sbt("mincl", [128, 128], BF16); r_mincl = P.res()
        UI = sbt("UI", [128, 128], BF16); r_UI = P.res()
        Lc = sbt("Lc", [128, 128], BF16); r_Lc = P.res()
        PB = [pst("PB%d" % i, [128, 512], F32) for i in range(7)]; r_PB = [P.res("PB%d" % i) for i in range(7)]
        PT = pst("PT", [128, 1024], BF16); r_PT = P.res("PT")
        PBx = [PB[i][:] for i in range(7)] + [PT[:].bitcast(F32)]
        r_PBx = r_PB + [r_PT]

        def mm(out, lhsT, rhs, start, stop, reads, writes):
            P.op("pe", lambda e: e.matmul(out, lhsT=lhsT, rhs=rhs, start=start, stop=stop, skip_group_check=True), reads, writes)

        def act(out, in_, func, reads, writes, **kw):
            P.op("act", lambda e: e.activation(out=out, in_=in_, func=func, **kw), reads, writes)

        def tcopy(eng, out, in_, reads, writes):
            P.op(eng, lambda e: e.tensor_copy(out=out, in_=in_), reads, writes)

        def tt(eng, out, in0, in1, op, reads, writes):
            P.op(eng, lambda e: e.tensor_tensor(out=out, in0=in0, in1=in1, op=op), reads, writes)

        def ts(eng, out, in0, s1, s2, op0, op1, reads, writes):
            if s2 is None:
                P.op(eng, lambda e: e.tensor_scalar(out=out, in0=in0, scalar1=s1, scalar2=None, op0=op0), reads, writes)
            else:
                P.op(eng, lambda e: e.tensor_scalar(out=out, in0=in0, scalar1=s1, scalar2=s2, op0=op0, op1=op1), reads, writes)

        def stt(eng, out, in0, scalar, in1, op0, op1, reads, writes):
            P.op(eng, lambda e: e.scalar_tensor_tensor(out=out, in0=in0, scalar=scalar, in1=in1, op0=op0, op1=op1), reads, writes)

        def memset(eng, ap, val, writes):
            P.op(eng, lambda e: e.memset(ap, val), (), writes)

        def dma(out, in_, semres, reads, writes, slow=False):
            if slow:
                P.dma("sp", lambda e: e.dma_start(out=out, in_=in_, allow_slow_non_contiguous=True), semres, reads, writes)
            else:
                P.dma("sp", lambda e: e.dma_start(out=out, in_=in_), semres, reads, writes)

        def aff(ap, pattern, cmp, cm, writes):
            P.op("pool", lambda e: e.affine_select(out=ap, in_=ap, pattern=pattern, compare_op=cmp, fill=0.0, base=0, channel_multiplier=cm), writes, writes)
        for (t_, r_) in ((ident, r_ident), (mstrict, r_mstrict), (mincl, r_mincl), (UI, r_UI), (Lc, r_Lc)):
            memset("pool", t_[:], 1.0, [r_])
        aff(ident[:], [[-1, 128]], ALU.is_equal, 1, [r_ident])
        aff(mstrict[:], [[1, 128]], ALU.is_gt, -1, [r_mstrict])
        aff(mincl[:], [[1, 128]], ALU.is_ge, -1, [r_mincl])
        aff(UI[:], [[-1, 128]], ALU.is_ge, 1, [r_UI])
        aff(Lc[:], [[1, 128]], ALU.is_gt, -1, [r_Lc])
        memset("pool", halo[:], 0.0, [r_halo] + r_haloc)
        for l in range(2):
            for k in range(3):
                dma(CW[l][:, k, :], conv_w[l, k].rearrange("(c p) -> p c", p=128), r_CW[l], [], [r_CW[l]], slow=True)
            dma(CW[l][:, 3, :], conv_b[l].rearrange("(c p) -> p c", p=128), r_CW[l], [], [r_CW[l]], slow=True)
        dma(bfneg[:], b_f.rearrange("(h o) -> h o", o=1), r_bf, [], [r_bf], slow=True)
        ts("dve", bfneg[:], bfneg[:], -1.0, None, ALU.mult, None, [r_bf], [r_bf])

        NSLOT = 4 if NT * D >= 4 * 4096 else 2
        if NSLOT == 4:
            stg32 = [Hbuf[:, i * 4096:(i + 1) * 4096] for i in range(4)]
        else:
            stg32 = [R1[:, i * 8192:(i + 1) * 8192].bitcast(F32) for i in range(2)]
        stg16 = [R2[:, i * 4096:(i + 1) * 4096] for i in range(NSLOT)]
        r_s32 = [P.res("s32_%d" % i, dma=True) for i in range(NSLOT)]
        r_s16 = [P.res("s16_%d" % i) for i in range(NSLOT)]
        r_s16d = [P.res("s16d_%d" % i, dma=True) for i in range(NSLOT)]
        cast_engs = ["dve", "pool", "act"]
        ucount = [0]

        def cast_unit(srcs, E, dst, outview=None):
            cast_list.append((srcs, E, dst, outview))

        cast_list = []

        def emit_casts():
            n = len(cast_list)
            LOOK = NSLOT - 1

            def emit_in(u):
                i = u % NSLOT
                for (vf, src) in cast_list[u][0]:
                    dma(vf(stg32[i]), src, r_s32[i], [], [r_s32[i]])

            def emit_cast_out(u):
                i = u % NSLOT
                srcs, E, dst, outview = cast_list[u]
                if u % 2 == 0:
                    act(stg16[i][:, 0:E], stg32[i][:, 0:E], AF.Copy, [r_s32[i]], [r_s16[i]])
                else:
                    tcopy("dve", stg16[i][:, 0:E], stg32[i][:, 0:E], [r_s32[i]], [r_s16[i]])
                src16 = stg16[i][:, 0:E] if outview is None else outview(stg16[i][:, 0:E])
                dma(dst, src16, r_s16d[i], [r_s16[i]], [r_s16d[i]])
            for u in range(n + LOOK):
                if u < n:
                    emit_in(u)
                if u - LOOK >= 0:
                    emit_cast_out(u - LOOK)

        def img4(stage, a, b, c):
            return stage[:, 0:a * b * c].rearrange("p (a b c) -> p a b c", a=a, b=b)

        def img3(stage, a, b):
            return stage[:, 0:a * b].rearrange("p (a b) -> p a b", a=a)

        def cast_cols(src2d, col0, dst_units):
            for half in range(2):
                srcs = []
                for hp4 in range(4):
                    hp = half * 4 + hp4
                    srcs.append((lambda s, hp4=hp4: img4(s, 4, 8, 128)[:, hp4],
                                 src2d[:, col0 + hp * 128: col0 + (hp + 1) * 128].rearrange("(k p) j -> p k j", p=128)))
                cast_unit(srcs, 4096, dst_units[half * 4:half * 4 + 4].rearrange("u p e -> p u e"),
                          outview=lambda v: v.rearrange("p (u e) -> p u e", u=4))
        cast_cols(sb_w_qkv[0], 0, ws_qkv[0])
        cast_cols(sb_w_qkv[0], D, ws_qkv[1])
        cast_cols(sb_w_qkv[0], 2 * D, ws_qkv[2])

        def cast_wo(src2d, dst):
            for half in range(2):
                srcs = [(lambda s: img3(s, 4, 1024),
                         src2d[half * 512:(half + 1) * 512, :].rearrange("(k p) n -> p k n", p=128))]
                cast_unit(srcs, 4096, dst[:, half * 4096:(half + 1) * 4096])
        cast_wo(sb_w_o[0], ws_o[0])

        def cast_ffn(l):
            for c0 in range(0, NCH, 2):
                srcs = []
                for gu in range(2):
                    for ci in range(2):
                        c = c0 + ci
                        srcs.append((lambda s, gu=gu, ci=ci: s[:, 0:4096].rearrange("p (c g k j) -> p c g k j", c=2, g=2, k=8)[:, ci, gu],
                                     w_up[l][:, gu * FF + c * 128: gu * FF + (c + 1) * 128].rearrange("(k p) j -> p k j", p=128)))
                cast_unit(srcs, 4096, ws_up[l, c0:c0 + 2].rearrange("c p e -> p c e"),
                          outview=lambda v: v.rearrange("p (c e) -> p c e", c=2))
            for c0 in range(0, NCH, 4):
                n = min(4, NCH - c0)
                srcs = [(lambda s, n=n: img3(s, n, 1024),
                         w_down[l][c0 * 128:(c0 + n) * 128, :].rearrange("(c p) n -> p c n", p=128))]
                cast_unit(srcs, n * 1024, ws_dn[l, c0:c0 + n].rearrange("c p e -> p c e"),
                          outview=lambda v, n=n: v.rearrange("p (c e) -> p c e", c=n))
        cast_ffn(0)
        cast_cols(w_kvf, 0, ws_kv[0])
        cast_cols(w_kvf, D, ws_kv[1])
        cast_unit([(lambda s: img3(s, 8, 16), w_kvf[:, 2 * D:2 * D + H].rearrange("(k p) j -> p k j", p=128))], 128, ws_f)
        cast_cols(fox_w_q[0], 0, ws_qkv[3])
        cast_wo(fox_w_o[0], ws_o[1])
        cast_ffn(1)
        emit_casts()
        r_scr = r_s16d
        P.alias(r_s32 + r_s16 + r_s16d, [r_R1, r_R2] + r_h)

        def prenorm(g_ap, dstT, r_dst, first_alias=None):
            dma(Gpre[:], g_ap.partition_broadcast(128), r_Gpre, [], [r_Gpre])
            for t in range(NT):
                act(junk[:], hview[:, t, :], AF.Square, [r_h[t]], [r_junk, r_pst], accum_out=pst_[:, t:t + 1])
            act(pst_[:, 16:16 + NT], pst_[:, 0:NT], AF.Ln, [r_pst], [r_pst], scale=1.0 / D, bias=EPS)
            act(pst_[:, 32:32 + NT], pst_[:, 16:16 + NT], AF.Exp, [r_pst], [r_pst], scale=-0.5)
            xns = [(xn, r_xn), (junk, r_junk)]
            pts = [(PT[:], r_PT), (PB[6][:].bitcast(BF16), r_PB[6])]
            for t in range(NT):
                xt, r_xt = xns[t % 2]
                pt, r_pt = pts[t % 2]
                stt("dve", xt[:], hview[:, t, :], pst_[:, 32 + t:33 + t], Gpre[:], ALU.mult, ALU.mult, [r_h[t], r_pst, r_Gpre], [r_xt])
                for kc in range(8):
                    P.op("pe", lambda e, kc=kc, xt=xt, pt=pt: e.transpose(pt[:, kc * 128:(kc + 1) * 128], xt[:, kc * 128:(kc + 1) * 128], ident[:]),
                         [r_xt, r_ident], [r_pt])
                tcopy("dve", dstT[:, :, t * 128:(t + 1) * 128], pt.rearrange("p (k j) -> p k j", j=128), [r_pt], [r_dst])

        def postnorm_residual(t, ba, bb, g_res):
            act(junk[:, 0:512], PBx[ba], AF.Square, [r_PBx[ba]], [r_junk, r_st], accum_out=st[:, 3:4])
            act(junk[:, 512:1024], PBx[bb], AF.Square, [r_PBx[bb]], [r_junk, r_st], accum_out=st[:, 4:5])
            tt("dve", st[:, 5:6], st[:, 3:4], st[:, 4:5], ALU.add, [r_st], [r_st])
            act(st[:, 6:7], st[:, 5:6], AF.Ln, [r_st], [r_st], scale=1.0 / D, bias=EPS)
            act(st[:, 7:8], st[:, 6:7], AF.Exp, [r_st], [r_st], scale=-0.5)
            stt("dve", tmpf[:, 0:512], PBx[ba], st[:, 7:8], Gpost[:, 0:512], ALU.mult, ALU.mult, [r_PBx[ba], r_st, g_res], [r_tmpA])
            stt("dve", tmpf[:, 512:1024], PBx[bb], st[:, 7:8], Gpost[:, 512:1024], ALU.mult, ALU.mult, [r_PBx[bb], r_st, g_res], [r_tmpB])
            tt("pool", hview[:, t, :], hview[:, t, :], tmpf[:], ALU.add, [r_h[t], r_tmpA, r_tmpB], [r_h[t]])

        def out_proj(widx, g_ap, srcT, r_src):
            P.alias(r_attn_all, [r_Wo])
            dma(WoT[:, 0:4, :], ws_o[widx][:, 0:4096].rearrange("p (k n) -> p k n", n=D), r_Wo, r_scr, [r_Wo])
            dma(WoT[:, 4:8, :], ws_o[widx][:, 4096:8192].rearrange("p (k n) -> p k n", n=D), r_Wo, r_scr, [r_Wo])
            dma(Gpost[:], g_ap.partition_broadcast(128), r_Gpost, [], [r_Gpost])
            for t in range(NT):
                ba, bb = ((0, 1), (2, 3), (4, 5))[t % 3]
                for n, bk in ((0, ba), (1, bb)):
                    for kc in range(8):
                        mm(PB[bk][:], srcT[:, kc, t * 128:(t + 1) * 128], WoT[:, kc, n * 512:(n + 1) * 512],
                           kc == 0, kc == 7, [r_src, r_Wo], [r_PB[bk]])
                postnorm_residual(t, ba, bb, r_Gpost)
            P.alias([r_Wo], r_attn_all)

        def pipeline(n, stages, forward=False):
            ns = len(stages)
            for tick in range(n + ns - 1):
                for k in (range(ns) if forward else range(ns - 1, -1, -1)):
                    i = tick - k
                    if 0 <= i < n:
                        stages[k](i)

        def load_pair_weights(qsrc, ksrc, vsrc, hp):
            if qsrc is not None:
                dma(Wq[:], qsrc[hp].rearrange("p (k j) -> p k j", j=128), r_Wq, r_scr, [r_Wq])
            dma(Wk[:], ksrc[hp].rearrange("p (k j) -> p k j", j=128), r_Wk, r_scr, [r_Wk])
            dma(Wv[:], vsrc[hp].rearrange("p (k j) -> p k j", j=128), r_Wv, r_scr, [r_Wv])

        pbank = [0]

        def next_pbank():
            pbank[0] += 1
            return (6, 0, 1)[pbank[0] % 3]

        def proj_featmajor(Wt, r_W, srcT, r_src, bank, sink):
            for tg in range(NG):
                bank = next_pbank()
                for kc in range(8):
                    mm(PB[bank][:], Wt[:, kc, :], srcT[:, kc, tg * 512:(tg + 1) * 512], kc == 0, kc == 7, [r_W, r_src], [r_PB[bank]])
                sink(tg, bank)

        def proj_v(srcT, r_src, bank, padval):
            memset("pool", vpad[:, :, 0, 64:128], padval, [r_vpad])
            memset("pool", vpad[:, :, 1, 0:64], padval, [r_vpad])
            for t4 in range(0, NT, 4):
                bank = next_pbank()
                for ti in range(4):
                    t = t4 + ti
                    for kc in range(8):
                        mm(PB[bank][:, ti * 128:(ti + 1) * 128], srcT[:, kc, t * 128:(t + 1) * 128], Wv[:, kc, :], kc == 0, kc == 7,
                           [r_src, r_Wv], [r_PB[bank]])
                pv = PB[bank][:].rearrange("p (t c) -> p t c", c=128)
                tcopy("dve", vpad[:, t4:t4 + 4, 0, 0:64], pv[:, :, 0:64], [r_PB[bank]], [r_vpad])
                tcopy("dve", vpad[:, t4:t4 + 4, 1, 64:128], pv[:, :, 64:128], [r_PB[bank]], [r_vpad])

        def attn_sb_pair(hp, obank_base):
            units = []
            for g in range(NG):
                for sb in range(4 * g + 3, -1, -1):
                    for hd in range(2):
                        units.append((g, sb, hd))
            n = len(units)

            def geom(u):
                g, sb, hd = units[u]
                c0 = max(sb * 128, g * 512) - g * 512
                return g, sb, hd, c0, (sb >= 4 * g)

            def s_z(u):
                g, sb, hd, c0, diag = geom(u)
                zb = (0, 1, 6)[u % 3]
                mm(PB[zb][:, c0:512], ka[hd][0:128, sb * 128:(sb + 1) * 128], qa[hd][0:128, g * 512 + c0:(g + 1) * 512], True, True,
                   [r_ka[hd], r_qa[hd]], [r_PB[zb]])

            def s_e(u):
                g, sb, hd, c0, diag = geom(u)
                zb = (0, 1, 6)[u % 3]
                eb = u % 4
                act(e32[eb][:, c0:512], PB[zb][:, c0:512], AF.Exp, [r_PB[zb]], [r_e32[eb]])

            def s_esp(u):
                g, sb, hd, c0, diag = geom(u)
                zb = (0, 1, 6)[u % 3]
                eb = u % 4
                act(spb[eb][:, c0:512], e32[eb][:, c0:512], AF.Ln, [r_e32[eb]], [r_sp[eb]], bias=1.0)
                if diag:
                    tt("pool", spb[eb][:, c0:c0 + 128], spb[eb][:, c0:c0 + 128], mstrict[:], ALU.mult, [r_sp[eb], r_mstrict], [r_sp[eb]])

            def s_x(u):
                g, sb, hd, c0, diag = geom(u)
                eb = u % 4
                xb = 2 + hd
                mm(PB[xb][:, c0:512], UI[:], spb[eb][:, c0:512], sb == 4 * g + 3, False, [r_UI, r_sp[eb]], [r_PB[xb]])

            def s_a(u):
                g, sb, hd, c0, diag = geom(u)
                eb = u % 4
                xb = 2 + hd
                ab = u % 3
                act(E2[hd][:, c0:512], PB[xb][:, c0:512], AF.Exp, [r_PB[xb]], [r_E2[hd]], scale=-1.0)
                if sb > 0:
                    mm(PB[xb][:, c0:512], Lc[:], spb[eb][:, c0:512], False, False, [r_Lc, r_sp[eb]], [r_PB[xb]])
                tt("dve", Ab[ab][:, c0:512], e32[eb][:, c0:512], E2[hd][:, c0:512], ALU.mult, [r_e32[eb], r_E2[hd]], [r_A[ab]])
                if diag:
                    tt("pool", Ab[ab][:, c0:c0 + 128], Ab[ab][:, c0:c0 + 128], mstrict[:], ALU.mult, [r_A[ab], r_mstrict], [r_A[ab]])

            def s_pv(u):
                g, sb, hd, c0, diag = geom(u)
                ab = u % 3
                ob = obank_base + (g % 2)
                firstm = (sb == 4 * g + 3 and hd == 0)
                mm(PB[ob][:, c0:512], vpad[:, sb, hd, :], Ab[ab][:, c0:512], firstm, False, [r_vpad, r_A[ab]], [r_PB[ob]])
                if sb == 0 and hd == 1:
                    tcopy("dve", bT[:, hp, g * 512:(g + 1) * 512], PB[ob][:], [r_PB[ob]], [r_R2])

            pipeline(n, [s_z, s_e, s_esp, s_x, s_a, s_pv])

        def attn_fox_pair(hp):
            units = []
            for g in range(NG):
                for sb in range(4 * g + 3, -1, -1):
                    for hd in range(2):
                        units.append((g, sb, hd))
            n = len(units)

            def geom(u):
                g, sb, hd = units[u]
                c0 = max(sb * 128, g * 512) - g * 512
                return g, sb, hd, c0, (sb >= 4 * g)

            def obank(g, hd):
                return (4 + hd) if g % 2 == 0 else (2 + hd)

            def s_z(u):
                g, sb, hd, c0, diag = geom(u)
                zb = (0, 1, 6)[u % 3]
                mm(PB[zb][:, c0:512], ka[hd][0:128, sb * 128:(sb + 1) * 128], qa[hd][0:128, g * 512 + c0:(g + 1) * 512], True, True,
                   [r_ka[hd], r_qa[hd]] + r_qrow[hd] + r_krow[hd], [r_PB[zb]])

            def s_a(u):
                g, sb, hd, c0, diag = geom(u)
                zb = (0, 1, 6)[u % 3]
                ab = u % 3
                act(Ab[ab][:, c0:512], PB[zb][:, c0:512], AF.Exp, [r_PB[zb]], [r_A[ab]])
                if diag:
                    tt("pool", Ab[ab][:, c0:c0 + 128], Ab[ab][:, c0:c0 + 128], mincl[:], ALU.mult, [r_A[ab], r_mincl], [r_A[ab]])

            def s_pv(u):
                g, sb, hd, c0, diag = geom(u)
                ab = u % 3
                ob = obank(g, hd)
                mm(PB[ob][:, c0:512], vpad[:, sb, hd, :], Ab[ab][:, c0:512], sb == 4 * g + 3, False, [r_vpad, r_A[ab]], [r_PB[ob]])
                if sb == 0:
                    if hd == 0:
                        P.op("dve", lambda e: e.reciprocal(out=rinv[0:64, :], in_=PB[ob][64:128, :]), [r_PB[ob]], [r_rinv])
                        tt("dve", bT[0:64, hp, g * 512:(g + 1) * 512], PB[ob][0:64, :], rinv[0:64, :], ALU.mult, [r_PB[ob], r_rinv], [r_R2])
                    else:
                        P.op("dve", lambda e: e.reciprocal(out=rinv[64:128, :], in_=PB[ob][0:64, :]), [r_PB[ob]], [r_rinv])
                        tt("dve", bT[64:128, hp, g * 512:(g + 1) * 512], PB[ob][64:128, :], rinv[64:128, :], ALU.mult, [r_PB[ob], r_rinv], [r_R2])

            pipeline(n, [s_z, s_a, s_pv], forward=True)

        def ffn(l, after_tile=None):
            prenorm(ffn_pre_g[l], aT, r_R1)
            dma(Gpost[:], ffn_post_g[l].partition_broadcast(128), r_Gpost, [], [r_Gpost])
            P.alias([r_R2], r_gT + r_wup)
            P.alias(r_attn_all, r_wdn)
            P.alias(r_e32, r_hh)
            LAG, PRE = 2, NWD - 2
            for tg in range(NG):
                def load_up(c):
                    dma(wupb[c % 2][:], ws_up[l, c].rearrange("p (g k j) -> p g k j", g=2, k=8), r_wup[c % 2], r_scr, [r_wup[c % 2]])
                steps = [(half, c) for half in range(2) for c in range(NCH)]
                loaded = [0]
                if tg % 2 == 0:
                    upb, hb0, hb1 = (4, 5), (0, 1, 2, 3), (4, 5, 6, 7)
                else:
                    upb, hb0, hb1 = (0, 1), (4, 5, 6, 7), (0, 1, 2, 3)

                def ensure_loaded(k):
                    while loaded[0] <= min(k, len(steps) - 1):
                        i = loaded[0] % NWD
                        dma(wdnb[i], ws_dn[l, steps[loaded[0]][1]], r_wdn[i], r_scr, [r_wdn[i]])
                        loaded[0] += 1

                def down_step(k):
                    ensure_loaded(k + PRE)
                    half, c = steps[k]
                    i = k % NWD
                    dbanks = hb0 if half == 0 else hb1
                    for t2 in range(2):
                        tl = 2 * half + t2
                        for nn in range(2):
                            bk = dbanks[2 * t2 + nn]
                            mm(PBx[bk], gT[:, c, tl * 128:(tl + 1) * 128], wdnb[i][:, nn * 512:(nn + 1) * 512], c == 0, c == NCH - 1,
                               [r_gT[c], r_wdn[i]], [r_PBx[bk]])
                    if c == NCH - 1:
                        for t2 in range(2):
                            postnorm_residual(tg * 4 + 2 * half + t2, dbanks[2 * t2], dbanks[2 * t2 + 1], r_Gpost)
                            if after_tile is not None:
                                after_tile(tg * 4 + 2 * half + t2)

                def gate_mul(cc):
                    cg, cu = 2 * (cc % 2), 2 * (cc % 2) + 1
                    act(gl[:], cacc[cg], AF.Gelu_apprx_tanh, [r_cacc[cg]], [r_gl])
                    tt("pool", gT[:, cc, :], gl[:], cacc[cu], ALU.mult, [r_gl, r_cacc[cu]], [r_gT[cc]])

                if tg == 0:
                    load_up(0)
                ensure_loaded(PRE - 1)
                kdown = 0
                for c in range(NCH):
                    if c + 1 < NCH:
                        load_up(c + 1)
                    for gu in range(2):
                        bk = upb[gu]
                        for kc in range(8):
                            mm(PB[bk][:], wupb[c % 2][:, gu, kc, :], aT[:, kc, tg * 512:(tg + 1) * 512], kc == 0, kc == 7,
                               [r_wup[c % 2], r_R1], [r_PB[bk]])
                    if c >= LAG:
                        down_step(kdown)
                        kdown += 1
                    for gu in range(2):
                        bk = upb[gu]
                        ci = gu * NCH + c
                        hb = 2 * (c % 2) + gu
                        hr, rh = hraw[hb], r_hraw[hb]
                        ca, rca = cacc[hb], r_cacc[hb]
                        if tg == 0:
                            memset("pool", hr[:, 2:4], 0.0, [r_hh[hb]])
                        else:
                            tcopy("pool", hr[:, 2:4], halo[:, ci, :], [r_haloc[ci]], [r_hh[hb]])
                        act(hr[:, 4:516], PB[bk][:], AF.Copy, [r_PB[bk]], [rh])
                        tcopy("pool", halo[:, ci, :], hr[:, 514:516], [rh], [r_haloc[ci]])
                        act(ca, PB[bk][:], AF.Identity, [r_PB[bk], r_CW[l]], [rca],
                            scale=CW[l][:, 2, ci:ci + 1], bias=CW[l][:, 3, ci:ci + 1])
                    for tap in (1, 0):
                        for gu in range(2):
                            ci = gu * NCH + c
                            hb = 2 * (c % 2) + gu
                            hr, rh = hraw[hb], r_hraw[hb]
                            ca, rca = cacc[hb], r_cacc[hb]
                            stt("dve", ca, hr[:, 2 + tap:514 + tap], CW[l][:, tap, ci:ci + 1], ca, ALU.mult, ALU.add,
                                [rh, r_hh[hb], r_CW[l], rca], [rca])
                    if c > 0:
                        gate_mul(c - 1)
                    if c == NCH - 1:
                        gate_mul(c)
                if tg + 1 < NG:
                    load_up(0)
                while kdown < len(steps):
                    down_step(kdown)
                    kdown += 1
            P.alias(r_gT + r_wup, [r_R2])
            P.alias(r_wdn, r_attn_all)
            P.alias(r_hh, r_e32)

        for t in range(NT):
            dma(hview[:, t, :], x[0, t * 128:(t + 1) * 128, :], r_h[t], [], [r_h[t]])
        for b in range(NB):
            prenorm(sb_pre_g[0], aT, r_R1)
            for hd in range(2):
                memset("pool", qa[hd][64:128, :], 0.0, [r_qa[hd]])
                memset("pool", ka[hd][64:128, :], 0.0, [r_ka[hd]])
            load_pair_weights(ws_qkv[0], ws_qkv[1], ws_qkv[2], 0)
            for hp in range(8):

                def sink_q(tg, bank):
                    for hd in range(2):
                        ts("dve", qa[hd][0:64, tg * 512:(tg + 1) * 512], PB[bank][hd * 64:(hd + 1) * 64, :], 0.125, None, ALU.mult, None,
                           [r_PB[bank]], [r_qa[hd]])

                def sink_k(tg, bank):
                    for hd in range(2):
                        tcopy("dve", ka[hd][0:64, tg * 512:(tg + 1) * 512], PB[bank][hd * 64:(hd + 1) * 64, :], [r_PB[bank]], [r_ka[hd]])
                proj_featmajor(Wq, r_Wq, aT, r_R1, 6, sink_q)
                proj_featmajor(Wk, r_Wk, aT, r_R1, 6, sink_k)
                proj_v(aT, r_R1, 6, 0.0)
                if hp + 1 < 8:
                    load_pair_weights(ws_qkv[0], ws_qkv[1], ws_qkv[2], hp + 1)
                attn_sb_pair(hp, 4)
            out_proj(0, sb_post_g[0], bT, r_R2)
            ffn(0)
            prenorm(fox_pre_g[0], aT, r_R1)
            for hp in range(8):
                dma(Wq[:], ws_qkv[3][hp].rearrange("p (k j) -> p k j", j=128), r_Wq, r_scr, [r_Wq])

                def sink_qall(tg, bank, hp=hp):
                    ts("dve", bT[:, hp, tg * 512:(tg + 1) * 512], PB[bank][:], 0.125, None, ALU.mult, None, [r_PB[bank]], [r_R2])
                proj_featmajor(Wq, r_Wq, aT, r_R1, 6, sink_qall)
            prenorm(kv_norm_g, aT, r_R1)
            dma(Wf[:], ws_f.rearrange("p (k j) -> p k j", j=16), r_Wf, r_scr, [r_Wf])
            memset("pool", E2[1][0:16, 0:512], 1.0, [r_E2[1]])
            for tg in range(NG):
                seg = slice(tg * 512, (tg + 1) * 512)
                for kc in range(8):
                    mm(PB[6][0:16, :], Wf[:, kc, :], aT[:, kc, seg], kc == 0, kc == 7, [r_Wf, r_R1], [r_PB[6]])
                act(e32[0][0:16, 0:512], PB[6][0:16, :], AF.Exp, [r_PB[6], r_bf], [r_e32[0]], scale=-1.0, bias=bfneg[:, 0:1])
                act(e32[1][0:16, 0:512], e32[0][0:16, 0:512], AF.Ln, [r_e32[0]], [r_e32[1]], bias=1.0)
                if tg == 0:
                    P.op("dve", lambda e: e.tensor_tensor_scan(out=E2[0][0:16, 0:512], data0=E2[1][0:16, 0:512], data1=e32[1][0:16, 0:512],
                                                               initial=0.0, op0=ALU.mult, op1=ALU.add),
                         [r_E2[1], r_e32[1]], [r_E2[0]])
                else:
                    P.op("dve", lambda e: e.tensor_tensor_scan(out=E2[0][0:16, 0:512], data0=E2[1][0:16, 0:512], data1=e32[1][0:16, 0:512],
                                                               initial=carry[:, 0:1], op0=ALU.mult, op1=ALU.add),
                         [r_E2[1], r_e32[1], r_carry], [r_E2[0]])
                tcopy("dve", carry[:, 0:1], E2[0][0:16, 511:512], [r_E2[0]], [r_carry])
                tcopy("dve", frow[0][:, seg], E2[0][0:16, 0:512], [r_E2[0]], [r_frow])
                tt("dve", Ls32[0][0:16, :], E2[0][0:16, 0:512], frow[0][:, seg], ALU.subtract, [r_E2[0], r_frow], [r_Ls32[0]])
                tcopy("dve", frow[1][:, seg], Ls32[0][0:16, :], [r_Ls32[0]], [r_frow])
                ts("dve", frow[2][:, seg], E2[0][0:16, 0:512], -1.0, None, ALU.mult, None, [r_E2[0]], [r_frow])
                ts("dve", frow[3][:, seg], Ls32[0][0:16, :], -1.0, None, ALU.mult, None, [r_Ls32[0]], [r_frow])
            for hd in range(2):
                memset("pool", qa[hd][64:128, :], 0.0, [r_qa[hd]])
                memset("pool", ka[hd][64:128, :], 0.0, [r_ka[hd]])
                memset("pool", qa[hd][64:68, :], 1.0, [r_qa[hd]])
                memset("pool", ka[hd][64:68, :], 1.0, [r_ka[hd]])
            load_pair_weights(None, ws_kv[0], ws_kv[1], 0)
            for hp in range(8):
                for hd in range(2):
                    hh = 2 * hp + hd
                    if hd == 0:
                        act(qa[hd][0:64, :], bT[0:64, hp, :], AF.Copy, [r_R2], [r_qa[hd]])
                    else:
                        tcopy("dve", qa[hd][0:64, :], bT[64:128, hp, :], [r_R2], [r_qa[hd]])
                    dma(qa[hd][64:65, :], frow[2][hh:hh + 1, :], r_qrow[hd][0], [r_frow, r_qa[hd]], [r_qrow[hd][0]])
                    dma(qa[hd][65:66, :], frow[3][hh:hh + 1, :], r_qrow[hd][1], [r_frow, r_qa[hd]], [r_qrow[hd][1]])
                    dma(ka[hd][66:67, :], frow[0][hh:hh + 1, :], r_krow[hd][0], [r_frow, r_ka[hd]], [r_krow[hd][0]])
                    dma(ka[hd][67:68, :], frow[1][hh:hh + 1, :], r_krow[hd][1], [r_frow, r_ka[hd]], [r_krow[hd][1]])

                def sink_k1(tg, bank):
                    for hd in range(2):
                        tcopy("dve", ka[hd][0:64, tg * 512:(tg + 1) * 512], PB[bank][hd * 64:(hd + 1) * 64, :], [r_PB[bank]], [r_ka[hd]])
                proj_featmajor(Wk, r_Wk, aT, r_R1, 6, sink_k1)
                proj_v(aT, r_R1, 6, 1.0)
                if hp + 1 < 8:
                    load_pair_weights(None, ws_kv[0], ws_kv[1], hp + 1)
                attn_fox_pair(hp)
            out_proj(1, fox_post_g[0], bT, r_R2)
            def stream_io(t, b=b):
                dma(y[b, t * 128:(t + 1) * 128, :], hview[:, t, :], r_h[t], [r_h[t]], [])
                if b + 1 < NB:
                    dma(hview[:, t, :], x[b + 1, t * 128:(t + 1) * 128, :], r_h[t], [], [r_h[t]])
            ffn(1, after_tile=stream_io)
        P.wait_all("sp", r_h)
        P.emit()
    return nc


_NC_CACHE = {}


def kernel(**inputs):
    x = np.ascontiguousarray(inputs["x"], dtype=np.float32)
    B, S, _ = x.shape
    NB = B // N_CORES
    key = (NB, S)
    if key not in _NC_CACHE:
        _NC_CACHE[key] = build_nc(NB, S)
    nc = _NC_CACHE[key]
    wnames = ["sb_pre_g", "sb_w_qkv", "sb_w_o", "sb_post_g", "kv_norm_g", "w_kvf", "b_f", "fox_pre_g", "fox_w_q",
              "fox_w_o", "fox_post_g", "ffn_pre_g", "w_up", "conv_w", "conv_b", "w_down", "ffn_post_g"]
    ws = {k: np.ascontiguousarray(inputs[k], dtype=np.float32) for k in wnames}
    in_maps = []
    for c in range(N_CORES):
        m = {"x": x[c * NB:(c + 1) * NB]}
        m.update(ws)
        in_maps.append(m)
    res = run_bass_kernel_spmd(nc, in_maps, core_ids=list(range(N_CORES)))
    return np.concatenate([r["y"] for r in res.results], axis=0)
```
